# Optimizing a Trainium2 kernel written in Bass

```python
import math
import jax, jax.numpy as jnp
from jax import lax
import numpy as np

D_MODEL = 1024
BATCH = 8
SEQ = 2048
DEPTH = 1
DEC_BATCH = 128
DEC_SEQ = 8
PAST_LEN = 16384
PAGE_SIZE = 128

D_INNER = D_MODEL
HEAD_DIM = 64
N_HEADS = D_INNER // HEAD_DIM
N_GROUPS = 2
HEADS_PER_GROUP = N_HEADS // N_GROUPS
D_STATE = 128
CONV_WIDTH = 4
CONV_DIM = D_INNER + 2 * N_GROUPS * D_STATE
SSD_CHUNK = 128
D_POOL = D_MODEL // 2
POOL_WINDOWS = (2, 4, 8, 16)
N_POOL_GROUPS = len(POOL_WINDOWS)
POOL_GROUP = D_POOL // N_POOL_GROUPS
POOL_HIST = max(POOL_WINDOWS) - 1
N_BRANCHES = 2
EPS = 1e-6
IN_SPLITS = (D_INNER, CONV_DIM, N_HEADS, D_POOL, D_POOL, N_BRANCHES * D_MODEL)
D_IN_PROJ = sum(IN_SPLITS)
IN_OFFSETS = tuple(int(v) for v in np.cumsum(IN_SPLITS)[:-1])

kernel_name = 'hybrid_ssd_pool_gated_decoder_step'

F32 = jnp.float32


def _rmsnorm(x, g):
    xf = x.astype(F32)
    xf = xf * lax.rsqrt(jnp.mean(xf * xf, axis=-1, keepdims=True) + EPS)
    return (xf * g.astype(F32)).astype(x.dtype)


def _causal_conv(xbc, buf, conv_w, conv_b):
    L = xbc.shape[1]
    ext = jnp.concatenate([buf.astype(xbc.dtype), xbc], axis=1)
    extf = ext.astype(F32)
    w = conv_w.astype(F32)
    out = conv_b.astype(F32)
    for k in range(CONV_WIDTH):
        out = out + extf[:, k:k + L] * w[k]
    return jax.nn.silu(out), ext[:, -(CONV_WIDTH - 1):]


def _ssd(x, dt, a, b_in, c_in, h0):
    bsz, l = x.shape[0], x.shape[1]
    q = math.gcd(l, SSD_CHUNK)
    nc = l // q
    xr = (x * dt[..., None]).reshape(bsz, nc, q, N_GROUPS, HEADS_PER_GROUP, HEAD_DIM)
    adt = (dt * a).reshape(bsz, nc, q, N_GROUPS, HEADS_PER_GROUP)
    acum = jnp.cumsum(adt, axis=2)
    bc = b_in.reshape(bsz, nc, q, N_GROUPS, D_STATE)
    cc = c_in.reshape(bsz, nc, q, N_GROUPS, D_STATE)
    causal = jnp.tril(jnp.ones((q, q), dtype=bool))[None, None, :, :, None, None]
    seg = acum[:, :, :, None] - acum[:, :, None, :]
    decay = jnp.exp(jnp.where(causal, seg, -jnp.inf))
    cb = jnp.einsum('bclgn,bcsgn->bclsg', cc, bc)
    y_diag = jnp.einsum('bclsg,bclsgr,bcsgrp->bclgrp', cb, decay, xr)
    decay_to_end = jnp.exp(acum[:, :, -1:] - acum)
    chunk_states = jnp.einsum('bcsgn,bcsgr,bcsgrp->bcgrpn', bc, decay_to_end, xr)
    chunk_decay = jnp.exp(acum[:, :, -1])

    def step(h, inp):
        st, dec = inp
        return h * dec[..., None, None] + st, h

    h0r = h0.reshape(bsz, N_GROUPS, HEADS_PER_GROUP, HEAD_DIM, D_STATE)
    h_final, h_in = lax.scan(step, h0r, (jnp.moveaxis(chunk_states, 1, 0), jnp.moveaxis(chunk_decay, 1, 0)))
    h_in = jnp.moveaxis(h_in, 0, 1)
    y_off = jnp.einsum('bclgn,bcgrpn,bclgr->bclgrp', cc, h_in, jnp.exp(acum))
    y = (y_diag + y_off).reshape(bsz, l, N_HEADS, HEAD_DIM)
    return y, h_final.reshape(bsz, N_HEADS, HEAD_DIM, D_STATE)


def _mamba_branch(z, xbc, dt_raw, conv_buf, ssm_state, conv_w, conv_b, dt_bias, a_log, d_skip, ssm_norm):
    bsz, L, _ = z.shape
    xbc_act, new_conv = _causal_conv(xbc, conv_buf, conv_w, conv_b)
    xs = xbc_act[..., :D_INNER].reshape(bsz, L, N_HEADS, HEAD_DIM)
    bm = xbc_act[..., D_INNER:D_INNER + N_GROUPS * D_STATE].reshape(bsz, L, N_GROUPS, D_STATE)
    cm = xbc_act[..., D_INNER + N_GROUPS * D_STATE:].reshape(bsz, L, N_GROUPS, D_STATE)
    dt = jax.nn.softplus(dt_raw.astype(F32) + dt_bias.astype(F32))
    a = -jnp.exp(a_log.astype(F32))
    y, new_ssm = _ssd(xs, dt, a, bm, cm, ssm_state.astype(F32))
    y = y + d_skip.astype(F32)[:, None] * xs
    y = y.reshape(bsz, L, D_INNER) * jax.nn.silu(z.astype(F32))
    yg = y.reshape(bsz, L, N_GROUPS, D_INNER // N_GROUPS)
    yg = yg * lax.rsqrt(jnp.mean(yg * yg, axis=-1, keepdims=True) + EPS)
    y = yg.reshape(bsz, L, D_INNER) * ssm_norm.astype(F32)
    return y, new_conv, new_ssm


def _pool_branch(u, g, buf, pos0, pool_mix, pool_scale):
    bsz, L, _ = u.shape
    ext = jnp.concatenate([buf.astype(u.dtype), u], axis=1)
    extf = ext.astype(F32)
    csum = jnp.concatenate([jnp.zeros((bsz, 1, D_POOL), F32), jnp.cumsum(extf, axis=1)], axis=1)
    hi = csum[:, POOL_HIST + 1:]
    pos = jnp.arange(L, dtype=jnp.int32) + pos0
    means = []
    for k, w in enumerate(POOL_WINDOWS):
        sl = slice(k * POOL_GROUP, (k + 1) * POOL_GROUP)
        lo = csum[:, POOL_HIST + 1 - w:POOL_HIST + 1 - w + L, sl]
        cnt = jnp.minimum(pos + 1, w).astype(F32)
        means.append((hi[..., sl] - lo) / cnt[None, :, None])
    tok = extf[:, POOL_HIST:].reshape(bsz, L, N_POOL_GROUPS, POOL_GROUP)
    pooled = jnp.stack(means, axis=2) - tok
    mixed = jnp.einsum('blkc,kcd->blkd', pooled, pool_mix.astype(F32)).reshape(bsz, L, D_POOL)
    out = mixed * pool_scale.astype(F32) * jax.nn.silu(g.astype(F32))
    return out, ext[:, -POOL_HIST:]


def _layer(x, conv_buf, ssm_state, pool_buf, pos0, norm_pre, w_in, conv_w, conv_b, dt_bias, a_log,
           d_skip, ssm_norm, w_branch_a, pool_mix, pool_scale, w_branch_b, w_out, norm_post):
    h = _rmsnorm(x, norm_pre)
    proj = jnp.einsum('bld,de->ble', h, w_in)
    z, xbc, dt_raw, u, g, gates = jnp.split(proj, IN_OFFSETS, axis=-1)
    ya, new_conv, new_ssm = _mamba_branch(z, xbc, dt_raw, conv_buf, ssm_state, conv_w, conv_b,
                                          dt_bias, a_log, d_skip, ssm_norm)
    yb, new_pool = _pool_branch(u, g, pool_buf, pos0, pool_mix, pool_scale)
    pa = jnp.einsum('ble,ed->bld', ya.astype(x.dtype), w_branch_a)
    pb = jnp.einsum('ble,ed->bld', yb.astype(x.dtype), w_branch_b)
    gate_a, gate_b = jnp.split(gates, N_BRANCHES, axis=-1)
    merged = jax.nn.sigmoid(gate_a) * pa + jax.nn.sigmoid(gate_b) * pb
    out = jnp.einsum('bld,de->ble', merged, w_out)
    y = x + _rmsnorm(out, norm_post)
    return y, new_conv, new_ssm.astype(x.dtype), new_pool


def setup_inputs(seed: int = 0) -> dict:
    key = jax.random.key(seed)
    ks = jax.random.split(key, 24)
    nrm = lambda k, shp, s: s * jax.random.normal(k, shp, F32)
    dt0 = jnp.exp(jax.random.uniform(ks[9], (DEPTH, N_HEADS), F32, math.log(1e-3), math.log(1e-1)))
    return {
        'x_prompt': nrm(ks[0], (BATCH, SEQ, D_MODEL), 1.0),
        'x_sample': nrm(ks[1], (DEC_BATCH, DEC_SEQ, D_MODEL), 1.0),
        'state_conv': nrm(ks[2], (DEPTH, DEC_BATCH, CONV_WIDTH - 1, CONV_DIM), 1.0),
        'state_ssm': nrm(ks[3], (DEPTH, DEC_BATCH, N_HEADS, HEAD_DIM, D_STATE), 0.1),
        'state_pool': nrm(ks[4], (DEPTH, DEC_BATCH, POOL_HIST, D_POOL), 1.0),
        'norm_pre': 1.0 + nrm(ks[5], (DEPTH, D_MODEL), 0.02),
        'w_in': nrm(ks[6], (DEPTH, D_MODEL, D_IN_PROJ), D_MODEL ** -0.5),
        'conv_w': nrm(ks[7], (DEPTH, CONV_WIDTH, CONV_DIM), CONV_WIDTH ** -0.5),
        'conv_b': nrm(ks[8], (DEPTH, CONV_DIM), 0.02),
        'dt_bias': dt0 + jnp.log(-jnp.expm1(-dt0)),
        'a_log': jnp.log(jax.random.uniform(ks[10], (DEPTH, N_HEADS), F32, 1.0, 16.0)),
        'd_skip': 1.0 + nrm(ks[11], (DEPTH, N_HEADS), 0.02),
        'ssm_norm': 1.0 + nrm(ks[12], (DEPTH, D_INNER), 0.02),
        'w_branch_a': nrm(ks[13], (DEPTH, D_INNER, D_MODEL), D_INNER ** -0.5),
        'pool_mix': nrm(ks[14], (DEPTH, N_POOL_GROUPS, POOL_GROUP, POOL_GROUP), POOL_GROUP ** -0.5),
        'pool_scale': 1.0 + nrm(ks[15], (DEPTH, D_POOL), 0.02),
        'w_branch_b': nrm(ks[16], (DEPTH, D_POOL, D_MODEL), D_POOL ** -0.5),
        'w_out': nrm(ks[17], (DEPTH, D_MODEL, D_MODEL), D_MODEL ** -0.5),
        'norm_post': 1.0 + nrm(ks[18], (DEPTH, D_MODEL), 0.02),
    }


def reference(x_prompt, x_sample, state_conv, state_ssm, state_pool, norm_pre, w_in, conv_w, conv_b,
              dt_bias, a_log, d_skip, ssm_norm, w_branch_a, pool_mix, pool_scale, w_branch_b, w_out,
              norm_post):
    yp, ys = x_prompt, x_sample
    conv_p, ssm_p, pool_p, conv_s, ssm_s, pool_s = [], [], [], [], [], []
    for i in range(DEPTH):
        params = (norm_pre[i], w_in[i], conv_w[i], conv_b[i], dt_bias[i], a_log[i], d_skip[i], ssm_norm[i],
                  w_branch_a[i], pool_mix[i], pool_scale[i], w_branch_b[i], w_out[i], norm_post[i])
        dt_ = x_prompt.dtype
        zc = jnp.zeros((BATCH, CONV_WIDTH - 1, CONV_DIM), dt_)
        zs = jnp.zeros((BATCH, N_HEADS, HEAD_DIM, D_STATE), dt_)
        zq = jnp.zeros((BATCH, POOL_HIST, D_POOL), dt_)
        yp, c1, s1, q1 = _layer(yp, zc, zs, zq, 0, *params)
        ys, c2, s2, q2 = _layer(ys, state_conv[i], state_ssm[i], state_pool[i], PAST_LEN, *params)
        conv_p.append(c1); ssm_p.append(s1); pool_p.append(q1)
        conv_s.append(c2); ssm_s.append(s2); pool_s.append(q2)
    return (yp, ys, jnp.stack(conv_p), jnp.stack(ssm_p), jnp.stack(pool_p),
            jnp.stack(conv_s), jnp.stack(ssm_s), jnp.stack(pool_s))
```

```python
import os
import numpy as np
from contextlib import ExitStack
import concourse.bass as bass
import concourse.mybir as mybir
from concourse.bass_utils import run_bass_kernel_spmd

F32 = mybir.dt.float32
BF16 = mybir.dt.bfloat16
AF = mybir.ActivationFunctionType
ALU = mybir.AluOpType

NCHUNK = 17
XLAT = float(os.environ.get("XLAT", 0.9))
SAME_ENG_ALL = int(os.environ.get("SAME_ENG_ALL", 1))
EPS = 1e-6


class Tile:
    __slots__ = ("name", "writers", "readers", "sem", "cnt", "collector", "psum")

    def __init__(self, name, collector=False, psum=False):
        self.psum = psum
        self.name = name
        self.writers = []
        self.readers = []
        self.sem = None
        self.cnt = 0
        self.collector = collector


class Op:
    __slots__ = ("eng", "fn", "reads", "writes", "acc", "dma", "depc", "depdma",
                 "need_inc", "ticket", "dticket", "seq", "dsem", "t_end")


class Sched:
    ENGS = ("pe", "act", "dve", "pool", "sp")
    DMA_SEM_MAX = 224
    ROT = int(os.environ.get("ROT", 1000))

    def __init__(self):
        self.ops = []
        self.per = {e: [] for e in self.ENGS}
        self.dma_keys = []
        self.cur = None
        self.eng_t = {e: 0.0 for e in self.ENGS}
        self.act_set = None

    def begin(self):
        self.cur = []
        return self.cur

    def end(self):
        st = self.cur
        self.cur = None
        return st

    def _est_start(self, a):
        eng, fn, reads, writes, acc, dma, cost, aset = a
        t = self.eng_t[eng]
        if aset is not None and aset != self.act_set:
            t += 1.3
        for tl_ in reads:
            for w in tl_.writers:
                te = w.t_end + (0.0 if w.eng == eng else XLAT)
                if te > t:
                    t = te
            if tl_.psum:
                for r in tl_.readers:
                    if r.eng != eng and r.t_end > t:
                        t = r.t_end
        for tl_ in writes:
            if tl_.collector:
                continue
            for w in tl_.writers:
                te = w.t_end + (0.0 if w.eng == eng else XLAT)
                if te > t:
                    t = te
            for r in tl_.readers:
                te = r.t_end + (0.0 if r.eng == eng else XLAT)
                if te > t:
                    t = te
        return t

    def run_merged(self, streams):
        streams = [st for st in streams if st]
        pos = [0] * len(streams)
        while True:
            best = -1
            bt = 1e30
            for i, st in enumerate(streams):
                if pos[i] < len(st):
                    t = self._est_start(st[pos[i]])
                    t += 0.05 * pos[i] / len(st)
                    if t < bt:
                        bt = t
                        best = i
            if best < 0:
                break
            a = streams[best][pos[best]]
            pos[best] += 1
            self.op(*a)

    def op(self, eng, fn, reads=(), writes=(), acc=False, dma=None, cost=0.3, aset=None):
        if self.cur is not None:
            self.cur.append((eng, fn, list(reads), list(writes), acc, dma, cost, aset))
            return None
        t_start = self._est_start((eng, fn, reads, writes, acc, dma, cost, aset))
        if aset is not None:
            self.act_set = aset
        o = Op()
        if dma is not None:
            self.eng_t[eng] = t_start + 0.1
            o.t_end = t_start + cost
        else:
            o.t_end = t_start + cost
            self.eng_t[eng] = o.t_end
        o.eng = eng
        o.fn = fn
        o.reads = list(reads)
        o.writes = list(writes)
        o.acc = acc
        o.dma = dma
        o.depc = []
        o.depdma = []
        o.need_inc = False
        o.ticket = 0
        o.dticket = 0
        if dma is not None:
            if dma.cnt == 0 and dma not in self.dma_keys:
                self.dma_keys.append(dma)
            o.dsem = dma.cnt // self.DMA_SEM_MAX
            dma.cnt += 16
            o.dticket = dma.cnt - o.dsem * self.DMA_SEM_MAX
        deps = {}
        for t in o.reads:
            for w in t.writers:
                deps[id(w)] = (w, True)
            if t.psum:
                for r in t.readers:
                    if r.eng != eng and id(r) not in deps:
                        deps[id(r)] = (r, False)
        for t in o.writes:
            if t.collector:
                continue
            for w in t.writers:
                if id(w) not in deps:
                    deps[id(w)] = (w, False)
            for r in t.readers:
                if id(r) not in deps:
                    deps[id(r)] = (r, None)
        for t in o.reads:
            t.readers.append(o)
        for t in o.writes:
            if t.collector or acc:
                t.writers.append(o)
            else:
                t.writers = [o]
                t.readers = []
        latest = {}
        for d, raw in deps.values():
            if d is o:
                continue
            if d.dma is not None:
                o.depdma.append(d)
                continue
            if d.eng == eng:
                if eng == "pe":
                    continue
                if raw is None and SAME_ENG_ALL < 1:
                    continue
                if raw is False and SAME_ENG_ALL < 1 and SAME_ENG_ALL > -1:
                    pass
                if raw is False and SAME_ENG_ALL < 0:
                    continue
                if raw is False and acc and d.acc:
                    continue
            c_ = latest.get(d.eng)
            if c_ is None or d.seq > c_.seq:
                latest[d.eng] = d
        for d in latest.values():
            d.need_inc = True
            o.depc.append(d)
        o.seq = len(self.ops)
        self.ops.append(o)
        self.per[eng].append(o)
        return o

    def finalize(self, nc, es):
        self.engsem = {e: [] for e in self.ENGS}
        nsem = 0
        for i, k in enumerate(self.dma_keys):
            n = (k.cnt + self.DMA_SEM_MAX - 1) // self.DMA_SEM_MAX
            k.sem = [es.enter_context(nc.semaphore("dq%d_%d" % (i, j))) for j in range(n)]
            nsem += n
        print("dma sems", nsem)
        for e in self.ENGS:
            c = 0
            for o in self.per[e]:
                if o.need_inc:
                    c += 1
                    o.ticket = c
            print("engine", e, "ops", len(self.per[e]), "tickets", c)
            nse = (c + self.ROT - 1) // self.ROT
            self.engsem[e] = [es.enter_context(nc.semaphore("sem_%s%d" % (e, j))) for j in range(max(nse, 1))]

    def emit_engine(self, e, eng):
        waited = {}
        nw = 0
        for o in self.per[eng]:
            waits = {}
            for d in o.depc:
                s = self.engsem[d.eng][(d.ticket - 1) // self.ROT]
                k = id(s)
                tv = (d.ticket - 1) % self.ROT + 1
                if waits.get(k, (None, 0))[1] < tv:
                    waits[k] = (s, tv)
            for d in o.depdma:
                s = d.dma.sem[d.dsem]
                k = id(s)
                if waits.get(k, (None, 0))[1] < d.dticket:
                    waits[k] = (s, d.dticket)
            for k, (s, v) in waits.items():
                if waited.get(k, 0) >= v:
                    continue
                waited[k] = v
                e.wait_ge(s, v)
                nw += 1
                if os.environ.get("DUMPW") and o.seq >= int(os.environ.get("DUMPW")):
                    print("W", eng, o.seq, getattr(s, "name", s), v)
            if os.environ.get("DUMPW") and o.seq >= int(os.environ.get("DUMPW")):
                print("OP", eng, o.seq, "inc" if o.need_inc else "", o.ticket)
            ins = o.fn(e) if o.fn is not None else None
            if o.need_inc:
                assert ins is not None
                ins.then_inc(self.engsem[eng][(o.ticket - 1) // self.ROT], 1)
            if o.dma is not None:
                ins.then_inc(o.dma.sem[o.dsem], 16)
        print("emit", eng, "ops", len(self.per[eng]), "waits", nw)


class Buf:
    __slots__ = ("h", "t")

    def __init__(self, h, t):
        self.h = h
        self.t = t


def build_program(dbg=None, nprompt=16, do_pstate=True, do_sample=True):
    nc = bass.Bass("TRN2", target_bir_lowering=False)
    S = Sched()
    es = ExitStack()

    def din(name, shape, dt=F32):
        return nc.dram_tensor(name, shape, dt, kind="ExternalInput").ap()

    def dout(name, shape, dt=F32):
        return nc.dram_tensor(name, shape, dt, kind="ExternalOutput").ap()

    x_d = din("x", [NCHUNK, 128, 1024])
    sconv_d = din("sconv", [48, 1536])
    sssm_d = din("sssm", [16, 1024, 128])
    spool_d = din("spool", [240, 512])
    win_d = din("w_in", [1024, 5648])
    wa_d = din("w_a", [1024, 1024])
    wb_d = din("w_b", [512, 1024])
    wo_d = din("w_out", [1024, 1024])
    pmix_d = din("pool_mix", [4, 128, 128])
    npre_d = din("npre_col", [128, 8])
    convw_d = din("convw_col", [128, 12, 4])
    convb_d = din("convb_col", [128, 12])
    ssmn_d = din("ssmn_col", [128, 8])
    pscale_d = din("pscale_col", [128, 4])
    dtb_d = din("dtb_bc", [128, 16])
    alog_d = din("alog_bc", [128, 16])
    dskip_d = din("dskip_bc", [128, 16])
    npost_d = din("npost_bc", [128, 1024])

    y_d = dout("y", [NCHUNK, 128, 1024])
    convp_d = dout("convp", [3, 1536])
    ssmp_d = dout("ssmp", [1024, 128])
    poolp_d = dout("poolp", [15, 512])
    convs_d = dout("convs", [48, 1536])
    ssms_d = dout("ssms", [16, 1024, 128])
    pools_d = dout("pools", [240, 512])
    dbg_outs = {}
    OUTT = Tile("dram_out", collector=True)

    ARENA = 212800
    arena = nc.alloc_sbuf_tensor("arena", [128, ARENA // 4], F32)
    base = nc.lookup_mloc(arena).addr
    cur = [0]

    def nbytes(shape, dt):
        n = 1
        for s in shape[1:]:
            n *= s
        return n * (2 if dt == BF16 else 4)

    def alloc(name, shape, dt, at=None, tile=None):
        sz = (nbytes(shape, dt) + 31) // 32 * 32
        if at is None:
            off = cur[0]
            cur[0] += sz
            assert cur[0] <= ARENA, (name, cur[0])
        else:
            off = at
        h = nc.alloc_sbuf_tensor_at(name, shape, dt, offset=base + off)
        b = Buf(h, tile if tile is not None else Tile(name))
        b_off[name] = off
        return b

    b_off = {}

    WINX = alloc("winx", [128, 8, 2576], BF16)
    WINZ = alloc("winz", [128, 8, 1024], BF16)
    WING = alloc("wing", [128, 8, 2048], BF16)
    WA = alloc("wa", [128, 8, 1024], BF16)
    WB = alloc("wb", [128, 4, 1024], BF16)
    WO = alloc("wo", [128, 8, 1024], BF16)
    PMIX = alloc("pmix", [128, 4, 128], BF16)
    CT_ = Tile("consts", collector=True)
    IDB = alloc("idb", [128, 128], BF16, tile=CT_)
    IDF = alloc("idf", [128, 128], F32, tile=CT_)
    TRI = alloc("tri", [128, 128], BF16, tile=CT_)
    LST = alloc("lst", [128, 128], BF16, tile=CT_)
    ONE = alloc("one", [128, 128], BF16, tile=CT_)
    TRIS = alloc("tris", [128, 128], BF16, tile=CT_)
    LSTS = alloc("lsts", [128, 128], BF16, tile=CT_)
    SSQ = alloc("ssq", [128, 128], BF16, tile=CT_)
    SQM = alloc("sqm", [128, 16], F32, tile=CT_)
    NPRE = alloc("npre", [128, 8], F32, tile=CT_)
    CONVW = alloc("convw", [128, 12, 4], F32, tile=CT_)
    CONVB = alloc("convb", [128, 12], F32, tile=CT_)
    SSMN = alloc("ssmn", [128, 8], F32, tile=CT_)
    PSC = alloc("psc", [128, 4], F32, tile=CT_)
    DTB = alloc("dtb", [128, 16], F32, tile=CT_)
    ABC = alloc("abc", [128, 16], F32, tile=CT_)
    DSK = alloc("dsk", [128, 16], F32, tile=CT_)
    NPOST = alloc("npostb", [128, 1024], F32, tile=CT_)
    INVC = alloc("invc", [128, 16], F32, tile=CT_)
    SCR = alloc("scr", [128, 16], F32)
    NEGM = alloc("negm", [128, 128], BF16, tile=CT_)
    NEGMS = alloc("negms", [128, 128], BF16, tile=CT_)
    NACS = alloc("nacs", [128, 16], F32)
    XB = [alloc("xb0", [128, 1024], F32), alloc("xb1", [128, 1024], F32)]
    BFA = alloc("bfa", [128, 1024], BF16)
    XN = alloc("xn", [128, 1024], BF16)
    HTs = [alloc("ht0", [128, 8, 128], BF16), alloc("ht1", [128, 8, 128], BF16)]
    XBC = alloc("xbc", [128, 12, 176], F32)
    UX = alloc("ux", [128, 4, 143], F32)
    CACC = [alloc("cacc%d" % i, [128, 128], F32) for i in range(4)]
    XSTs = [alloc("xst0", [128, 8, 128], BF16), alloc("xst1", [128, 8, 128], BF16)]
    BTs = [alloc("bt0", [128, 2, 128], BF16), alloc("bt1", [128, 2, 128], BF16)]
    CTTs = [alloc("ct0", [128, 2, 128], BF16), alloc("ct1", [128, 2, 128], BF16)]
    GS = alloc("gs", [128, 4, 128], F32)
    DTRs = [alloc("dtr0", [128, 16], F32), alloc("dtr1", [128, 16], F32)]
    DTs_ = [alloc("dt%d" % i, [128, 16], F32) for i in range(2)]
    ADTs_ = [alloc("adt%d" % i, [128, 16], F32) for i in range(2)]
    DIF = alloc("dif", [128, 16], F32)
    EACs_ = [alloc("eac%d" % i, [128, 16], F32) for i in range(2)]
    DTE = alloc("dte", [128, 16], F32)
    CDs_ = [alloc("cd%d" % i, [128, 16], F32) for i in range(2)]
    DDTs_ = [alloc("ddt%d" % i, [128, 16], F32) for i in range(2)]
    AHIs_ = [alloc("ahi%d" % i, [128, 16], BF16) for i in range(2)]
    ALOs_ = [alloc("alo%d" % i, [128, 16], BF16) for i in range(2)]
    BTOK = alloc("btok", [128, 256], BF16)
    XR = alloc("xr", [128, 1024], BF16)
    XRD = alloc("xrd", [128, 1024], BF16)
    MG = Buf(XR.h, XR.t)
    MGT_h = nc.alloc_sbuf_tensor_at("mgt", [128, 8, 128], BF16, offset=base + b_off["xrd"])
    MGT = Buf(MGT_h, XRD.t)
    R1T = [Tile("r1_%d" % i) for i in range(5)]
    r1 = cur[0]
    cur[0] += 10240
    AMH = alloc("amh", [128, 8, 128], BF16, at=r1, tile=R1T[0])
    AML = alloc("aml", [128, 8, 128], BF16, at=r1 + 2048, tile=R1T[1])
    DEC = alloc("dec", [128, 8, 128], BF16, at=r1 + 4096, tile=R1T[2])
    MMT = alloc("mmt", [128, 8, 128], BF16, at=r1 + 6144, tile=R1T[3])
    YT = alloc("yt", [128, 512], F32, at=r1 + 8192, tile=R1T[4])
    TA = alloc("ta", [128, 512], F32, at=r1, tile=R1T[0])
    TAJ = alloc("taj", [128, 1024], BF16, at=r1, tile=R1T[0])
    TB = alloc("tb", [128, 512], F32, at=r1 + 2048, tile=R1T[1])
    QA = alloc("qa", [128, 512], F32, at=r1 + 4096, tile=R1T[2])
    QB = alloc("qb", [128, 512], F32, at=r1 + 6144, tile=R1T[3])
    OT = alloc("ot", [128, 512], F32, at=r1 + 8192, tile=R1T[4])
    CBM = alloc("cbm", [128, 2, 128], BF16)
    STATE = alloc("state", [128, 1024], F32)
    STBF = alloc("stbf", [128, 1024], BF16)
    Y = alloc("y", [128, 1024], F32)
    ZS = alloc("zs", [128, 512], F32)
    AMH0 = alloc("amh0", [128, 8, 128], BF16, at=b_off["y"], tile=Y.t)
    AML0 = alloc("aml0", [128, 8, 128], BF16, at=b_off["y"] + 2048, tile=Y.t)
    P2p = alloc("p2", [128, 144], F32)
    P4p = alloc("p4", [128, 144], F32)
    P8p = alloc("p8", [128, 144], F32)
    P16 = alloc("p16", [128, 128], F32)
    PLD = alloc("pld", [128, 4, 128], BF16)
    YBTs = [alloc("ybt0", [128, 4, 128], BF16), alloc("ybt1", [128, 4, 128], BF16)]
    SSPRE = alloc("sspre", [128, 1], F32)
    RPRE = alloc("rpre", [128, 1], F32)
    SSG = alloc("ssg", [128, 2], F32)
    RG = alloc("rg", [128, 2], F32)
    SSP = alloc("ssp", [128, 2], F32)
    RP = alloc("rp", [128, 1], F32)
    UXS_h = nc.alloc_sbuf_tensor_at("uxs", [128, 4, 368], F32, offset=base + b_off["state"])
    assert b_off["stbf"] == b_off["state"] + 4096
    UXST = Tile("uxs")
    wx = b_off["winx"]
    H0 = [alloc("h0_%d" % i, [128, 8, 128], F32, at=wx + i * 4096) for i in range(2)]
    H0B = [alloc("h0b_%d" % i, [128, 8, 128], BF16, at=wx + 8192 + i * 2048) for i in range(2)]
    H0T = [alloc("h0t_%d" % i, [128, 1024], BF16, at=wx + 12288 + i * 2048) for i in range(2)]
    CTM = [alloc("ctm_%d" % i, [128, 2, 128], BF16, at=wx + 16384 + i * 512) for i in range(2)]
    BM = [alloc("bm_%d" % i, [128, 256], BF16, at=wx + 17408 + i * 512) for i in range(2)]
    H0.append(alloc("h0_2", [128, 8, 128], F32, at=r1))
    H0B.append(alloc("h0b_2", [128, 8, 128], BF16, at=r1 + 4096))
    CDC = alloc("cdc", [128, 8, 16], F32, at=wx + 18432)
    CSTG = alloc("cstg", [128, 12, 48], F32, at=wx + 18944)
    PSTG = alloc("pstg", [128, 4, 240], F32, at=wx + 18944 + 2304)
    OSTG = alloc("ostg", [128, 1536], F32, at=wx + 18944 + 2304 + 3840)
    ADTX = alloc("adtx", [128, 1024], F32, at=wx + 18944 + 2304 + 3840 + 6144)
    P2s = alloc("p2s", [128, 368], F32, at=wx + 35328)
    P4s = alloc("p4s", [128, 368], F32, at=wx + 35328 + 1472)
    P8s = alloc("p8s", [128, 368], F32, at=wx + 35328 + 2944)
    assert 35328 + 3 * 1472 <= 41216
    SCV = alloc("scv", [128, 1536], F32, at=r1, tile=R1T[0])
    SPL = alloc("spl", [128, 2, 512], F32, at=r1 + 6144, tile=R1T[3])
    print("SBUF used", cur[0], "of", ARENA)

    PB = []
    pb45 = es.enter_context(nc.psum_tensor("pb45", [128, 1024], F32))
    for i in range(8):
        if i == 2:
            h = es.enter_context(nc.psum_tensor("pb2", [128, 1024], BF16))
        elif i == 4:
            h = pb45[:, 0:512]
        elif i == 5:
            h = pb45[:, 512:1024]
        else:
            h = es.enter_context(nc.psum_tensor("pb%d" % i, [128, 512], F32))
        PB.append(Buf(h, Tile("pb%d" % i, psum=True)))
    PT = PB[2]
    B3 = PB[3]
    PT2 = PB[7].h[:, :].bitcast(BF16)
    PT2t = PB[7].t
    B3dt = B3.t
    B3ac = B3.t
    B3cb = B3.t

    def tl(bufs):
        return [b.t if isinstance(b, Buf) else b for b in bufs]

    def fsz(ap):
        n = 1
        for d in ap.shape[1:]:
            n *= d
        return n

    def MM(out, lhsT, rhs, start, stop, r, w, first):
        n = fsz(rhs)
        S.op("pe", lambda e: e.matmul(out, lhsT=lhsT, rhs=rhs, start=start, stop=stop),
             tl(r), tl(w), acc=not first, cost=0.06 + max(n, 64) / 2000.0)

    def TR(out, in_, ident, r, w, first):
        S.op("pe", lambda e: e.transpose(out, in_, ident), tl(r) + [CT_], tl(w), acc=not first, cost=0.12)

    def ecost(eng, out):
        n = fsz(out)
        if eng == "pool":
            return 0.25 + n * 0.0017
        if eng == "act":
            return 0.25 + n * 0.00085
        return 0.15 + n * 0.00105

    def VT(eng, out, in0, in1, op, r, w, acc=False):
        S.op(eng, lambda e: e.tensor_tensor(out=out, in0=in0, in1=in1, op=op), tl(r), tl(w), acc=acc,
             cost=ecost(eng, out))

    def VS(eng, out, in0, s1, s2, op0, op1, r, w, acc=False):
        if s2 is None:
            S.op(eng, lambda e: e.tensor_scalar(out=out, in0=in0, scalar1=s1, scalar2=None, op0=op0),
                 tl(r), tl(w), acc=acc, cost=ecost(eng, out))
        else:
            S.op(eng, lambda e: e.tensor_scalar(out=out, in0=in0, scalar1=s1, scalar2=s2, op0=op0, op1=op1),
                 tl(r), tl(w), acc=acc, cost=ecost(eng, out))

    def STT(eng, out, in0, sc, in1, op0, op1, r, w, acc=False):
        S.op(eng, lambda e: e.scalar_tensor_tensor(out=out, in0=in0, scalar=sc, in1=in1, op0=op0, op1=op1),
             tl(r), tl(w), acc=acc, cost=ecost(eng, out))

    def CP(eng, out, in_, r, w, acc=False):
        if eng == "act":
            S.op(eng, lambda e: e.activation(out=out, in_=in_, func=AF.Copy), tl(r), tl(w), acc=acc,
                 cost=ecost(eng, out))
        else:
            S.op(eng, lambda e: e.tensor_copy(out=out, in_=in_), tl(r), tl(w), acc=acc, cost=ecost(eng, out))

    def ACT(out, in_, func, r, w, bias=None, scale=None, accum=None, acc=False):
        kw = {}
        if bias is not None:
            kw["bias"] = bias
        if scale is not None:
            kw["scale"] = scale
        if accum is not None:
            kw["accum_out"] = accum
        aset = "A" if func in (AF.Silu, AF.Tanh) else ("B" if func in (AF.Exp, AF.Ln) else None)
        S.op("act", lambda e: e.activation(out=out, in_=in_, func=func, **kw), tl(r), tl(w), acc=acc,
             cost=ecost("act", out), aset=aset)

    def MEMSET(eng, ap, val, w, acc=False):
        S.op(eng, lambda e: e.memset(ap, val), [], tl(w), acc=acc)

    def DMA(eng, out, in_, r, w, key, acc=False):
        S.op(eng, lambda e: e.dma_start(out=out, in_=in_), tl(r), tl(w), acc=acc, dma=key, cost=3.0)

    def bc(ap, shape, axis):
        return ap.unsqueeze(axis).to_broadcast(shape)

    winv = win_d.rearrange("(kc p) e -> p kc e", p=128)
    WXB = Tile("winx_b")
    DMA("pool", WINX.h[:, :, 0:1536], winv[:, :, 1024:2560], [], [WINX], WINX.t)
    DMA("pool", WINX.h[:, :, 1536:2576], winv[:, :, 2560:3600], [], [WXB], WXB)
    small = [(NPRE, npre_d), (CONVW, convw_d), (CONVB, convb_d), (SSMN, ssmn_d), (PSC, pscale_d),
             (DTB, dtb_d), (ABC, alog_d), (DSK, dskip_d), (NPOST, npost_d)]
    ctl = {}

    def ct(bf):
        k = id(bf)
        if k not in ctl:
            ctl[k] = Tile("c%d" % len(ctl))
        return ctl[k]

    for b_, d_ in small:
        if len(d_.shape) == 3:
            DMA("act", b_.h[:, :, :], d_[:, :, :], [], [ct(b_)], ct(b_))
        else:
            DMA("act", b_.h[:, :], d_[:, :], [], [ct(b_)], ct(b_))
    DMA("pool", WINZ.h[:, :, :], winv[:, :, 0:1024], [], [WINZ], WINZ.t)
    DMA("pool", PMIX.h[:, :, :], pmix_d.rearrange("k c d -> c k d"), [], [PMIX], PMIX.t)
    DMA("pool", WA.h[:, :, :], wa_d.rearrange("(kc p) e -> p kc e", p=128), [], [WA], WA.t)
    DMA("pool", WB.h[:, :, :], wb_d.rearrange("(kc p) e -> p kc e", p=128), [], [WB], WB.t)
    DMA("pool", WING.h[:, :, :], winv[:, :, 3600:5648], [], [WING], WING.t)
    DMA("pool", WO.h[:, :, :], wo_d.rearrange("(kc p) e -> p kc e", p=128), [], [WO], WO.t)

    def aff(bf, eng_ap, pattern, cmp, fill, base_, cm):
        S.op("pool", lambda e: e.affine_select(out=eng_ap, in_=eng_ap, pattern=pattern, compare_op=cmp,
                                                fill=fill, base=base_, channel_multiplier=cm), [ct(bf)], [ct(bf)])

    for I_ in (IDB, IDF):
        MEMSET("pool", I_.h[:, :], 0.0, [ct(I_)])
        aff(I_, I_.h[:, :], [[-1, 128]], ALU.not_equal, 1.0, 0, 1)
    MEMSET("pool", TRI.h[:, :], 1.0, [ct(TRI)])
    aff(TRI, TRI.h[:, :], [[1, 128]], ALU.is_ge, 0.0, 0, -1)
    MEMSET("pool", LST.h[:, :], 1.0, [ct(LST)])
    aff(LST, LST.h[:, :], [[-1, 128]], ALU.is_gt, 0.0, 0, 1)
    MEMSET("pool", ONE.h[:, :], 1.0, [ct(ONE)])
    MEMSET("pool", SSQ.h[:, :], 1.0, [ct(SSQ)])
    ssq3 = SSQ.h[:, :].rearrange("p (b j) -> p b j", j=8)
    aff(SSQ, ssq3, [[-8, 16], [0, 8]], ALU.is_ge, 0.0, 0, 1)
    aff(SSQ, ssq3, [[8, 16], [0, 8]], ALU.is_ge, 0.0, 7, -1)
    MEMSET("pool", SQM.h[:, :], 1.0, [ct(SQM)])
    aff(SQM, SQM.h[:, :], [[-8, 16]], ALU.is_ge, 0.0, 0, 1)
    aff(SQM, SQM.h[:, :], [[8, 16]], ALU.is_ge, 0.0, 7, -1)
    S.op("pool", lambda e: e.tensor_tensor(out=TRIS.h[:, :], in0=TRI.h[:, :], in1=SSQ.h[:, :], op=ALU.mult),
         [ct(TRI), ct(SSQ)], [ct(TRIS)])
    S.op("pool", lambda e: e.tensor_tensor(out=LSTS.h[:, :], in0=LST.h[:, :], in1=SSQ.h[:, :], op=ALU.mult),
         [ct(LST), ct(SSQ)], [ct(LSTS)])
    for N_, T_ in ((NEGM, TRI), (NEGMS, TRIS)):
        S.op("pool", (lambda N_, T_: (lambda e: e.tensor_scalar(out=N_.h[:, :], in0=T_.h[:, :], scalar1=-1.0,
                                                                  scalar2=30000.0, op0=ALU.add, op1=ALU.mult)))(N_, T_),
             [ct(T_)], [ct(N_)])
    for t_ in range(16):
        MEMSET("pool", INVC.h[:, t_:t_ + 1], 1.0 / (t_ + 1), [ct(INVC)], acc=(t_ > 0))
    S.op("act", lambda e: e.activation(out=ABC.h[:, :], in_=ABC.h[:, :], func=AF.Exp), [ct(ABC)], [ct(ABC)], aset="B")
    S.op("act", lambda e: e.mul(ABC.h[:, :], ABC.h[:, :], -1.0), [ct(ABC)], [ct(ABC)])
    S.op("pool", lambda e: e.memset(SCR.h[:, 3:4], 0.0), list(ctl.values()), [CT_, SCR.t])

    class _C:
        pass
    c = _C()

    def setpar(p):
        c.HT = HTs[p]
        c.XST = XSTs[p]
        c.YAT = Buf(XSTs[p].h, XSTs[p].t)
        c.BT = BTs[p]
        c.CTT = CTTs[p]
        c.YBT = YBTs[p]
        c.DTR = DTRs[p]
        c.DT = DTs_[p]
        c.ADT = ADTs_[p]
        c.EAC = EACs_[p]
        c.CD = CDs_[p]
        c.DDT = DDTs_[p]
        c.AHI = AHIs_[p]
        c.ALO = ALOs_[p]

    def load_x(ci):
        xb = XB[ci % 2]
        DMA("sp", xb.h[:, :], x_d[ci], [], [xb], xb.t)

    def proj_feature_major(ci, sample):
        L = 8 if sample else 128
        nseq = 16 if sample else 1
        hc = 3
        hp = 15
        if sample:
            xbc4 = XBC.h[:, :, :].rearrange("p t (b j) -> p t b j", j=11)
            ux4 = UXS_h[:, :, :].rearrange("p t (b j) -> p t b j", j=23)
            uxt = UXST
        else:
            xbc4 = XBC.h[:, :, 0:131].rearrange("p t (b j) -> p t b j", b=1)
            ux4 = UX.h[:, :, :].rearrange("p t (b j) -> p t b j", b=1)
            uxt = UX.t
        nb = 0
        for bl in range(3):
            bank = PB[nb % 2]
            nb += 1
            for j in range(4):
                tile = bl * 4 + j
                for kc in range(8):
                    MM(bank.h[:, j * 128:(j + 1) * 128], WINX.h[:, kc, tile * 128:(tile + 1) * 128], c.HT.h[:, kc, :],
                       kc == 0, kc == 7, [WINX, c.HT], [bank], first=(j == 0 and kc == 0))
            for j in range(4):
                tile = bl * 4 + j
                eng = "act" if bl % 2 == 0 else "dve"
                CP(eng, xbc4[:, tile, :, hc:hc + L],
                   bank.h[:, j * 128:(j + 1) * 128].rearrange("p (b j) -> p b j", j=L),
                   [bank], [XBC], acc=True)
        for kc in range(8):
            MM(B3.h[:, 0:16], c.HT.h[:, kc, :], WINX.h[:, kc, 1536:1552], kc == 0, kc == 7, [WXB, c.HT], [B3dt],
               first=(kc == 0))
        VT("dve", c.DTR.h[:, :], B3.h[:, 0:16], DTB.h[:, :], ALU.add, [B3dt, CT_], [c.DTR])
        bank = PB[nb % 2]
        nb += 1
        for j in range(4):
            for kc in range(8):
                MM(bank.h[:, j * 128:(j + 1) * 128], WINX.h[:, kc, 1552 + j * 128:1552 + (j + 1) * 128],
                   c.HT.h[:, kc, :], kc == 0, kc == 7, [WXB, c.HT], [bank], first=(j == 0 and kc == 0))
        for j in range(4):
            CP("dve", ux4[:, j, :, hp:hp + L], bank.h[:, j * 128:(j + 1) * 128].rearrange("p (b j) -> p b j", j=L),
               [bank], [uxt], acc=True)
        bank = PB[nb % 2]
        nb += 1
        for j in range(4):
            for kc in range(8):
                MM(bank.h[:, j * 128:(j + 1) * 128], WINX.h[:, kc, 2064 + j * 128:2064 + (j + 1) * 128],
                   c.HT.h[:, kc, :], kc == 0, kc == 7, [WXB, c.HT], [bank], first=(j == 0 and kc == 0))
        ACT(GS.h[:, :, :], bank.h[:, :].rearrange("p (k t) -> p k t", k=4), AF.Silu, [bank], [GS])
        return xbc4, ux4, uxt

    def conv_and_silu(xbc4, sample):
        L = 8 if sample else 128
        for pair in range(6):
            tiles = (2 * pair, 2 * pair + 1)
            accs = [CACC[(2 * pair) % 4], CACC[(2 * pair + 1) % 4]]
            a3s = [a.h[:, :].rearrange("p (b j) -> p b j", j=L) for a in accs]
            for i_, tile in enumerate(tiles):
                S.op("act", (lambda o_, i__, sc_, bi_: (lambda e: e.activation(out=o_, in_=i__, func=AF.Identity,
                                                                                   scale=sc_, bias=bi_)))(
                    a3s[i_], xbc4[:, tile, :, 0:L], CONVW.h[:, tile, 0:1], CONVB.h[:, tile:tile + 1]),
                    tl([XBC, CT_]), tl([accs[i_]]))
            for k in range(1, 4):
                for i_, tile in enumerate(tiles):
                    STT("dve", a3s[i_], xbc4[:, tile, :, k:k + L], CONVW.h[:, tile, k:k + 1], a3s[i_], ALU.mult,
                        ALU.add, [XBC, CT_, accs[i_]], [accs[i_]], acc=True)
            for i_, tile in enumerate(tiles):
                acc = accs[i_]
                if tile < 8:
                    ACT(c.XST.h[:, tile, :], acc.h[:, :], AF.Silu, [acc], [c.XST], acc=True)
                elif tile < 10:
                    ACT(c.BT.h[:, tile - 8, :], acc.h[:, :], AF.Silu, [acc], [c.BT], acc=True)
                else:
                    ACT(c.CTT.h[:, tile - 10, :], acc.h[:, :], AF.Silu, [acc], [c.CTT], acc=True)

    def pool_branch(ci, ux4, uxt, sample):
        L = 8 if sample else 128
        nseq = 16 if sample else 1
        E = 15 + L
        P2, P4, P8 = (P2s, P4s, P8s) if sample else (P2p, P4p, P8p)
        p2 = P2.h[:, 0:nseq * (E - 1)].rearrange("p (b j) -> p b j", b=nseq)
        p4 = P4.h[:, 0:nseq * (E - 3)].rearrange("p (b j) -> p b j", b=nseq)
        p8 = P8.h[:, 0:nseq * (E - 7)].rearrange("p (b j) -> p b j", b=nseq)
        p16 = P16.h[:, :].rearrange("p (b j) -> p b j", b=nseq)
        for k, w in enumerate((2, 4, 8, 16)):
            u = ux4[:, k, :, :]
            eng = "dve" if k % 2 == 0 else "pool"
            VT(eng, p2, u[:, :, 1:E], u[:, :, 0:E - 1], ALU.add, [uxt], [P2])
            Sv = p2[:, :, 14:14 + L]
            rd = [P2]
            if w >= 4:
                VT(eng, p4, p2[:, :, 2:E - 1], p2[:, :, 0:E - 3], ALU.add, [P2], [P4])
                Sv = p4[:, :, 12:12 + L]
                rd = [P4]
            if w >= 8:
                VT(eng, p8, p4[:, :, 4:E - 3], p4[:, :, 0:E - 7], ALU.add, [P4], [P8])
                Sv = p8[:, :, 8:8 + L]
                rd = [P8]
            if w >= 16:
                VT(eng, p16, p8[:, :, 8:E - 7], p8[:, :, 0:E - 15], ALU.add, [P8], [P16])
                Sv = p16
                rd = [P16]
            pld3 = PLD.h[:, k, :].rearrange("p (b j) -> p b j", b=nseq)
            STT("dve", pld3, Sv, 1.0 / w, u[:, :, 15:15 + L], ALU.mult, ALU.subtract, rd + [uxt], [PLD], acc=True)
            if ci == 0 and not sample:
                VT(eng, SCR.h[:, 0:w - 1], Sv[:, 0, 0:w - 1], INVC.h[:, 0:w - 1], ALU.mult, rd + [CT_], [SCR])
                VT(eng, PLD.h[:, k, 0:w - 1], SCR.h[:, 0:w - 1], u[:, 0, 15:15 + w - 1], ALU.subtract,
                   [SCR, uxt], [PLD], acc=True)
        bank = PB[0]
        for k in range(4):
            MM(bank.h[:, k * 128:(k + 1) * 128], PMIX.h[:, k, :], PLD.h[:, k, :], True, True, [PMIX, PLD], [bank],
               first=(k == 0))
        for k in range(4):
            STT("dve", c.YBT.h[:, k, :], bank.h[:, k * 128:(k + 1) * 128], PSC.h[:, k:k + 1], GS.h[:, k, :],
                ALU.mult, ALU.mult, [bank, GS, CT_], [c.YBT], acc=True)

    def dt_path(sample):
        tri = TRIS if sample else TRI
        ssq = SSQ if sample else ONE
        ACT(c.DTR.h[:, :], c.DTR.h[:, :], AF.Exp, [c.DTR], [c.DTR])
        ACT(c.DT.h[:, :], c.DTR.h[:, :], AF.Ln, [c.DTR], [c.DT], bias=1.0)
        VT("dve", c.ADT.h[:, :], c.DT.h[:, :], ABC.h[:, :], ALU.mult, [c.DT, CT_], [c.ADT])
        CP("dve", c.AHI.h[:, :], c.ADT.h[:, :], [c.ADT], [c.AHI])
        VT("dve", c.ALO.h[:, :], c.ADT.h[:, :], c.AHI.h[:, :], ALU.subtract, [c.ADT, c.AHI], [c.ALO])
        MM(B3.h[:, 16:32], tri.h[:, :], c.AHI.h[:, :], True, False, [CT_, c.AHI], [B3ac], first=True)
        MM(B3.h[:, 16:32], tri.h[:, :], c.ALO.h[:, :], False, True, [CT_, c.ALO], [B3ac], first=False)
        MM(B3.h[:, 32:48], ssq.h[:, :], c.AHI.h[:, :], True, False, [CT_, c.AHI], [B3ac], first=False)
        MM(B3.h[:, 32:48], ssq.h[:, :], c.ALO.h[:, :], False, True, [CT_, c.ALO], [B3ac], first=False)
        VS("dve", NACS.h[:, :], B3.h[:, 16:32], -1.0, None, ALU.mult, None, [B3ac], [NACS])
        ACT(c.EAC.h[:, :], B3.h[:, 16:32], AF.Exp, [B3ac], [c.EAC])
        ACT(c.CD.h[:, :], B3.h[:, 32:48], AF.Exp, [B3ac], [c.CD])
        VT("dve", DIF.h[:, :], B3.h[:, 32:48], NACS.h[:, :], ALU.add, [B3ac, NACS], [DIF])
        ACT(DTE.h[:, :], DIF.h[:, :], AF.Exp, [DIF], [DTE])
        VT("dve", c.DDT.h[:, :], c.DT.h[:, :], DTE.h[:, :], ALU.mult, [c.DT, DTE], [c.DDT])

    def to_token_major():
        for j in range(8):
            TR(PT2[:, j * 128:(j + 1) * 128], c.XST.h[:, j, :], IDB.h[:, :], [c.XST], [PT2t], first=(j == 0))
        CP("dve", BFA.h[:, :], PT2[:, :], [PT2t], [BFA])
        for g in range(2):
            TR(PT2[:, g * 128:(g + 1) * 128], c.BT.h[:, g, :], IDB.h[:, :], [c.BT], [PT2t], first=(g == 0))
        CP("act", BTOK.h[:, :], PT2[:, 0:256], [PT2t], [BTOK])

    def build_masks(sample, g):
        tri = TRIS if sample else TRI
        AMH_, AML_ = (AMH0, AML0) if g == 0 else (AMH, AML)
        VT("pool", AMH_.h[:, :, :], bc(tri.h[:, :], [128, 8, 128], 1), bc(c.AHI.h[:, 8 * g:8 * g + 8], [128, 8, 128], 2),
           ALU.mult, [CT_, c.AHI], [AMH_], acc=(g == 0))
        VT("pool", AML_.h[:, :, :], bc(tri.h[:, :], [128, 8, 128], 1), bc(c.ALO.h[:, 8 * g:8 * g + 8], [128, 8, 128], 2),
           ALU.mult, [CT_, c.ALO], [AML_], acc=(g == 0))

    def ssd_intra(ci, sample, g):
        tri = TRIS if sample else TRI
        lst = LSTS if sample else LST
        AMH_, AML_ = (AMH0, AML0) if g == 0 else (AMH, AML)
        for q in range(2):
            bank = PB[4 + q]
            MM(bank.h[:, :], lst.h[:, :], AMH_.h[:, 4 * q:4 * q + 4, :].rearrange("p h l -> p (h l)"), True, False,
               [CT_, AMH_], [bank], first=True)
            MM(bank.h[:, :], lst.h[:, :], AML_.h[:, 4 * q:4 * q + 4, :].rearrange("p h l -> p (h l)"), False, True,
               [CT_, AML_], [bank], first=False)
            ACT(DEC.h[:, 4 * q:4 * q + 4, :], bank.h[:, :].rearrange("p (h l) -> p h l", h=4), AF.Exp, [bank], [DEC],
                acc=(q == 1))
        VT("dve", MMT.h[:, :, :], DEC.h[:, :, :], bc(CBM.h[:, g, :], [128, 8, 128], 1), ALU.mult, [DEC, CBM], [MMT])
        bank = PB[6]
        for hh in range(8):
            h = 8 * g + hh
            MM(bank.h[:, hh * 64:(hh + 1) * 64], MMT.h[:, hh, :], XR.h[:, h * 64:(h + 1) * 64], True, True,
               [MMT, XR], [bank], first=(hh == 0))

    def ssd_prompt(ci):
        build_masks(False, 0)
        to_token_major()
        xs3 = BFA.h[:, :].rearrange("p (h d) -> p h d", d=64)
        VT("pool", XR.h[:, :].rearrange("p (h d) -> p h d", d=64), xs3, bc(c.DT.h[:, :], [128, 16, 64], 2), ALU.mult,
           [BFA, c.DT], [XR])
        build_masks(False, 1)
        VT("pool", XRD.h[:, :].rearrange("p (h d) -> p h d", d=64), xs3, bc(c.DDT.h[:, :], [128, 16, 64], 2), ALU.mult,
           [BFA, c.DDT], [XRD])
        for g in range(2):
            MM(B3.h[:, 64 + g * 128:64 + (g + 1) * 128], c.BT.h[:, g, :], c.CTT.h[:, g, :], True, True, [c.BT, c.CTT], [B3cb],
               first=(g == 0))
        VT("dve", CBM.h[:, :, :], B3.h[:, 64:320].rearrange("p (g l) -> p g l", g=2),
           bc(TRI.h[:, :], [128, 2, 128], 1), ALU.mult, [B3cb, CT_], [CBM])
        for g in range(2):
            ssd_intra(ci, False, g)
            ysl = Y.h[:, g * 512:(g + 1) * 512]
            y3 = ysl.rearrange("p (h d) -> p h d", d=64)
            if ci > 0:
                MM(PB[7].h[:, :], c.CTT.h[:, g, :], STBF.h[:, g * 512:(g + 1) * 512], True, True, [c.CTT, STBF], [PB[7]],
                   first=True)
                VT("dve", y3, PB[7].h[:, :].rearrange("p (h d) -> p h d", d=64),
                   bc(c.EAC.h[:, 8 * g:8 * g + 8], [128, 8, 64], 2), ALU.mult, [PB[7], c.EAC], [Y], acc=(g == 1))
                VT("dve", ysl, ysl, PB[6].h[:, :], ALU.add, [Y, PB[6]], [Y], acc=True)
            else:
                CP("dve", ysl, PB[6].h[:, :], [PB[6]], [Y], acc=(g == 1))
            VT("pool", YT.h[:, :].rearrange("p (h d) -> p h d", d=64),
               BFA.h[:, g * 512:(g + 1) * 512].rearrange("p (h d) -> p h d", d=64),
               bc(DSK.h[:, 8 * g:8 * g + 8], [128, 8, 64], 2), ALU.mult, [BFA, CT_], [YT])
            VT("dve", ysl, ysl, YT.h[:, :], ALU.add, [Y, YT], [Y], acc=True)
            MM(PB[7].h[:, :], BTOK.h[:, g * 128:(g + 1) * 128], XRD.h[:, g * 512:(g + 1) * 512], True, True,
               [BTOK, XRD], [PB[7]], first=True)
            ssl = STATE.h[:, g * 512:(g + 1) * 512]
            if ci > 0:
                s3 = ssl.rearrange("p (h d) -> p h d", d=64)
                VT("dve", s3, s3, bc(c.CD.h[:, 8 * g:8 * g + 8], [128, 8, 64], 2), ALU.mult, [STATE, c.CD], [STATE],
                   acc=True)
                VT("dve", ssl, ssl, PB[7].h[:, :], ALU.add, [STATE, PB[7]], [STATE], acc=True)
            else:
                CP("dve", ssl, PB[7].h[:, :], [PB[7]], [STATE], acc=(g == 1))
            if ci < 15:
                CP("act", STBF.h[:, g * 512:(g + 1) * 512], ssl, [STATE], [STBF], acc=(g == 1))

    def gate_norm_transpose():
        for g in range(2):
            bank = PB[4 + g]
            for kc in range(8):
                MM(bank.h[:, :], c.HT.h[:, kc, :], WINZ.h[:, kc, g * 512:(g + 1) * 512], kc == 0, kc == 7, [c.HT, WINZ],
                   [bank], first=(kc == 0))
            ACT(ZS.h[:, :], bank.h[:, :], AF.Silu, [bank], [ZS])
            ysl = Y.h[:, g * 512:(g + 1) * 512]
            VT("dve", ysl, ysl, ZS.h[:, :], ALU.mult, [Y, ZS], [Y], acc=True)
            ACT(ZS.h[:, :], ysl, AF.Square, [Y], [ZS, SSG], accum=SSG.h[:, g:g + 1], acc=(g == 1))
        ACT(RG.h[:, :], SSG.h[:, :], AF.Ln, [SSG], [RG], bias=EPS, scale=1.0 / 512)
        ACT(RG.h[:, :], RG.h[:, :], AF.Exp, [RG], [RG], scale=-0.5)
        for g in range(2):
            VS("dve", BFA.h[:, g * 512:(g + 1) * 512], Y.h[:, g * 512:(g + 1) * 512], RG.h[:, g:g + 1], None, ALU.mult,
               None, [Y, RG], [BFA], acc=(g == 1))
        for j in range(8):
            TR(PT2[:, j * 128:(j + 1) * 128], BFA.h[:, j * 128:(j + 1) * 128], IDB.h[:, :], [BFA], [PT2t], first=(j == 0))
        VT("dve", c.YAT.h[:, :, :], PT2[:, :].rearrange("p (k t) -> p k t", k=8), bc(SSMN.h[:, :], [128, 8, 128], 2),
           ALU.mult, [PT2t, CT_], [c.YAT])

    def tail(ci):
        xb = XB[ci % 2]
        for cb in range(2):
            cs = slice(cb * 512, (cb + 1) * 512)
            for kc in range(8):
                MM(PB[4].h[:, :], c.YAT.h[:, kc, :], WA.h[:, kc, cs], kc == 0, kc == 7, [c.YAT, WA], [PB[4]], first=(kc == 0))
            for kc in range(4):
                MM(PB[5].h[:, :], c.YBT.h[:, kc, :], WB.h[:, kc, cs], kc == 0, kc == 3, [c.YBT, WB], [PB[5]], first=(kc == 0))
            for kc in range(8):
                MM(PB[6].h[:, :], c.HT.h[:, kc, :], WING.h[:, kc, cs], kc == 0, kc == 7, [c.HT, WING], [PB[6]],
                   first=(kc == 0))
            for kc in range(8):
                MM(PB[7].h[:, :], c.HT.h[:, kc, :], WING.h[:, kc, 1024 + cb * 512:1024 + (cb + 1) * 512], kc == 0, kc == 7,
                   [c.HT, WING], [PB[7]], first=(kc == 0))
            ACT(TA.h[:, :], PB[6].h[:, :], AF.Tanh, [PB[6]], [TA], scale=0.5)
            ACT(TB.h[:, :], PB[7].h[:, :], AF.Tanh, [PB[7]], [TB], scale=0.5)
            STT("dve", QA.h[:, :], TA.h[:, :], 1.0, PB[4].h[:, :], ALU.add, ALU.mult, [TA, PB[4]], [QA])
            STT("dve", QB.h[:, :], TB.h[:, :], 1.0, PB[5].h[:, :], ALU.add, ALU.mult, [TB, PB[5]], [QB])
            VT("dve", MG.h[:, cs], QA.h[:, :], QB.h[:, :], ALU.add, [QA, QB], [MG], acc=(cb == 1))
        for j in range(8):
            TR(PT2[:, j * 128:(j + 1) * 128], MG.h[:, j * 128:(j + 1) * 128], IDB.h[:, :], [MG], [PT2t], first=(j == 0))
        S.op("act", lambda e: e.mul(MGT.h[:, :, :], PT2[:, :].rearrange("p (k t) -> p k t", k=8), 0.5),
             [PT2t], [MGT.t])
        for ob in range(2):
            bank = PB[4 + ob]
            for kc in range(8):
                MM(bank.h[:, :], MGT.h[:, kc, :], WO.h[:, kc, ob * 512:(ob + 1) * 512], kc == 0, kc == 7, [MGT, WO],
                   [bank], first=(kc == 0))
        ACT(TAJ.h[:, :], pb45[:, :], AF.Square, [PB[4], PB[5]], [TAJ, SSP], accum=SSP.h[:, 0:1])
        ACT(RP.h[:, :], SSP.h[:, 0:1], AF.Ln, [SSP], [RP], bias=EPS, scale=1.0 / 1024)
        ACT(RP.h[:, :], RP.h[:, :], AF.Exp, [RP], [RP], scale=-0.5)
        for ob in range(2):
            cs = slice(ob * 512, (ob + 1) * 512)
            STT("dve", OT.h[:, :], PB[4 + ob].h[:, :], RP.h[:, 0:1], NPOST.h[:, cs], ALU.mult, ALU.mult,
                [PB[4 + ob], RP, CT_], [OT])
            VT("dve", xb.h[:, cs], xb.h[:, cs], OT.h[:, :], ALU.add, [xb, OT], [xb], acc=True)
        DMA("sp", y_d[ci], xb.h[:, :], [xb], [OUTT], xb.t)

    def norm_pre(ci):
        xb = XB[ci % 2]
        ACT(XN.h[:, :], xb.h[:, :], AF.Square, [xb], [XN, SSPRE], accum=SSPRE.h[:, :])
        ACT(RPRE.h[:, :], SSPRE.h[:, :], AF.Ln, [SSPRE], [RPRE], bias=EPS, scale=1.0 / 1024)
        ACT(RPRE.h[:, :], RPRE.h[:, :], AF.Exp, [RPRE], [RPRE], scale=-0.5)
        VS("dve", XN.h[:, :], xb.h[:, :], RPRE.h[:, 0:1], None, ALU.mult, None, [xb, RPRE], [XN])
        for j in range(8):
            TR(PT.h[:, j * 128:(j + 1) * 128], XN.h[:, j * 128:(j + 1) * 128], IDB.h[:, :], [XN], [PT], first=(j == 0))
        VT("dve", c.HT.h[:, :, :], PT.h[:, :].rearrange("p (k t) -> p k t", k=8), bc(NPRE.h[:, :], [128, 8, 128], 2),
           ALU.mult, [PT, CT_], [c.HT])

    def out_fp32_T(src_fn, ncols, tiles, stage_ap_fn, stage_tiles, dram_ap, reads, key, banks=(0, 1)):
        done = 0
        nb = 0
        nt = len(tiles)
        while done < nt:
            n = min(4, nt - done)
            bank = PB[banks[nb % 2]]
            nb += 1
            for j in range(n):
                TR(bank.h[0:ncols, j * 128:(j + 1) * 128], src_fn(tiles[done + j]), IDF.h[:, :], reads, [bank],
                   first=(j == 0))
            CP("dve", stage_ap_fn(done * 128, (done + n) * 128), bank.h[0:ncols, 0:n * 128], [bank], stage_tiles,
               acc=(done > 0))
            done += n
        DMA("sp", dram_ap, stage_ap_fn(0, nt * 128), stage_tiles, [OUTT], key)

    def stage1(ci):
        setpar(ci % 2)
        norm_pre(ci)
        xbc4, ux4, uxt = proj_feature_major(ci, False)
        dt_path(False)
        conv_and_silu(xbc4, False)
        pool_branch(ci, ux4, uxt, False)
        if ci < 15:
            CP("pool", XBC.h[:, :, 0:3], XBC.h[:, :, 128:131], [XBC], [XBC], acc=True)
            CP("pool", UX.h[:, :, 0:15], UX.h[:, :, 128:143], [UX], [UX], acc=True)

    def stage23(ci):
        setpar(ci % 2)
        ssd_prompt(ci)
        gate_norm_transpose()
        tail(ci)

    MEMSET("dve", XBC.h[:, :, 0:3], 0.0, [XBC])
    MEMSET("dve", UX.h[:, :, 0:15], 0.0, [UX])
    load_x(0)
    stage1(0)
    for ci in range(nprompt):
        sa = None
        if ci + 1 < nprompt or do_sample:
            load_x(ci + 1)
        if ci + 1 < nprompt:
            S.begin()
            stage1(ci + 1)
            sa = S.end()
        S.begin()
        stage23(ci)
        sb = S.end()
        S.run_merged([sa, sb])

    XRF = alloc("xrf", [128, 512], F32, at=b_off["xr"], tile=XR.t)
    XRDF = alloc("xrdf", [128, 512], F32, at=b_off["xrd"], tile=XRD.t)

    def prompt_state_outputs():
        out_fp32_T(lambda t: XBC.h[:, t, 128:131], 3, list(range(8)), lambda a, b: Y.h[0:3, a:b], [Y.t],
                   convp_d[:, 0:1024], [XBC], Y.t, banks=(6, 7))
        out_fp32_T(lambda t: XBC.h[:, t, 128:131], 3, list(range(8, 12)), lambda a, b: ZS.h[0:3, a:b], [ZS.t],
                   convp_d[:, 1024:1536], [XBC], ZS.t, banks=(6, 7))
        out_fp32_T(lambda t: UX.h[:, t, 128:143], 15, list(range(4)), lambda a, b: XRF.h[0:15, a:b], [XR.t],
                   poolp_d[:, :], [UX], XR.t, banks=(6, 7))
        ssmp_v = ssmp_d.rearrange("(j p) n -> p j n", p=128)
        for half in range(2):
            bank = PB[6 + half]
            for j in range(4):
                jj = half * 4 + j
                TR(bank.h[:, j * 128:(j + 1) * 128], STATE.h[:, jj * 128:(jj + 1) * 128], IDF.h[:, :], [STATE], [bank],
                   first=(j == 0))
            CP("dve", XRDF.h[:, :], bank.h[:, :], [bank], [XRD])
            DMA("sp", ssmp_v[:, half * 4:(half + 1) * 4, :], XRDF.h[:, :].rearrange("p (j n) -> p j n", j=4), [XRD],
                [OUTT], XRD.t)

    sctx = {}

    def sample_front():
        ci = 16
        setpar(0)
        norm_pre(ci)
        SCVT = [R1T[0], R1T[1], R1T[2]]
        SPLT = [R1T[3], R1T[4]]
        DMA("act", SCV.h[0:48, :], sconv_d[:, :], [], SCVT, R1T[0])
        DMA("act", SPL.h[0:120, :, :], spool_d.rearrange("(h r) c -> r h c", h=2), [], SPLT, R1T[3])
        xbc4s = XBC.h[:, :, :].rearrange("p t (b j) -> p t b j", j=11)
        ux4s = UXS_h[:, :, :].rearrange("p t (b j) -> p t b j", j=23)
        for tile in range(12):
            bank = PB[4 + (tile % 2)]
            TR(bank.h[:, 0:48], SCV.h[0:48, tile * 128:(tile + 1) * 128], IDF.h[0:48, 0:48], SCVT, [bank], first=True)
            CP("dve", xbc4s[:, tile, :, 0:3], bank.h[:, 0:48].rearrange("p (b k) -> p b k", k=3), [bank], [XBC], acc=True)
        first_ux = True
        for k in range(4):
            for half in range(2):
                bank = PB[4 + half]
                TR(bank.h[:, 0:120], SPL.h[0:120, half, k * 128:(k + 1) * 128], IDF.h[0:120, 0:120], SPLT, [bank],
                   first=True)
                CP("dve", ux4s[:, k, 8 * half:8 * half + 8, 0:15], bank.h[:, 0:120].rearrange("p (b k) -> p b k", k=15),
                   [bank, STATE, STBF], [UXST, STATE, STBF], acc=not first_ux)
                first_ux = False
        xbc4, ux4, uxt = proj_feature_major(ci, True)
        conv_and_silu(xbc4, True)
        alias_tiles = [b.t for b in H0 + H0B + H0T + CTM + BM] + [CDC.t, CSTG.t, PSTG.t, OSTG.t, ADTX.t, P2s.t, P4s.t, P8s.t]
        S.op("pool", lambda e: e.memset(SCR.h[:, 0:1], 0.0), [], [WINX.t, WXB] + alias_tiles + [SCR.t])
        for tile in range(12):
            CP("pool", CSTG.h[:, tile, :].rearrange("p (b k) -> p b k", k=3), xbc4s[:, tile, :, 8:11], [XBC], [CSTG],
               acc=(tile > 0))
        out_fp32_T(lambda t: CSTG.h[:, t, :], 48, list(range(12)), lambda a, b: OSTG.h[0:48, a:b], [OSTG.t],
                   convs_d[:, :], [CSTG], OSTG.t)
        sctx.update(ux4=ux4, uxt=uxt, ux4s=ux4s)

    def sample_rest():
        ci = 16
        setpar(0)
        ux4, uxt, ux4s = sctx["ux4"], sctx["uxt"], sctx["ux4s"]
        ssv = sso = None
        dt_path(True)
        build_masks(True, 0)
        build_masks(True, 1)
        to_token_major()
        xs3 = BFA.h[:, :].rearrange("p (h d) -> p h d", d=64)
        VT("pool", XR.h[:, :].rearrange("p (h d) -> p h d", d=64), xs3, bc(c.DT.h[:, :], [128, 16, 64], 2), ALU.mult,
           [BFA, c.DT], [XR])
        VT("pool", XRD.h[:, :].rearrange("p (h d) -> p h d", d=64), xs3, bc(c.DDT.h[:, :], [128, 16, 64], 2), ALU.mult,
           [BFA, c.DDT], [XRD])
        for g in range(2):
            MM(B3.h[:, 64 + g * 128:64 + (g + 1) * 128], c.BT.h[:, g, :], c.CTT.h[:, g, :], True, True, [c.BT, c.CTT], [B3cb],
               first=(g == 0))
        VT("dve", CBM.h[:, :, :], B3.h[:, 64:320].rearrange("p (g l) -> p g l", g=2), bc(TRIS.h[:, :], [128, 2, 128], 1),
           ALU.mult, [B3cb, CT_], [CBM])
        for g in range(2):
            ssd_intra(ci, True, g)
            ysl = Y.h[:, g * 512:(g + 1) * 512]
            CP("dve", ysl, PB[6].h[:, :], [PB[6]], [Y], acc=(g == 1))
            VT("pool", YT.h[:, :].rearrange("p (h d) -> p h d", d=64),
               BFA.h[:, g * 512:(g + 1) * 512].rearrange("p (h d) -> p h d", d=64),
               bc(DSK.h[:, 8 * g:8 * g + 8], [128, 8, 64], 2), ALU.mult, [BFA, CT_], [YT])
            VT("dve", ysl, ysl, YT.h[:, :], ALU.add, [Y, YT], [Y], acc=True)
        VT("dve", ADTX.h[:, :].rearrange("p (h d) -> p h d", d=64), bc(c.ADT.h[:, :], [128, 16, 64], 2),
           bc(ONE.h[:, 0:16], [128, 16, 64], 2), ALU.mult, [c.ADT, CT_], [ADTX])
        for j in range(8):
            MM(PB[7].h[:, j * 16:(j + 1) * 16], ADTX.h[:, j * 128:(j + 1) * 128], SQM.h[:, :], True, True,
               [ADTX, CT_], [PB[7]], first=(j == 0))
        ACT(CDC.h[:, :, :], PB[7].h[:, 0:128].rearrange("p (j b) -> p j b", j=8), AF.Exp, [PB[7]], [CDC])
        for i in range(2):
            MEMSET("pool", CTM[i].h[:, :, :], 0.0, [CTM[i]])
        ssv = sssm_d.rearrange("b (j p) n -> b p j n", p=128)
        sso = ssms_d.rearrange("b (j p) n -> b p j n", p=128)
        NB = 3
        S.op("pool", lambda e: e.memset(SCR.h[:, 1:2], 0.0), [], [R1T[0], R1T[1], R1T[2], H0[2].t, H0B[2].t, SCR.t])

        def h0_load(b):
            rb = b % NB
            DMA("sp", H0[rb].h[:, :, :], ssv[b], [], [H0[rb]], H0[rb].t)
            DMA("pool", H0B[rb].h[:, :, :], ssv[b], [], [H0B[rb]], H0B[rb].t)

        def stage_a(b):
            rb = b % NB
            r2_ = b % 2
            for j in range(8):
                TR(PT.h[:, j * 128:(j + 1) * 128], H0B[rb].h[:, j, :], IDB.h[:, :], [H0B[rb]], [PT], first=(j == 0))
            CP("act", H0T[r2_].h[:, :], PT.h[:, :], [PT], [H0T[r2_]])
            CP("act", CTM[r2_].h[:, :, 8 * b:8 * b + 8], c.CTT.h[:, :, 8 * b:8 * b + 8], [c.CTT], [CTM[r2_]], acc=True)
            ACT(BM[r2_].h[:, :], BTOK.h[:, :], AF.Copy, [BTOK, CT_], [BM[r2_]], scale=SQM.h[:, b:b + 1])

        def stage_b(b):
            rb = b % NB
            r2_ = b % 2
            for g in range(2):
                MM(PB[g].h[:, :], CTM[r2_].h[:, g, :], H0T[r2_].h[:, g * 512:(g + 1) * 512], b == 0, b == 15,
                   [CTM[r2_], H0T[r2_]], [PB[g]], first=(b == 0))
            for j in range(8):
                bank = PB[4 + j // 4]
                MM(bank.h[:, (j % 4) * 128:(j % 4 + 1) * 128], XRD.h[:, j * 128:(j + 1) * 128],
                   BM[r2_].h[:, (j // 4) * 128:(j // 4 + 1) * 128], True, True, [XRD, BM[r2_]], [bank], first=(j % 4 == 0))
            ACT(CTM[r2_].h[:, :, 8 * b:8 * b + 8], CTM[r2_].h[:, :, 8 * b:8 * b + 8], AF.Copy, [CTM[r2_]], [CTM[r2_]],
                scale=0.0, acc=True)
            for j in range(8):
                bank = PB[4 + j // 4]
                STT("dve", H0[rb].h[:, j, :], H0[rb].h[:, j, :], CDC.h[:, j, b:b + 1],
                    bank.h[:, (j % 4) * 128:(j % 4 + 1) * 128], ALU.mult, ALU.add, [H0[rb], CDC, bank], [H0[rb]], acc=True)
            DMA("act", sso[b], H0[rb].h[:, :, :], [H0[rb]], [OUTT], H0[rb].t)

        h0_load(0)
        h0_load(1)
        stage_a(0)
        for b in range(16):
            if b + 1 < 16:
                stage_a(b + 1)
            stage_b(b)
            if b + 2 < 16:
                h0_load(b + 2)
        S.op("pool", lambda e: e.memset(SCR.h[:, 2:3], 0.0), [], [R1T[0], R1T[1], R1T[2], H0[2].t, H0B[2].t, SCR.t])
        for g in range(2):
            ysl = Y.h[:, g * 512:(g + 1) * 512]
            VT("dve", YT.h[:, :].rearrange("p (h d) -> p h d", d=64), PB[g].h[:, :].rearrange("p (h d) -> p h d", d=64),
               bc(c.EAC.h[:, 8 * g:8 * g + 8], [128, 8, 64], 2), ALU.mult, [PB[g], c.EAC], [YT])
            VT("dve", ysl, ysl, YT.h[:, :], ALU.add, [Y, YT], [Y], acc=True)
        pool_branch(ci, ux4, uxt, True)
        for k in range(4):
            CP("pool", PSTG.h[:, k, :].rearrange("p (b r) -> p b r", r=15), ux4s[:, k, :, 8:23], [UXST], [PSTG],
               acc=(k > 0))
        for half in range(2):
            bank = PB[4 + half]
            for k in range(4):
                TR(bank.h[0:120, k * 128:(k + 1) * 128], PSTG.h[:, k, half * 120:(half + 1) * 120], IDF.h[:, :], [PSTG],
                   [bank], first=(k == 0))
            CP("dve", OSTG.h[0:120, half * 512:(half + 1) * 512], bank.h[0:120, :], [bank], [OSTG], acc=True)
        DMA("sp", pools_d.rearrange("(h r) c -> r h c", h=2), OSTG.h[0:120, 0:1024].rearrange("r (h c) -> r h c", h=2),
            [OSTG], [OUTT], OSTG.t)
        gate_norm_transpose()
        tail(ci)

    sa = sb = None
    if do_pstate:
        S.begin()
        prompt_state_outputs()
        sa = S.end()
    if do_sample:
        S.begin()
        sample_front()
        sb = S.end()
    S.run_merged([sa, sb])
    if do_sample:
        sample_rest()
    S.op("sp", None, [OUTT], [])

    S.finalize(nc, es)
    block = es.enter_context(nc.Block())

    @block.tensor
    def _(e):
        S.emit_engine(e, "pe")

    @block.scalar
    def _(e):
        S.emit_engine(e, "act")

    @block.vector
    def _(e):
        S.emit_engine(e, "dve")

    @block.gpsimd
    def _(e):
        S.emit_engine(e, "pool")

    @block.sync
    def _(e):
        S.emit_engine(e, "sp")

    es.close()
    return nc


_NC_CACHE = {}


def _prep_inputs(inp):
    f = lambda a: np.ascontiguousarray(np.asarray(a, dtype=np.float32))
    xp = f(inp["x_prompt"])
    xs = f(inp["x_sample"])
    sc = f(inp["state_conv"])[0]
    ss = f(inp["state_ssm"])[0]
    sp = f(inp["state_pool"])[0]
    shared = {
        "w_in": f(inp["w_in"])[0],
        "w_a": f(inp["w_branch_a"])[0],
        "w_b": f(inp["w_branch_b"])[0],
        "w_out": f(inp["w_out"])[0],
        "pool_mix": f(inp["pool_mix"])[0],
        "npre_col": f(f(inp["norm_pre"])[0].reshape(8, 128).T),
        "convw_col": f(f(inp["conv_w"])[0].reshape(4, 12, 128).transpose(2, 1, 0)),
        "convb_col": f(f(inp["conv_b"])[0].reshape(12, 128).T),
        "ssmn_col": f(f(inp["ssm_norm"])[0].reshape(8, 128).T),
        "pscale_col": f(f(inp["pool_scale"])[0].reshape(4, 128).T),
        "dtb_bc": f(np.broadcast_to(f(inp["dt_bias"])[0][None, :], (128, 16))),
        "alog_bc": f(np.broadcast_to(f(inp["a_log"])[0][None, :], (128, 16))),
        "dskip_bc": f(np.broadcast_to(f(inp["d_skip"])[0][None, :], (128, 16))),
        "npost_bc": f(np.broadcast_to(f(inp["norm_post"])[0][None, :], (128, 1024))),
    }
    maps = []
    for c in range(8):
        x = np.concatenate([xp[c].reshape(16, 128, 1024), xs[16 * c:16 * c + 16].reshape(1, 128, 1024)], axis=0)
        m = dict(shared)
        m["x"] = f(x)
        m["sconv"] = f(sc[16 * c:16 * c + 16].reshape(48, 1536))
        m["sssm"] = f(ss[16 * c:16 * c + 16].reshape(16, 1024, 128))
        m["spool"] = f(sp[16 * c:16 * c + 16].reshape(240, 512))
        maps.append(m)
    return maps


def kernel(**inputs):
    if "nc" not in _NC_CACHE:
        _NC_CACHE["nc"] = build_program()
    nc = _NC_CACHE["nc"]
    maps = _prep_inputs(inputs)
    res = run_bass_kernel_spmd(nc, maps, core_ids=list(range(8)))
    R = res.results
    yp = np.stack([R[c]["y"][:16].reshape(2048, 1024) for c in range(8)], axis=0)
    ys = np.concatenate([R[c]["y"][16].reshape(16, 8, 1024) for c in range(8)], axis=0)
    convp = np.stack([R[c]["convp"] for c in range(8)], axis=0)[None]
    ssmp = np.stack([R[c]["ssmp"].reshape(16, 64, 128) for c in range(8)], axis=0)[None]
    poolp = np.stack([R[c]["poolp"] for c in range(8)], axis=0)[None]
    convs = np.concatenate([R[c]["convs"].reshape(16, 3, 1536) for c in range(8)], axis=0)[None]
    ssms = np.concatenate([R[c]["ssms"].reshape(16, 16, 64, 128) for c in range(8)], axis=0)[None]
    pools = np.concatenate([R[c]["pools"].reshape(16, 15, 512) for c in range(8)], axis=0)[None]
    out = (yp, ys, convp, ssmp, poolp, convs, ssms, pools)
    return tuple(np.ascontiguousarray(o, dtype=np.float32) for o in out)
```

```python
import os
import numpy as np
from contextlib import ExitStack
import concourse.bass as bass
import concourse.mybir as mybir
from concourse.bass_utils import run_bass_kernel_spmd

F32 = mybir.dt.float32
BF16 = mybir.dt.bfloat16
AF = mybir.ActivationFunctionType
ALU = mybir.AluOpType

NCHUNK = 17
XLAT = float(os.environ.get("XLAT", 0.9))
SAME_ENG_ALL = int(os.environ.get("SAME_ENG_ALL", 1))
EPS = 1e-6


class Tile:
    __slots__ = ("name", "writers", "readers", "sem", "cnt", "collector", "psum")

    def __init__(self, name, collector=False, psum=False):
        self.psum = psum
        self.name = name
        self.writers = []
        self.readers = []
        self.sem = None
        self.cnt = 0
        self.collector = collector


class Op:
    __slots__ = ("eng", "fn", "reads", "writes", "acc", "dma", "depc", "depdma",
                 "need_inc", "ticket", "dticket", "seq", "dsem", "t_end")


class Sched:
    ENGS = ("pe", "act", "dve", "pool", "sp")
    DMA_SEM_MAX = 224
    ROT = int(os.environ.get("ROT", 1000))

    def __init__(self):
        self.ops = []
        self.per = {e: [] for e in self.ENGS}
        self.dma_keys = []
        self.cur = None
        self.eng_t = {e: 0.0 for e in self.ENGS}
        self.act_set = None

    def begin(self):
        self.cur = []
        return self.cur

    def end(self):
        st = self.cur
        self.cur = None
        return st

    def _est_start(self, a):
        eng, fn, reads, writes, acc, dma, cost, aset = a
        t = self.eng_t[eng]
        if aset is not None and aset != self.act_set:
            t += 1.3
        for tl_ in reads:
            for w in tl_.writers:
                te = w.t_end + (0.0 if w.eng == eng else XLAT)
                if te > t:
                    t = te
            if tl_.psum:
                for r in tl_.readers:
                    if r.eng != eng and r.t_end > t:
                        t = r.t_end
        for tl_ in writes:
            if tl_.collector:
                continue
            for w in tl_.writers:
                te = w.t_end + (0.0 if w.eng == eng else XLAT)
                if te > t:
                    t = te
            for r in tl_.readers:
                te = r.t_end + (0.0 if r.eng == eng else XLAT)
                if te > t:
                    t = te
        return t

    def run_merged(self, streams):
        streams = [st for st in streams if st]
        pos = [0] * len(streams)
        while True:
            best = -1
            bt = 1e30
            for i, st in enumerate(streams):
                if pos[i] < len(st):
                    t = self._est_start(st[pos[i]])
                    t += 0.05 * pos[i] / len(st)
                    if t < bt:
                        bt = t
                        best = i
            if best < 0:
                break
            a = streams[best][pos[best]]
            pos[best] += 1
            self.op(*a)

    def op(self, eng, fn, reads=(), writes=(), acc=False, dma=None, cost=0.3, aset=None):
        if self.cur is not None:
            self.cur.append((eng, fn, list(reads), list(writes), acc, dma, cost, aset))
            return None
        t_start = self._est_start((eng, fn, reads, writes, acc, dma, cost, aset))
        if aset is not None:
            self.act_set = aset
        o = Op()
        if dma is not None:
            self.eng_t[eng] = t_start + 0.1
            o.t_end = t_start + cost
        else:
            o.t_end = t_start + cost
            self.eng_t[eng] = o.t_end
        o.eng = eng
        o.fn = fn
        o.reads = list(reads)
        o.writes = list(writes)
        o.acc = acc
        o.dma = dma
        o.depc = []
        o.depdma = []
        o.need_inc = False
        o.ticket = 0
        o.dticket = 0
        if dma is not None:
            if dma.cnt == 0 and dma not in self.dma_keys:
                self.dma_keys.append(dma)
            o.dsem = dma.cnt // self.DMA_SEM_MAX
            dma.cnt += 16
            o.dticket = dma.cnt - o.dsem * self.DMA_SEM_MAX
        deps = {}
        for t in o.reads:
            for w in t.writers:
                deps[id(w)] = (w, True)
            if t.psum:
                for r in t.readers:
                    if r.eng != eng and id(r) not in deps:
                        deps[id(r)] = (r, False)
        for t in o.writes:
            if t.collector:
                continue
            for w in t.writers:
                if id(w) not in deps:
                    deps[id(w)] = (w, False)
            for r in t.readers:
                if id(r) not in deps:
                    deps[id(r)] = (r, None)
        for t in o.reads:
            t.readers.append(o)
        for t in o.writes:
            if t.collector or acc:
                t.writers.append(o)
            else:
                t.writers = [o]
                t.readers = []
        latest = {}
        for d, raw in deps.values():
            if d is o:
                continue
            if d.dma is not None:
                o.depdma.append(d)
                continue
            if d.eng == eng:
                if eng == "pe":
                    continue
                if raw is None and SAME_ENG_ALL < 1:
                    continue
                if raw is False and SAME_ENG_ALL < 1 and SAME_ENG_ALL > -1:
                    pass
                if raw is False and SAME_ENG_ALL < 0:
                    continue
                if raw is False and acc and d.acc:
                    continue
            c_ = latest.get(d.eng)
            if c_ is None or d.seq > c_.seq:
                latest[d.eng] = d
        for d in latest.values():
            d.need_inc = True
            o.depc.append(d)
        o.seq = len(self.ops)
        self.ops.append(o)
        self.per[eng].append(o)
        return o

    def finalize(self, nc, es):
        self.engsem = {e: [] for e in self.ENGS}
        nsem = 0
        for i, k in enumerate(self.dma_keys):
            n = (k.cnt + self.DMA_SEM_MAX - 1) // self.DMA_SEM_MAX
            k.sem = [es.enter_context(nc.semaphore("dq%d_%d" % (i, j))) for j in range(n)]
            nsem += n
        print("dma sems", nsem)
        for e in self.ENGS:
            c = 0
            for o in self.per[e]:
                if o.need_inc:
                    c += 1
                    o.ticket = c
            print("engine", e, "ops", len(self.per[e]), "tickets", c)
            nse = (c + self.ROT - 1) // self.ROT
            self.engsem[e] = [es.enter_context(nc.semaphore("sem_%s%d" % (e, j))) for j in range(max(nse, 1))]

    def emit_engine(self, e, eng):
        waited = {}
        nw = 0
        for o in self.per[eng]:
            waits = {}
            for d in o.depc:
                s = self.engsem[d.eng][(d.ticket - 1) // self.ROT]
                k = id(s)
                tv = (d.ticket - 1) % self.ROT + 1
                if waits.get(k, (None, 0))[1] < tv:
                    waits[k] = (s, tv)
            for d in o.depdma:
                s = d.dma.sem[d.dsem]
                k = id(s)
                if waits.get(k, (None, 0))[1] < d.dticket:
                    waits[k] = (s, d.dticket)
            for k, (s, v) in waits.items():
                if waited.get(k, 0) >= v:
                    continue
                waited[k] = v
                e.wait_ge(s, v)
                nw += 1
                if os.environ.get("DUMPW") and o.seq >= int(os.environ.get("DUMPW")):
                    print("W", eng, o.seq, getattr(s, "name", s), v)
            if os.environ.get("DUMPW") and o.seq >= int(os.environ.get("DUMPW")):
                print("OP", eng, o.seq, "inc" if o.need_inc else "", o.ticket)
            ins = o.fn(e) if o.fn is not None else None
            if o.need_inc:
                assert ins is not None
                ins.then_inc(self.engsem[eng][(o.ticket - 1) // self.ROT], 1)
            if o.dma is not None:
                ins.then_inc(o.dma.sem[o.dsem], 16)
        print("emit", eng, "ops", len(self.per[eng]), "waits", nw)


class Buf:
    __slots__ = ("h", "t")

    def __init__(self, h, t):
        self.h = h
        self.t = t


def build_program(dbg=None, nprompt=16, do_pstate=True, do_sample=True):
    nc = bass.Bass("TRN2", target_bir_lowering=False)
    S = Sched()
    es = ExitStack()

    def din(name, shape, dt=F32):
        return nc.dram_tensor(name, shape, dt, kind="ExternalInput").ap()

    def dout(name, shape, dt=F32):
        return nc.dram_tensor(name, shape, dt, kind="ExternalOutput").ap()

    x_d = din("x", [NCHUNK, 128, 1024])
    sconv_d = din("sconv", [48, 1536])
    sssm_d = din("sssm", [16, 1024, 128])
    spool_d = din("spool", [240, 512])
    win_d = din("w_in", [1024, 5648])
    wa_d = din("w_a", [1024, 1024])
    wb_d = din("w_b", [512, 1024])
    wo_d = din("w_out", [1024, 1024])
    pmix_d = din("pool_mix", [4, 128, 128])
    npre_d = din("npre_col", [128, 8])
    convw_d = din("convw_col", [128, 12, 4])
    convb_d = din("convb_col", [128, 12])
    ssmn_d = din("ssmn_col", [128, 8])
    pscale_d = din("pscale_col", [128, 4])
    dtb_d = din("dtb_bc", [128, 16])
    alog_d = din("alog_bc", [128, 16])
    dskip_d = din("dskip_bc", [128, 16])
    npost_d = din("npost_bc", [128, 1024])

    y_d = dout("y", [NCHUNK, 128, 1024])
    convp_d = dout("convp", [3, 1536])
    ssmp_d = dout("ssmp", [1024, 128])
    poolp_d = dout("poolp", [15, 512])
    convs_d = dout("convs", [48, 1536])
    ssms_d = dout("ssms", [16, 1024, 128])
    pools_d = dout("pools", [240, 512])
    dbg_outs = {}
    OUTT = Tile("dram_out", collector=True)

    ARENA = 212800
    arena = nc.alloc_sbuf_tensor("arena", [128, ARENA // 4], F32)
    base = nc.lookup_mloc(arena).addr
    cur = [0]

    def nbytes(shape, dt):
        n = 1
        for s in shape[1:]:
            n *= s
        return n * (2 if dt == BF16 else 4)

    def alloc(name, shape, dt, at=None, tile=None):
        sz = (nbytes(shape, dt) + 31) // 32 * 32
        if at is None:
            off = cur[0]
            cur[0] += sz
            assert cur[0] <= ARENA, (name, cur[0])
        else:
            off = at
        h = nc.alloc_sbuf_tensor_at(name, shape, dt, offset=base + off)
        b = Buf(h, tile if tile is not None else Tile(name))
        b_off[name] = off
        return b

    b_off = {}

    WINX = alloc("winx", [128, 8, 2576], BF16)
    WINZ = alloc("winz", [128, 8, 1024], BF16)
    WING = alloc("wing", [128, 8, 2048], BF16)
    WA = alloc("wa", [128, 8, 1024], BF16)
    WB = alloc("wb", [128, 4, 1024], BF16)
    WO = alloc("wo", [128, 8, 1024], BF16)
    PMIX = alloc("pmix", [128, 4, 128], BF16)
    CT_ = Tile("consts", collector=True)
    IDB = alloc("idb", [128, 128], BF16, tile=CT_)
    IDF = alloc("idf", [128, 128], F32, tile=CT_)
    TRI = alloc("tri", [128, 128], BF16, tile=CT_)
    LST = alloc("lst", [128, 128], BF16, tile=CT_)
    ONE = alloc("one", [128, 128], BF16, tile=CT_)
    TRIS = alloc("tris", [128, 128], BF16, tile=CT_)
    LSTS = alloc("lsts", [128, 128], BF16, tile=CT_)
    SSQ = alloc("ssq", [128, 128], BF16, tile=CT_)
    SQM = alloc("sqm", [128, 16], F32, tile=CT_)
    NPRE = alloc("npre", [128, 8], F32, tile=CT_)
    CONVW = alloc("convw", [128, 12, 4], F32, tile=CT_)
    CONVB = alloc("convb", [128, 12], F32, tile=CT_)
    SSMN = alloc("ssmn", [128, 8], F32, tile=CT_)
    PSC = alloc("psc", [128, 4], F32, tile=CT_)
    DTB = alloc("dtb", [128, 16], F32, tile=CT_)
    ABC = alloc("abc", [128, 16], F32, tile=CT_)
    DSK = alloc("dsk", [128, 16], F32, tile=CT_)
    NPOST = alloc("npostb", [128, 1024], F32, tile=CT_)
    INVC = alloc("invc", [128, 16], F32, tile=CT_)
    SCR = alloc("scr", [128, 16], F32)
    NEGM = alloc("negm", [128, 128], BF16, tile=CT_)
    NEGMS = alloc("negms", [128, 128], BF16, tile=CT_)
    NACS = alloc("nacs", [128, 16], F32)
    XB = [alloc("xb0", [128, 1024], F32), alloc("xb1", [128, 1024], F32)]
    BFA = alloc("bfa", [128, 1024], BF16)
    XN = alloc("xn", [128, 1024], BF16)
    HTs = [alloc("ht0", [128, 8, 128], BF16), alloc("ht1", [128, 8, 128], BF16)]
    XBC = alloc("xbc", [128, 12, 176], F32)
    UX = alloc("ux", [128, 4, 143], F32)
    CACC = [alloc("cacc%d" % i, [128, 128], F32) for i in range(4)]
    XSTs = [alloc("xst0", [128, 8, 128], BF16), alloc("xst1", [128, 8, 128], BF16)]
    BTs = [alloc("bt0", [128, 2, 128], BF16), alloc("bt1", [128, 2, 128], BF16)]
    CTTs = [alloc("ct0", [128, 2, 128], BF16), alloc("ct1", [128, 2, 128], BF16)]
    GS = alloc("gs", [128, 4, 128], F32)
    DTRs = [alloc("dtr0", [128, 16], F32), alloc("dtr1", [128, 16], F32)]
    DTs_ = [alloc("dt%d" % i, [128, 16], F32) for i in range(2)]
    ADTs_ = [alloc("adt%d" % i, [128, 16], F32) for i in range(2)]
    DIF = alloc("dif", [128, 16], F32)
    EACs_ = [alloc("eac%d" % i, [128, 16], F32) for i in range(2)]
    DTE = alloc("dte", [128, 16], F32)
    CDs_ = [alloc("cd%d" % i, [128, 16], F32) for i in range(2)]
    DDTs_ = [alloc("ddt%d" % i, [128, 16], F32) for i in range(2)]
    AHIs_ = [alloc("ahi%d" % i, [128, 16], BF16) for i in range(2)]
    ALOs_ = [alloc("alo%d" % i, [128, 16], BF16) for i in range(2)]
    BTOK = alloc("btok", [128, 256], BF16)
    XR = alloc("xr", [128, 1024], BF16)
    XRD = alloc("xrd", [128, 1024], BF16)
    MG = Buf(XR.h, XR.t)
    MGT_h = nc.alloc_sbuf_tensor_at("mgt", [128, 8, 128], BF16, offset=base + b_off["xrd"])
    MGT = Buf(MGT_h, XRD.t)
    R1T = [Tile("r1_%d" % i) for i in range(5)]
    r1 = cur[0]
    cur[0] += 10240
    AMH = alloc("amh", [128, 8, 128], BF16, at=r1, tile=R1T[0])
    AML = alloc("aml", [128, 8, 128], BF16, at=r1 + 2048, tile=R1T[1])
    DEC = alloc("dec", [128, 8, 128], BF16, at=r1 + 4096, tile=R1T[2])
    MMT = alloc("mmt", [128, 8, 128], BF16, at=r1 + 6144, tile=R1T[3])
    YT = alloc("yt", [128, 512], F32, at=r1 + 8192, tile=R1T[4])
    TA = alloc("ta", [128, 512], F32, at=r1, tile=R1T[0])
    TAJ = alloc("taj", [128, 1024], BF16, at=r1, tile=R1T[0])
    QAB = alloc("qab", [128, 1024], F32, at=r1 + 4096, tile=R1T[2])
    TB = alloc("tb", [128, 512], F32, at=r1 + 2048, tile=R1T[1])
    QA = alloc("qa", [128, 512], F32, at=r1 + 4096, tile=R1T[2])
    QB = alloc("qb", [128, 512], F32, at=r1 + 6144, tile=R1T[3])
    OT = alloc("ot", [128, 512], F32, at=r1 + 8192, tile=R1T[4])
    CBM = alloc("cbm", [128, 2, 128], BF16)
    STATE = alloc("state", [128, 1024], F32)
    STBF = alloc("stbf", [128, 1024], BF16)
    Y = alloc("y", [128, 1024], F32)
    ZS = alloc("zs", [128, 512], F32)
    AMH0 = alloc("amh0", [128, 8, 128], BF16, at=b_off["y"], tile=Y.t)
    AML0 = alloc("aml0", [128, 8, 128], BF16, at=b_off["y"] + 2048, tile=Y.t)
    P2p = alloc("p2", [128, 144], F32)
    P4p = alloc("p4", [128, 144], F32)
    P8p = alloc("p8", [128, 144], F32)
    P16 = alloc("p16", [128, 128], F32)
    PLD = alloc("pld", [128, 4, 128], BF16)
    YBTs = [alloc("ybt0", [128, 4, 128], BF16), alloc("ybt1", [128, 4, 128], BF16)]
    SSPRE = alloc("sspre", [128, 1], F32)
    RPRE = alloc("rpre", [128, 1], F32)
    SSG = alloc("ssg", [128, 2], F32)
    RG = alloc("rg", [128, 2], F32)
    SSP = alloc("ssp", [128, 2], F32)
    RP = alloc("rp", [128, 1], F32)
    UXS_h = nc.alloc_sbuf_tensor_at("uxs", [128, 4, 368], F32, offset=base + b_off["state"])
    assert b_off["stbf"] == b_off["state"] + 4096
    UXST = Tile("uxs")
    wx = b_off["winx"]
    H0 = [alloc("h0_%d" % i, [128, 8, 128], F32, at=wx + i * 4096) for i in range(2)]
    H0B = [alloc("h0b_%d" % i, [128, 8, 128], BF16, at=wx + 8192 + i * 2048) for i in range(2)]
    H0T = [alloc("h0t_%d" % i, [128, 1024], BF16, at=wx + 12288 + i * 2048) for i in range(2)]
    CTM = [alloc("ctm_%d" % i, [128, 2, 128], BF16, at=wx + 16384 + i * 512) for i in range(2)]
    BM = [alloc("bm_%d" % i, [128, 256], BF16, at=wx + 17408 + i * 512) for i in range(2)]
    H0.append(alloc("h0_2", [128, 8, 128], F32, at=r1))
    H0B.append(alloc("h0b_2", [128, 8, 128], BF16, at=r1 + 4096))
    CDC = alloc("cdc", [128, 8, 16], F32, at=wx + 18432)
    CSTG = alloc("cstg", [128, 12, 48], F32, at=wx + 18944)
    PSTG = alloc("pstg", [128, 4, 240], F32, at=wx + 18944 + 2304)
    OSTG = alloc("ostg", [128, 1536], F32, at=wx + 18944 + 2304 + 3840)
    ADTX = alloc("adtx", [128, 1024], F32, at=wx + 18944 + 2304 + 3840 + 6144)
    P2s = alloc("p2s", [128, 368], F32, at=wx + 35328)
    P4s = alloc("p4s", [128, 368], F32, at=wx + 35328 + 1472)
    P8s = alloc("p8s", [128, 368], F32, at=wx + 35328 + 2944)
    assert 35328 + 3 * 1472 <= 41216
    SCV = alloc("scv", [128, 1536], F32, at=r1, tile=R1T[0])
    SPL = alloc("spl", [128, 2, 512], F32, at=r1 + 6144, tile=R1T[3])
    print("SBUF used", cur[0], "of", ARENA)

    PB = []
    pb45 = es.enter_context(nc.psum_tensor("pb45", [128, 1024], F32))
    for i in range(8):
        if i == 2:
            h = es.enter_context(nc.psum_tensor("pb2", [128, 1024], BF16))
        elif i == 4:
            h = pb45[:, 0:512]
        elif i == 5:
            h = pb45[:, 512:1024]
        else:
            h = es.enter_context(nc.psum_tensor("pb%d" % i, [128, 512], F32))
        PB.append(Buf(h, Tile("pb%d" % i, psum=True)))
    PT = PB[2]
    B3 = PB[3]
    PT2 = PB[7].h[:, :].bitcast(BF16)
    PT2t = PB[7].t
    B3dt = B3.t
    B3ac = B3.t
    B3cb = B3.t

    def tl(bufs):
        return [b.t if isinstance(b, Buf) else b for b in bufs]

    def fsz(ap):
        n = 1
        for d in ap.shape[1:]:
            n *= d
        return n

    def MM(out, lhsT, rhs, start, stop, r, w, first):
        n = fsz(rhs)
        S.op("pe", lambda e: e.matmul(out, lhsT=lhsT, rhs=rhs, start=start, stop=stop),
             tl(r), tl(w), acc=not first, cost=0.06 + max(n, 64) / 2000.0)

    def TR(out, in_, ident, r, w, first):
        S.op("pe", lambda e: e.transpose(out, in_, ident), tl(r) + [CT_], tl(w), acc=not first, cost=0.12)

    def ecost(eng, out):
        n = fsz(out)
        if eng == "pool":
            return 0.25 + n * 0.0017
        if eng == "act":
            return 0.25 + n * 0.00085
        return 0.15 + n * 0.00105

    def VT(eng, out, in0, in1, op, r, w, acc=False):
        S.op(eng, lambda e: e.tensor_tensor(out=out, in0=in0, in1=in1, op=op), tl(r), tl(w), acc=acc,
             cost=ecost(eng, out))

    def VS(eng, out, in0, s1, s2, op0, op1, r, w, acc=False):
        if s2 is None:
            S.op(eng, lambda e: e.tensor_scalar(out=out, in0=in0, scalar1=s1, scalar2=None, op0=op0),
                 tl(r), tl(w), acc=acc, cost=ecost(eng, out))
        else:
            S.op(eng, lambda e: e.tensor_scalar(out=out, in0=in0, scalar1=s1, scalar2=s2, op0=op0, op1=op1),
                 tl(r), tl(w), acc=acc, cost=ecost(eng, out))

    def STT(eng, out, in0, sc, in1, op0, op1, r, w, acc=False):
        S.op(eng, lambda e: e.scalar_tensor_tensor(out=out, in0=in0, scalar=sc, in1=in1, op0=op0, op1=op1),
             tl(r), tl(w), acc=acc, cost=ecost(eng, out))

    def CP(eng, out, in_, r, w, acc=False):
        if eng == "act":
            S.op(eng, lambda e: e.activation(out=out, in_=in_, func=AF.Copy), tl(r), tl(w), acc=acc,
                 cost=ecost(eng, out))
        else:
            S.op(eng, lambda e: e.tensor_copy(out=out, in_=in_), tl(r), tl(w), acc=acc, cost=ecost(eng, out))

    def ACT(out, in_, func, r, w, bias=None, scale=None, accum=None, acc=False):
        kw = {}
        if bias is not None:
            kw["bias"] = bias
        if scale is not None:
            kw["scale"] = scale
        if accum is not None:
            kw["accum_out"] = accum
        aset = "A" if func in (AF.Silu, AF.Tanh) else ("B" if func in (AF.Exp, AF.Ln) else None)
        S.op("act", lambda e: e.activation(out=out, in_=in_, func=func, **kw), tl(r), tl(w), acc=acc,
             cost=ecost("act", out), aset=aset)

    def MEMSET(eng, ap, val, w, acc=False):
        S.op(eng, lambda e: e.memset(ap, val), [], tl(w), acc=acc)

    def DMA(eng, out, in_, r, w, key, acc=False):
        S.op(eng, lambda e: e.dma_start(out=out, in_=in_), tl(r), tl(w), acc=acc, dma=key, cost=3.0)

    def bc(ap, shape, axis):
        return ap.unsqueeze(axis).to_broadcast(shape)

    winv = win_d.rearrange("(kc p) e -> p kc e", p=128)
    WXB = Tile("winx_b")
    DMA("pool", WINX.h[:, :, 0:1536], winv[:, :, 1024:2560], [], [WINX], WINX.t)
    DMA("pool", WINX.h[:, :, 1536:2576], winv[:, :, 2560:3600], [], [WXB], WXB)
    small = [(NPRE, npre_d), (CONVW, convw_d), (CONVB, convb_d), (SSMN, ssmn_d), (PSC, pscale_d),
             (DTB, dtb_d), (ABC, alog_d), (DSK, dskip_d), (NPOST, npost_d)]
    ctl = {}

    def ct(bf):
        k = id(bf)
        if k not in ctl:
            ctl[k] = Tile("c%d" % len(ctl))
        return ctl[k]

    for b_, d_ in small:
        if len(d_.shape) == 3:
            DMA("act", b_.h[:, :, :], d_[:, :, :], [], [ct(b_)], ct(b_))
        else:
            DMA("act", b_.h[:, :], d_[:, :], [], [ct(b_)], ct(b_))
    DMA("pool", WINZ.h[:, :, :], winv[:, :, 0:1024], [], [WINZ], WINZ.t)
    DMA("pool", PMIX.h[:, :, :], pmix_d.rearrange("k c d -> c k d"), [], [PMIX], PMIX.t)
    DMA("pool", WA.h[:, :, :], wa_d.rearrange("(kc p) e -> p kc e", p=128), [], [WA], WA.t)
    DMA("pool", WB.h[:, :, :], wb_d.rearrange("(kc p) e -> p kc e", p=128), [], [WB], WB.t)
    DMA("pool", WING.h[:, :, :], winv[:, :, 3600:5648], [], [WING], WING.t)
    DMA("pool", WO.h[:, :, :], wo_d.rearrange("(kc p) e -> p kc e", p=128), [], [WO], WO.t)

    def aff(bf, eng_ap, pattern, cmp, fill, base_, cm):
        S.op("pool", lambda e: e.affine_select(out=eng_ap, in_=eng_ap, pattern=pattern, compare_op=cmp,
                                                fill=fill, base=base_, channel_multiplier=cm), [ct(bf)], [ct(bf)])

    for I_ in (IDB, IDF):
        MEMSET("pool", I_.h[:, :], 0.0, [ct(I_)])
        aff(I_, I_.h[:, :], [[-1, 128]], ALU.not_equal, 1.0, 0, 1)
    MEMSET("pool", TRI.h[:, :], 1.0, [ct(TRI)])
    aff(TRI, TRI.h[:, :], [[1, 128]], ALU.is_ge, 0.0, 0, -1)
    MEMSET("pool", LST.h[:, :], 1.0, [ct(LST)])
    aff(LST, LST.h[:, :], [[-1, 128]], ALU.is_gt, 0.0, 0, 1)
    MEMSET("pool", ONE.h[:, :], 1.0, [ct(ONE)])
    MEMSET("pool", SSQ.h[:, :], 1.0, [ct(SSQ)])
    ssq3 = SSQ.h[:, :].rearrange("p (b j) -> p b j", j=8)
    aff(SSQ, ssq3, [[-8, 16], [0, 8]], ALU.is_ge, 0.0, 0, 1)
    aff(SSQ, ssq3, [[8, 16], [0, 8]], ALU.is_ge, 0.0, 7, -1)
    MEMSET("pool", SQM.h[:, :], 1.0, [ct(SQM)])
    aff(SQM, SQM.h[:, :], [[-8, 16]], ALU.is_ge, 0.0, 0, 1)
    aff(SQM, SQM.h[:, :], [[8, 16]], ALU.is_ge, 0.0, 7, -1)
    S.op("pool", lambda e: e.tensor_tensor(out=TRIS.h[:, :], in0=TRI.h[:, :], in1=SSQ.h[:, :], op=ALU.mult),
         [ct(TRI), ct(SSQ)], [ct(TRIS)])
    S.op("pool", lambda e: e.tensor_tensor(out=LSTS.h[:, :], in0=LST.h[:, :], in1=SSQ.h[:, :], op=ALU.mult),
         [ct(LST), ct(SSQ)], [ct(LSTS)])
    for N_, T_ in ((NEGM, TRI), (NEGMS, TRIS)):
        S.op("pool", (lambda N_, T_: (lambda e: e.tensor_scalar(out=N_.h[:, :], in0=T_.h[:, :], scalar1=-1.0,
                                                                  scalar2=30000.0, op0=ALU.add, op1=ALU.mult)))(N_, T_),
             [ct(T_)], [ct(N_)])
    for t_ in range(16):
        MEMSET("pool", INVC.h[:, t_:t_ + 1], 1.0 / (t_ + 1), [ct(INVC)], acc=(t_ > 0))
    S.op("act", lambda e: e.activation(out=ABC.h[:, :], in_=ABC.h[:, :], func=AF.Exp), [ct(ABC)], [ct(ABC)], aset="B")
    S.op("act", lambda e: e.mul(ABC.h[:, :], ABC.h[:, :], -1.0), [ct(ABC)], [ct(ABC)])
    S.op("pool", lambda e: e.memset(SCR.h[:, 3:4], 0.0), list(ctl.values()), [CT_, SCR.t])

    class _C:
        pass
    c = _C()

    def setpar(p):
        c.HT = HTs[p]
        c.XST = XSTs[p]
        c.YAT = Buf(XSTs[p].h, XSTs[p].t)
        c.BT = BTs[p]
        c.CTT = CTTs[p]
        c.YBT = YBTs[p]
        c.DTR = DTRs[p]
        c.DT = DTs_[p]
        c.ADT = ADTs_[p]
        c.EAC = EACs_[p]
        c.CD = CDs_[p]
        c.DDT = DDTs_[p]
        c.AHI = AHIs_[p]
        c.ALO = ALOs_[p]

    def load_x(ci):
        xb = XB[ci % 2]
        DMA("sp", xb.h[:, :], x_d[ci], [], [xb], xb.t)

    def proj_feature_major(ci, sample):
        L = 8 if sample else 128
        nseq = 16 if sample else 1
        hc = 3
        hp = 15
        if sample:
            xbc4 = XBC.h[:, :, :].rearrange("p t (b j) -> p t b j", j=11)
            ux4 = UXS_h[:, :, :].rearrange("p t (b j) -> p t b j", j=23)
            uxt = UXST
        else:
            xbc4 = XBC.h[:, :, 0:131].rearrange("p t (b j) -> p t b j", b=1)
            ux4 = UX.h[:, :, :].rearrange("p t (b j) -> p t b j", b=1)
            uxt = UX.t
        nb = 0
        for bl in range(3):
            bank = PB[nb % 2]
            nb += 1
            for j in range(4):
                tile = bl * 4 + j
                for kc in range(8):
                    MM(bank.h[:, j * 128:(j + 1) * 128], WINX.h[:, kc, tile * 128:(tile + 1) * 128], c.HT.h[:, kc, :],
                       kc == 0, kc == 7, [WINX, c.HT], [bank], first=(j == 0 and kc == 0))
            for j in range(4):
                tile = bl * 4 + j
                eng = "act" if bl % 2 == 0 else "dve"
                CP(eng, xbc4[:, tile, :, hc:hc + L],
                   bank.h[:, j * 128:(j + 1) * 128].rearrange("p (b j) -> p b j", j=L),
                   [bank], [XBC], acc=True)
        for kc in range(8):
            MM(B3.h[:, 0:16], c.HT.h[:, kc, :], WINX.h[:, kc, 1536:1552], kc == 0, kc == 7, [WXB, c.HT], [B3dt],
               first=(kc == 0))
        VT("dve", c.DTR.h[:, :], B3.h[:, 0:16], DTB.h[:, :], ALU.add, [B3dt, CT_], [c.DTR])
        bank = PB[nb % 2]
        nb += 1
        for j in range(4):
            for kc in range(8):
                MM(bank.h[:, j * 128:(j + 1) * 128], WINX.h[:, kc, 1552 + j * 128:1552 + (j + 1) * 128],
                   c.HT.h[:, kc, :], kc == 0, kc == 7, [WXB, c.HT], [bank], first=(j == 0 and kc == 0))
        for j in range(4):
            CP("dve", ux4[:, j, :, hp:hp + L], bank.h[:, j * 128:(j + 1) * 128].rearrange("p (b j) -> p b j", j=L),
               [bank], [uxt], acc=True)
        bank = PB[nb % 2]
        nb += 1
        for j in range(4):
            for kc in range(8):
                MM(bank.h[:, j * 128:(j + 1) * 128], WINX.h[:, kc, 2064 + j * 128:2064 + (j + 1) * 128],
                   c.HT.h[:, kc, :], kc == 0, kc == 7, [WXB, c.HT], [bank], first=(j == 0 and kc == 0))
        ACT(GS.h[:, :, :], bank.h[:, :].rearrange("p (k t) -> p k t", k=4), AF.Silu, [bank], [GS])
        return xbc4, ux4, uxt

    def conv_and_silu(xbc4, sample):
        L = 8 if sample else 128
        for pair in range(6):
            tiles = (2 * pair, 2 * pair + 1)
            accs = [CACC[(2 * pair) % 4], CACC[(2 * pair + 1) % 4]]
            a3s = [a.h[:, :].rearrange("p (b j) -> p b j", j=L) for a in accs]
            for i_, tile in enumerate(tiles):
                S.op("act", (lambda o_, i__, sc_, bi_: (lambda e: e.activation(out=o_, in_=i__, func=AF.Identity,
                                                                                   scale=sc_, bias=bi_)))(
                    a3s[i_], xbc4[:, tile, :, 0:L], CONVW.h[:, tile, 0:1], CONVB.h[:, tile:tile + 1]),
                    tl([XBC, CT_]), tl([accs[i_]]))
            for k in range(1, 4):
                for i_, tile in enumerate(tiles):
                    STT("dve", a3s[i_], xbc4[:, tile, :, k:k + L], CONVW.h[:, tile, k:k + 1], a3s[i_], ALU.mult,
                        ALU.add, [XBC, CT_, accs[i_]], [accs[i_]], acc=True)
            for i_, tile in enumerate(tiles):
                acc = accs[i_]
                if tile < 8:
                    ACT(c.XST.h[:, tile, :], acc.h[:, :], AF.Silu, [acc], [c.XST], acc=True)
                elif tile < 10:
                    ACT(c.BT.h[:, tile - 8, :], acc.h[:, :], AF.Silu, [acc], [c.BT], acc=True)
                else:
                    ACT(c.CTT.h[:, tile - 10, :], acc.h[:, :], AF.Silu, [acc], [c.CTT], acc=True)

    def pool_branch(ci, ux4, uxt, sample):
        L = 8 if sample else 128
        nseq = 16 if sample else 1
        E = 15 + L
        P2, P4, P8 = (P2s, P4s, P8s) if sample else (P2p, P4p, P8p)
        p2 = P2.h[:, 0:nseq * (E - 1)].rearrange("p (b j) -> p b j", b=nseq)
        p4 = P4.h[:, 0:nseq * (E - 3)].rearrange("p (b j) -> p b j", b=nseq)
        p8 = P8.h[:, 0:nseq * (E - 7)].rearrange("p (b j) -> p b j", b=nseq)
        p16 = P16.h[:, :].rearrange("p (b j) -> p b j", b=nseq)
        for k, w in enumerate((2, 4, 8, 16)):
            u = ux4[:, k, :, :]
            eng = "dve" if k % 2 == 0 else "pool"
            VT(eng, p2, u[:, :, 1:E], u[:, :, 0:E - 1], ALU.add, [uxt], [P2])
            Sv = p2[:, :, 14:14 + L]
            rd = [P2]
            if w >= 4:
                VT(eng, p4, p2[:, :, 2:E - 1], p2[:, :, 0:E - 3], ALU.add, [P2], [P4])
                Sv = p4[:, :, 12:12 + L]
                rd = [P4]
            if w >= 8:
                VT(eng, p8, p4[:, :, 4:E - 3], p4[:, :, 0:E - 7], ALU.add, [P4], [P8])
                Sv = p8[:, :, 8:8 + L]
                rd = [P8]
            if w >= 16:
                VT(eng, p16, p8[:, :, 8:E - 7], p8[:, :, 0:E - 15], ALU.add, [P8], [P16])
                Sv = p16
                rd = [P16]
            pld3 = PLD.h[:, k, :].rearrange("p (b j) -> p b j", b=nseq)
            STT("dve", pld3, Sv, 1.0 / w, u[:, :, 15:15 + L], ALU.mult, ALU.subtract, rd + [uxt], [PLD], acc=True)
            if ci == 0 and not sample:
                VT(eng, SCR.h[:, 0:w - 1], Sv[:, 0, 0:w - 1], INVC.h[:, 0:w - 1], ALU.mult, rd + [CT_], [SCR])
                VT(eng, PLD.h[:, k, 0:w - 1], SCR.h[:, 0:w - 1], u[:, 0, 15:15 + w - 1], ALU.subtract,
                   [SCR, uxt], [PLD], acc=True)
        bank = PB[0]
        for k in range(4):
            MM(bank.h[:, k * 128:(k + 1) * 128], PMIX.h[:, k, :], PLD.h[:, k, :], True, True, [PMIX, PLD], [bank],
               first=(k == 0))
        for k in range(4):
            STT("dve", c.YBT.h[:, k, :], bank.h[:, k * 128:(k + 1) * 128], PSC.h[:, k:k + 1], GS.h[:, k, :],
                ALU.mult, ALU.mult, [bank, GS, CT_], [c.YBT], acc=True)

    def dt_path(sample):
        tri = TRIS if sample else TRI
        ssq = SSQ if sample else ONE
        ACT(c.DTR.h[:, :], c.DTR.h[:, :], AF.Exp, [c.DTR], [c.DTR])
        ACT(c.DT.h[:, :], c.DTR.h[:, :], AF.Ln, [c.DTR], [c.DT], bias=1.0)
        VT("dve", c.ADT.h[:, :], c.DT.h[:, :], ABC.h[:, :], ALU.mult, [c.DT, CT_], [c.ADT])
        CP("dve", c.AHI.h[:, :], c.ADT.h[:, :], [c.ADT], [c.AHI])
        VT("dve", c.ALO.h[:, :], c.ADT.h[:, :], c.AHI.h[:, :], ALU.subtract, [c.ADT, c.AHI], [c.ALO])
        MM(B3.h[:, 16:32], tri.h[:, :], c.AHI.h[:, :], True, False, [CT_, c.AHI], [B3ac], first=True)
        MM(B3.h[:, 16:32], tri.h[:, :], c.ALO.h[:, :], False, True, [CT_, c.ALO], [B3ac], first=False)
        MM(B3.h[:, 32:48], ssq.h[:, :], c.AHI.h[:, :], True, False, [CT_, c.AHI], [B3ac], first=False)
        MM(B3.h[:, 32:48], ssq.h[:, :], c.ALO.h[:, :], False, True, [CT_, c.ALO], [B3ac], first=False)
        VS("dve", NACS.h[:, :], B3.h[:, 16:32], -1.0, None, ALU.mult, None, [B3ac], [NACS])
        ACT(c.EAC.h[:, :], B3.h[:, 16:32], AF.Exp, [B3ac], [c.EAC])
        ACT(c.CD.h[:, :], B3.h[:, 32:48], AF.Exp, [B3ac], [c.CD])
        VT("dve", DIF.h[:, :], B3.h[:, 32:48], NACS.h[:, :], ALU.add, [B3ac, NACS], [DIF])
        ACT(DTE.h[:, :], DIF.h[:, :], AF.Exp, [DIF], [DTE])
        VT("dve", c.DDT.h[:, :], c.DT.h[:, :], DTE.h[:, :], ALU.mult, [c.DT, DTE], [c.DDT])

    def to_token_major():
        for j in range(8):
            TR(PT2[:, j * 128:(j + 1) * 128], c.XST.h[:, j, :], IDB.h[:, :], [c.XST], [PT2t], first=(j == 0))
        CP("dve", BFA.h[:, :], PT2[:, :], [PT2t], [BFA])
        for g in range(2):
            TR(PT2[:, g * 128:(g + 1) * 128], c.BT.h[:, g, :], IDB.h[:, :], [c.BT], [PT2t], first=(g == 0))
        CP("act", BTOK.h[:, :], PT2[:, 0:256], [PT2t], [BTOK])

    def build_masks(sample, g):
        tri = TRIS if sample else TRI
        AMH_, AML_ = (AMH0, AML0) if g == 0 else (AMH, AML)
        VT("pool", AMH_.h[:, :, :], bc(tri.h[:, :], [128, 8, 128], 1), bc(c.AHI.h[:, 8 * g:8 * g + 8], [128, 8, 128], 2),
           ALU.mult, [CT_, c.AHI], [AMH_], acc=(g == 0))
        VT("pool", AML_.h[:, :, :], bc(tri.h[:, :], [128, 8, 128], 1), bc(c.ALO.h[:, 8 * g:8 * g + 8], [128, 8, 128], 2),
           ALU.mult, [CT_, c.ALO], [AML_], acc=(g == 0))

    def ssd_intra(ci, sample, g):
        tri = TRIS if sample else TRI
        lst = LSTS if sample else LST
        AMH_, AML_ = (AMH0, AML0) if g == 0 else (AMH, AML)
        for q in range(2):
            bank = PB[4 + q]
            MM(bank.h[:, :], lst.h[:, :], AMH_.h[:, 4 * q:4 * q + 4, :].rearrange("p h l -> p (h l)"), True, False,
               [CT_, AMH_], [bank], first=True)
            MM(bank.h[:, :], lst.h[:, :], AML_.h[:, 4 * q:4 * q + 4, :].rearrange("p h l -> p (h l)"), False, True,
               [CT_, AML_], [bank], first=False)
            ACT(DEC.h[:, 4 * q:4 * q + 4, :], bank.h[:, :].rearrange("p (h l) -> p h l", h=4), AF.Exp, [bank], [DEC],
                acc=(q == 1))
        VT("dve", MMT.h[:, :, :], DEC.h[:, :, :], bc(CBM.h[:, g, :], [128, 8, 128], 1), ALU.mult, [DEC, CBM], [MMT])
        bank = PB[6]
        for hh in range(8):
            h = 8 * g + hh
            MM(bank.h[:, hh * 64:(hh + 1) * 64], MMT.h[:, hh, :], XR.h[:, h * 64:(h + 1) * 64], True, True,
               [MMT, XR], [bank], first=(hh == 0))

    def ssd_prompt(ci):
        build_masks(False, 0)
        to_token_major()
        xs3 = BFA.h[:, :].rearrange("p (h d) -> p h d", d=64)
        VT("pool", XR.h[:, :].rearrange("p (h d) -> p h d", d=64), xs3, bc(c.DT.h[:, :], [128, 16, 64], 2), ALU.mult,
           [BFA, c.DT], [XR])
        build_masks(False, 1)
        VT("pool", XRD.h[:, :].rearrange("p (h d) -> p h d", d=64), xs3, bc(c.DDT.h[:, :], [128, 16, 64], 2), ALU.mult,
           [BFA, c.DDT], [XRD])
        for g in range(2):
            MM(B3.h[:, 64 + g * 128:64 + (g + 1) * 128], c.BT.h[:, g, :], c.CTT.h[:, g, :], True, True, [c.BT, c.CTT], [B3cb],
               first=(g == 0))
        VT("dve", CBM.h[:, :, :], B3.h[:, 64:320].rearrange("p (g l) -> p g l", g=2),
           bc(TRI.h[:, :], [128, 2, 128], 1), ALU.mult, [B3cb, CT_], [CBM])
        for g in range(2):
            ssd_intra(ci, False, g)
            ysl = Y.h[:, g * 512:(g + 1) * 512]
            y3 = ysl.rearrange("p (h d) -> p h d", d=64)
            if ci > 0:
                MM(PB[7].h[:, :], c.CTT.h[:, g, :], STBF.h[:, g * 512:(g + 1) * 512], True, True, [c.CTT, STBF], [PB[7]],
                   first=True)
                VT("dve", y3, PB[7].h[:, :].rearrange("p (h d) -> p h d", d=64),
                   bc(c.EAC.h[:, 8 * g:8 * g + 8], [128, 8, 64], 2), ALU.mult, [PB[7], c.EAC], [Y], acc=(g == 1))
                VT("dve", ysl, ysl, PB[6].h[:, :], ALU.add, [Y, PB[6]], [Y], acc=True)
            else:
                CP("dve", ysl, PB[6].h[:, :], [PB[6]], [Y], acc=(g == 1))
            VT("pool", YT.h[:, :].rearrange("p (h d) -> p h d", d=64),
               BFA.h[:, g * 512:(g + 1) * 512].rearrange("p (h d) -> p h d", d=64),
               bc(DSK.h[:, 8 * g:8 * g + 8], [128, 8, 64], 2), ALU.mult, [BFA, CT_], [YT])
            VT("dve", ysl, ysl, YT.h[:, :], ALU.add, [Y, YT], [Y], acc=True)
            MM(PB[7].h[:, :], BTOK.h[:, g * 128:(g + 1) * 128], XRD.h[:, g * 512:(g + 1) * 512], True, True,
               [BTOK, XRD], [PB[7]], first=True)
            ssl = STATE.h[:, g * 512:(g + 1) * 512]
            if ci > 0:
                s3 = ssl.rearrange("p (h d) -> p h d", d=64)
                VT("dve", s3, s3, bc(c.CD.h[:, 8 * g:8 * g + 8], [128, 8, 64], 2), ALU.mult, [STATE, c.CD], [STATE],
                   acc=True)
                VT("dve", ssl, ssl, PB[7].h[:, :], ALU.add, [STATE, PB[7]], [STATE], acc=True)
            else:
                CP("dve", ssl, PB[7].h[:, :], [PB[7]], [STATE], acc=(g == 1))
            if ci < 15:
                CP("act", STBF.h[:, g * 512:(g + 1) * 512], ssl, [STATE], [STBF], acc=(g == 1))

    def gate_norm_transpose():
        for g in range(2):
            bank = PB[4 + g]
            for kc in range(8):
                MM(bank.h[:, :], c.HT.h[:, kc, :], WINZ.h[:, kc, g * 512:(g + 1) * 512], kc == 0, kc == 7, [c.HT, WINZ],
                   [bank], first=(kc == 0))
            ACT(ZS.h[:, :], bank.h[:, :], AF.Silu, [bank], [ZS])
            ysl = Y.h[:, g * 512:(g + 1) * 512]
            VT("dve", ysl, ysl, ZS.h[:, :], ALU.mult, [Y, ZS], [Y], acc=True)
            ACT(ZS.h[:, :], ysl, AF.Square, [Y], [ZS, SSG], accum=SSG.h[:, g:g + 1], acc=(g == 1))
        ACT(RG.h[:, :], SSG.h[:, :], AF.Ln, [SSG], [RG], bias=EPS, scale=1.0 / 512)
        ACT(RG.h[:, :], RG.h[:, :], AF.Exp, [RG], [RG], scale=-0.5)
        for g in range(2):
            VS("dve", BFA.h[:, g * 512:(g + 1) * 512], Y.h[:, g * 512:(g + 1) * 512], RG.h[:, g:g + 1], None, ALU.mult,
               None, [Y, RG], [BFA], acc=(g == 1))
        for j in range(8):
            TR(PT2[:, j * 128:(j + 1) * 128], BFA.h[:, j * 128:(j + 1) * 128], IDB.h[:, :], [BFA], [PT2t], first=(j == 0))
        VT("dve", c.YAT.h[:, :, :], PT2[:, :].rearrange("p (k t) -> p k t", k=8), bc(SSMN.h[:, :], [128, 8, 128], 2),
           ALU.mult, [PT2t, CT_], [c.YAT])

    def tail(ci):
        xb = XB[ci % 2]
        for cb in range(2):
            cs = slice(cb * 512, (cb + 1) * 512)
            for kc in range(8):
                MM(PB[4].h[:, :], c.YAT.h[:, kc, :], WA.h[:, kc, cs], kc == 0, kc == 7, [c.YAT, WA], [PB[4]], first=(kc == 0))
            for kc in range(4):
                MM(PB[5].h[:, :], c.YBT.h[:, kc, :], WB.h[:, kc, cs], kc == 0, kc == 3, [c.YBT, WB], [PB[5]], first=(kc == 0))
            for kc in range(8):
                MM(PB[6].h[:, :], c.HT.h[:, kc, :], WING.h[:, kc, cs], kc == 0, kc == 7, [c.HT, WING], [PB[6]],
                   first=(kc == 0))
            for kc in range(8):
                MM(PB[7].h[:, :], c.HT.h[:, kc, :], WING.h[:, kc, 1024 + cb * 512:1024 + (cb + 1) * 512], kc == 0, kc == 7,
                   [c.HT, WING], [PB[7]], first=(kc == 0))
            ACT(TA.h[:, :], PB[6].h[:, :], AF.Tanh, [PB[6]], [TA], scale=0.5)
            ACT(TB.h[:, :], PB[7].h[:, :], AF.Tanh, [PB[7]], [TB], scale=0.5)
            STT("dve", QA.h[:, :], TA.h[:, :], 1.0, PB[4].h[:, :], ALU.add, ALU.mult, [TA, PB[4]], [QA])
            STT("dve", QB.h[:, :], TB.h[:, :], 1.0, PB[5].h[:, :], ALU.add, ALU.mult, [TB, PB[5]], [QB])
            VT("dve", MG.h[:, cs], QA.h[:, :], QB.h[:, :], ALU.add, [QA, QB], [MG], acc=(cb == 1))
        for j in range(8):
            TR(PT2[:, j * 128:(j + 1) * 128], MG.h[:, j * 128:(j + 1) * 128], IDB.h[:, :], [MG], [PT2t], first=(j == 0))
        S.op("act", lambda e: e.mul(MGT.h[:, :, :], PT2[:, :].rearrange("p (k t) -> p k t", k=8), 0.5),
             [PT2t], [MGT.t])
        for ob in range(2):
            bank = PB[4 + ob]
            for kc in range(8):
                MM(bank.h[:, :], MGT.h[:, kc, :], WO.h[:, kc, ob * 512:(ob + 1) * 512], kc == 0, kc == 7, [MGT, WO],
                   [bank], first=(kc == 0))
        ACT(TAJ.h[:, :], pb45[:, :], AF.Square, [PB[4], PB[5]], [TAJ, SSP], accum=SSP.h[:, 0:1])
        ACT(RP.h[:, :], SSP.h[:, 0:1], AF.Ln, [SSP], [RP], bias=EPS, scale=1.0 / 1024)
        ACT(RP.h[:, :], RP.h[:, :], AF.Exp, [RP], [RP], scale=-0.5)
        STT("dve", QAB.h[:, :], pb45[:, :], RP.h[:, 0:1], NPOST.h[:, :], ALU.mult, ALU.mult,
            [PB[4], PB[5], RP, CT_], [R1T[2], R1T[3]])
        VT("dve", xb.h[:, :], xb.h[:, :], QAB.h[:, :], ALU.add, [xb, R1T[2], R1T[3]], [xb], acc=True)
        DMA("sp", y_d[ci], xb.h[:, :], [xb], [OUTT], xb.t)

    def norm_pre(ci):
        xb = XB[ci % 2]
        ACT(XN.h[:, :], xb.h[:, :], AF.Square, [xb], [XN, SSPRE], accum=SSPRE.h[:, :])
        ACT(RPRE.h[:, :], SSPRE.h[:, :], AF.Ln, [SSPRE], [RPRE], bias=EPS, scale=1.0 / 1024)
        ACT(RPRE.h[:, :], RPRE.h[:, :], AF.Exp, [RPRE], [RPRE], scale=-0.5)
        VS("dve", XN.h[:, :], xb.h[:, :], RPRE.h[:, 0:1], None, ALU.mult, None, [xb, RPRE], [XN])
        for j in range(8):
            TR(PT.h[:, j * 128:(j + 1) * 128], XN.h[:, j * 128:(j + 1) * 128], IDB.h[:, :], [XN], [PT], first=(j == 0))
        VT("dve", c.HT.h[:, :, :], PT.h[:, :].rearrange("p (k t) -> p k t", k=8), bc(NPRE.h[:, :], [128, 8, 128], 2),
           ALU.mult, [PT, CT_], [c.HT])

    def out_fp32_T(src_fn, ncols, tiles, stage_ap_fn, stage_tiles, dram_ap, reads, key, banks=(0, 1)):
        done = 0
        nb = 0
        nt = len(tiles)
        while done < nt:
            n = min(4, nt - done)
            bank = PB[banks[nb % 2]]
            nb += 1
            for j in range(n):
                TR(bank.h[0:ncols, j * 128:(j + 1) * 128], src_fn(tiles[done + j]), IDF.h[:, :], reads, [bank],
                   first=(j == 0))
            CP("dve", stage_ap_fn(done * 128, (done + n) * 128), bank.h[0:ncols, 0:n * 128], [bank], stage_tiles,
               acc=(done > 0))
            done += n
        DMA("sp", dram_ap, stage_ap_fn(0, nt * 128), stage_tiles, [OUTT], key)

    def stage1(ci):
        setpar(ci % 2)
        norm_pre(ci)
        xbc4, ux4, uxt = proj_feature_major(ci, False)
        dt_path(False)
        conv_and_silu(xbc4, False)
        pool_branch(ci, ux4, uxt, False)
        if ci < 15:
            CP("pool", XBC.h[:, :, 0:3], XBC.h[:, :, 128:131], [XBC], [XBC], acc=True)
            CP("pool", UX.h[:, :, 0:15], UX.h[:, :, 128:143], [UX], [UX], acc=True)

    def stage23(ci):
        setpar(ci % 2)
        ssd_prompt(ci)
        gate_norm_transpose()
        tail(ci)

    MEMSET("dve", XBC.h[:, :, 0:3], 0.0, [XBC])
    MEMSET("dve", UX.h[:, :, 0:15], 0.0, [UX])
    load_x(0)
    stage1(0)
    for ci in range(nprompt):
        sa = None
        if ci + 1 < nprompt or do_sample:
            load_x(ci + 1)
        if ci + 1 < nprompt:
            S.begin()
            stage1(ci + 1)
            sa = S.end()
        S.begin()
        stage23(ci)
        sb = S.end()
        S.run_merged([sa, sb])

    XRF = alloc("xrf", [128, 512], F32, at=b_off["xr"], tile=XR.t)
    XRDF = alloc("xrdf", [128, 512], F32, at=b_off["xrd"], tile=XRD.t)

    def prompt_state_outputs():
        out_fp32_T(lambda t: XBC.h[:, t, 128:131], 3, list(range(8)), lambda a, b: Y.h[0:3, a:b], [Y.t],
                   convp_d[:, 0:1024], [XBC], Y.t, banks=(6, 7))
        out_fp32_T(lambda t: XBC.h[:, t, 128:131], 3, list(range(8, 12)), lambda a, b: ZS.h[0:3, a:b], [ZS.t],
                   convp_d[:, 1024:1536], [XBC], ZS.t, banks=(6, 7))
        out_fp32_T(lambda t: UX.h[:, t, 128:143], 15, list(range(4)), lambda a, b: XRF.h[0:15, a:b], [XR.t],
                   poolp_d[:, :], [UX], XR.t, banks=(6, 7))
        ssmp_v = ssmp_d.rearrange("(j p) n -> p j n", p=128)
        for half in range(2):
            bank = PB[6 + half]
            for j in range(4):
                jj = half * 4 + j
                TR(bank.h[:, j * 128:(j + 1) * 128], STATE.h[:, jj * 128:(jj + 1) * 128], IDF.h[:, :], [STATE], [bank],
                   first=(j == 0))
            CP("dve", XRDF.h[:, :], bank.h[:, :], [bank], [XRD])
            DMA("sp", ssmp_v[:, half * 4:(half + 1) * 4, :], XRDF.h[:, :].rearrange("p (j n) -> p j n", j=4), [XRD],
                [OUTT], XRD.t)

    sctx = {}

    def sample_front():
        ci = 16
        setpar(0)
        norm_pre(ci)
        SCVT = [R1T[0], R1T[1], R1T[2]]
        SPLT = [R1T[3], R1T[4]]
        DMA("act", SCV.h[0:48, :], sconv_d[:, :], [], SCVT, R1T[0])
        DMA("act", SPL.h[0:120, :, :], spool_d.rearrange("(h r) c -> r h c", h=2), [], SPLT, R1T[3])
        xbc4s = XBC.h[:, :, :].rearrange("p t (b j) -> p t b j", j=11)
        ux4s = UXS_h[:, :, :].rearrange("p t (b j) -> p t b j", j=23)
        for tile in range(12):
            bank = PB[4 + (tile % 2)]
            TR(bank.h[:, 0:48], SCV.h[0:48, tile * 128:(tile + 1) * 128], IDF.h[0:48, 0:48], SCVT, [bank], first=True)
            CP("dve", xbc4s[:, tile, :, 0:3], bank.h[:, 0:48].rearrange("p (b k) -> p b k", k=3), [bank], [XBC], acc=True)
        first_ux = True
        for k in range(4):
            for half in range(2):
                bank = PB[4 + half]
                TR(bank.h[:, 0:120], SPL.h[0:120, half, k * 128:(k + 1) * 128], IDF.h[0:120, 0:120], SPLT, [bank],
                   first=True)
                CP("dve", ux4s[:, k, 8 * half:8 * half + 8, 0:15], bank.h[:, 0:120].rearrange("p (b k) -> p b k", k=15),
                   [bank, STATE, STBF], [UXST, STATE, STBF], acc=not first_ux)
                first_ux = False
        xbc4, ux4, uxt = proj_feature_major(ci, True)
        conv_and_silu(xbc4, True)
        alias_tiles = [b.t for b in H0 + H0B + H0T + CTM + BM] + [CDC.t, CSTG.t, PSTG.t, OSTG.t, ADTX.t, P2s.t, P4s.t, P8s.t]
        S.op("pool", lambda e: e.memset(SCR.h[:, 0:1], 0.0), [], [WINX.t, WXB] + alias_tiles + [SCR.t])
        for tile in range(12):
            CP("pool", CSTG.h[:, tile, :].rearrange("p (b k) -> p b k", k=3), xbc4s[:, tile, :, 8:11], [XBC], [CSTG],
               acc=(tile > 0))
        out_fp32_T(lambda t: CSTG.h[:, t, :], 48, list(range(12)), lambda a, b: OSTG.h[0:48, a:b], [OSTG.t],
                   convs_d[:, :], [CSTG], OSTG.t)
        sctx.update(ux4=ux4, uxt=uxt, ux4s=ux4s)

    def sample_rest():
        ci = 16
        setpar(0)
        ux4, uxt, ux4s = sctx["ux4"], sctx["uxt"], sctx["ux4s"]
        ssv = sso = None
        dt_path(True)
        build_masks(True, 0)
        build_masks(True, 1)
        to_token_major()
        xs3 = BFA.h[:, :].rearrange("p (h d) -> p h d", d=64)
        VT("pool", XR.h[:, :].rearrange("p (h d) -> p h d", d=64), xs3, bc(c.DT.h[:, :], [128, 16, 64], 2), ALU.mult,
           [BFA, c.DT], [XR])
        VT("pool", XRD.h[:, :].rearrange("p (h d) -> p h d", d=64), xs3, bc(c.DDT.h[:, :], [128, 16, 64], 2), ALU.mult,
           [BFA, c.DDT], [XRD])
        for g in range(2):
            MM(B3.h[:, 64 + g * 128:64 + (g + 1) * 128], c.BT.h[:, g, :], c.CTT.h[:, g, :], True, True, [c.BT, c.CTT], [B3cb],
               first=(g == 0))
        VT("dve", CBM.h[:, :, :], B3.h[:, 64:320].rearrange("p (g l) -> p g l", g=2), bc(TRIS.h[:, :], [128, 2, 128], 1),
           ALU.mult, [B3cb, CT_], [CBM])
        for g in range(2):
            ssd_intra(ci, True, g)
            ysl = Y.h[:, g * 512:(g + 1) * 512]
            CP("dve", ysl, PB[6].h[:, :], [PB[6]], [Y], acc=(g == 1))
            VT("pool", YT.h[:, :].rearrange("p (h d) -> p h d", d=64),
               BFA.h[:, g * 512:(g + 1) * 512].rearrange("p (h d) -> p h d", d=64),
               bc(DSK.h[:, 8 * g:8 * g + 8], [128, 8, 64], 2), ALU.mult, [BFA, CT_], [YT])
            VT("dve", ysl, ysl, YT.h[:, :], ALU.add, [Y, YT], [Y], acc=True)
        VT("dve", ADTX.h[:, :].rearrange("p (h d) -> p h d", d=64), bc(c.ADT.h[:, :], [128, 16, 64], 2),
           bc(ONE.h[:, 0:16], [128, 16, 64], 2), ALU.mult, [c.ADT, CT_], [ADTX])
        for j in range(8):
            MM(PB[7].h[:, j * 16:(j + 1) * 16], ADTX.h[:, j * 128:(j + 1) * 128], SQM.h[:, :], True, True,
               [ADTX, CT_], [PB[7]], first=(j == 0))
        ACT(CDC.h[:, :, :], PB[7].h[:, 0:128].rearrange("p (j b) -> p j b", j=8), AF.Exp, [PB[7]], [CDC])
        for i in range(2):
            MEMSET("pool", CTM[i].h[:, :, :], 0.0, [CTM[i]])
        ssv = sssm_d.rearrange("b (j p) n -> b p j n", p=128)
        sso = ssms_d.rearrange("b (j p) n -> b p j n", p=128)
        NB = 3
        S.op("pool", lambda e: e.memset(SCR.h[:, 1:2], 0.0), [], [R1T[0], R1T[1], R1T[2], H0[2].t, H0B[2].t, SCR.t])

        def h0_load(b):
            rb = b % NB
            DMA("sp", H0[rb].h[:, :, :], ssv[b], [], [H0[rb]], H0[rb].t)
            DMA("pool", H0B[rb].h[:, :, :], ssv[b], [], [H0B[rb]], H0B[rb].t)

        def stage_a(b):
            rb = b % NB
            r2_ = b % 2
            for j in range(8):
                TR(PT.h[:, j * 128:(j + 1) * 128], H0B[rb].h[:, j, :], IDB.h[:, :], [H0B[rb]], [PT], first=(j == 0))
            CP("act", H0T[r2_].h[:, :], PT.h[:, :], [PT], [H0T[r2_]])
            CP("act", CTM[r2_].h[:, :, 8 * b:8 * b + 8], c.CTT.h[:, :, 8 * b:8 * b + 8], [c.CTT], [CTM[r2_]], acc=True)
            ACT(BM[r2_].h[:, :], BTOK.h[:, :], AF.Copy, [BTOK, CT_], [BM[r2_]], scale=SQM.h[:, b:b + 1])

        def stage_b(b):
            rb = b % NB
            r2_ = b % 2
            for g in range(2):
                MM(PB[g].h[:, :], CTM[r2_].h[:, g, :], H0T[r2_].h[:, g * 512:(g + 1) * 512], b == 0, b == 15,
                   [CTM[r2_], H0T[r2_]], [PB[g]], first=(b == 0))
            for j in range(8):
                bank = PB[4 + j // 4]
                MM(bank.h[:, (j % 4) * 128:(j % 4 + 1) * 128], XRD.h[:, j * 128:(j + 1) * 128],
                   BM[r2_].h[:, (j // 4) * 128:(j // 4 + 1) * 128], True, True, [XRD, BM[r2_]], [bank], first=(j % 4 == 0))
            ACT(CTM[r2_].h[:, :, 8 * b:8 * b + 8], CTM[r2_].h[:, :, 8 * b:8 * b + 8], AF.Copy, [CTM[r2_]], [CTM[r2_]],
                scale=0.0, acc=True)
            for j in range(8):
                bank = PB[4 + j // 4]
                STT("dve", H0[rb].h[:, j, :], H0[rb].h[:, j, :], CDC.h[:, j, b:b + 1],
                    bank.h[:, (j % 4) * 128:(j % 4 + 1) * 128], ALU.mult, ALU.add, [H0[rb], CDC, bank], [H0[rb]], acc=True)
            DMA("act", sso[b], H0[rb].h[:, :, :], [H0[rb]], [OUTT], H0[rb].t)

        h0_load(0)
        h0_load(1)
        stage_a(0)
        for b in range(16):
            if b + 1 < 16:
                stage_a(b + 1)
            stage_b(b)
            if b + 2 < 16:
                h0_load(b + 2)
        S.op("pool", lambda e: e.memset(SCR.h[:, 2:3], 0.0), [], [R1T[0], R1T[1], R1T[2], H0[2].t, H0B[2].t, SCR.t])
        for g in range(2):
            ysl = Y.h[:, g * 512:(g + 1) * 512]
            VT("dve", YT.h[:, :].rearrange("p (h d) -> p h d", d=64), PB[g].h[:, :].rearrange("p (h d) -> p h d", d=64),
               bc(c.EAC.h[:, 8 * g:8 * g + 8], [128, 8, 64], 2), ALU.mult, [PB[g], c.EAC], [YT])
            VT("dve", ysl, ysl, YT.h[:, :], ALU.add, [Y, YT], [Y], acc=True)
        pool_branch(ci, ux4, uxt, True)
        for k in range(4):
            CP("pool", PSTG.h[:, k, :].rearrange("p (b r) -> p b r", r=15), ux4s[:, k, :, 8:23], [UXST], [PSTG],
               acc=(k > 0))
        for half in range(2):
            bank = PB[4 + half]
            for k in range(4):
                TR(bank.h[0:120, k * 128:(k + 1) * 128], PSTG.h[:, k, half * 120:(half + 1) * 120], IDF.h[:, :], [PSTG],
                   [bank], first=(k == 0))
            CP("dve", OSTG.h[0:120, half * 512:(half + 1) * 512], bank.h[0:120, :], [bank], [OSTG], acc=True)
        DMA("sp", pools_d.rearrange("(h r) c -> r h c", h=2), OSTG.h[0:120, 0:1024].rearrange("r (h c) -> r h c", h=2),
            [OSTG], [OUTT], OSTG.t)
        gate_norm_transpose()
        tail(ci)

    sa = sb = None
    if do_pstate:
        S.begin()
        prompt_state_outputs()
        sa = S.end()
    if do_sample:
        S.begin()
        sample_front()
        sb = S.end()
    S.run_merged([sa, sb])
    if do_sample:
        sample_rest()
    S.op("sp", None, [OUTT], [])

    S.finalize(nc, es)
    block = es.enter_context(nc.Block())

    @block.tensor
    def _(e):
        S.emit_engine(e, "pe")

    @block.scalar
    def _(e):
        S.emit_engine(e, "act")

    @block.vector
    def _(e):
        S.emit_engine(e, "dve")

    @block.gpsimd
    def _(e):
        S.emit_engine(e, "pool")

    @block.sync
    def _(e):
        S.emit_engine(e, "sp")

    es.close()
    return nc


_NC_CACHE = {}


def _prep_inputs(inp):
    f = lambda a: np.ascontiguousarray(np.asarray(a, dtype=np.float32))
    xp = f(inp["x_prompt"])
    xs = f(inp["x_sample"])
    sc = f(inp["state_conv"])[0]
    ss = f(inp["state_ssm"])[0]
    sp = f(inp["state_pool"])[0]
    shared = {
        "w_in": f(inp["w_in"])[0],
        "w_a": f(inp["w_branch_a"])[0],
        "w_b": f(inp["w_branch_b"])[0],
        "w_out": f(inp["w_out"])[0],
        "pool_mix": f(inp["pool_mix"])[0],
        "npre_col": f(f(inp["norm_pre"])[0].reshape(8, 128).T),
        "convw_col": f(f(inp["conv_w"])[0].reshape(4, 12, 128).transpose(2, 1, 0)),
        "convb_col": f(f(inp["conv_b"])[0].reshape(12, 128).T),
        "ssmn_col": f(f(inp["ssm_norm"])[0].reshape(8, 128).T),
        "pscale_col": f(f(inp["pool_scale"])[0].reshape(4, 128).T),
        "dtb_bc": f(np.broadcast_to(f(inp["dt_bias"])[0][None, :], (128, 16))),
        "alog_bc": f(np.broadcast_to(f(inp["a_log"])[0][None, :], (128, 16))),
        "dskip_bc": f(np.broadcast_to(f(inp["d_skip"])[0][None, :], (128, 16))),
        "npost_bc": f(np.broadcast_to(f(inp["norm_post"])[0][None, :], (128, 1024))),
    }
    maps = []
    for c in range(8):
        x = np.concatenate([xp[c].reshape(16, 128, 1024), xs[16 * c:16 * c + 16].reshape(1, 128, 1024)], axis=0)
        m = dict(shared)
        m["x"] = f(x)
        m["sconv"] = f(sc[16 * c:16 * c + 16].reshape(48, 1536))
        m["sssm"] = f(ss[16 * c:16 * c + 16].reshape(16, 1024, 128))
        m["spool"] = f(sp[16 * c:16 * c + 16].reshape(240, 512))
        maps.append(m)
    return maps


def kernel(**inputs):
    if "nc" not in _NC_CACHE:
        _NC_CACHE["nc"] = build_program()
    nc = _NC_CACHE["nc"]
    maps = _prep_inputs(inputs)
    res = run_bass_kernel_spmd(nc, maps, core_ids=list(range(8)))
    R = res.results
    yp = np.stack([R[c]["y"][:16].reshape(2048, 1024) for c in range(8)], axis=0)
    ys = np.concatenate([R[c]["y"][16].reshape(16, 8, 1024) for c in range(8)], axis=0)
    convp = np.stack([R[c]["convp"] for c in range(8)], axis=0)[None]
    ssmp = np.stack([R[c]["ssmp"].reshape(16, 64, 128) for c in range(8)], axis=0)[None]
    poolp = np.stack([R[c]["poolp"] for c in range(8)], axis=0)[None]
    convs = np.concatenate([R[c]["convs"].reshape(16, 3, 1536) for c in range(8)], axis=0)[None]
    ssms = np.concatenate([R[c]["ssms"].reshape(16, 16, 64, 128) for c in range(8)], axis=0)[None]
    pools = np.concatenate([R[c]["pools"].reshape(16, 15, 512) for c in range(8)], axis=0)[None]
    out = (yp, ys, convp, ssmp, poolp, convs, ssms, pools)
    return tuple(np.ascontiguousarray(o, dtype=np.float32) for o in out)
```

```python
import os
import numpy as np
from contextlib import ExitStack
import concourse.bass as bass
import concourse.mybir as mybir
from concourse.bass_utils import run_bass_kernel_spmd

F32 = mybir.dt.float32
BF16 = mybir.dt.bfloat16
AF = mybir.ActivationFunctionType
ALU = mybir.AluOpType

NCHUNK = 17
XLAT = float(os.environ.get("XLAT", 0.9))
SAME_ENG_ALL = int(os.environ.get("SAME_ENG_ALL", 1))
EPS = 1e-6


class Tile:
    __slots__ = ("name", "writers", "readers", "sem", "cnt", "collector", "psum")

    def __init__(self, name, collector=False, psum=False):
        self.psum = psum
        self.name = name
        self.writers = []
        self.readers = []
        self.sem = None
        self.cnt = 0
        self.collector = collector


class Op:
    __slots__ = ("eng", "fn", "reads", "writes", "acc", "dma", "depc", "depdma",
                 "need_inc", "ticket", "dticket", "seq", "dsem", "t_end")


class Sched:
    ENGS = ("pe", "act", "dve", "pool", "sp")
    DMA_SEM_MAX = 224
    ROT = int(os.environ.get("ROT", 1000))

    def __init__(self):
        self.ops = []
        self.per = {e: [] for e in self.ENGS}
        self.dma_keys = []
        self.cur = None
        self.eng_t = {e: 0.0 for e in self.ENGS}
        self.act_set = None

    def begin(self):
        self.cur = []
        return self.cur

    def end(self):
        st = self.cur
        self.cur = None
        return st

    def _est_start(self, a):
        eng, fn, reads, writes, acc, dma, cost, aset = a
        t = self.eng_t[eng]
        if aset is not None and aset != self.act_set:
            t += 1.3
        for tl_ in reads:
            for w in tl_.writers:
                te = w.t_end + (0.0 if w.eng == eng else XLAT)
                if te > t:
                    t = te
            if tl_.psum:
                for r in tl_.readers:
                    if r.eng != eng and r.t_end > t:
                        t = r.t_end
        for tl_ in writes:
            if tl_.collector:
                continue
            for w in tl_.writers:
                te = w.t_end + (0.0 if w.eng == eng else XLAT)
                if te > t:
                    t = te
            for r in tl_.readers:
                te = r.t_end + (0.0 if r.eng == eng else XLAT)
                if te > t:
                    t = te
        return t

    def run_merged(self, streams):
        streams = [st for st in streams if st]
        pos = [0] * len(streams)
        while True:
            best = -1
            bt = 1e30
            for i, st in enumerate(streams):
                if pos[i] < len(st):
                    t = self._est_start(st[pos[i]])
                    t += 0.05 * pos[i] / len(st)
                    if t < bt:
                        bt = t
                        best = i
            if best < 0:
                break
            a = streams[best][pos[best]]
            pos[best] += 1
            self.op(*a)

    def op(self, eng, fn, reads=(), writes=(), acc=False, dma=None, cost=0.3, aset=None):
        if self.cur is not None:
            self.cur.append((eng, fn, list(reads), list(writes), acc, dma, cost, aset))
            return None
        t_start = self._est_start((eng, fn, reads, writes, acc, dma, cost, aset))
        if aset is not None:
            self.act_set = aset
        o = Op()
        if dma is not None:
            self.eng_t[eng] = t_start + 0.1
            o.t_end = t_start + cost
        else:
            o.t_end = t_start + cost
            self.eng_t[eng] = o.t_end
        o.eng = eng
        o.fn = fn
        o.reads = list(reads)
        o.writes = list(writes)
        o.acc = acc
        o.dma = dma
        o.depc = []
        o.depdma = []
        o.need_inc = False
        o.ticket = 0
        o.dticket = 0
        if dma is not None:
            if dma.cnt == 0 and dma not in self.dma_keys:
                self.dma_keys.append(dma)
            o.dsem = dma.cnt // self.DMA_SEM_MAX
            dma.cnt += 16
            o.dticket = dma.cnt - o.dsem * self.DMA_SEM_MAX
        deps = {}
        for t in o.reads:
            for w in t.writers:
                deps[id(w)] = (w, True)
            if t.psum:
                for r in t.readers:
                    if r.eng != eng and id(r) not in deps:
                        deps[id(r)] = (r, False)
        for t in o.writes:
            if t.collector:
                continue
            for w in t.writers:
                if id(w) not in deps:
                    deps[id(w)] = (w, False)
            for r in t.readers:
                if id(r) not in deps:
                    deps[id(r)] = (r, None)
        for t in o.reads:
            t.readers.append(o)
        for t in o.writes:
            if t.collector or acc:
                t.writers.append(o)
            else:
                t.writers = [o]
                t.readers = []
        latest = {}
        for d, raw in deps.values():
            if d is o:
                continue
            if d.dma is not None:
                o.depdma.append(d)
                continue
            if d.eng == eng:
                if eng == "pe":
                    continue
                if raw is None and SAME_ENG_ALL < 1:
                    continue
                if raw is False and SAME_ENG_ALL < 1 and SAME_ENG_ALL > -1:
                    pass
                if raw is False and SAME_ENG_ALL < 0:
                    continue
                if raw is False and acc and d.acc:
                    continue
            c_ = latest.get(d.eng)
            if c_ is None or d.seq > c_.seq:
                latest[d.eng] = d
        for d in latest.values():
            d.need_inc = True
            o.depc.append(d)
        o.seq = len(self.ops)
        self.ops.append(o)
        self.per[eng].append(o)
        return o

    def finalize(self, nc, es):
        self.engsem = {e: [] for e in self.ENGS}
        nsem = 0
        for i, k in enumerate(self.dma_keys):
            n = (k.cnt + self.DMA_SEM_MAX - 1) // self.DMA_SEM_MAX
            k.sem = [es.enter_context(nc.semaphore("dq%d_%d" % (i, j))) for j in range(n)]
            nsem += n
        print("dma sems", nsem)
        for e in self.ENGS:
            c = 0
            for o in self.per[e]:
                if o.need_inc:
                    c += 1
                    o.ticket = c
            print("engine", e, "ops", len(self.per[e]), "tickets", c)
            nse = (c + self.ROT - 1) // self.ROT
            self.engsem[e] = [es.enter_context(nc.semaphore("sem_%s%d" % (e, j))) for j in range(max(nse, 1))]

    def emit_engine(self, e, eng):
        waited = {}
        nw = 0
        for o in self.per[eng]:
            waits = {}
            for d in o.depc:
                s = self.engsem[d.eng][(d.ticket - 1) // self.ROT]
                k = id(s)
                tv = (d.ticket - 1) % self.ROT + 1
                if waits.get(k, (None, 0))[1] < tv:
                    waits[k] = (s, tv)
            for d in o.depdma:
                s = d.dma.sem[d.dsem]
                k = id(s)
                if waits.get(k, (None, 0))[1] < d.dticket:
                    waits[k] = (s, d.dticket)
            for k, (s, v) in waits.items():
                if waited.get(k, 0) >= v:
                    continue
                waited[k] = v
                e.wait_ge(s, v)
                nw += 1
                if os.environ.get("DUMPW") and o.seq >= int(os.environ.get("DUMPW")):
                    print("W", eng, o.seq, getattr(s, "name", s), v)
            if os.environ.get("DUMPW") and o.seq >= int(os.environ.get("DUMPW")):
                print("OP", eng, o.seq, "inc" if o.need_inc else "", o.ticket)
            ins = o.fn(e) if o.fn is not None else None
            if o.need_inc:
                assert ins is not None
                ins.then_inc(self.engsem[eng][(o.ticket - 1) // self.ROT], 1)
            if o.dma is not None:
                ins.then_inc(o.dma.sem[o.dsem], 16)
        print("emit", eng, "ops", len(self.per[eng]), "waits", nw)


class Buf:
    __slots__ = ("h", "t")

    def __init__(self, h, t):
        self.h = h
        self.t = t


def build_program(dbg=None, nprompt=16, do_pstate=True, do_sample=True):
    nc = bass.Bass("TRN2", target_bir_lowering=False)
    S = Sched()
    es = ExitStack()

    def din(name, shape, dt=F32):
        return nc.dram_tensor(name, shape, dt, kind="ExternalInput").ap()

    def dout(name, shape, dt=F32):
        return nc.dram_tensor(name, shape, dt, kind="ExternalOutput").ap()

    x_d = din("x", [NCHUNK, 128, 1024])
    sconv_d = din("sconv", [48, 1536])
    sssm_d = din("sssm", [16, 1024, 128])
    spool_d = din("spool", [240, 512])
    win_d = din("w_in", [1024, 5648])
    wa_d = din("w_a", [1024, 1024])
    wb_d = din("w_b", [512, 1024])
    wo_d = din("w_out", [1024, 1024])
    pmix_d = din("pool_mix", [4, 128, 128])
    npre_d = din("npre_col", [128, 8])
    convw_d = din("convw_col", [128, 12, 4])
    convb_d = din("convb_col", [128, 12])
    ssmn_d = din("ssmn_col", [128, 8])
    pscale_d = din("pscale_col", [128, 4])
    dtb_d = din("dtb_bc", [128, 16])
    alog_d = din("alog_bc", [128, 16])
    dskip_d = din("dskip_bc", [128, 16])
    npost_d = din("npost_bc", [128, 1024])

    y_d = dout("y", [NCHUNK, 128, 1024])
    convp_d = dout("convp", [3, 1536])
    ssmp_d = dout("ssmp", [1024, 128])
    poolp_d = dout("poolp", [15, 512])
    convs_d = dout("convs", [48, 1536])
    ssms_d = dout("ssms", [16, 1024, 128])
    pools_d = dout("pools", [240, 512])
    dbg_outs = {}
    OUTT = Tile("dram_out", collector=True)

    ARENA = 212800
    arena = nc.alloc_sbuf_tensor("arena", [128, ARENA // 4], F32)
    base = nc.lookup_mloc(arena).addr
    cur = [0]

    def nbytes(shape, dt):
        n = 1
        for s in shape[1:]:
            n *= s
        return n * (2 if dt == BF16 else 4)

    def alloc(name, shape, dt, at=None, tile=None):
        sz = (nbytes(shape, dt) + 31) // 32 * 32
        if at is None:
            off = cur[0]
            cur[0] += sz
            assert cur[0] <= ARENA, (name, cur[0])
        else:
            off = at
        h = nc.alloc_sbuf_tensor_at(name, shape, dt, offset=base + off)
        b = Buf(h, tile if tile is not None else Tile(name))
        b_off[name] = off
        return b

    b_off = {}

    WINX = alloc("winx", [128, 8, 2576], BF16)
    WINZ = alloc("winz", [128, 8, 1024], BF16)
    WING = alloc("wing", [128, 8, 2048], BF16)
    WA = alloc("wa", [128, 8, 1024], BF16)
    WB = alloc("wb", [128, 4, 1024], BF16)
    WO = alloc("wo", [128, 8, 1024], BF16)
    PMIX = alloc("pmix", [128, 4, 128], BF16)
    CT_ = Tile("consts", collector=True)
    IDB = alloc("idb", [128, 128], BF16, tile=CT_)
    IDF = alloc("idf", [128, 128], F32, tile=CT_)
    TRI = alloc("tri", [128, 128], BF16, tile=CT_)
    LST = alloc("lst", [128, 128], BF16, tile=CT_)
    ONE = alloc("one", [128, 128], BF16, tile=CT_)
    TRIS = alloc("tris", [128, 128], BF16, tile=CT_)
    LSTS = alloc("lsts", [128, 128], BF16, tile=CT_)
    SSQ = alloc("ssq", [128, 128], BF16, tile=CT_)
    SQM = alloc("sqm", [128, 16], F32, tile=CT_)
    NPRE = alloc("npre", [128, 8], F32, tile=CT_)
    CONVW = alloc("convw", [128, 12, 4], F32, tile=CT_)
    CONVB = alloc("convb", [128, 12], F32, tile=CT_)
    SSMN = alloc("ssmn", [128, 8], F32, tile=CT_)
    PSC = alloc("psc", [128, 4], F32, tile=CT_)
    DTB = alloc("dtb", [128, 16], F32, tile=CT_)
    ABC = alloc("abc", [128, 16], F32, tile=CT_)
    DSK = alloc("dsk", [128, 16], F32, tile=CT_)
    NPOST = alloc("npostb", [128, 1024], F32, tile=CT_)
    INVC = alloc("invc", [128, 16], F32, tile=CT_)
    SCR = alloc("scr", [128, 16], F32)
    NEGM = alloc("negm", [128, 128], BF16, tile=CT_)
    NEGMS = alloc("negms", [128, 128], BF16, tile=CT_)
    NACS = alloc("nacs", [128, 16], F32)
    XB = [alloc("xb0", [128, 1024], F32), alloc("xb1", [128, 1024], F32)]
    BFA = alloc("bfa", [128, 1024], BF16)
    XN = alloc("xn", [128, 1024], BF16)
    HTs = [alloc("ht0", [128, 8, 128], BF16), alloc("ht1", [128, 8, 128], BF16)]
    XBC = alloc("xbc", [128, 12, 176], F32)
    UX = alloc("ux", [128, 4, 143], F32)
    CACC = [alloc("cacc%d" % i, [128, 128], F32) for i in range(4)]
    XSTs = [alloc("xst0", [128, 8, 128], BF16), alloc("xst1", [128, 8, 128], BF16)]
    BTs = [alloc("bt0", [128, 2, 128], BF16), alloc("bt1", [128, 2, 128], BF16)]
    CTTs = [alloc("ct0", [128, 2, 128], BF16), alloc("ct1", [128, 2, 128], BF16)]
    GS = alloc("gs", [128, 4, 128], F32)
    DTRs = [alloc("dtr0", [128, 16], F32), alloc("dtr1", [128, 16], F32)]
    DTs_ = [alloc("dt%d" % i, [128, 16], F32) for i in range(2)]
    ADTs_ = [alloc("adt%d" % i, [128, 16], F32) for i in range(2)]
    DIF = alloc("dif", [128, 16], F32)
    EACs_ = [alloc("eac%d" % i, [128, 16], F32) for i in range(2)]
    DTE = alloc("dte", [128, 16], F32)
    CDs_ = [alloc("cd%d" % i, [128, 16], F32) for i in range(2)]
    DDTs_ = [alloc("ddt%d" % i, [128, 16], F32) for i in range(2)]
    AHIs_ = [alloc("ahi%d" % i, [128, 16], BF16) for i in range(2)]
    ALOs_ = [alloc("alo%d" % i, [128, 16], BF16) for i in range(2)]
    BTOK = alloc("btok", [128, 256], BF16)
    XR = alloc("xr", [128, 1024], BF16)
    XRD = alloc("xrd", [128, 1024], BF16)
    MG = Buf(XR.h, XR.t)
    MGT_h = nc.alloc_sbuf_tensor_at("mgt", [128, 8, 128], BF16, offset=base + b_off["xrd"])
    MGT = Buf(MGT_h, XRD.t)
    R1T = [Tile("r1_%d" % i) for i in range(5)]
    r1 = cur[0]
    cur[0] += 10240
    AMH = alloc("amh", [128, 8, 128], BF16, at=r1, tile=R1T[0])
    AML = alloc("aml", [128, 8, 128], BF16, at=r1 + 2048, tile=R1T[1])
    DEC = alloc("dec", [128, 8, 128], BF16, at=r1 + 4096, tile=R1T[2])
    MMT = alloc("mmt", [128, 8, 128], BF16, at=r1 + 6144, tile=R1T[3])
    YT = alloc("yt", [128, 512], F32, at=r1 + 8192, tile=R1T[4])
    TA = alloc("ta", [128, 512], F32, at=r1, tile=R1T[0])
    TAJ = alloc("taj", [128, 1024], BF16, at=r1, tile=R1T[0])
    TB = alloc("tb", [128, 512], F32, at=r1 + 2048, tile=R1T[1])
    QA = alloc("qa", [128, 512], F32, at=r1 + 4096, tile=R1T[2])
    QB = alloc("qb", [128, 512], F32, at=r1 + 6144, tile=R1T[3])
    OT = alloc("ot", [128, 512], F32, at=r1 + 8192, tile=R1T[4])
    CBM = alloc("cbm", [128, 2, 128], BF16)
    STATE = alloc("state", [128, 1024], F32)
    STBF = alloc("stbf", [128, 1024], BF16)
    Y = alloc("y", [128, 1024], F32)
    ZS = alloc("zs", [128, 512], F32)
    AMH0 = alloc("amh0", [128, 8, 128], BF16, at=b_off["y"], tile=Y.t)
    AML0 = alloc("aml0", [128, 8, 128], BF16, at=b_off["y"] + 2048, tile=Y.t)
    P2p = alloc("p2", [128, 144], F32)
    P4p = alloc("p4", [128, 144], F32)
    P8p = alloc("p8", [128, 144], F32)
    P16 = alloc("p16", [128, 128], F32)
    PLD = alloc("pld", [128, 4, 128], BF16)
    YBTs = [alloc("ybt0", [128, 4, 128], BF16), alloc("ybt1", [128, 4, 128], BF16)]
    SSPRE = alloc("sspre", [128, 1], F32)
    RPRE = alloc("rpre", [128, 1], F32)
    SSG = alloc("ssg", [128, 2], F32)
    RG = alloc("rg", [128, 2], F32)
    SSP = alloc("ssp", [128, 2], F32)
    RP = alloc("rp", [128, 1], F32)
    UXS_h = nc.alloc_sbuf_tensor_at("uxs", [128, 4, 368], F32, offset=base + b_off["state"])
    assert b_off["stbf"] == b_off["state"] + 4096
    UXST = Tile("uxs")
    wx = b_off["winx"]
    H0 = [alloc("h0_%d" % i, [128, 8, 128], F32, at=wx + i * 4096) for i in range(2)]
    H0B = [alloc("h0b_%d" % i, [128, 8, 128], BF16, at=wx + 8192 + i * 2048) for i in range(2)]
    H0T = [alloc("h0t_%d" % i, [128, 1024], BF16, at=wx + 12288 + i * 2048) for i in range(2)]
    CTM = [alloc("ctm_%d" % i, [128, 2, 128], BF16, at=wx + 16384 + i * 512) for i in range(2)]
    BM = [alloc("bm_%d" % i, [128, 256], BF16, at=wx + 17408 + i * 512) for i in range(2)]
    H0.append(alloc("h0_2", [128, 8, 128], F32, at=r1))
    H0B.append(alloc("h0b_2", [128, 8, 128], BF16, at=r1 + 4096))
    CDC = alloc("cdc", [128, 8, 16], F32, at=wx + 18432)
    CSTG = alloc("cstg", [128, 12, 48], F32, at=wx + 18944)
    PSTG = alloc("pstg", [128, 4, 240], F32, at=wx + 18944 + 2304)
    OSTG = alloc("ostg", [128, 1536], F32, at=wx + 18944 + 2304 + 3840)
    ADTX = alloc("adtx", [128, 1024], F32, at=wx + 18944 + 2304 + 3840 + 6144)
    P2s = alloc("p2s", [128, 368], F32, at=wx + 35328)
    P4s = alloc("p4s", [128, 368], F32, at=wx + 35328 + 1472)
    P8s = alloc("p8s", [128, 368], F32, at=wx + 35328 + 2944)
    assert 35328 + 3 * 1472 <= 41216
    SCV = alloc("scv", [128, 1536], F32, at=r1, tile=R1T[0])
    SPL = alloc("spl", [128, 2, 512], F32, at=r1 + 6144, tile=R1T[3])
    print("SBUF used", cur[0], "of", ARENA)

    PB = []
    pb45 = es.enter_context(nc.psum_tensor("pb45", [128, 1024], F32))
    for i in range(8):
        if i == 2:
            h = es.enter_context(nc.psum_tensor("pb2", [128, 1024], BF16))
        elif i == 4:
            h = pb45[:, 0:512]
        elif i == 5:
            h = pb45[:, 512:1024]
        else:
            h = es.enter_context(nc.psum_tensor("pb%d" % i, [128, 512], F32))
        PB.append(Buf(h, Tile("pb%d" % i, psum=True)))
    PT = PB[2]
    B3 = PB[3]
    PT2 = PB[7].h[:, :].bitcast(BF16)
    PT2t = PB[7].t
    B3dt = B3.t
    B3ac = B3.t
    B3cb = B3.t

    def tl(bufs):
        return [b.t if isinstance(b, Buf) else b for b in bufs]

    def fsz(ap):
        n = 1
        for d in ap.shape[1:]:
            n *= d
        return n

    def MM(out, lhsT, rhs, start, stop, r, w, first):
        n = fsz(rhs)
        S.op("pe", lambda e: e.matmul(out, lhsT=lhsT, rhs=rhs, start=start, stop=stop),
             tl(r), tl(w), acc=not first, cost=0.06 + max(n, 64) / 2000.0)

    def TR(out, in_, ident, r, w, first):
        S.op("pe", lambda e: e.transpose(out, in_, ident), tl(r) + [CT_], tl(w), acc=not first, cost=0.12)

    def ecost(eng, out):
        n = fsz(out)
        if eng == "pool":
            return 0.25 + n * 0.0017
        if eng == "act":
            return 0.25 + n * 0.00085
        return 0.15 + n * 0.00105

    def VT(eng, out, in0, in1, op, r, w, acc=False):
        S.op(eng, lambda e: e.tensor_tensor(out=out, in0=in0, in1=in1, op=op), tl(r), tl(w), acc=acc,
             cost=ecost(eng, out))

    def VS(eng, out, in0, s1, s2, op0, op1, r, w, acc=False):
        if s2 is None:
            S.op(eng, lambda e: e.tensor_scalar(out=out, in0=in0, scalar1=s1, scalar2=None, op0=op0),
                 tl(r), tl(w), acc=acc, cost=ecost(eng, out))
        else:
            S.op(eng, lambda e: e.tensor_scalar(out=out, in0=in0, scalar1=s1, scalar2=s2, op0=op0, op1=op1),
                 tl(r), tl(w), acc=acc, cost=ecost(eng, out))

    def STT(eng, out, in0, sc, in1, op0, op1, r, w, acc=False):
        S.op(eng, lambda e: e.scalar_tensor_tensor(out=out, in0=in0, scalar=sc, in1=in1, op0=op0, op1=op1),
             tl(r), tl(w), acc=acc, cost=ecost(eng, out))

    def CP(eng, out, in_, r, w, acc=False):
        if eng == "act":
            S.op(eng, lambda e: e.activation(out=out, in_=in_, func=AF.Copy), tl(r), tl(w), acc=acc,
                 cost=ecost(eng, out))
        else:
            S.op(eng, lambda e: e.tensor_copy(out=out, in_=in_), tl(r), tl(w), acc=acc, cost=ecost(eng, out))

    def ACT(out, in_, func, r, w, bias=None, scale=None, accum=None, acc=False):
        kw = {}
        if bias is not None:
            kw["bias"] = bias
        if scale is not None:
            kw["scale"] = scale
        if accum is not None:
            kw["accum_out"] = accum
        aset = "A" if func in (AF.Silu, AF.Tanh) else ("B" if func in (AF.Exp, AF.Ln) else None)
        S.op("act", lambda e: e.activation(out=out, in_=in_, func=func, **kw), tl(r), tl(w), acc=acc,
             cost=ecost("act", out), aset=aset)

    def MEMSET(eng, ap, val, w, acc=False):
        S.op(eng, lambda e: e.memset(ap, val), [], tl(w), acc=acc)

    def DMA(eng, out, in_, r, w, key, acc=False):
        S.op(eng, lambda e: e.dma_start(out=out, in_=in_), tl(r), tl(w), acc=acc, dma=key, cost=3.0)

    def bc(ap, shape, axis):
        return ap.unsqueeze(axis).to_broadcast(shape)

    winv = win_d.rearrange("(kc p) e -> p kc e", p=128)
    WXB = Tile("winx_b")
    DMA("pool", WINX.h[:, :, 0:1536], winv[:, :, 1024:2560], [], [WINX], WINX.t)
    DMA("pool", WINX.h[:, :, 1536:2576], winv[:, :, 2560:3600], [], [WXB], WXB)
    small = [(NPRE, npre_d), (CONVW, convw_d), (CONVB, convb_d), (SSMN, ssmn_d), (PSC, pscale_d),
             (DTB, dtb_d), (ABC, alog_d), (DSK, dskip_d), (NPOST, npost_d)]
    ctl = {}

    def ct(bf):
        k = id(bf)
        if k not in ctl:
            ctl[k] = Tile("c%d" % len(ctl))
        return ctl[k]

    for b_, d_ in small:
        if len(d_.shape) == 3:
            DMA("act", b_.h[:, :, :], d_[:, :, :], [], [ct(b_)], ct(b_))
        else:
            DMA("act", b_.h[:, :], d_[:, :], [], [ct(b_)], ct(b_))
    DMA("pool", WINZ.h[:, :, :], winv[:, :, 0:1024], [], [WINZ], WINZ.t)
    DMA("pool", PMIX.h[:, :, :], pmix_d.rearrange("k c d -> c k d"), [], [PMIX], PMIX.t)
    DMA("pool", WA.h[:, :, :], wa_d.rearrange("(kc p) e -> p kc e", p=128), [], [WA], WA.t)
    DMA("pool", WB.h[:, :, :], wb_d.rearrange("(kc p) e -> p kc e", p=128), [], [WB], WB.t)
    DMA("pool", WING.h[:, :, :], winv[:, :, 3600:5648], [], [WING], WING.t)
    DMA("pool", WO.h[:, :, :], wo_d.rearrange("(kc p) e -> p kc e", p=128), [], [WO], WO.t)

    def aff(bf, eng_ap, pattern, cmp, fill, base_, cm):
        S.op("pool", lambda e: e.affine_select(out=eng_ap, in_=eng_ap, pattern=pattern, compare_op=cmp,
                                                fill=fill, base=base_, channel_multiplier=cm), [ct(bf)], [ct(bf)])

    for I_ in (IDB, IDF):
        MEMSET("pool", I_.h[:, :], 0.0, [ct(I_)])
        aff(I_, I_.h[:, :], [[-1, 128]], ALU.not_equal, 1.0, 0, 1)
    MEMSET("pool", TRI.h[:, :], 1.0, [ct(TRI)])
    aff(TRI, TRI.h[:, :], [[1, 128]], ALU.is_ge, 0.0, 0, -1)
    MEMSET("pool", LST.h[:, :], 1.0, [ct(LST)])
    aff(LST, LST.h[:, :], [[-1, 128]], ALU.is_gt, 0.0, 0, 1)
    MEMSET("pool", ONE.h[:, :], 1.0, [ct(ONE)])
    MEMSET("pool", SSQ.h[:, :], 1.0, [ct(SSQ)])
    ssq3 = SSQ.h[:, :].rearrange("p (b j) -> p b j", j=8)
    aff(SSQ, ssq3, [[-8, 16], [0, 8]], ALU.is_ge, 0.0, 0, 1)
    aff(SSQ, ssq3, [[8, 16], [0, 8]], ALU.is_ge, 0.0, 7, -1)
    MEMSET("pool", SQM.h[:, :], 1.0, [ct(SQM)])
    aff(SQM, SQM.h[:, :], [[-8, 16]], ALU.is_ge, 0.0, 0, 1)
    aff(SQM, SQM.h[:, :], [[8, 16]], ALU.is_ge, 0.0, 7, -1)
    S.op("pool", lambda e: e.tensor_tensor(out=TRIS.h[:, :], in0=TRI.h[:, :], in1=SSQ.h[:, :], op=ALU.mult),
         [ct(TRI), ct(SSQ)], [ct(TRIS)])
    S.op("pool", lambda e: e.tensor_tensor(out=LSTS.h[:, :], in0=LST.h[:, :], in1=SSQ.h[:, :], op=ALU.mult),
         [ct(LST), ct(SSQ)], [ct(LSTS)])
    for N_, T_ in ((NEGM, TRI), (NEGMS, TRIS)):
        S.op("pool", (lambda N_, T_: (lambda e: e.tensor_scalar(out=N_.h[:, :], in0=T_.h[:, :], scalar1=-1.0,
                                                                  scalar2=30000.0, op0=ALU.add, op1=ALU.mult)))(N_, T_),
             [ct(T_)], [ct(N_)])
    for t_ in range(16):
        MEMSET("pool", INVC.h[:, t_:t_ + 1], 1.0 / (t_ + 1), [ct(INVC)], acc=(t_ > 0))
    S.op("act", lambda e: e.activation(out=ABC.h[:, :], in_=ABC.h[:, :], func=AF.Exp), [ct(ABC)], [ct(ABC)], aset="B")
    S.op("act", lambda e: e.mul(ABC.h[:, :], ABC.h[:, :], -1.0), [ct(ABC)], [ct(ABC)])
    S.op("pool", lambda e: e.memset(SCR.h[:, 3:4], 0.0), list(ctl.values()), [CT_, SCR.t])

    class _C:
        pass
    c = _C()

    def setpar(p):
        c.HT = HTs[p]
        c.XST = XSTs[p]
        c.YAT = Buf(XSTs[p].h, XSTs[p].t)
        c.BT = BTs[p]
        c.CTT = CTTs[p]
        c.YBT = YBTs[p]
        c.DTR = DTRs[p]
        c.DT = DTs_[p]
        c.ADT = ADTs_[p]
        c.EAC = EACs_[p]
        c.CD = CDs_[p]
        c.DDT = DDTs_[p]
        c.AHI = AHIs_[p]
        c.ALO = ALOs_[p]

    def load_x(ci):
        xb = XB[ci % 2]
        DMA("sp", xb.h[:, :], x_d[ci], [], [xb], xb.t)

    def proj_feature_major(ci, sample):
        L = 8 if sample else 128
        nseq = 16 if sample else 1
        hc = 3
        hp = 15
        if sample:
            xbc4 = XBC.h[:, :, :].rearrange("p t (b j) -> p t b j", j=11)
            ux4 = UXS_h[:, :, :].rearrange("p t (b j) -> p t b j", j=23)
            uxt = UXST
        else:
            xbc4 = XBC.h[:, :, 0:131].rearrange("p t (b j) -> p t b j", b=1)
            ux4 = UX.h[:, :, :].rearrange("p t (b j) -> p t b j", b=1)
            uxt = UX.t
        nb = 0
        for bl in range(3):
            bank = PB[nb % 2]
            nb += 1
            for j in range(4):
                tile = bl * 4 + j
                for kc in range(8):
                    MM(bank.h[:, j * 128:(j + 1) * 128], WINX.h[:, kc, tile * 128:(tile + 1) * 128], c.HT.h[:, kc, :],
                       kc == 0, kc == 7, [WINX, c.HT], [bank], first=(j == 0 and kc == 0))
            for j in range(4):
                tile = bl * 4 + j
                eng = "act" if bl % 2 == 0 else "dve"
                CP(eng, xbc4[:, tile, :, hc:hc + L],
                   bank.h[:, j * 128:(j + 1) * 128].rearrange("p (b j) -> p b j", j=L),
                   [bank], [XBC], acc=True)
        for kc in range(8):
            MM(B3.h[:, 0:16], c.HT.h[:, kc, :], WINX.h[:, kc, 1536:1552], kc == 0, kc == 7, [WXB, c.HT], [B3dt],
               first=(kc == 0))
        VT("dve", c.DTR.h[:, :], B3.h[:, 0:16], DTB.h[:, :], ALU.add, [B3dt, CT_], [c.DTR])
        bank = PB[nb % 2]
        nb += 1
        for j in range(4):
            for kc in range(8):
                MM(bank.h[:, j * 128:(j + 1) * 128], WINX.h[:, kc, 1552 + j * 128:1552 + (j + 1) * 128],
                   c.HT.h[:, kc, :], kc == 0, kc == 7, [WXB, c.HT], [bank], first=(j == 0 and kc == 0))
        for j in range(4):
            CP("dve", ux4[:, j, :, hp:hp + L], bank.h[:, j * 128:(j + 1) * 128].rearrange("p (b j) -> p b j", j=L),
               [bank], [uxt], acc=True)
        bank = PB[nb % 2]
        nb += 1
        for j in range(4):
            for kc in range(8):
                MM(bank.h[:, j * 128:(j + 1) * 128], WINX.h[:, kc, 2064 + j * 128:2064 + (j + 1) * 128],
                   c.HT.h[:, kc, :], kc == 0, kc == 7, [WXB, c.HT], [bank], first=(j == 0 and kc == 0))
        ACT(GS.h[:, :, :], bank.h[:, :].rearrange("p (k t) -> p k t", k=4), AF.Silu, [bank], [GS])
        return xbc4, ux4, uxt

    def conv_and_silu(xbc4, sample):
        L = 8 if sample else 128
        for pair in range(6):
            tiles = (2 * pair, 2 * pair + 1)
            accs = [CACC[(2 * pair) % 4], CACC[(2 * pair + 1) % 4]]
            a3s = [a.h[:, :].rearrange("p (b j) -> p b j", j=L) for a in accs]
            for i_, tile in enumerate(tiles):
                S.op("act", (lambda o_, i__, sc_, bi_: (lambda e: e.activation(out=o_, in_=i__, func=AF.Identity,
                                                                                   scale=sc_, bias=bi_)))(
                    a3s[i_], xbc4[:, tile, :, 0:L], CONVW.h[:, tile, 0:1], CONVB.h[:, tile:tile + 1]),
                    tl([XBC, CT_]), tl([accs[i_]]))
            for k in range(1, 4):
                for i_, tile in enumerate(tiles):
                    STT("dve", a3s[i_], xbc4[:, tile, :, k:k + L], CONVW.h[:, tile, k:k + 1], a3s[i_], ALU.mult,
                        ALU.add, [XBC, CT_, accs[i_]], [accs[i_]], acc=True)
            for i_, tile in enumerate(tiles):
                acc = accs[i_]
                if tile < 8:
                    ACT(c.XST.h[:, tile, :], acc.h[:, :], AF.Silu, [acc], [c.XST], acc=True)
                elif tile < 10:
                    ACT(c.BT.h[:, tile - 8, :], acc.h[:, :], AF.Silu, [acc], [c.BT], acc=True)
                else:
                    ACT(c.CTT.h[:, tile - 10, :], acc.h[:, :], AF.Silu, [acc], [c.CTT], acc=True)

    def pool_branch(ci, ux4, uxt, sample):
        L = 8 if sample else 128
        nseq = 16 if sample else 1
        E = 15 + L
        P2, P4, P8 = (P2s, P4s, P8s) if sample else (P2p, P4p, P8p)
        p2 = P2.h[:, 0:nseq * (E - 1)].rearrange("p (b j) -> p b j", b=nseq)
        p4 = P4.h[:, 0:nseq * (E - 3)].rearrange("p (b j) -> p b j", b=nseq)
        p8 = P8.h[:, 0:nseq * (E - 7)].rearrange("p (b j) -> p b j", b=nseq)
        p16 = P16.h[:, :].rearrange("p (b j) -> p b j", b=nseq)
        for k, w in enumerate((2, 4, 8, 16)):
            u = ux4[:, k, :, :]
            eng = "dve" if k % 2 == 0 else "pool"
            VT(eng, p2, u[:, :, 1:E], u[:, :, 0:E - 1], ALU.add, [uxt], [P2])
            Sv = p2[:, :, 14:14 + L]
            rd = [P2]
            if w >= 4:
                VT(eng, p4, p2[:, :, 2:E - 1], p2[:, :, 0:E - 3], ALU.add, [P2], [P4])
                Sv = p4[:, :, 12:12 + L]
                rd = [P4]
            if w >= 8:
                VT(eng, p8, p4[:, :, 4:E - 3], p4[:, :, 0:E - 7], ALU.add, [P4], [P8])
                Sv = p8[:, :, 8:8 + L]
                rd = [P8]
            if w >= 16:
                VT(eng, p16, p8[:, :, 8:E - 7], p8[:, :, 0:E - 15], ALU.add, [P8], [P16])
                Sv = p16
                rd = [P16]
            pld3 = PLD.h[:, k, :].rearrange("p (b j) -> p b j", b=nseq)
            STT("dve", pld3, Sv, 1.0 / w, u[:, :, 15:15 + L], ALU.mult, ALU.subtract, rd + [uxt], [PLD], acc=True)
            if ci == 0 and not sample:
                VT(eng, SCR.h[:, 0:w - 1], Sv[:, 0, 0:w - 1], INVC.h[:, 0:w - 1], ALU.mult, rd + [CT_], [SCR])
                VT(eng, PLD.h[:, k, 0:w - 1], SCR.h[:, 0:w - 1], u[:, 0, 15:15 + w - 1], ALU.subtract,
                   [SCR, uxt], [PLD], acc=True)
        bank = PB[0]
        for k in range(4):
            MM(bank.h[:, k * 128:(k + 1) * 128], PMIX.h[:, k, :], PLD.h[:, k, :], True, True, [PMIX, PLD], [bank],
               first=(k == 0))
        for k in range(4):
            STT("dve", c.YBT.h[:, k, :], bank.h[:, k * 128:(k + 1) * 128], PSC.h[:, k:k + 1], GS.h[:, k, :],
                ALU.mult, ALU.mult, [bank, GS, CT_], [c.YBT], acc=True)

    def dt_path(sample):
        tri = TRIS if sample else TRI
        ssq = SSQ if sample else ONE
        ACT(c.DTR.h[:, :], c.DTR.h[:, :], AF.Exp, [c.DTR], [c.DTR])
        ACT(c.DT.h[:, :], c.DTR.h[:, :], AF.Ln, [c.DTR], [c.DT], bias=1.0)
        VT("dve", c.ADT.h[:, :], c.DT.h[:, :], ABC.h[:, :], ALU.mult, [c.DT, CT_], [c.ADT])
        CP("dve", c.AHI.h[:, :], c.ADT.h[:, :], [c.ADT], [c.AHI])
        VT("dve", c.ALO.h[:, :], c.ADT.h[:, :], c.AHI.h[:, :], ALU.subtract, [c.ADT, c.AHI], [c.ALO])
        MM(B3.h[:, 16:32], tri.h[:, :], c.AHI.h[:, :], True, False, [CT_, c.AHI], [B3ac], first=True)
        MM(B3.h[:, 16:32], tri.h[:, :], c.ALO.h[:, :], False, True, [CT_, c.ALO], [B3ac], first=False)
        MM(B3.h[:, 32:48], ssq.h[:, :], c.AHI.h[:, :], True, False, [CT_, c.AHI], [B3ac], first=False)
        MM(B3.h[:, 32:48], ssq.h[:, :], c.ALO.h[:, :], False, True, [CT_, c.ALO], [B3ac], first=False)
        VS("dve", NACS.h[:, :], B3.h[:, 16:32], -1.0, None, ALU.mult, None, [B3ac], [NACS])
        ACT(c.EAC.h[:, :], B3.h[:, 16:32], AF.Exp, [B3ac], [c.EAC])
        ACT(c.CD.h[:, :], B3.h[:, 32:48], AF.Exp, [B3ac], [c.CD])
        VT("dve", DIF.h[:, :], B3.h[:, 32:48], NACS.h[:, :], ALU.add, [B3ac, NACS], [DIF])
        ACT(DTE.h[:, :], DIF.h[:, :], AF.Exp, [DIF], [DTE])
        VT("dve", c.DDT.h[:, :], c.DT.h[:, :], DTE.h[:, :], ALU.mult, [c.DT, DTE], [c.DDT])

    def to_token_major():
        for j in range(8):
            TR(PT2[:, j * 128:(j + 1) * 128], c.XST.h[:, j, :], IDB.h[:, :], [c.XST], [PT2t], first=(j == 0))
        CP("dve", BFA.h[:, :], PT2[:, :], [PT2t], [BFA])
        for g in range(2):
            TR(PT2[:, g * 128:(g + 1) * 128], c.BT.h[:, g, :], IDB.h[:, :], [c.BT], [PT2t], first=(g == 0))
        CP("act", BTOK.h[:, :], PT2[:, 0:256], [PT2t], [BTOK])

    def build_masks(sample, g):
        tri = TRIS if sample else TRI
        AMH_, AML_ = (AMH0, AML0) if g == 0 else (AMH, AML)
        VT("pool", AMH_.h[:, :, :], bc(tri.h[:, :], [128, 8, 128], 1), bc(c.AHI.h[:, 8 * g:8 * g + 8], [128, 8, 128], 2),
           ALU.mult, [CT_, c.AHI], [AMH_], acc=(g == 0))
        VT("pool", AML_.h[:, :, :], bc(tri.h[:, :], [128, 8, 128], 1), bc(c.ALO.h[:, 8 * g:8 * g + 8], [128, 8, 128], 2),
           ALU.mult, [CT_, c.ALO], [AML_], acc=(g == 0))

    def ssd_intra(ci, sample, g):
        tri = TRIS if sample else TRI
        lst = LSTS if sample else LST
        AMH_, AML_ = (AMH0, AML0) if g == 0 else (AMH, AML)
        for q in range(2):
            bank = PB[4 + q]
            MM(bank.h[:, :], lst.h[:, :], AMH_.h[:, 4 * q:4 * q + 4, :].rearrange("p h l -> p (h l)"), True, False,
               [CT_, AMH_], [bank], first=True)
            MM(bank.h[:, :], lst.h[:, :], AML_.h[:, 4 * q:4 * q + 4, :].rearrange("p h l -> p (h l)"), False, True,
               [CT_, AML_], [bank], first=False)
            ACT(DEC.h[:, 4 * q:4 * q + 4, :], bank.h[:, :].rearrange("p (h l) -> p h l", h=4), AF.Exp, [bank], [DEC],
                acc=(q == 1))
        VT("dve", MMT.h[:, :, :], DEC.h[:, :, :], bc(CBM.h[:, g, :], [128, 8, 128], 1), ALU.mult, [DEC, CBM], [MMT])
        bank = PB[6]
        for hh in range(8):
            h = 8 * g + hh
            MM(bank.h[:, hh * 64:(hh + 1) * 64], MMT.h[:, hh, :], XR.h[:, h * 64:(h + 1) * 64], True, True,
               [MMT, XR], [bank], first=(hh == 0))

    def ssd_prompt(ci):
        build_masks(False, 0)
        to_token_major()
        xs3 = BFA.h[:, :].rearrange("p (h d) -> p h d", d=64)
        VT("pool", XR.h[:, :].rearrange("p (h d) -> p h d", d=64), xs3, bc(c.DT.h[:, :], [128, 16, 64], 2), ALU.mult,
           [BFA, c.DT], [XR])
        build_masks(False, 1)
        VT("pool", XRD.h[:, :].rearrange("p (h d) -> p h d", d=64), xs3, bc(c.DDT.h[:, :], [128, 16, 64], 2), ALU.mult,
           [BFA, c.DDT], [XRD])
        for g in range(2):
            MM(B3.h[:, 64 + g * 128:64 + (g + 1) * 128], c.BT.h[:, g, :], c.CTT.h[:, g, :], True, True, [c.BT, c.CTT], [B3cb],
               first=(g == 0))
        VT("dve", CBM.h[:, :, :], B3.h[:, 64:320].rearrange("p (g l) -> p g l", g=2),
           bc(TRI.h[:, :], [128, 2, 128], 1), ALU.mult, [B3cb, CT_], [CBM])
        for g in range(2):
            ssd_intra(ci, False, g)
            ysl = Y.h[:, g * 512:(g + 1) * 512]
            y3 = ysl.rearrange("p (h d) -> p h d", d=64)
            if ci > 0:
                MM(PB[7].h[:, :], c.CTT.h[:, g, :], STBF.h[:, g * 512:(g + 1) * 512], True, True, [c.CTT, STBF], [PB[7]],
                   first=True)
                VT("dve", y3, PB[7].h[:, :].rearrange("p (h d) -> p h d", d=64),
                   bc(c.EAC.h[:, 8 * g:8 * g + 8], [128, 8, 64], 2), ALU.mult, [PB[7], c.EAC], [Y], acc=(g == 1))
                VT("dve", ysl, ysl, PB[6].h[:, :], ALU.add, [Y, PB[6]], [Y], acc=True)
            else:
                CP("dve", ysl, PB[6].h[:, :], [PB[6]], [Y], acc=(g == 1))
            VT("pool", YT.h[:, :].rearrange("p (h d) -> p h d", d=64),
               BFA.h[:, g * 512:(g + 1) * 512].rearrange("p (h d) -> p h d", d=64),
               bc(DSK.h[:, 8 * g:8 * g + 8], [128, 8, 64], 2), ALU.mult, [BFA, CT_], [YT])
            VT("dve", ysl, ysl, YT.h[:, :], ALU.add, [Y, YT], [Y], acc=True)
            MM(PB[7].h[:, :], BTOK.h[:, g * 128:(g + 1) * 128], XRD.h[:, g * 512:(g + 1) * 512], True, True,
               [BTOK, XRD], [PB[7]], first=True)
            ssl = STATE.h[:, g * 512:(g + 1) * 512]
            if ci > 0:
                s3 = ssl.rearrange("p (h d) -> p h d", d=64)
                VT("dve", s3, s3, bc(c.CD.h[:, 8 * g:8 * g + 8], [128, 8, 64], 2), ALU.mult, [STATE, c.CD], [STATE],
                   acc=True)
                VT("dve", ssl, ssl, PB[7].h[:, :], ALU.add, [STATE, PB[7]], [STATE], acc=True)
            else:
                CP("dve", ssl, PB[7].h[:, :], [PB[7]], [STATE], acc=(g == 1))
            if ci < 15:
                CP("act", STBF.h[:, g * 512:(g + 1) * 512], ssl, [STATE], [STBF], acc=(g == 1))

    def gates(cb):
        cs = slice(cb * 512, (cb + 1) * 512)
        for kc in range(8):
            MM(PB[6].h[:, :], c.HT.h[:, kc, :], WING.h[:, kc, cs], kc == 0, kc == 7, [c.HT, WING], [PB[6]],
               first=(kc == 0))
        for kc in range(8):
            MM(PB[7].h[:, :], c.HT.h[:, kc, :], WING.h[:, kc, 1024 + cb * 512:1024 + (cb + 1) * 512], kc == 0, kc == 7,
               [c.HT, WING], [PB[7]], first=(kc == 0))
        ACT(TA.h[:, :], PB[6].h[:, :], AF.Tanh, [PB[6]], [TA], scale=0.5)
        ACT(TB.h[:, :], PB[7].h[:, :], AF.Tanh, [PB[7]], [TB], scale=0.5)

    def gate_norm_transpose():
        for g in range(2):
            bank = PB[4 + g]
            for kc in range(8):
                MM(bank.h[:, :], c.HT.h[:, kc, :], WINZ.h[:, kc, g * 512:(g + 1) * 512], kc == 0, kc == 7, [c.HT, WINZ],
                   [bank], first=(kc == 0))
            ACT(ZS.h[:, :], bank.h[:, :], AF.Silu, [bank], [ZS])
            ysl = Y.h[:, g * 512:(g + 1) * 512]
            VT("dve", ysl, ysl, ZS.h[:, :], ALU.mult, [Y, ZS], [Y], acc=True)
            ACT(ZS.h[:, :], ysl, AF.Square, [Y], [ZS, SSG], accum=SSG.h[:, g:g + 1], acc=(g == 1))
        gates(0)
        ACT(RG.h[:, :], SSG.h[:, :], AF.Ln, [SSG], [RG], bias=EPS, scale=1.0 / 512)
        ACT(RG.h[:, :], RG.h[:, :], AF.Exp, [RG], [RG], scale=-0.5)
        for g in range(2):
            VS("dve", BFA.h[:, g * 512:(g + 1) * 512], Y.h[:, g * 512:(g + 1) * 512], RG.h[:, g:g + 1], None, ALU.mult,
               None, [Y, RG], [BFA], acc=(g == 1))
        for j in range(8):
            TR(PT2[:, j * 128:(j + 1) * 128], BFA.h[:, j * 128:(j + 1) * 128], IDB.h[:, :], [BFA], [PT2t], first=(j == 0))
        VT("dve", c.YAT.h[:, :, :], PT2[:, :].rearrange("p (k t) -> p k t", k=8), bc(SSMN.h[:, :], [128, 8, 128], 2),
           ALU.mult, [PT2t, CT_], [c.YAT])

    def tail(ci):
        xb = XB[ci % 2]
        for cb in range(2):
            cs = slice(cb * 512, (cb + 1) * 512)
            for kc in range(8):
                MM(PB[4].h[:, :], c.YAT.h[:, kc, :], WA.h[:, kc, cs], kc == 0, kc == 7, [c.YAT, WA], [PB[4]], first=(kc == 0))
            for kc in range(4):
                MM(PB[5].h[:, :], c.YBT.h[:, kc, :], WB.h[:, kc, cs], kc == 0, kc == 3, [c.YBT, WB], [PB[5]], first=(kc == 0))
            STT("dve", QA.h[:, :], TA.h[:, :], 1.0, PB[4].h[:, :], ALU.add, ALU.mult, [TA, PB[4]], [QA])
            STT("dve", QB.h[:, :], TB.h[:, :], 1.0, PB[5].h[:, :], ALU.add, ALU.mult, [TB, PB[5]], [QB])
            if cb == 0:
                gates(1)
            VT("dve", MG.h[:, cs], QA.h[:, :], QB.h[:, :], ALU.add, [QA, QB], [MG], acc=(cb == 1))
        for j in range(8):
            TR(PT2[:, j * 128:(j + 1) * 128], MG.h[:, j * 128:(j + 1) * 128], IDB.h[:, :], [MG], [PT2t], first=(j == 0))
        S.op("act", lambda e: e.mul(MGT.h[:, :, :], PT2[:, :].rearrange("p (k t) -> p k t", k=8), 0.5),
             [PT2t], [MGT.t])
        for ob in range(2):
            bank = PB[4 + ob]
            for kc in range(8):
                MM(bank.h[:, :], MGT.h[:, kc, :], WO.h[:, kc, ob * 512:(ob + 1) * 512], kc == 0, kc == 7, [MGT, WO],
                   [bank], first=(kc == 0))
        ACT(TAJ.h[:, :], pb45[:, :], AF.Square, [PB[4], PB[5]], [TAJ, SSP], accum=SSP.h[:, 0:1])
        ACT(RP.h[:, :], SSP.h[:, 0:1], AF.Ln, [SSP], [RP], bias=EPS, scale=1.0 / 1024)
        ACT(RP.h[:, :], RP.h[:, :], AF.Exp, [RP], [RP], scale=-0.5)
        for ob in range(2):
            cs = slice(ob * 512, (ob + 1) * 512)
            STT("dve", OT.h[:, :], PB[4 + ob].h[:, :], RP.h[:, 0:1], NPOST.h[:, cs], ALU.mult, ALU.mult,
                [PB[4 + ob], RP, CT_], [OT])
            VT("dve", xb.h[:, cs], xb.h[:, cs], OT.h[:, :], ALU.add, [xb, OT], [xb], acc=True)
        DMA("sp", y_d[ci], xb.h[:, :], [xb], [OUTT], xb.t)

    def norm_pre(ci):
        xb = XB[ci % 2]
        ACT(XN.h[:, :], xb.h[:, :], AF.Square, [xb], [XN, SSPRE], accum=SSPRE.h[:, :])
        ACT(RPRE.h[:, :], SSPRE.h[:, :], AF.Ln, [SSPRE], [RPRE], bias=EPS, scale=1.0 / 1024)
        ACT(RPRE.h[:, :], RPRE.h[:, :], AF.Exp, [RPRE], [RPRE], scale=-0.5)
        VS("dve", XN.h[:, :], xb.h[:, :], RPRE.h[:, 0:1], None, ALU.mult, None, [xb, RPRE], [XN])
        for j in range(8):
            TR(PT.h[:, j * 128:(j + 1) * 128], XN.h[:, j * 128:(j + 1) * 128], IDB.h[:, :], [XN], [PT], first=(j == 0))
        VT("dve", c.HT.h[:, :, :], PT.h[:, :].rearrange("p (k t) -> p k t", k=8), bc(NPRE.h[:, :], [128, 8, 128], 2),
           ALU.mult, [PT, CT_], [c.HT])

    def out_fp32_T(src_fn, ncols, tiles, stage_ap_fn, stage_tiles, dram_ap, reads, key, banks=(0, 1)):
        done = 0
        nb = 0
        nt = len(tiles)
        while done < nt:
            n = min(4, nt - done)
            bank = PB[banks[nb % 2]]
            nb += 1
            for j in range(n):
                TR(bank.h[0:ncols, j * 128:(j + 1) * 128], src_fn(tiles[done + j]), IDF.h[:, :], reads, [bank],
                   first=(j == 0))
            CP("dve", stage_ap_fn(done * 128, (done + n) * 128), bank.h[0:ncols, 0:n * 128], [bank], stage_tiles,
               acc=(done > 0))
            done += n
        DMA("sp", dram_ap, stage_ap_fn(0, nt * 128), stage_tiles, [OUTT], key)

    def stage1(ci):
        setpar(ci % 2)
        norm_pre(ci)
        xbc4, ux4, uxt = proj_feature_major(ci, False)
        dt_path(False)
        conv_and_silu(xbc4, False)
        pool_branch(ci, ux4, uxt, False)
        if ci < 15:
            CP("pool", XBC.h[:, :, 0:3], XBC.h[:, :, 128:131], [XBC], [XBC], acc=True)
            CP("pool", UX.h[:, :, 0:15], UX.h[:, :, 128:143], [UX], [UX], acc=True)

    def stage23(ci):
        setpar(ci % 2)
        ssd_prompt(ci)
        gate_norm_transpose()
        tail(ci)

    MEMSET("dve", XBC.h[:, :, 0:3], 0.0, [XBC])
    MEMSET("dve", UX.h[:, :, 0:15], 0.0, [UX])
    load_x(0)
    stage1(0)
    for ci in range(nprompt):
        sa = None
        if ci + 1 < nprompt or do_sample:
            load_x(ci + 1)
        if ci + 1 < nprompt:
            S.begin()
            stage1(ci + 1)
            sa = S.end()
        S.begin()
        stage23(ci)
        sb = S.end()
        S.run_merged([sa, sb])

    XRF = alloc("xrf", [128, 512], F32, at=b_off["xr"], tile=XR.t)
    XRDF = alloc("xrdf", [128, 512], F32, at=b_off["xrd"], tile=XRD.t)

    def prompt_state_outputs():
        out_fp32_T(lambda t: XBC.h[:, t, 128:131], 3, list(range(8)), lambda a, b: Y.h[0:3, a:b], [Y.t],
                   convp_d[:, 0:1024], [XBC], Y.t, banks=(6, 7))
        out_fp32_T(lambda t: XBC.h[:, t, 128:131], 3, list(range(8, 12)), lambda a, b: ZS.h[0:3, a:b], [ZS.t],
                   convp_d[:, 1024:1536], [XBC], ZS.t, banks=(6, 7))
        out_fp32_T(lambda t: UX.h[:, t, 128:143], 15, list(range(4)), lambda a, b: XRF.h[0:15, a:b], [XR.t],
                   poolp_d[:, :], [UX], XR.t, banks=(6, 7))
        ssmp_v = ssmp_d.rearrange("(j p) n -> p j n", p=128)
        for half in range(2):
            bank = PB[6 + half]
            for j in range(4):
                jj = half * 4 + j
                TR(bank.h[:, j * 128:(j + 1) * 128], STATE.h[:, jj * 128:(jj + 1) * 128], IDF.h[:, :], [STATE], [bank],
                   first=(j == 0))
            CP("dve", XRDF.h[:, :], bank.h[:, :], [bank], [XRD])
            DMA("sp", ssmp_v[:, half * 4:(half + 1) * 4, :], XRDF.h[:, :].rearrange("p (j n) -> p j n", j=4), [XRD],
                [OUTT], XRD.t)

    sctx = {}

    def sample_front():
        ci = 16
        setpar(0)
        norm_pre(ci)
        SCVT = [R1T[0], R1T[1], R1T[2]]
        SPLT = [R1T[3], R1T[4]]
        DMA("act", SCV.h[0:48, :], sconv_d[:, :], [], SCVT, R1T[0])
        DMA("act", SPL.h[0:120, :, :], spool_d.rearrange("(h r) c -> r h c", h=2), [], SPLT, R1T[3])
        xbc4s = XBC.h[:, :, :].rearrange("p t (b j) -> p t b j", j=11)
        ux4s = UXS_h[:, :, :].rearrange("p t (b j) -> p t b j", j=23)
        for tile in range(12):
            bank = PB[4 + (tile % 2)]
            TR(bank.h[:, 0:48], SCV.h[0:48, tile * 128:(tile + 1) * 128], IDF.h[0:48, 0:48], SCVT, [bank], first=True)
            CP("dve", xbc4s[:, tile, :, 0:3], bank.h[:, 0:48].rearrange("p (b k) -> p b k", k=3), [bank], [XBC], acc=True)
        first_ux = True
        for k in range(4):
            for half in range(2):
                bank = PB[4 + half]
                TR(bank.h[:, 0:120], SPL.h[0:120, half, k * 128:(k + 1) * 128], IDF.h[0:120, 0:120], SPLT, [bank],
                   first=True)
                CP("dve", ux4s[:, k, 8 * half:8 * half + 8, 0:15], bank.h[:, 0:120].rearrange("p (b k) -> p b k", k=15),
                   [bank, STATE, STBF], [UXST, STATE, STBF], acc=not first_ux)
                first_ux = False
        xbc4, ux4, uxt = proj_feature_major(ci, True)
        conv_and_silu(xbc4, True)
        alias_tiles = [b.t for b in H0 + H0B + H0T + CTM + BM] + [CDC.t, CSTG.t, PSTG.t, OSTG.t, ADTX.t, P2s.t, P4s.t, P8s.t]
        S.op("pool", lambda e: e.memset(SCR.h[:, 0:1], 0.0), [], [WINX.t, WXB] + alias_tiles + [SCR.t])
        for tile in range(12):
            CP("pool", CSTG.h[:, tile, :].rearrange("p (b k) -> p b k", k=3), xbc4s[:, tile, :, 8:11], [XBC], [CSTG],
               acc=(tile > 0))
        out_fp32_T(lambda t: CSTG.h[:, t, :], 48, list(range(12)), lambda a, b: OSTG.h[0:48, a:b], [OSTG.t],
                   convs_d[:, :], [CSTG], OSTG.t)
        sctx.update(ux4=ux4, uxt=uxt, ux4s=ux4s)

    def sample_rest():
        ci = 16
        setpar(0)
        ux4, uxt, ux4s = sctx["ux4"], sctx["uxt"], sctx["ux4s"]
        ssv = sso = None
        dt_path(True)
        build_masks(True, 0)
        build_masks(True, 1)
        to_token_major()
        xs3 = BFA.h[:, :].rearrange("p (h d) -> p h d", d=64)
        VT("pool", XR.h[:, :].rearrange("p (h d) -> p h d", d=64), xs3, bc(c.DT.h[:, :], [128, 16, 64], 2), ALU.mult,
           [BFA, c.DT], [XR])
        VT("pool", XRD.h[:, :].rearrange("p (h d) -> p h d", d=64), xs3, bc(c.DDT.h[:, :], [128, 16, 64], 2), ALU.mult,
           [BFA, c.DDT], [XRD])
        for g in range(2):
            MM(B3.h[:, 64 + g * 128:64 + (g + 1) * 128], c.BT.h[:, g, :], c.CTT.h[:, g, :], True, True, [c.BT, c.CTT], [B3cb],
               first=(g == 0))
        VT("dve", CBM.h[:, :, :], B3.h[:, 64:320].rearrange("p (g l) -> p g l", g=2), bc(TRIS.h[:, :], [128, 2, 128], 1),
           ALU.mult, [B3cb, CT_], [CBM])
        for g in range(2):
            ssd_intra(ci, True, g)
            ysl = Y.h[:, g * 512:(g + 1) * 512]
            CP("dve", ysl, PB[6].h[:, :], [PB[6]], [Y], acc=(g == 1))
            VT("pool", YT.h[:, :].rearrange("p (h d) -> p h d", d=64),
               BFA.h[:, g * 512:(g + 1) * 512].rearrange("p (h d) -> p h d", d=64),
               bc(DSK.h[:, 8 * g:8 * g + 8], [128, 8, 64], 2), ALU.mult, [BFA, CT_], [YT])
            VT("dve", ysl, ysl, YT.h[:, :], ALU.add, [Y, YT], [Y], acc=True)
        VT("dve", ADTX.h[:, :].rearrange("p (h d) -> p h d", d=64), bc(c.ADT.h[:, :], [128, 16, 64], 2),
           bc(ONE.h[:, 0:16], [128, 16, 64], 2), ALU.mult, [c.ADT, CT_], [ADTX])
        for j in range(8):
            MM(PB[7].h[:, j * 16:(j + 1) * 16], ADTX.h[:, j * 128:(j + 1) * 128], SQM.h[:, :], True, True,
               [ADTX, CT_], [PB[7]], first=(j == 0))
        ACT(CDC.h[:, :, :], PB[7].h[:, 0:128].rearrange("p (j b) -> p j b", j=8), AF.Exp, [PB[7]], [CDC])
        for i in range(2):
            MEMSET("pool", CTM[i].h[:, :, :], 0.0, [CTM[i]])
        ssv = sssm_d.rearrange("b (j p) n -> b p j n", p=128)
        sso = ssms_d.rearrange("b (j p) n -> b p j n", p=128)
        NB = 3
        S.op("pool", lambda e: e.memset(SCR.h[:, 1:2], 0.0), [], [R1T[0], R1T[1], R1T[2], H0[2].t, H0B[2].t, SCR.t])

        def h0_load(b):
            rb = b % NB
            DMA("sp", H0[rb].h[:, :, :], ssv[b], [], [H0[rb]], H0[rb].t)
            DMA("pool", H0B[rb].h[:, :, :], ssv[b], [], [H0B[rb]], H0B[rb].t)

        def stage_a(b):
            rb = b % NB
            r2_ = b % 2
            for j in range(8):
                TR(PT.h[:, j * 128:(j + 1) * 128], H0B[rb].h[:, j, :], IDB.h[:, :], [H0B[rb]], [PT], first=(j == 0))
            CP("act", H0T[r2_].h[:, :], PT.h[:, :], [PT], [H0T[r2_]])
            CP("act", CTM[r2_].h[:, :, 8 * b:8 * b + 8], c.CTT.h[:, :, 8 * b:8 * b + 8], [c.CTT], [CTM[r2_]], acc=True)
            ACT(BM[r2_].h[:, :], BTOK.h[:, :], AF.Copy, [BTOK, CT_], [BM[r2_]], scale=SQM.h[:, b:b + 1])

        def stage_b(b):
            rb = b % NB
            r2_ = b % 2
            for g in range(2):
                MM(PB[g].h[:, :], CTM[r2_].h[:, g, :], H0T[r2_].h[:, g * 512:(g + 1) * 512], b == 0, b == 15,
                   [CTM[r2_], H0T[r2_]], [PB[g]], first=(b == 0))
            for j in range(8):
                bank = PB[4 + j // 4]
                MM(bank.h[:, (j % 4) * 128:(j % 4 + 1) * 128], XRD.h[:, j * 128:(j + 1) * 128],
                   BM[r2_].h[:, (j // 4) * 128:(j // 4 + 1) * 128], True, True, [XRD, BM[r2_]], [bank], first=(j % 4 == 0))
            ACT(CTM[r2_].h[:, :, 8 * b:8 * b + 8], CTM[r2_].h[:, :, 8 * b:8 * b + 8], AF.Copy, [CTM[r2_]], [CTM[r2_]],
                scale=0.0, acc=True)
            for j in range(8):
                bank = PB[4 + j // 4]
                STT("dve", H0[rb].h[:, j, :], H0[rb].h[:, j, :], CDC.h[:, j, b:b + 1],
                    bank.h[:, (j % 4) * 128:(j % 4 + 1) * 128], ALU.mult, ALU.add, [H0[rb], CDC, bank], [H0[rb]], acc=True)
            DMA("act", sso[b], H0[rb].h[:, :, :], [H0[rb]], [OUTT], H0[rb].t)

        h0_load(0)
        h0_load(1)
        stage_a(0)
        for b in range(16):
            if b + 1 < 16:
                stage_a(b + 1)
            stage_b(b)
            if b + 2 < 16:
                h0_load(b + 2)
        S.op("pool", lambda e: e.memset(SCR.h[:, 2:3], 0.0), [], [R1T[0], R1T[1], R1T[2], H0[2].t, H0B[2].t, SCR.t])
        for g in range(2):
            ysl = Y.h[:, g * 512:(g + 1) * 512]
            VT("dve", YT.h[:, :].rearrange("p (h d) -> p h d", d=64), PB[g].h[:, :].rearrange("p (h d) -> p h d", d=64),
               bc(c.EAC.h[:, 8 * g:8 * g + 8], [128, 8, 64], 2), ALU.mult, [PB[g], c.EAC], [YT])
            VT("dve", ysl, ysl, YT.h[:, :], ALU.add, [Y, YT], [Y], acc=True)
        pool_branch(ci, ux4, uxt, True)
        for k in range(4):
            CP("pool", PSTG.h[:, k, :].rearrange("p (b r) -> p b r", r=15), ux4s[:, k, :, 8:23], [UXST], [PSTG],
               acc=(k > 0))
        for half in range(2):
            bank = PB[4 + half]
            for k in range(4):
                TR(bank.h[0:120, k * 128:(k + 1) * 128], PSTG.h[:, k, half * 120:(half + 1) * 120], IDF.h[:, :], [PSTG],
                   [bank], first=(k == 0))
            CP("dve", OSTG.h[0:120, half * 512:(half + 1) * 512], bank.h[0:120, :], [bank], [OSTG], acc=True)
        DMA("sp", pools_d.rearrange("(h r) c -> r h c", h=2), OSTG.h[0:120, 0:1024].rearrange("r (h c) -> r h c", h=2),
            [OSTG], [OUTT], OSTG.t)
        gate_norm_transpose()
        tail(ci)

    sa = sb = None
    if do_pstate:
        S.begin()
        prompt_state_outputs()
        sa = S.end()
    if do_sample:
        S.begin()
        sample_front()
        sb = S.end()
    S.run_merged([sa, sb])
    if do_sample:
        sample_rest()
    S.op("sp", None, [OUTT], [])

    S.finalize(nc, es)
    block = es.enter_context(nc.Block())

    @block.tensor
    def _(e):
        S.emit_engine(e, "pe")

    @block.scalar
    def _(e):
        S.emit_engine(e, "act")

    @block.vector
    def _(e):
        S.emit_engine(e, "dve")

    @block.gpsimd
    def _(e):
        S.emit_engine(e, "pool")

    @block.sync
    def _(e):
        S.emit_engine(e, "sp")

    es.close()
    return nc


_NC_CACHE = {}


def _prep_inputs(inp):
    f = lambda a: np.ascontiguousarray(np.asarray(a, dtype=np.float32))
    xp = f(inp["x_prompt"])
    xs = f(inp["x_sample"])
    sc = f(inp["state_conv"])[0]
    ss = f(inp["state_ssm"])[0]
    sp = f(inp["state_pool"])[0]
    shared = {
        "w_in": f(inp["w_in"])[0],
        "w_a": f(inp["w_branch_a"])[0],
        "w_b": f(inp["w_branch_b"])[0],
        "w_out": f(inp["w_out"])[0],
        "pool_mix": f(inp["pool_mix"])[0],
        "npre_col": f(f(inp["norm_pre"])[0].reshape(8, 128).T),
        "convw_col": f(f(inp["conv_w"])[0].reshape(4, 12, 128).transpose(2, 1, 0)),
        "convb_col": f(f(inp["conv_b"])[0].reshape(12, 128).T),
        "ssmn_col": f(f(inp["ssm_norm"])[0].reshape(8, 128).T),
        "pscale_col": f(f(inp["pool_scale"])[0].reshape(4, 128).T),
        "dtb_bc": f(np.broadcast_to(f(inp["dt_bias"])[0][None, :], (128, 16))),
        "alog_bc": f(np.broadcast_to(f(inp["a_log"])[0][None, :], (128, 16))),
        "dskip_bc": f(np.broadcast_to(f(inp["d_skip"])[0][None, :], (128, 16))),
        "npost_bc": f(np.broadcast_to(f(inp["norm_post"])[0][None, :], (128, 1024))),
    }
    maps = []
    for c in range(8):
        x = np.concatenate([xp[c].reshape(16, 128, 1024), xs[16 * c:16 * c + 16].reshape(1, 128, 1024)], axis=0)
        m = dict(shared)
        m["x"] = f(x)
        m["sconv"] = f(sc[16 * c:16 * c + 16].reshape(48, 1536))
        m["sssm"] = f(ss[16 * c:16 * c + 16].reshape(16, 1024, 128))
        m["spool"] = f(sp[16 * c:16 * c + 16].reshape(240, 512))
        maps.append(m)
    return maps


def kernel(**inputs):
    if "nc" not in _NC_CACHE:
        _NC_CACHE["nc"] = build_program()
    nc = _NC_CACHE["nc"]
    maps = _prep_inputs(inputs)
    res = run_bass_kernel_spmd(nc, maps, core_ids=list(range(8)))
    R = res.results
    yp = np.stack([R[c]["y"][:16].reshape(2048, 1024) for c in range(8)], axis=0)
    ys = np.concatenate([R[c]["y"][16].reshape(16, 8, 1024) for c in range(8)], axis=0)
    convp = np.stack([R[c]["convp"] for c in range(8)], axis=0)[None]
    ssmp = np.stack([R[c]["ssmp"].reshape(16, 64, 128) for c in range(8)], axis=0)[None]
    poolp = np.stack([R[c]["poolp"] for c in range(8)], axis=0)[None]
    convs = np.concatenate([R[c]["convs"].reshape(16, 3, 1536) for c in range(8)], axis=0)[None]
    ssms = np.concatenate([R[c]["ssms"].reshape(16, 16, 64, 128) for c in range(8)], axis=0)[None]
    pools = np.concatenate([R[c]["pools"].reshape(16, 15, 512) for c in range(8)], axis=0)[None]
    out = (yp, ys, convp, ssmp, poolp, convs, ssms, pools)
    return tuple(np.ascontiguousarray(o, dtype=np.float32) for o in out)
```

```python
import os
import numpy as np
from contextlib import ExitStack
import concourse.bass as bass
import concourse.mybir as mybir
from concourse.bass_utils import run_bass_kernel_spmd

F32 = mybir.dt.float32
BF16 = mybir.dt.bfloat16
AF = mybir.ActivationFunctionType
ALU = mybir.AluOpType

NCHUNK = 17
XLAT = float(os.environ.get("XLAT", 0.9))
SAME_ENG_ALL = int(os.environ.get("SAME_ENG_ALL", 1))
EPS = 1e-6


class Tile:
    __slots__ = ("name", "writers", "readers", "sem", "cnt", "collector", "psum")

    def __init__(self, name, collector=False, psum=False):
        self.psum = psum
        self.name = name
        self.writers = []
        self.readers = []
        self.sem = None
        self.cnt = 0
        self.collector = collector


class Op:
    __slots__ = ("eng", "fn", "reads", "writes", "acc", "dma", "depc", "depdma",
                 "need_inc", "ticket", "dticket", "seq", "dsem", "t_end")


class Sched:
    ENGS = ("pe", "act", "dve", "pool", "sp")
    DMA_SEM_MAX = 224
    ROT = int(os.environ.get("ROT", 1000))

    def __init__(self):
        self.ops = []
        self.per = {e: [] for e in self.ENGS}
        self.dma_keys = []
        self.cur = None
        self.eng_t = {e: 0.0 for e in self.ENGS}
        self.act_set = None

    def begin(self):
        self.cur = []
        return self.cur

    def end(self):
        st = self.cur
        self.cur = None
        return st

    def _est_start(self, a):
        eng, fn, reads, writes, acc, dma, cost, aset = a
        t = self.eng_t[eng]
        if aset is not None and aset != self.act_set:
            t += 1.3
        for tl_ in reads:
            for w in tl_.writers:
                te = w.t_end + (0.0 if w.eng == eng else XLAT)
                if te > t:
                    t = te
            if tl_.psum:
                for r in tl_.readers:
                    if r.eng != eng and r.t_end > t:
                        t = r.t_end
        for tl_ in writes:
            if tl_.collector:
                continue
            for w in tl_.writers:
                te = w.t_end + (0.0 if w.eng == eng else XLAT)
                if te > t:
                    t = te
            for r in tl_.readers:
                te = r.t_end + (0.0 if r.eng == eng else XLAT)
                if te > t:
                    t = te
        return t

    def run_merged(self, streams):
        streams = [st for st in streams if st]
        pos = [0] * len(streams)
        while True:
            best = -1
            bt = 1e30
            for i, st in enumerate(streams):
                if pos[i] < len(st):
                    t = self._est_start(st[pos[i]])
                    t += 0.05 * pos[i] / len(st)
                    if t < bt:
                        bt = t
                        best = i
            if best < 0:
                break
            a = streams[best][pos[best]]
            pos[best] += 1
            self.op(*a)

    def op(self, eng, fn, reads=(), writes=(), acc=False, dma=None, cost=0.3, aset=None):
        if self.cur is not None:
            self.cur.append((eng, fn, list(reads), list(writes), acc, dma, cost, aset))
            return None
        t_start = self._est_start((eng, fn, reads, writes, acc, dma, cost, aset))
        if aset is not None:
            self.act_set = aset
        o = Op()
        if dma is not None:
            self.eng_t[eng] = t_start + 0.1
            o.t_end = t_start + cost
        else:
            o.t_end = t_start + cost
            self.eng_t[eng] = o.t_end
        o.eng = eng
        o.fn = fn
        o.reads = list(reads)
        o.writes = list(writes)
        o.acc = acc
        o.dma = dma
        o.depc = []
        o.depdma = []
        o.need_inc = False
        o.ticket = 0
        o.dticket = 0
        if dma is not None:
            if dma.cnt == 0 and dma not in self.dma_keys:
                self.dma_keys.append(dma)
            o.dsem = dma.cnt // self.DMA_SEM_MAX
            dma.cnt += 16
            o.dticket = dma.cnt - o.dsem * self.DMA_SEM_MAX
        deps = {}
        for t in o.reads:
            for w in t.writers:
                deps[id(w)] = (w, True)
            if t.psum:
                for r in t.readers:
                    if r.eng != eng and id(r) not in deps:
                        deps[id(r)] = (r, False)
        for t in o.writes:
            if t.collector:
                continue
            for w in t.writers:
                if id(w) not in deps:
                    deps[id(w)] = (w, False)
            for r in t.readers:
                if id(r) not in deps:
                    deps[id(r)] = (r, None)
        for t in o.reads:
            t.readers.append(o)
        for t in o.writes:
            if t.collector or acc:
                t.writers.append(o)
            else:
                t.writers = [o]
                t.readers = []
        latest = {}
        for d, raw in deps.values():
            if d is o:
                continue
            if d.dma is not None:
                o.depdma.append(d)
                continue
            if d.eng == eng:
                if eng == "pe":
                    continue
                if raw is None and SAME_ENG_ALL < 1:
                    continue
                if raw is False and SAME_ENG_ALL < 1 and SAME_ENG_ALL > -1:
                    pass
                if raw is False and SAME_ENG_ALL < 0:
                    continue
                if raw is False and acc and d.acc:
                    continue
            c_ = latest.get(d.eng)
            if c_ is None or d.seq > c_.seq:
                latest[d.eng] = d
        for d in latest.values():
            d.need_inc = True
            o.depc.append(d)
        o.seq = len(self.ops)
        self.ops.append(o)
        self.per[eng].append(o)
        return o

    def finalize(self, nc, es):
        self.engsem = {e: [] for e in self.ENGS}
        nsem = 0
        for i, k in enumerate(self.dma_keys):
            n = (k.cnt + self.DMA_SEM_MAX - 1) // self.DMA_SEM_MAX
            k.sem = [es.enter_context(nc.semaphore("dq%d_%d" % (i, j))) for j in range(n)]
            nsem += n
        print("dma sems", nsem)
        for e in self.ENGS:
            c = 0
            for o in self.per[e]:
                if o.need_inc:
                    c += 1
                    o.ticket = c
            print("engine", e, "ops", len(self.per[e]), "tickets", c)
            nse = (c + self.ROT - 1) // self.ROT
            self.engsem[e] = [es.enter_context(nc.semaphore("sem_%s%d" % (e, j))) for j in range(max(nse, 1))]

    def emit_engine(self, e, eng):
        waited = {}
        nw = 0
        for o in self.per[eng]:
            waits = {}
            for d in o.depc:
                s = self.engsem[d.eng][(d.ticket - 1) // self.ROT]
                k = id(s)
                tv = (d.ticket - 1) % self.ROT + 1
                if waits.get(k, (None, 0))[1] < tv:
                    waits[k] = (s, tv)
            for d in o.depdma:
                s = d.dma.sem[d.dsem]
                k = id(s)
                if waits.get(k, (None, 0))[1] < d.dticket:
                    waits[k] = (s, d.dticket)
            for k, (s, v) in waits.items():
                if waited.get(k, 0) >= v:
                    continue
                waited[k] = v
                e.wait_ge(s, v)
                nw += 1
                if os.environ.get("DUMPW") and o.seq >= int(os.environ.get("DUMPW")):
                    print("W", eng, o.seq, getattr(s, "name", s), v)
            if os.environ.get("DUMPW") and o.seq >= int(os.environ.get("DUMPW")):
                print("OP", eng, o.seq, "inc" if o.need_inc else "", o.ticket)
            ins = o.fn(e) if o.fn is not None else None
            if o.need_inc:
                assert ins is not None
                ins.then_inc(self.engsem[eng][(o.ticket - 1) // self.ROT], 1)
            if o.dma is not None:
                ins.then_inc(o.dma.sem[o.dsem], 16)
        print("emit", eng, "ops", len(self.per[eng]), "waits", nw)


class Buf:
    __slots__ = ("h", "t")

    def __init__(self, h, t):
        self.h = h
        self.t = t


def build_program(dbg=None, nprompt=16, do_pstate=True, do_sample=True):
    nc = bass.Bass("TRN2", target_bir_lowering=False)
    S = Sched()
    es = ExitStack()

    def din(name, shape, dt=F32):
        return nc.dram_tensor(name, shape, dt, kind="ExternalInput").ap()

    def dout(name, shape, dt=F32):
        return nc.dram_tensor(name, shape, dt, kind="ExternalOutput").ap()

    x_d = din("x", [NCHUNK, 128, 1024])
    sconv_d = din("sconv", [48, 1536])
    sssm_d = din("sssm", [16, 1024, 128])
    spool_d = din("spool", [240, 512])
    win_d = din("w_in", [1024, 5648])
    wa_d = din("w_a", [1024, 1024])
    wb_d = din("w_b", [512, 1024])
    wo_d = din("w_out", [1024, 1024])
    pmix_d = din("pool_mix", [4, 128, 128])
    npre_d = din("npre_col", [128, 8])
    convw_d = din("convw_col", [128, 12, 4])
    convb_d = din("convb_col", [128, 12])
    ssmn_d = din("ssmn_col", [128, 8])
    pscale_d = din("pscale_col", [128, 4])
    dtb_d = din("dtb_bc", [128, 16])
    alog_d = din("alog_bc", [128, 16])
    dskip_d = din("dskip_bc", [128, 16])
    npost_d = din("npost_bc", [128, 1024])

    y_d = dout("y", [NCHUNK, 128, 1024])
    convp_d = dout("convp", [3, 1536])
    ssmp_d = dout("ssmp", [1024, 128])
    poolp_d = dout("poolp", [15, 512])
    convs_d = dout("convs", [48, 1536])
    ssms_d = dout("ssms", [16, 1024, 128])
    pools_d = dout("pools", [240, 512])
    dbg_outs = {}
    OUTT = Tile("dram_out", collector=True)

    ARENA = 212800
    arena = nc.alloc_sbuf_tensor("arena", [128, ARENA // 4], F32)
    base = nc.lookup_mloc(arena).addr
    cur = [0]

    def nbytes(shape, dt):
        n = 1
        for s in shape[1:]:
            n *= s
        return n * (2 if dt == BF16 else 4)

    def alloc(name, shape, dt, at=None, tile=None):
        sz = (nbytes(shape, dt) + 31) // 32 * 32
        if at is None:
            off = cur[0]
            cur[0] += sz
            assert cur[0] <= ARENA, (name, cur[0])
        else:
            off = at
        h = nc.alloc_sbuf_tensor_at(name, shape, dt, offset=base + off)
        b = Buf(h, tile if tile is not None else Tile(name))
        b_off[name] = off
        return b

    b_off = {}

    WINX = alloc("winx", [128, 8, 2576], BF16)
    WINZ = alloc("winz", [128, 8, 1024], BF16)
    WING = alloc("wing", [128, 8, 2048], BF16)
    WA = alloc("wa", [128, 8, 1024], BF16)
    WB = alloc("wb", [128, 4, 1024], BF16)
    WO = alloc("wo", [128, 8, 1024], BF16)
    PMIX = alloc("pmix", [128, 4, 128], BF16)
    CT_ = Tile("consts", collector=True)
    IDB = alloc("idb", [128, 128], BF16, tile=CT_)
    IDF = alloc("idf", [128, 128], F32, tile=CT_)
    TRI = alloc("tri", [128, 128], BF16, tile=CT_)
    LST = alloc("lst", [128, 128], BF16, tile=CT_)
    ONE = alloc("one", [128, 128], BF16, tile=CT_)
    TRIS = alloc("tris", [128, 128], BF16, tile=CT_)
    LSTS = alloc("lsts", [128, 128], BF16, tile=CT_)
    SSQ = alloc("ssq", [128, 128], BF16, tile=CT_)
    SQM = alloc("sqm", [128, 16], F32, tile=CT_)
    NPRE = alloc("npre", [128, 8], F32, tile=CT_)
    CONVW = alloc("convw", [128, 12, 4], F32, tile=CT_)
    CONVB = alloc("convb", [128, 12], F32, tile=CT_)
    SSMN = alloc("ssmn", [128, 8], F32, tile=CT_)
    PSC = alloc("psc", [128, 4], F32, tile=CT_)
    DTB = alloc("dtb", [128, 16], F32, tile=CT_)
    ABC = alloc("abc", [128, 16], F32, tile=CT_)
    DSK = alloc("dsk", [128, 16], F32, tile=CT_)
    NPOST = alloc("npostb", [128, 1024], F32, tile=CT_)
    INVC = alloc("invc", [128, 16], F32, tile=CT_)
    SCR = alloc("scr", [128, 16], F32)
    NEGM = alloc("negm", [128, 128], BF16, tile=CT_)
    NEGMS = alloc("negms", [128, 128], BF16, tile=CT_)
    NACS = alloc("nacs", [128, 16], F32)
    XB = [alloc("xb0", [128, 1024], F32), alloc("xb1", [128, 1024], F32)]
    BFA = alloc("bfa", [128, 1024], BF16)
    XN = alloc("xn", [128, 1024], BF16)
    HTs = [alloc("ht0", [128, 8, 128], BF16), alloc("ht1", [128, 8, 128], BF16)]
    XBC = alloc("xbc", [128, 12, 176], F32)
    UX = alloc("ux", [128, 4, 143], F32)
    CACC = [alloc("cacc%d" % i, [128, 128], F32) for i in range(4)]
    XSTs = [alloc("xst0", [128, 8, 128], BF16), alloc("xst1", [128, 8, 128], BF16)]
    BTs = [alloc("bt0", [128, 2, 128], BF16), alloc("bt1", [128, 2, 128], BF16)]
    CTTs = [alloc("ct0", [128, 2, 128], BF16), alloc("ct1", [128, 2, 128], BF16)]
    GS = alloc("gs", [128, 4, 128], F32)
    DTRs = [alloc("dtr0", [128, 16], F32), alloc("dtr1", [128, 16], F32)]
    DTs_ = [alloc("dt%d" % i, [128, 16], F32) for i in range(2)]
    ADTs_ = [alloc("adt%d" % i, [128, 16], F32) for i in range(2)]
    DIF = alloc("dif", [128, 16], F32)
    EACs_ = [alloc("eac%d" % i, [128, 16], F32) for i in range(2)]
    DTE = alloc("dte", [128, 16], F32)
    CDs_ = [alloc("cd%d" % i, [128, 16], F32) for i in range(2)]
    DDTs_ = [alloc("ddt%d" % i, [128, 16], F32) for i in range(2)]
    AHIs_ = [alloc("ahi%d" % i, [128, 16], BF16) for i in range(2)]
    ALOs_ = [alloc("alo%d" % i, [128, 16], BF16) for i in range(2)]
    BTOK = alloc("btok", [128, 256], BF16)
    XR = alloc("xr", [128, 1024], BF16)
    XRD = alloc("xrd", [128, 1024], BF16)
    MG = Buf(XR.h, XR.t)
    MGT_h = nc.alloc_sbuf_tensor_at("mgt", [128, 8, 128], BF16, offset=base + b_off["xrd"])
    MGT = Buf(MGT_h, XRD.t)
    R1T = [Tile("r1_%d" % i) for i in range(5)]
    r1 = cur[0]
    cur[0] += 10240
    AMH = alloc("amh", [128, 8, 128], BF16, at=r1, tile=R1T[0])
    AML = alloc("aml", [128, 8, 128], BF16, at=r1 + 2048, tile=R1T[1])
    DEC = alloc("dec", [128, 8, 128], BF16, at=r1 + 4096, tile=R1T[2])
    MMT = alloc("mmt", [128, 8, 128], BF16, at=r1 + 6144, tile=R1T[3])
    YT = alloc("yt", [128, 512], F32, at=r1 + 8192, tile=R1T[4])
    TA = alloc("ta", [128, 512], F32, at=r1, tile=R1T[0])
    TAJ = alloc("taj", [128, 1024], BF16, at=r1, tile=R1T[0])
    TB = alloc("tb", [128, 512], F32, at=r1 + 2048, tile=R1T[1])
    QA = alloc("qa", [128, 512], F32, at=r1 + 4096, tile=R1T[2])
    QB = alloc("qb", [128, 512], F32, at=r1 + 6144, tile=R1T[3])
    OT = alloc("ot", [128, 512], F32, at=r1 + 8192, tile=R1T[4])
    CBM = alloc("cbm", [128, 2, 128], BF16)
    STATE = alloc("state", [128, 1024], F32)
    STBF = alloc("stbf", [128, 1024], BF16)
    Y = alloc("y", [128, 1024], F32)
    ZS = alloc("zs", [128, 512], F32)
    AMH0 = alloc("amh0", [128, 8, 128], BF16, at=b_off["y"], tile=Y.t)
    AML0 = alloc("aml0", [128, 8, 128], BF16, at=b_off["y"] + 2048, tile=Y.t)
    P2p = alloc("p2", [128, 144], F32)
    P4p = alloc("p4", [128, 144], F32)
    P8p = alloc("p8", [128, 144], F32)
    P16 = alloc("p16", [128, 128], F32)
    PLD = alloc("pld", [128, 4, 128], BF16)
    YBTs = [alloc("ybt0", [128, 4, 128], BF16), alloc("ybt1", [128, 4, 128], BF16)]
    SSPRE = alloc("sspre", [128, 1], F32)
    RPRE = alloc("rpre", [128, 1], F32)
    SSG = alloc("ssg", [128, 2], F32)
    RG = alloc("rg", [128, 2], F32)
    SSP = alloc("ssp", [128, 2], F32)
    RP = alloc("rp", [128, 1], F32)
    UXS_h = nc.alloc_sbuf_tensor_at("uxs", [128, 4, 368], F32, offset=base + b_off["state"])
    assert b_off["stbf"] == b_off["state"] + 4096
    UXST = Tile("uxs")
    wx = b_off["winx"]
    H0 = [alloc("h0_%d" % i, [128, 8, 128], F32, at=wx + i * 4096) for i in range(2)]
    H0B = [alloc("h0b_%d" % i, [128, 8, 128], BF16, at=wx + 8192 + i * 2048) for i in range(2)]
    H0T = [alloc("h0t_%d" % i, [128, 1024], BF16, at=wx + 12288 + i * 2048) for i in range(2)]
    CTM = [alloc("ctm_%d" % i, [128, 2, 128], BF16, at=wx + 16384 + i * 512) for i in range(2)]
    BM = [alloc("bm_%d" % i, [128, 256], BF16, at=wx + 17408 + i * 512) for i in range(2)]
    H0.append(alloc("h0_2", [128, 8, 128], F32, at=r1))
    H0B.append(alloc("h0b_2", [128, 8, 128], BF16, at=r1 + 4096))
    CDC = alloc("cdc", [128, 8, 16], F32, at=wx + 18432)
    CSTG = alloc("cstg", [128, 12, 48], F32, at=wx + 18944)
    PSTG = alloc("pstg", [128, 4, 240], F32, at=wx + 18944 + 2304)
    OSTG = alloc("ostg", [128, 1536], F32, at=wx + 18944 + 2304 + 3840)
    ADTX = alloc("adtx", [128, 1024], F32, at=wx + 18944 + 2304 + 3840 + 6144)
    P2s = alloc("p2s", [128, 368], F32, at=wx + 35328)
    P4s = alloc("p4s", [128, 368], F32, at=wx + 35328 + 1472)
    P8s = alloc("p8s", [128, 368], F32, at=wx + 35328 + 2944)
    assert 35328 + 3 * 1472 <= 41216
    SCV = alloc("scv", [128, 1536], F32, at=r1, tile=R1T[0])
    SPL = alloc("spl", [128, 2, 512], F32, at=r1 + 6144, tile=R1T[3])
    print("SBUF used", cur[0], "of", ARENA)

    PB = []
    pb45 = es.enter_context(nc.psum_tensor("pb45", [128, 1024], F32))
    for i in range(8):
        if i == 2:
            h = es.enter_context(nc.psum_tensor("pb2", [128, 1024], BF16))
        elif i == 4:
            h = pb45[:, 0:512]
        elif i == 5:
            h = pb45[:, 512:1024]
        else:
            h = es.enter_context(nc.psum_tensor("pb%d" % i, [128, 512], F32))
        PB.append(Buf(h, Tile("pb%d" % i, psum=True)))
    PT = PB[2]
    B3 = PB[3]
    PT2 = PB[7].h[:, :].bitcast(BF16)
    PT2t = PB[7].t
    B3dt = B3.t
    B3ac = B3.t
    B3cb = B3.t

    def tl(bufs):
        return [b.t if isinstance(b, Buf) else b for b in bufs]

    def fsz(ap):
        n = 1
        for d in ap.shape[1:]:
            n *= d
        return n

    def MM(out, lhsT, rhs, start, stop, r, w, first):
        n = fsz(rhs)
        S.op("pe", lambda e: e.matmul(out, lhsT=lhsT, rhs=rhs, start=start, stop=stop),
             tl(r), tl(w), acc=not first, cost=0.06 + max(n, 64) / 2000.0)

    def TR(out, in_, ident, r, w, first):
        S.op("pe", lambda e: e.transpose(out, in_, ident), tl(r) + [CT_], tl(w), acc=not first, cost=0.12)

    def ecost(eng, out):
        n = fsz(out)
        if eng == "pool":
            return 0.25 + n * 0.0017
        if eng == "act":
            return 0.25 + n * 0.00085
        return 0.15 + n * 0.00105

    def VT(eng, out, in0, in1, op, r, w, acc=False):
        S.op(eng, lambda e: e.tensor_tensor(out=out, in0=in0, in1=in1, op=op), tl(r), tl(w), acc=acc,
             cost=ecost(eng, out))

    def VS(eng, out, in0, s1, s2, op0, op1, r, w, acc=False):
        if s2 is None:
            S.op(eng, lambda e: e.tensor_scalar(out=out, in0=in0, scalar1=s1, scalar2=None, op0=op0),
                 tl(r), tl(w), acc=acc, cost=ecost(eng, out))
        else:
            S.op(eng, lambda e: e.tensor_scalar(out=out, in0=in0, scalar1=s1, scalar2=s2, op0=op0, op1=op1),
                 tl(r), tl(w), acc=acc, cost=ecost(eng, out))

    def STT(eng, out, in0, sc, in1, op0, op1, r, w, acc=False):
        S.op(eng, lambda e: e.scalar_tensor_tensor(out=out, in0=in0, scalar=sc, in1=in1, op0=op0, op1=op1),
             tl(r), tl(w), acc=acc, cost=ecost(eng, out))

    def CP(eng, out, in_, r, w, acc=False):
        if eng == "act":
            S.op(eng, lambda e: e.activation(out=out, in_=in_, func=AF.Copy), tl(r), tl(w), acc=acc,
                 cost=ecost(eng, out))
        else:
            S.op(eng, lambda e: e.tensor_copy(out=out, in_=in_), tl(r), tl(w), acc=acc, cost=ecost(eng, out))

    def ACT(out, in_, func, r, w, bias=None, scale=None, accum=None, acc=False):
        kw = {}
        if bias is not None:
            kw["bias"] = bias
        if scale is not None:
            kw["scale"] = scale
        if accum is not None:
            kw["accum_out"] = accum
        aset = "A" if func in (AF.Silu, AF.Tanh) else ("B" if func in (AF.Exp, AF.Ln) else None)
        S.op("act", lambda e: e.activation(out=out, in_=in_, func=func, **kw), tl(r), tl(w), acc=acc,
             cost=ecost("act", out), aset=aset)

    def MEMSET(eng, ap, val, w, acc=False):
        S.op(eng, lambda e: e.memset(ap, val), [], tl(w), acc=acc)

    def DMA(eng, out, in_, r, w, key, acc=False):
        S.op(eng, lambda e: e.dma_start(out=out, in_=in_), tl(r), tl(w), acc=acc, dma=key, cost=3.0)

    def bc(ap, shape, axis):
        return ap.unsqueeze(axis).to_broadcast(shape)

    winv = win_d.rearrange("(kc p) e -> p kc e", p=128)
    WXB = Tile("winx_b")
    DMA("pool", WINX.h[:, :, 0:1536], winv[:, :, 1024:2560], [], [WINX], WINX.t)
    DMA("pool", WINX.h[:, :, 1536:2576], winv[:, :, 2560:3600], [], [WXB], WXB)
    small = [(NPRE, npre_d), (CONVW, convw_d), (CONVB, convb_d), (SSMN, ssmn_d), (PSC, pscale_d),
             (DTB, dtb_d), (ABC, alog_d), (DSK, dskip_d), (NPOST, npost_d)]
    ctl = {}

    def ct(bf):
        k = id(bf)
        if k not in ctl:
            ctl[k] = Tile("c%d" % len(ctl))
        return ctl[k]

    for b_, d_ in small:
        if len(d_.shape) == 3:
            DMA("act", b_.h[:, :, :], d_[:, :, :], [], [ct(b_)], ct(b_))
        else:
            DMA("act", b_.h[:, :], d_[:, :], [], [ct(b_)], ct(b_))
    DMA("pool", PMIX.h[:, :, :], pmix_d.rearrange("k c d -> c k d"), [], [PMIX], PMIX.t)
    DMA("pool", WINZ.h[:, :, :], winv[:, :, 0:1024], [], [WINZ], WINZ.t)
    DMA("pool", WING.h[:, :, :], winv[:, :, 3600:5648], [], [WING], WING.t)
    DMA("pool", WB.h[:, :, :], wb_d.rearrange("(kc p) e -> p kc e", p=128), [], [WB], WB.t)
    DMA("pool", WA.h[:, :, :], wa_d.rearrange("(kc p) e -> p kc e", p=128), [], [WA], WA.t)
    DMA("pool", WO.h[:, :, :], wo_d.rearrange("(kc p) e -> p kc e", p=128), [], [WO], WO.t)

    def aff(bf, eng_ap, pattern, cmp, fill, base_, cm):
        S.op("pool", lambda e: e.affine_select(out=eng_ap, in_=eng_ap, pattern=pattern, compare_op=cmp,
                                                fill=fill, base=base_, channel_multiplier=cm), [ct(bf)], [ct(bf)])

    for I_ in (IDB, IDF):
        MEMSET("pool", I_.h[:, :], 0.0, [ct(I_)])
        aff(I_, I_.h[:, :], [[-1, 128]], ALU.not_equal, 1.0, 0, 1)
    MEMSET("pool", TRI.h[:, :], 1.0, [ct(TRI)])
    aff(TRI, TRI.h[:, :], [[1, 128]], ALU.is_ge, 0.0, 0, -1)
    MEMSET("pool", LST.h[:, :], 1.0, [ct(LST)])
    aff(LST, LST.h[:, :], [[-1, 128]], ALU.is_gt, 0.0, 0, 1)
    MEMSET("pool", ONE.h[:, :], 1.0, [ct(ONE)])
    MEMSET("pool", SSQ.h[:, :], 1.0, [ct(SSQ)])
    ssq3 = SSQ.h[:, :].rearrange("p (b j) -> p b j", j=8)
    aff(SSQ, ssq3, [[-8, 16], [0, 8]], ALU.is_ge, 0.0, 0, 1)
    aff(SSQ, ssq3, [[8, 16], [0, 8]], ALU.is_ge, 0.0, 7, -1)
    MEMSET("pool", SQM.h[:, :], 1.0, [ct(SQM)])
    aff(SQM, SQM.h[:, :], [[-8, 16]], ALU.is_ge, 0.0, 0, 1)
    aff(SQM, SQM.h[:, :], [[8, 16]], ALU.is_ge, 0.0, 7, -1)
    S.op("pool", lambda e: e.tensor_tensor(out=TRIS.h[:, :], in0=TRI.h[:, :], in1=SSQ.h[:, :], op=ALU.mult),
         [ct(TRI), ct(SSQ)], [ct(TRIS)])
    S.op("pool", lambda e: e.tensor_tensor(out=LSTS.h[:, :], in0=LST.h[:, :], in1=SSQ.h[:, :], op=ALU.mult),
         [ct(LST), ct(SSQ)], [ct(LSTS)])
    for N_, T_ in ((NEGM, TRI), (NEGMS, TRIS)):
        S.op("pool", (lambda N_, T_: (lambda e: e.tensor_scalar(out=N_.h[:, :], in0=T_.h[:, :], scalar1=-1.0,
                                                                  scalar2=30000.0, op0=ALU.add, op1=ALU.mult)))(N_, T_),
             [ct(T_)], [ct(N_)])
    for t_ in range(16):
        MEMSET("pool", INVC.h[:, t_:t_ + 1], 1.0 / (t_ + 1), [ct(INVC)], acc=(t_ > 0))
    S.op("act", lambda e: e.activation(out=ABC.h[:, :], in_=ABC.h[:, :], func=AF.Exp), [ct(ABC)], [ct(ABC)], aset="B")
    S.op("act", lambda e: e.mul(ABC.h[:, :], ABC.h[:, :], -1.0), [ct(ABC)], [ct(ABC)])
    S.op("pool", lambda e: e.memset(SCR.h[:, 3:4], 0.0), list(ctl.values()), [CT_, SCR.t])

    class _C:
        pass
    c = _C()

    def setpar(p):
        c.HT = HTs[p]
        c.XST = XSTs[p]
        c.YAT = Buf(XSTs[p].h, XSTs[p].t)
        c.BT = BTs[p]
        c.CTT = CTTs[p]
        c.YBT = YBTs[p]
        c.DTR = DTRs[p]
        c.DT = DTs_[p]
        c.ADT = ADTs_[p]
        c.EAC = EACs_[p]
        c.CD = CDs_[p]
        c.DDT = DDTs_[p]
        c.AHI = AHIs_[p]
        c.ALO = ALOs_[p]

    def load_x(ci):
        xb = XB[ci % 2]
        DMA("sp", xb.h[:, :], x_d[ci], [], [xb], xb.t)

    def proj_feature_major(ci, sample):
        L = 8 if sample else 128
        nseq = 16 if sample else 1
        hc = 3
        hp = 15
        if sample:
            xbc4 = XBC.h[:, :, :].rearrange("p t (b j) -> p t b j", j=11)
            ux4 = UXS_h[:, :, :].rearrange("p t (b j) -> p t b j", j=23)
            uxt = UXST
        else:
            xbc4 = XBC.h[:, :, 0:131].rearrange("p t (b j) -> p t b j", b=1)
            ux4 = UX.h[:, :, :].rearrange("p t (b j) -> p t b j", b=1)
            uxt = UX.t
        nb = 0
        for bl in range(3):
            bank = PB[nb % 2]
            nb += 1
            for j in range(4):
                tile = bl * 4 + j
                for kc in range(8):
                    MM(bank.h[:, j * 128:(j + 1) * 128], WINX.h[:, kc, tile * 128:(tile + 1) * 128], c.HT.h[:, kc, :],
                       kc == 0, kc == 7, [WINX, c.HT], [bank], first=(j == 0 and kc == 0))
            for j in range(4):
                tile = bl * 4 + j
                eng = "act" if bl % 2 == 0 else "dve"
                CP(eng, xbc4[:, tile, :, hc:hc + L],
                   bank.h[:, j * 128:(j + 1) * 128].rearrange("p (b j) -> p b j", j=L),
                   [bank], [XBC], acc=True)
        for kc in range(8):
            MM(B3.h[:, 0:16], c.HT.h[:, kc, :], WINX.h[:, kc, 1536:1552], kc == 0, kc == 7, [WXB, c.HT], [B3dt],
               first=(kc == 0))
        VT("dve", c.DTR.h[:, :], B3.h[:, 0:16], DTB.h[:, :], ALU.add, [B3dt, CT_], [c.DTR])
        bank = PB[nb % 2]
        nb += 1
        for j in range(4):
            for kc in range(8):
                MM(bank.h[:, j * 128:(j + 1) * 128], WINX.h[:, kc, 1552 + j * 128:1552 + (j + 1) * 128],
                   c.HT.h[:, kc, :], kc == 0, kc == 7, [WXB, c.HT], [bank], first=(j == 0 and kc == 0))
        for j in range(4):
            CP("dve", ux4[:, j, :, hp:hp + L], bank.h[:, j * 128:(j + 1) * 128].rearrange("p (b j) -> p b j", j=L),
               [bank], [uxt], acc=True)
        bank = PB[nb % 2]
        nb += 1
        for j in range(4):
            for kc in range(8):
                MM(bank.h[:, j * 128:(j + 1) * 128], WINX.h[:, kc, 2064 + j * 128:2064 + (j + 1) * 128],
                   c.HT.h[:, kc, :], kc == 0, kc == 7, [WXB, c.HT], [bank], first=(j == 0 and kc == 0))
        ACT(GS.h[:, :, :], bank.h[:, :].rearrange("p (k t) -> p k t", k=4), AF.Silu, [bank], [GS])
        return xbc4, ux4, uxt

    def conv_and_silu(xbc4, sample):
        L = 8 if sample else 128
        for pair in range(6):
            tiles = (2 * pair, 2 * pair + 1)
            accs = [CACC[(2 * pair) % 4], CACC[(2 * pair + 1) % 4]]
            a3s = [a.h[:, :].rearrange("p (b j) -> p b j", j=L) for a in accs]
            for i_, tile in enumerate(tiles):
                S.op("act", (lambda o_, i__, sc_, bi_: (lambda e: e.activation(out=o_, in_=i__, func=AF.Identity,
                                                                                   scale=sc_, bias=bi_)))(
                    a3s[i_], xbc4[:, tile, :, 0:L], CONVW.h[:, tile, 0:1], CONVB.h[:, tile:tile + 1]),
                    tl([XBC, CT_]), tl([accs[i_]]))
            for k in range(1, 4):
                for i_, tile in enumerate(tiles):
                    STT("dve", a3s[i_], xbc4[:, tile, :, k:k + L], CONVW.h[:, tile, k:k + 1], a3s[i_], ALU.mult,
                        ALU.add, [XBC, CT_, accs[i_]], [accs[i_]], acc=True)
            for i_, tile in enumerate(tiles):
                acc = accs[i_]
                if tile < 8:
                    ACT(c.XST.h[:, tile, :], acc.h[:, :], AF.Silu, [acc], [c.XST], acc=True)
                elif tile < 10:
                    ACT(c.BT.h[:, tile - 8, :], acc.h[:, :], AF.Silu, [acc], [c.BT], acc=True)
                else:
                    ACT(c.CTT.h[:, tile - 10, :], acc.h[:, :], AF.Silu, [acc], [c.CTT], acc=True)

    def pool_branch(ci, ux4, uxt, sample, mixbank=None):
        L = 8 if sample else 128
        nseq = 16 if sample else 1
        E = 15 + L
        P2, P4, P8 = (P2s, P4s, P8s) if sample else (P2p, P4p, P8p)
        p2 = P2.h[:, 0:nseq * (E - 1)].rearrange("p (b j) -> p b j", b=nseq)
        p4 = P4.h[:, 0:nseq * (E - 3)].rearrange("p (b j) -> p b j", b=nseq)
        p8 = P8.h[:, 0:nseq * (E - 7)].rearrange("p (b j) -> p b j", b=nseq)
        p16 = P16.h[:, :].rearrange("p (b j) -> p b j", b=nseq)
        for k, w in enumerate((2, 4, 8, 16)):
            u = ux4[:, k, :, :]
            eng = "dve" if k % 2 == 0 else "pool"
            VT(eng, p2, u[:, :, 1:E], u[:, :, 0:E - 1], ALU.add, [uxt], [P2])
            Sv = p2[:, :, 14:14 + L]
            rd = [P2]
            if w >= 4:
                VT(eng, p4, p2[:, :, 2:E - 1], p2[:, :, 0:E - 3], ALU.add, [P2], [P4])
                Sv = p4[:, :, 12:12 + L]
                rd = [P4]
            if w >= 8:
                VT(eng, p8, p4[:, :, 4:E - 3], p4[:, :, 0:E - 7], ALU.add, [P4], [P8])
                Sv = p8[:, :, 8:8 + L]
                rd = [P8]
            if w >= 16:
                VT(eng, p16, p8[:, :, 8:E - 7], p8[:, :, 0:E - 15], ALU.add, [P8], [P16])
                Sv = p16
                rd = [P16]
            pld3 = PLD.h[:, k, :].rearrange("p (b j) -> p b j", b=nseq)
            STT("dve", pld3, Sv, 1.0 / w, u[:, :, 15:15 + L], ALU.mult, ALU.subtract, rd + [uxt], [PLD], acc=True)
            if ci == 0 and not sample:
                VT(eng, SCR.h[:, 0:w - 1], Sv[:, 0, 0:w - 1], INVC.h[:, 0:w - 1], ALU.mult, rd + [CT_], [SCR])
                VT(eng, PLD.h[:, k, 0:w - 1], SCR.h[:, 0:w - 1], u[:, 0, 15:15 + w - 1], ALU.subtract,
                   [SCR, uxt], [PLD], acc=True)
        bank = mixbank if mixbank is not None else PB[0]
        for k in range(4):
            MM(bank.h[:, k * 128:(k + 1) * 128], PMIX.h[:, k, :], PLD.h[:, k, :], True, True, [PMIX, PLD], [bank],
               first=(k == 0))
        for k in range(4):
            STT("dve", c.YBT.h[:, k, :], bank.h[:, k * 128:(k + 1) * 128], PSC.h[:, k:k + 1], GS.h[:, k, :],
                ALU.mult, ALU.mult, [bank, GS, CT_], [c.YBT], acc=True)

    def dt_path(sample):
        tri = TRIS if sample else TRI
        ssq = SSQ if sample else ONE
        ACT(c.DTR.h[:, :], c.DTR.h[:, :], AF.Exp, [c.DTR], [c.DTR])
        ACT(c.DT.h[:, :], c.DTR.h[:, :], AF.Ln, [c.DTR], [c.DT], bias=1.0)
        VT("dve", c.ADT.h[:, :], c.DT.h[:, :], ABC.h[:, :], ALU.mult, [c.DT, CT_], [c.ADT])
        CP("dve", c.AHI.h[:, :], c.ADT.h[:, :], [c.ADT], [c.AHI])
        VT("dve", c.ALO.h[:, :], c.ADT.h[:, :], c.AHI.h[:, :], ALU.subtract, [c.ADT, c.AHI], [c.ALO])
        MM(B3.h[:, 16:32], tri.h[:, :], c.AHI.h[:, :], True, False, [CT_, c.AHI], [B3ac], first=True)
        MM(B3.h[:, 16:32], tri.h[:, :], c.ALO.h[:, :], False, True, [CT_, c.ALO], [B3ac], first=False)
        MM(B3.h[:, 32:48], ssq.h[:, :], c.AHI.h[:, :], True, False, [CT_, c.AHI], [B3ac], first=False)
        MM(B3.h[:, 32:48], ssq.h[:, :], c.ALO.h[:, :], False, True, [CT_, c.ALO], [B3ac], first=False)
        VS("dve", NACS.h[:, :], B3.h[:, 16:32], -1.0, None, ALU.mult, None, [B3ac], [NACS])
        ACT(c.EAC.h[:, :], B3.h[:, 16:32], AF.Exp, [B3ac], [c.EAC])
        ACT(c.CD.h[:, :], B3.h[:, 32:48], AF.Exp, [B3ac], [c.CD])
        VT("dve", DIF.h[:, :], B3.h[:, 32:48], NACS.h[:, :], ALU.add, [B3ac, NACS], [DIF])
        ACT(DTE.h[:, :], DIF.h[:, :], AF.Exp, [DIF], [DTE])
        VT("dve", c.DDT.h[:, :], c.DT.h[:, :], DTE.h[:, :], ALU.mult, [c.DT, DTE], [c.DDT])

    def to_token_major():
        for j in range(8):
            TR(PT2[:, j * 128:(j + 1) * 128], c.XST.h[:, j, :], IDB.h[:, :], [c.XST], [PT2t], first=(j == 0))
        CP("dve", BFA.h[:, :], PT2[:, :], [PT2t], [BFA])
        for g in range(2):
            TR(PT2[:, g * 128:(g + 1) * 128], c.BT.h[:, g, :], IDB.h[:, :], [c.BT], [PT2t], first=(g == 0))
        CP("act", BTOK.h[:, :], PT2[:, 0:256], [PT2t], [BTOK])

    def build_masks(sample, g):
        tri = TRIS if sample else TRI
        AMH_, AML_ = (AMH0, AML0) if g == 0 else (AMH, AML)
        VT("pool", AMH_.h[:, :, :], bc(tri.h[:, :], [128, 8, 128], 1), bc(c.AHI.h[:, 8 * g:8 * g + 8], [128, 8, 128], 2),
           ALU.mult, [CT_, c.AHI], [AMH_], acc=(g == 0))
        VT("pool", AML_.h[:, :, :], bc(tri.h[:, :], [128, 8, 128], 1), bc(c.ALO.h[:, 8 * g:8 * g + 8], [128, 8, 128], 2),
           ALU.mult, [CT_, c.ALO], [AML_], acc=(g == 0))

    def ssd_intra(ci, sample, g):
        tri = TRIS if sample else TRI
        lst = LSTS if sample else LST
        AMH_, AML_ = (AMH0, AML0) if g == 0 else (AMH, AML)
        for q in range(2):
            bank = PB[4 + q]
            MM(bank.h[:, :], lst.h[:, :], AMH_.h[:, 4 * q:4 * q + 4, :].rearrange("p h l -> p (h l)"), True, False,
               [CT_, AMH_], [bank], first=True)
            MM(bank.h[:, :], lst.h[:, :], AML_.h[:, 4 * q:4 * q + 4, :].rearrange("p h l -> p (h l)"), False, True,
               [CT_, AML_], [bank], first=False)
            ACT(DEC.h[:, 4 * q:4 * q + 4, :], bank.h[:, :].rearrange("p (h l) -> p h l", h=4), AF.Exp, [bank], [DEC],
                acc=(q == 1))
        VT("dve", MMT.h[:, :, :], DEC.h[:, :, :], bc(CBM.h[:, g, :], [128, 8, 128], 1), ALU.mult, [DEC, CBM], [MMT])
        bank = PB[6]
        for hh in range(8):
            h = 8 * g + hh
            MM(bank.h[:, hh * 64:(hh + 1) * 64], MMT.h[:, hh, :], XR.h[:, h * 64:(h + 1) * 64], True, True,
               [MMT, XR], [bank], first=(hh == 0))

    def ssd_prompt(ci):
        build_masks(False, 0)
        to_token_major()
        xs3 = BFA.h[:, :].rearrange("p (h d) -> p h d", d=64)
        VT("pool", XR.h[:, :].rearrange("p (h d) -> p h d", d=64), xs3, bc(c.DT.h[:, :], [128, 16, 64], 2), ALU.mult,
           [BFA, c.DT], [XR])
        build_masks(False, 1)
        VT("pool", XRD.h[:, :].rearrange("p (h d) -> p h d", d=64), xs3, bc(c.DDT.h[:, :], [128, 16, 64], 2), ALU.mult,
           [BFA, c.DDT], [XRD])
        for g in range(2):
            MM(B3.h[:, 64 + g * 128:64 + (g + 1) * 128], c.BT.h[:, g, :], c.CTT.h[:, g, :], True, True, [c.BT, c.CTT], [B3cb],
               first=(g == 0))
        VT("dve", CBM.h[:, :, :], B3.h[:, 64:320].rearrange("p (g l) -> p g l", g=2),
           bc(TRI.h[:, :], [128, 2, 128], 1), ALU.mult, [B3cb, CT_], [CBM])
        for g in range(2):
            ssd_intra(ci, False, g)
            ysl = Y.h[:, g * 512:(g + 1) * 512]
            y3 = ysl.rearrange("p (h d) -> p h d", d=64)
            if ci > 0:
                MM(PB[7].h[:, :], c.CTT.h[:, g, :], STBF.h[:, g * 512:(g + 1) * 512], True, True, [c.CTT, STBF], [PB[7]],
                   first=True)
                VT("dve", y3, PB[7].h[:, :].rearrange("p (h d) -> p h d", d=64),
                   bc(c.EAC.h[:, 8 * g:8 * g + 8], [128, 8, 64], 2), ALU.mult, [PB[7], c.EAC], [Y], acc=(g == 1))
                VT("dve", ysl, ysl, PB[6].h[:, :], ALU.add, [Y, PB[6]], [Y], acc=True)
            else:
                CP("dve", ysl, PB[6].h[:, :], [PB[6]], [Y], acc=(g == 1))
            VT("pool", YT.h[:, :].rearrange("p (h d) -> p h d", d=64),
               BFA.h[:, g * 512:(g + 1) * 512].rearrange("p (h d) -> p h d", d=64),
               bc(DSK.h[:, 8 * g:8 * g + 8], [128, 8, 64], 2), ALU.mult, [BFA, CT_], [YT])
            VT("dve", ysl, ysl, YT.h[:, :], ALU.add, [Y, YT], [Y], acc=True)
            MM(PB[7].h[:, :], BTOK.h[:, g * 128:(g + 1) * 128], XRD.h[:, g * 512:(g + 1) * 512], True, True,
               [BTOK, XRD], [PB[7]], first=True)
            ssl = STATE.h[:, g * 512:(g + 1) * 512]
            if ci > 0:
                s3 = ssl.rearrange("p (h d) -> p h d", d=64)
                VT("dve", s3, s3, bc(c.CD.h[:, 8 * g:8 * g + 8], [128, 8, 64], 2), ALU.mult, [STATE, c.CD], [STATE],
                   acc=True)
                VT("dve", ssl, ssl, PB[7].h[:, :], ALU.add, [STATE, PB[7]], [STATE], acc=True)
            else:
                CP("dve", ssl, PB[7].h[:, :], [PB[7]], [STATE], acc=(g == 1))
            if ci < 15:
                CP("act", STBF.h[:, g * 512:(g + 1) * 512], ssl, [STATE], [STBF], acc=(g == 1))

    def gates(cb):
        cs = slice(cb * 512, (cb + 1) * 512)
        for kc in range(8):
            MM(PB[6].h[:, :], c.HT.h[:, kc, :], WING.h[:, kc, cs], kc == 0, kc == 7, [c.HT, WING], [PB[6]],
               first=(kc == 0))
        for kc in range(8):
            MM(PB[7].h[:, :], c.HT.h[:, kc, :], WING.h[:, kc, 1024 + cb * 512:1024 + (cb + 1) * 512], kc == 0, kc == 7,
               [c.HT, WING], [PB[7]], first=(kc == 0))
        ACT(TA.h[:, :], PB[6].h[:, :], AF.Tanh, [PB[6]], [TA], scale=0.5)
        ACT(TB.h[:, :], PB[7].h[:, :], AF.Tanh, [PB[7]], [TB], scale=0.5)

    def gate_norm_transpose():
        for g in range(2):
            bank = PB[4 + g]
            for kc in range(8):
                MM(bank.h[:, :], c.HT.h[:, kc, :], WINZ.h[:, kc, g * 512:(g + 1) * 512], kc == 0, kc == 7, [c.HT, WINZ],
                   [bank], first=(kc == 0))
            ACT(ZS.h[:, :], bank.h[:, :], AF.Silu, [bank], [ZS])
            ysl = Y.h[:, g * 512:(g + 1) * 512]
            VT("dve", ysl, ysl, ZS.h[:, :], ALU.mult, [Y, ZS], [Y], acc=True)
            ACT(ZS.h[:, :], ysl, AF.Square, [Y], [ZS, SSG], accum=SSG.h[:, g:g + 1], acc=(g == 1))
        gates(0)
        ACT(RG.h[:, :], SSG.h[:, :], AF.Ln, [SSG], [RG], bias=EPS, scale=1.0 / 512)
        ACT(RG.h[:, :], RG.h[:, :], AF.Exp, [RG], [RG], scale=-0.5)
        for g in range(2):
            VS("dve", BFA.h[:, g * 512:(g + 1) * 512], Y.h[:, g * 512:(g + 1) * 512], RG.h[:, g:g + 1], None, ALU.mult,
               None, [Y, RG], [BFA], acc=(g == 1))
        for j in range(8):
            TR(PT2[:, j * 128:(j + 1) * 128], BFA.h[:, j * 128:(j + 1) * 128], IDB.h[:, :], [BFA], [PT2t], first=(j == 0))
        VT("dve", c.YAT.h[:, :, :], PT2[:, :].rearrange("p (k t) -> p k t", k=8), bc(SSMN.h[:, :], [128, 8, 128], 2),
           ALU.mult, [PT2t, CT_], [c.YAT])

    def tail(ci):
        xb = XB[ci % 2]
        for cb in range(2):
            cs = slice(cb * 512, (cb + 1) * 512)
            for kc in range(8):
                MM(PB[4].h[:, :], c.YAT.h[:, kc, :], WA.h[:, kc, cs], kc == 0, kc == 7, [c.YAT, WA], [PB[4]], first=(kc == 0))
            for kc in range(4):
                MM(PB[5].h[:, :], c.YBT.h[:, kc, :], WB.h[:, kc, cs], kc == 0, kc == 3, [c.YBT, WB], [PB[5]], first=(kc == 0))
            STT("dve", QA.h[:, :], TA.h[:, :], 1.0, PB[4].h[:, :], ALU.add, ALU.mult, [TA, PB[4]], [QA])
            STT("dve", QB.h[:, :], TB.h[:, :], 1.0, PB[5].h[:, :], ALU.add, ALU.mult, [TB, PB[5]], [QB])
            if cb == 0:
                gates(1)
            VT("dve", MG.h[:, cs], QA.h[:, :], QB.h[:, :], ALU.add, [QA, QB], [MG], acc=(cb == 1))
        for j in range(8):
            TR(PT2[:, j * 128:(j + 1) * 128], MG.h[:, j * 128:(j + 1) * 128], IDB.h[:, :], [MG], [PT2t], first=(j == 0))
        S.op("act", lambda e: e.mul(MGT.h[:, :, :], PT2[:, :].rearrange("p (k t) -> p k t", k=8), 0.5),
             [PT2t], [MGT.t])
        for ob in range(2):
            bank = PB[4 + ob]
            for kc in range(8):
                MM(bank.h[:, :], MGT.h[:, kc, :], WO.h[:, kc, ob * 512:(ob + 1) * 512], kc == 0, kc == 7, [MGT, WO],
                   [bank], first=(kc == 0))
        ACT(TAJ.h[:, :], pb45[:, :], AF.Square, [PB[4], PB[5]], [TAJ, SSP], accum=SSP.h[:, 0:1])
        ACT(RP.h[:, :], SSP.h[:, 0:1], AF.Ln, [SSP], [RP], bias=EPS, scale=1.0 / 1024)
        ACT(RP.h[:, :], RP.h[:, :], AF.Exp, [RP], [RP], scale=-0.5)
        for ob in range(2):
            cs = slice(ob * 512, (ob + 1) * 512)
            STT("dve", OT.h[:, :], PB[4 + ob].h[:, :], RP.h[:, 0:1], NPOST.h[:, cs], ALU.mult, ALU.mult,
                [PB[4 + ob], RP, CT_], [OT])
            VT("dve", xb.h[:, cs], xb.h[:, cs], OT.h[:, :], ALU.add, [xb, OT], [xb], acc=True)
        DMA("sp", y_d[ci], xb.h[:, :], [xb], [OUTT], xb.t)

    def norm_pre(ci):
        xb = XB[ci % 2]
        ACT(XN.h[:, :], xb.h[:, :], AF.Square, [xb], [XN, SSPRE], accum=SSPRE.h[:, :])
        ACT(RPRE.h[:, :], SSPRE.h[:, :], AF.Ln, [SSPRE], [RPRE], bias=EPS, scale=1.0 / 1024)
        ACT(RPRE.h[:, :], RPRE.h[:, :], AF.Exp, [RPRE], [RPRE], scale=-0.5)
        VS("dve", XN.h[:, :], xb.h[:, :], RPRE.h[:, 0:1], None, ALU.mult, None, [xb, RPRE], [XN])
        for j in range(8):
            TR(PT.h[:, j * 128:(j + 1) * 128], XN.h[:, j * 128:(j + 1) * 128], IDB.h[:, :], [XN], [PT], first=(j == 0))
        VT("dve", c.HT.h[:, :, :], PT.h[:, :].rearrange("p (k t) -> p k t", k=8), bc(NPRE.h[:, :], [128, 8, 128], 2),
           ALU.mult, [PT, CT_], [c.HT])

    def out_fp32_T(src_fn, ncols, tiles, stage_ap_fn, stage_tiles, dram_ap, reads, key, banks=(0, 1)):
        done = 0
        nb = 0
        nt = len(tiles)
        while done < nt:
            n = min(4, nt - done)
            bank = PB[banks[nb % 2]]
            nb += 1
            for j in range(n):
                TR(bank.h[0:ncols, j * 128:(j + 1) * 128], src_fn(tiles[done + j]), IDF.h[:, :], reads, [bank],
                   first=(j == 0))
            CP("dve", stage_ap_fn(done * 128, (done + n) * 128), bank.h[0:ncols, 0:n * 128], [bank], stage_tiles,
               acc=(done > 0))
            done += n
        DMA("sp", dram_ap, stage_ap_fn(0, nt * 128), stage_tiles, [OUTT], key)

    def stage1(ci):
        setpar(ci % 2)
        norm_pre(ci)
        xbc4, ux4, uxt = proj_feature_major(ci, False)
        dt_path(False)
        conv_and_silu(xbc4, False)
        pool_branch(ci, ux4, uxt, False)
        if ci < 15:
            CP("pool", XBC.h[:, :, 0:3], XBC.h[:, :, 128:131], [XBC], [XBC], acc=True)
            CP("pool", UX.h[:, :, 0:15], UX.h[:, :, 128:143], [UX], [UX], acc=True)

    def stage23(ci):
        setpar(ci % 2)
        ssd_prompt(ci)
        gate_norm_transpose()
        tail(ci)

    MEMSET("dve", XBC.h[:, :, 0:3], 0.0, [XBC])
    MEMSET("dve", UX.h[:, :, 0:15], 0.0, [UX])
    load_x(0)
    stage1(0)
    for ci in range(nprompt):
        sa = None
        if ci + 1 < nprompt or do_sample:
            load_x(ci + 1)
        if ci + 1 < nprompt:
            S.begin()
            stage1(ci + 1)
            sa = S.end()
        S.begin()
        stage23(ci)
        sb = S.end()
        S.run_merged([sa, sb])

    XRF = alloc("xrf", [128, 512], F32, at=b_off["xr"], tile=XR.t)
    XRDF = alloc("xrdf", [128, 512], F32, at=b_off["xrd"], tile=XRD.t)

    def prompt_state_outputs():
        out_fp32_T(lambda t: XBC.h[:, t, 128:131], 3, list(range(8)), lambda a, b: Y.h[0:3, a:b], [Y.t],
                   convp_d[:, 0:1024], [XBC], Y.t, banks=(6, 7))
        out_fp32_T(lambda t: XBC.h[:, t, 128:131], 3, list(range(8, 12)), lambda a, b: ZS.h[0:3, a:b], [ZS.t],
                   convp_d[:, 1024:1536], [XBC], ZS.t, banks=(6, 7))
        out_fp32_T(lambda t: UX.h[:, t, 128:143], 15, list(range(4)), lambda a, b: XRF.h[0:15, a:b], [XR.t],
                   poolp_d[:, :], [UX], XR.t, banks=(6, 7))
        ssmp_v = ssmp_d.rearrange("(j p) n -> p j n", p=128)
        for half in range(2):
            bank = PB[6 + half]
            for j in range(4):
                jj = half * 4 + j
                TR(bank.h[:, j * 128:(j + 1) * 128], STATE.h[:, jj * 128:(jj + 1) * 128], IDF.h[:, :], [STATE], [bank],
                   first=(j == 0))
            CP("dve", XRDF.h[:, :], bank.h[:, :], [bank], [XRD])
            DMA("sp", ssmp_v[:, half * 4:(half + 1) * 4, :], XRDF.h[:, :].rearrange("p (j n) -> p j n", j=4), [XRD],
                [OUTT], XRD.t)

    sctx = {}

    def sample_front():
        ci = 16
        setpar(0)
        norm_pre(ci)
        SCVT = [R1T[0], R1T[1], R1T[2]]
        SPLT = [R1T[3], R1T[4]]
        DMA("act", SCV.h[0:48, :], sconv_d[:, :], [], SCVT, R1T[0])
        DMA("act", SPL.h[0:120, :, :], spool_d.rearrange("(h r) c -> r h c", h=2), [], SPLT, R1T[3])
        xbc4s = XBC.h[:, :, :].rearrange("p t (b j) -> p t b j", j=11)
        ux4s = UXS_h[:, :, :].rearrange("p t (b j) -> p t b j", j=23)
        for tile in range(12):
            bank = PB[4 + (tile % 2)]
            TR(bank.h[:, 0:48], SCV.h[0:48, tile * 128:(tile + 1) * 128], IDF.h[0:48, 0:48], SCVT, [bank], first=True)
            CP("dve", xbc4s[:, tile, :, 0:3], bank.h[:, 0:48].rearrange("p (b k) -> p b k", k=3), [bank], [XBC], acc=True)
        first_ux = True
        for k in range(4):
            for half in range(2):
                bank = PB[4 + half]
                TR(bank.h[:, 0:120], SPL.h[0:120, half, k * 128:(k + 1) * 128], IDF.h[0:120, 0:120], SPLT, [bank],
                   first=True)
                CP("dve", ux4s[:, k, 8 * half:8 * half + 8, 0:15], bank.h[:, 0:120].rearrange("p (b k) -> p b k", k=15),
                   [bank, STATE, STBF], [UXST, STATE, STBF], acc=not first_ux)
                first_ux = False
        xbc4, ux4, uxt = proj_feature_major(ci, True)
        conv_and_silu(xbc4, True)
        alias_tiles = [b.t for b in H0 + H0B + H0T + CTM + BM] + [CDC.t, CSTG.t, PSTG.t, OSTG.t, ADTX.t, P2s.t, P4s.t, P8s.t]
        S.op("pool", lambda e: e.memset(SCR.h[:, 0:1], 0.0), [], [WINX.t, WXB] + alias_tiles + [SCR.t])
        for tile in range(12):
            CP("pool", CSTG.h[:, tile, :].rearrange("p (b k) -> p b k", k=3), xbc4s[:, tile, :, 8:11], [XBC], [CSTG],
               acc=(tile > 0))
        out_fp32_T(lambda t: CSTG.h[:, t, :], 48, list(range(12)), lambda a, b: OSTG.h[0:48, a:b], [OSTG.t],
                   convs_d[:, :], [CSTG], OSTG.t)
        sctx.update(ux4=ux4, uxt=uxt, ux4s=ux4s)

    def sample_rest():
        ci = 16
        setpar(0)
        ux4, uxt, ux4s = sctx["ux4"], sctx["uxt"], sctx["ux4s"]
        ssv = sso = None
        dt_path(True)
        build_masks(True, 0)
        build_masks(True, 1)
        to_token_major()
        xs3 = BFA.h[:, :].rearrange("p (h d) -> p h d", d=64)
        VT("pool", XR.h[:, :].rearrange("p (h d) -> p h d", d=64), xs3, bc(c.DT.h[:, :], [128, 16, 64], 2), ALU.mult,
           [BFA, c.DT], [XR])
        VT("pool", XRD.h[:, :].rearrange("p (h d) -> p h d", d=64), xs3, bc(c.DDT.h[:, :], [128, 16, 64], 2), ALU.mult,
           [BFA, c.DDT], [XRD])
        for g in range(2):
            MM(B3.h[:, 64 + g * 128:64 + (g + 1) * 128], c.BT.h[:, g, :], c.CTT.h[:, g, :], True, True, [c.BT, c.CTT], [B3cb],
               first=(g == 0))
        VT("dve", CBM.h[:, :, :], B3.h[:, 64:320].rearrange("p (g l) -> p g l", g=2), bc(TRIS.h[:, :], [128, 2, 128], 1),
           ALU.mult, [B3cb, CT_], [CBM])
        for g in range(2):
            ssd_intra(ci, True, g)
            ysl = Y.h[:, g * 512:(g + 1) * 512]
            CP("dve", ysl, PB[6].h[:, :], [PB[6]], [Y], acc=(g == 1))
            VT("pool", YT.h[:, :].rearrange("p (h d) -> p h d", d=64),
               BFA.h[:, g * 512:(g + 1) * 512].rearrange("p (h d) -> p h d", d=64),
               bc(DSK.h[:, 8 * g:8 * g + 8], [128, 8, 64], 2), ALU.mult, [BFA, CT_], [YT])
            VT("dve", ysl, ysl, YT.h[:, :], ALU.add, [Y, YT], [Y], acc=True)
        VT("dve", ADTX.h[:, :].rearrange("p (h d) -> p h d", d=64), bc(c.ADT.h[:, :], [128, 16, 64], 2),
           bc(ONE.h[:, 0:16], [128, 16, 64], 2), ALU.mult, [c.ADT, CT_], [ADTX])
        for j in range(8):
            MM(PB[7].h[:, j * 16:(j + 1) * 16], ADTX.h[:, j * 128:(j + 1) * 128], SQM.h[:, :], True, True,
               [ADTX, CT_], [PB[7]], first=(j == 0))
        ACT(CDC.h[:, :, :], PB[7].h[:, 0:128].rearrange("p (j b) -> p j b", j=8), AF.Exp, [PB[7]], [CDC])
        for i in range(2):
            MEMSET("pool", CTM[i].h[:, :, :], 0.0, [CTM[i]])
        S.begin()
        ssv = sssm_d.rearrange("b (j p) n -> b p j n", p=128)
        sso = ssms_d.rearrange("b (j p) n -> b p j n", p=128)
        NB = 3
        S.op("pool", lambda e: e.memset(SCR.h[:, 1:2], 0.0), [], [R1T[0], R1T[1], R1T[2], H0[2].t, H0B[2].t, SCR.t])

        def h0_load(b):
            rb = b % NB
            DMA("sp", H0[rb].h[:, :, :], ssv[b], [], [H0[rb]], H0[rb].t)
            DMA("pool", H0B[rb].h[:, :, :], ssv[b], [], [H0B[rb]], H0B[rb].t)

        def stage_a(b):
            rb = b % NB
            r2_ = b % 2
            for j in range(8):
                TR(PT.h[:, j * 128:(j + 1) * 128], H0B[rb].h[:, j, :], IDB.h[:, :], [H0B[rb]], [PT], first=(j == 0))
            CP("act", H0T[r2_].h[:, :], PT.h[:, :], [PT], [H0T[r2_]])
            CP("act", CTM[r2_].h[:, :, 8 * b:8 * b + 8], c.CTT.h[:, :, 8 * b:8 * b + 8], [c.CTT], [CTM[r2_]], acc=True)
            ACT(BM[r2_].h[:, :], BTOK.h[:, :], AF.Copy, [BTOK, CT_], [BM[r2_]], scale=SQM.h[:, b:b + 1])

        def stage_b(b):
            rb = b % NB
            r2_ = b % 2
            for g in range(2):
                MM(PB[g].h[:, :], CTM[r2_].h[:, g, :], H0T[r2_].h[:, g * 512:(g + 1) * 512], b == 0, b == 15,
                   [CTM[r2_], H0T[r2_]], [PB[g]], first=(b == 0))
            for j in range(8):
                bank = PB[4 + j // 4]
                MM(bank.h[:, (j % 4) * 128:(j % 4 + 1) * 128], XRD.h[:, j * 128:(j + 1) * 128],
                   BM[r2_].h[:, (j // 4) * 128:(j // 4 + 1) * 128], True, True, [XRD, BM[r2_]], [bank], first=(j % 4 == 0))
            ACT(CTM[r2_].h[:, :, 8 * b:8 * b + 8], CTM[r2_].h[:, :, 8 * b:8 * b + 8], AF.Copy, [CTM[r2_]], [CTM[r2_]],
                scale=0.0, acc=True)
            for j in range(8):
                bank = PB[4 + j // 4]
                STT("dve", H0[rb].h[:, j, :], H0[rb].h[:, j, :], CDC.h[:, j, b:b + 1],
                    bank.h[:, (j % 4) * 128:(j % 4 + 1) * 128], ALU.mult, ALU.add, [H0[rb], CDC, bank], [H0[rb]], acc=True)
            DMA("act", sso[b], H0[rb].h[:, :, :], [H0[rb]], [OUTT], H0[rb].t)

        h0_load(0)
        h0_load(1)
        stage_a(0)
        for b in range(16):
            if b + 1 < 16:
                stage_a(b + 1)
            stage_b(b)
            if b + 2 < 16:
                h0_load(b + 2)
        S.op("pool", lambda e: e.memset(SCR.h[:, 2:3], 0.0), [], [R1T[0], R1T[1], R1T[2], H0[2].t, H0B[2].t, SCR.t])
        st_loop = S.end()
        S.begin()
        pool_branch(ci, ux4, uxt, True, mixbank=B3)
        for k in range(4):
            CP("pool", PSTG.h[:, k, :].rearrange("p (b r) -> p b r", r=15), ux4s[:, k, :, 8:23], [UXST], [PSTG],
               acc=(k > 0))
        for half in range(2):
            bank = PB[6 + half]
            for k in range(4):
                TR(bank.h[0:120, k * 128:(k + 1) * 128], PSTG.h[:, k, half * 120:(half + 1) * 120], IDF.h[:, :], [PSTG],
                   [bank], first=(k == 0))
            CP("dve", OSTG.h[0:120, half * 512:(half + 1) * 512], bank.h[0:120, :], [bank], [OSTG], acc=True)
        DMA("sp", pools_d.rearrange("(h r) c -> r h c", h=2), OSTG.h[0:120, 0:1024].rearrange("r (h c) -> r h c", h=2),
            [OSTG], [OUTT], OSTG.t)
        st_pool = S.end()
        S.run_merged([st_loop, st_pool])
        for g in range(2):
            ysl = Y.h[:, g * 512:(g + 1) * 512]
            VT("dve", YT.h[:, :].rearrange("p (h d) -> p h d", d=64), PB[g].h[:, :].rearrange("p (h d) -> p h d", d=64),
               bc(c.EAC.h[:, 8 * g:8 * g + 8], [128, 8, 64], 2), ALU.mult, [PB[g], c.EAC], [YT])
            VT("dve", ysl, ysl, YT.h[:, :], ALU.add, [Y, YT], [Y], acc=True)
        gate_norm_transpose()
        tail(ci)

    sa = sb = None
    if do_pstate:
        S.begin()
        prompt_state_outputs()
        sa = S.end()
    if do_sample:
        S.begin()
        sample_front()
        sb = S.end()
    S.run_merged([sa, sb])
    if do_sample:
        sample_rest()
    S.op("sp", None, [OUTT], [])

    S.finalize(nc, es)
    block = es.enter_context(nc.Block())

    @block.tensor
    def _(e):
        S.emit_engine(e, "pe")

    @block.scalar
    def _(e):
        S.emit_engine(e, "act")

    @block.vector
    def _(e):
        S.emit_engine(e, "dve")

    @block.gpsimd
    def _(e):
        S.emit_engine(e, "pool")

    @block.sync
    def _(e):
        S.emit_engine(e, "sp")

    es.close()
    return nc


_NC_CACHE = {}


def _prep_inputs(inp):
    f = lambda a: np.ascontiguousarray(np.asarray(a, dtype=np.float32))
    xp = f(inp["x_prompt"])
    xs = f(inp["x_sample"])
    sc = f(inp["state_conv"])[0]
    ss = f(inp["state_ssm"])[0]
    sp = f(inp["state_pool"])[0]
    shared = {
        "w_in": f(inp["w_in"])[0],
        "w_a": f(inp["w_branch_a"])[0],
        "w_b": f(inp["w_branch_b"])[0],
        "w_out": f(inp["w_out"])[0],
        "pool_mix": f(inp["pool_mix"])[0],
        "npre_col": f(f(inp["norm_pre"])[0].reshape(8, 128).T),
        "convw_col": f(f(inp["conv_w"])[0].reshape(4, 12, 128).transpose(2, 1, 0)),
        "convb_col": f(f(inp["conv_b"])[0].reshape(12, 128).T),
        "ssmn_col": f(f(inp["ssm_norm"])[0].reshape(8, 128).T),
        "pscale_col": f(f(inp["pool_scale"])[0].reshape(4, 128).T),
        "dtb_bc": f(np.broadcast_to(f(inp["dt_bias"])[0][None, :], (128, 16))),
        "alog_bc": f(np.broadcast_to(f(inp["a_log"])[0][None, :], (128, 16))),
        "dskip_bc": f(np.broadcast_to(f(inp["d_skip"])[0][None, :], (128, 16))),
        "npost_bc": f(np.broadcast_to(f(inp["norm_post"])[0][None, :], (128, 1024))),
    }
    maps = []
    for c in range(8):
        x = np.concatenate([xp[c].reshape(16, 128, 1024), xs[16 * c:16 * c + 16].reshape(1, 128, 1024)], axis=0)
        m = dict(shared)
        m["x"] = f(x)
        m["sconv"] = f(sc[16 * c:16 * c + 16].reshape(48, 1536))
        m["sssm"] = f(ss[16 * c:16 * c + 16].reshape(16, 1024, 128))
        m["spool"] = f(sp[16 * c:16 * c + 16].reshape(240, 512))
        maps.append(m)
    return maps


def kernel(**inputs):
    if "nc" not in _NC_CACHE:
        _NC_CACHE["nc"] = build_program()
    nc = _NC_CACHE["nc"]
    maps = _prep_inputs(inputs)
    res = run_bass_kernel_spmd(nc, maps, core_ids=list(range(8)))
    R = res.results
    yp = np.stack([R[c]["y"][:16].reshape(2048, 1024) for c in range(8)], axis=0)
    ys = np.concatenate([R[c]["y"][16].reshape(16, 8, 1024) for c in range(8)], axis=0)
    convp = np.stack([R[c]["convp"] for c in range(8)], axis=0)[None]
    ssmp = np.stack([R[c]["ssmp"].reshape(16, 64, 128) for c in range(8)], axis=0)[None]
    poolp = np.stack([R[c]["poolp"] for c in range(8)], axis=0)[None]
    convs = np.concatenate([R[c]["convs"].reshape(16, 3, 1536) for c in range(8)], axis=0)[None]
    ssms = np.concatenate([R[c]["ssms"].reshape(16, 16, 64, 128) for c in range(8)], axis=0)[None]
    pools = np.concatenate([R[c]["pools"].reshape(16, 15, 512) for c in range(8)], axis=0)[None]
    out = (yp, ys, convp, ssmp, poolp, convs, ssms, pools)
    return tuple(np.ascontiguousarray(o, dtype=np.float32) for o in out)
```

```python
import os
import numpy as np
from contextlib import ExitStack
import concourse.bass as bass
import concourse.mybir as mybir
from concourse.bass_utils import run_bass_kernel_spmd

F32 = mybir.dt.float32
BF16 = mybir.dt.bfloat16
AF = mybir.ActivationFunctionType
ALU = mybir.AluOpType

NCHUNK = 17
XLAT = float(os.environ.get("XLAT", 0.9))
SAME_ENG_ALL = int(os.environ.get("SAME_ENG_ALL", 1))
EPS = 1e-6


class Tile:
    __slots__ = ("name", "writers", "readers", "sem", "cnt", "collector", "psum")

    def __init__(self, name, collector=False, psum=False):
        self.psum = psum
        self.name = name
        self.writers = []
        self.readers = []
        self.sem = None
        self.cnt = 0
        self.collector = collector


class Op:
    __slots__ = ("eng", "fn", "reads", "writes", "acc", "dma", "depc", "depdma",
                 "need_inc", "ticket", "dticket", "seq", "dsem", "t_end")


class Sched:
    ENGS = ("pe", "act", "dve", "pool", "sp")
    DMA_SEM_MAX = 224
    ROT = int(os.environ.get("ROT", 1000))

    def __init__(self):
        self.ops = []
        self.per = {e: [] for e in self.ENGS}
        self.dma_keys = []
        self.cur = None
        self.eng_t = {e: 0.0 for e in self.ENGS}
        self.act_set = None

    def begin(self):
        self.cur = []
        return self.cur

    def end(self):
        st = self.cur
        self.cur = None
        return st

    def _est_start(self, a):
        eng, fn, reads, writes, acc, dma, cost, aset = a
        t = self.eng_t[eng]
        if aset is not None and aset != self.act_set:
            t += 1.3
        for tl_ in reads:
            for w in tl_.writers:
                te = w.t_end + (0.0 if w.eng == eng else XLAT)
                if te > t:
                    t = te
            if tl_.psum:
                for r in tl_.readers:
                    if r.eng != eng and r.t_end > t:
                        t = r.t_end
        for tl_ in writes:
            if tl_.collector:
                continue
            for w in tl_.writers:
                te = w.t_end + (0.0 if w.eng == eng else XLAT)
                if te > t:
                    t = te
            for r in tl_.readers:
                te = r.t_end + (0.0 if r.eng == eng else XLAT)
                if te > t:
                    t = te
        return t

    def run_merged(self, streams):
        streams = [st for st in streams if st]
        pos = [0] * len(streams)
        while True:
            best = -1
            bt = 1e30
            for i, st in enumerate(streams):
                if pos[i] < len(st):
                    t = self._est_start(st[pos[i]])
                    t += 0.05 * pos[i] / len(st)
                    if t < bt:
                        bt = t
                        best = i
            if best < 0:
                break
            a = streams[best][pos[best]]
            pos[best] += 1
            self.op(*a)

    def op(self, eng, fn, reads=(), writes=(), acc=False, dma=None, cost=0.3, aset=None):
        if self.cur is not None:
            self.cur.append((eng, fn, list(reads), list(writes), acc, dma, cost, aset))
            return None
        t_start = self._est_start((eng, fn, reads, writes, acc, dma, cost, aset))
        if aset is not None:
            self.act_set = aset
        o = Op()
        if dma is not None:
            self.eng_t[eng] = t_start + 0.1
            o.t_end = t_start + cost
        else:
            o.t_end = t_start + cost
            self.eng_t[eng] = o.t_end
        o.eng = eng
        o.fn = fn
        o.reads = list(reads)
        o.writes = list(writes)
        o.acc = acc
        o.dma = dma
        o.depc = []
        o.depdma = []
        o.need_inc = False
        o.ticket = 0
        o.dticket = 0
        if dma is not None:
            if dma.cnt == 0 and dma not in self.dma_keys:
                self.dma_keys.append(dma)
            o.dsem = dma.cnt // self.DMA_SEM_MAX
            dma.cnt += 16
            o.dticket = dma.cnt - o.dsem * self.DMA_SEM_MAX
        deps = {}
        for t in o.reads:
            for w in t.writers:
                deps[id(w)] = (w, True)
            if t.psum:
                for r in t.readers:
                    if r.eng != eng and id(r) not in deps:
                        deps[id(r)] = (r, False)
        for t in o.writes:
            if t.collector:
                continue
            for w in t.writers:
                if id(w) not in deps:
                    deps[id(w)] = (w, False)
            for r in t.readers:
                if id(r) not in deps:
                    deps[id(r)] = (r, None)
        for t in o.reads:
            t.readers.append(o)
        for t in o.writes:
            if t.collector or acc:
                t.writers.append(o)
            else:
                t.writers = [o]
                t.readers = []
        latest = {}
        for d, raw in deps.values():
            if d is o:
                continue
            if d.dma is not None:
                o.depdma.append(d)
                continue
            if d.eng == eng:
                if eng == "pe":
                    continue
                if raw is None and SAME_ENG_ALL < 1:
                    continue
                if raw is False and SAME_ENG_ALL < 1 and SAME_ENG_ALL > -1:
                    pass
                if raw is False and SAME_ENG_ALL < 0:
                    continue
                if raw is False and acc and d.acc:
                    continue
            c_ = latest.get(d.eng)
            if c_ is None or d.seq > c_.seq:
                latest[d.eng] = d
        for d in latest.values():
            d.need_inc = True
            o.depc.append(d)
        o.seq = len(self.ops)
        self.ops.append(o)
        self.per[eng].append(o)
        return o

    def finalize(self, nc, es):
        self.engsem = {e: [] for e in self.ENGS}
        nsem = 0
        for i, k in enumerate(self.dma_keys):
            n = (k.cnt + self.DMA_SEM_MAX - 1) // self.DMA_SEM_MAX
            k.sem = [es.enter_context(nc.semaphore("dq%d_%d" % (i, j))) for j in range(n)]
            nsem += n
        print("dma sems", nsem)
        for e in self.ENGS:
            c = 0
            for o in self.per[e]:
                if o.need_inc:
                    c += 1
                    o.ticket = c
            print("engine", e, "ops", len(self.per[e]), "tickets", c)
            nse = (c + self.ROT - 1) // self.ROT
            self.engsem[e] = [es.enter_context(nc.semaphore("sem_%s%d" % (e, j))) for j in range(max(nse, 1))]

    def emit_engine(self, e, eng):
        waited = {}
        nw = 0
        for o in self.per[eng]:
            waits = {}
            for d in o.depc:
                s = self.engsem[d.eng][(d.ticket - 1) // self.ROT]
                k = id(s)
                tv = (d.ticket - 1) % self.ROT + 1
                if waits.get(k, (None, 0))[1] < tv:
                    waits[k] = (s, tv)
            for d in o.depdma:
                s = d.dma.sem[d.dsem]
                k = id(s)
                if waits.get(k, (None, 0))[1] < d.dticket:
                    waits[k] = (s, d.dticket)
            for k, (s, v) in waits.items():
                if waited.get(k, 0) >= v:
                    continue
                waited[k] = v
                e.wait_ge(s, v)
                nw += 1
                if os.environ.get("DUMPW") and o.seq >= int(os.environ.get("DUMPW")):
                    print("W", eng, o.seq, getattr(s, "name", s), v)
            if os.environ.get("DUMPW") and o.seq >= int(os.environ.get("DUMPW")):
                print("OP", eng, o.seq, "inc" if o.need_inc else "", o.ticket)
            ins = o.fn(e) if o.fn is not None else None
            if o.need_inc:
                assert ins is not None
                ins.then_inc(self.engsem[eng][(o.ticket - 1) // self.ROT], 1)
            if o.dma is not None:
                ins.then_inc(o.dma.sem[o.dsem], 16)
        print("emit", eng, "ops", len(self.per[eng]), "waits", nw)


class Buf:
    __slots__ = ("h", "t")

    def __init__(self, h, t):
        self.h = h
        self.t = t


def build_program(dbg=None, nprompt=16, do_pstate=True, do_sample=True):
    nc = bass.Bass("TRN2", target_bir_lowering=False)
    S = Sched()
    es = ExitStack()

    def din(name, shape, dt=F32):
        return nc.dram_tensor(name, shape, dt, kind="ExternalInput").ap()

    def dout(name, shape, dt=F32):
        return nc.dram_tensor(name, shape, dt, kind="ExternalOutput").ap()

    x_d = din("x", [NCHUNK, 128, 1024])
    sconv_d = din("sconv", [48, 1536])
    sssm_d = din("sssm", [16, 1024, 128])
    spool_d = din("spool", [240, 512])
    win_d = din("w_in", [1024, 5648])
    wa_d = din("w_a", [1024, 1024])
    wb_d = din("w_b", [512, 1024])
    wo_d = din("w_out", [1024, 1024])
    pmix_d = din("pool_mix", [4, 128, 128])
    npre_d = din("npre_col", [128, 8])
    convw_d = din("convw_col", [128, 12, 4])
    convb_d = din("convb_col", [128, 12])
    ssmn_d = din("ssmn_col", [128, 8])
    pscale_d = din("pscale_col", [128, 4])
    dtb_d = din("dtb_bc", [128, 16])
    alog_d = din("alog_bc", [128, 16])
    dskip_d = din("dskip_bc", [128, 16])
    npost_d = din("npost_bc", [128, 1024])

    y_d = dout("y", [NCHUNK, 128, 1024])
    convp_d = dout("convp", [3, 1536])
    ssmp_d = dout("ssmp", [1024, 128])
    poolp_d = dout("poolp", [15, 512])
    convs_d = dout("convs", [48, 1536])
    ssms_d = dout("ssms", [16, 1024, 128])
    pools_d = dout("pools", [240, 512])
    dbg_outs = {}
    OUTT = Tile("dram_out", collector=True)

    ARENA = 212800
    arena = nc.alloc_sbuf_tensor("arena", [128, ARENA // 4], F32)
    base = nc.lookup_mloc(arena).addr
    cur = [0]

    def nbytes(shape, dt):
        n = 1
        for s in shape[1:]:
            n *= s
        return n * (2 if dt == BF16 else 4)

    def alloc(name, shape, dt, at=None, tile=None):
        sz = (nbytes(shape, dt) + 31) // 32 * 32
        if at is None:
            off = cur[0]
            cur[0] += sz
            assert cur[0] <= ARENA, (name, cur[0])
        else:
            off = at
        h = nc.alloc_sbuf_tensor_at(name, shape, dt, offset=base + off)
        b = Buf(h, tile if tile is not None else Tile(name))
        b_off[name] = off
        return b

    b_off = {}

    WINX = alloc("winx", [128, 8, 2576], BF16)
    WINZ = alloc("winz", [128, 8, 1024], BF16)
    WING = alloc("wing", [128, 8, 2048], BF16)
    WA = alloc("wa", [128, 8, 1024], BF16)
    WB = alloc("wb", [128, 4, 1024], BF16)
    WO = alloc("wo", [128, 8, 1024], BF16)
    PMIX = alloc("pmix", [128, 4, 128], BF16)
    CT_ = Tile("consts", collector=True)
    IDB = alloc("idb", [128, 128], BF16, tile=CT_)
    IDF = alloc("idf", [128, 128], F32, tile=CT_)
    TRI = alloc("tri", [128, 128], BF16, tile=CT_)
    LST = alloc("lst", [128, 128], BF16, tile=CT_)
    ONE = alloc("one", [128, 128], BF16, tile=CT_)
    TRIS = alloc("tris", [128, 128], BF16, tile=CT_)
    LSTS = alloc("lsts", [128, 128], BF16, tile=CT_)
    SSQ = alloc("ssq", [128, 128], BF16, tile=CT_)
    SQM = alloc("sqm", [128, 16], F32, tile=CT_)
    NPRE = alloc("npre", [128, 8], F32, tile=CT_)
    CONVW = alloc("convw", [128, 12, 4], F32, tile=CT_)
    CONVB = alloc("convb", [128, 12], F32, tile=CT_)
    SSMN = alloc("ssmn", [128, 8], F32, tile=CT_)
    PSC = alloc("psc", [128, 4], F32, tile=CT_)
    DTB = alloc("dtb", [128, 16], F32, tile=CT_)
    ABC = alloc("abc", [128, 16], F32, tile=CT_)
    DSK = alloc("dsk", [128, 16], F32, tile=CT_)
    NPOST = alloc("npostb", [128, 1024], F32, tile=CT_)
    INVC = alloc("invc", [128, 16], F32, tile=CT_)
    SCR = alloc("scr", [128, 16], F32)
    NEGM = alloc("negm", [128, 128], BF16, tile=CT_)
    NEGMS = alloc("negms", [128, 128], BF16, tile=CT_)
    NACS = alloc("nacs", [128, 16], F32)
    XB = [alloc("xb0", [128, 1024], F32), alloc("xb1", [128, 1024], F32)]
    BFA = alloc("bfa", [128, 1024], BF16)
    XN = alloc("xn", [128, 1024], BF16)
    HTs = [alloc("ht0", [128, 8, 128], BF16), alloc("ht1", [128, 8, 128], BF16)]
    XBC = alloc("xbc", [128, 12, 176], F32)
    UX = alloc("ux", [128, 4, 143], F32)
    CACC = [alloc("cacc%d" % i, [128, 128], F32) for i in range(4)]
    XSTs = [alloc("xst0", [128, 8, 128], BF16), alloc("xst1", [128, 8, 128], BF16)]
    BTs = [alloc("bt0", [128, 2, 128], BF16), alloc("bt1", [128, 2, 128], BF16)]
    CTTs = [alloc("ct0", [128, 2, 128], BF16), alloc("ct1", [128, 2, 128], BF16)]
    GS = alloc("gs", [128, 4, 128], F32)
    DTRs = [alloc("dtr0", [128, 16], F32), alloc("dtr1", [128, 16], F32)]
    DTs_ = [alloc("dt%d" % i, [128, 16], F32) for i in range(2)]
    ADTs_ = [alloc("adt%d" % i, [128, 16], F32) for i in range(2)]
    DIF = alloc("dif", [128, 16], F32)
    EACs_ = [alloc("eac%d" % i, [128, 16], F32) for i in range(2)]
    DTE = alloc("dte", [128, 16], F32)
    CDs_ = [alloc("cd%d" % i, [128, 16], F32) for i in range(2)]
    DDTs_ = [alloc("ddt%d" % i, [128, 16], F32) for i in range(2)]
    AHIs_ = [alloc("ahi%d" % i, [128, 16], BF16) for i in range(2)]
    ALOs_ = [alloc("alo%d" % i, [128, 16], BF16) for i in range(2)]
    BTOK = alloc("btok", [128, 256], BF16)
    XR = alloc("xr", [128, 1024], BF16)
    XRD = alloc("xrd", [128, 1024], BF16)
    MG = Buf(XR.h, XR.t)
    MGT_h = nc.alloc_sbuf_tensor_at("mgt", [128, 8, 128], BF16, offset=base + b_off["xrd"])
    MGT = Buf(MGT_h, XRD.t)
    R1T = [Tile("r1_%d" % i) for i in range(5)]
    r1 = cur[0]
    cur[0] += 10240
    AMH = alloc("amh", [128, 8, 128], BF16, at=r1, tile=R1T[0])
    AML = alloc("aml", [128, 8, 128], BF16, at=r1 + 2048, tile=R1T[1])
    DEC = alloc("dec", [128, 8, 128], BF16, at=r1 + 4096, tile=R1T[2])
    MMT = alloc("mmt", [128, 8, 128], BF16, at=r1 + 6144, tile=R1T[3])
    YT = alloc("yt", [128, 512], F32, at=r1 + 8192, tile=R1T[4])
    TA = alloc("ta", [128, 512], F32, at=r1, tile=R1T[0])
    TAJ = alloc("taj", [128, 1024], BF16, at=r1, tile=R1T[0])
    TB = alloc("tb", [128, 512], F32, at=r1 + 2048, tile=R1T[1])
    QA = alloc("qa", [128, 512], F32, at=r1 + 4096, tile=R1T[2])
    QB = alloc("qb", [128, 512], F32, at=r1 + 6144, tile=R1T[3])
    OT = alloc("ot", [128, 512], F32, at=r1 + 8192, tile=R1T[4])
    CBM = alloc("cbm", [128, 2, 128], BF16)
    STATE = alloc("state", [128, 1024], F32)
    STBF = alloc("stbf", [128, 1024], BF16)
    Y = alloc("y", [128, 1024], F32)
    ZS = alloc("zs", [128, 512], F32)
    AMH0 = alloc("amh0", [128, 8, 128], BF16, at=b_off["y"], tile=Y.t)
    AML0 = alloc("aml0", [128, 8, 128], BF16, at=b_off["y"] + 2048, tile=Y.t)
    P2p = alloc("p2", [128, 144], F32)
    P4p = alloc("p4", [128, 144], F32)
    P8p = alloc("p8", [128, 144], F32)
    P16 = alloc("p16", [128, 128], F32)
    PLD = alloc("pld", [128, 4, 128], BF16)
    YBTs = [alloc("ybt0", [128, 4, 128], BF16), alloc("ybt1", [128, 4, 128], BF16)]
    SSPRE = alloc("sspre", [128, 1], F32)
    RPRE = alloc("rpre", [128, 1], F32)
    SSG = alloc("ssg", [128, 2], F32)
    RG = alloc("rg", [128, 2], F32)
    SSP = alloc("ssp", [128, 2], F32)
    RP = alloc("rp", [128, 1], F32)
    UXS_h = nc.alloc_sbuf_tensor_at("uxs", [128, 4, 368], F32, offset=base + b_off["state"])
    assert b_off["stbf"] == b_off["state"] + 4096
    UXST = Tile("uxs")
    wx = b_off["winx"]
    H0 = [alloc("h0_%d" % i, [128, 8, 128], F32, at=wx + i * 4096) for i in range(2)]
    H0B = [alloc("h0b_%d" % i, [128, 8, 128], BF16, at=wx + 8192 + i * 2048) for i in range(2)]
    H0T = [alloc("h0t_%d" % i, [128, 1024], BF16, at=wx + 12288 + i * 2048) for i in range(2)]
    CTM = [alloc("ctm_%d" % i, [128, 2, 128], BF16, at=wx + 16384 + i * 512) for i in range(2)]
    BM = [alloc("bm_%d" % i, [128, 256], BF16, at=wx + 17408 + i * 512) for i in range(2)]
    H0.append(alloc("h0_2", [128, 8, 128], F32, at=r1))
    H0B.append(alloc("h0b_2", [128, 8, 128], BF16, at=r1 + 4096))
    CDC = alloc("cdc", [128, 8, 16], F32, at=wx + 18432)
    CSTG = alloc("cstg", [128, 12, 48], F32, at=wx + 18944)
    PSTG = alloc("pstg", [128, 4, 240], F32, at=wx + 18944 + 2304)
    OSTG = alloc("ostg", [128, 1536], F32, at=wx + 18944 + 2304 + 3840)
    ADTX = alloc("adtx", [128, 1024], F32, at=wx + 18944 + 2304 + 3840 + 6144)
    P2s = alloc("p2s", [128, 368], F32, at=wx + 35328)
    P4s = alloc("p4s", [128, 368], F32, at=wx + 35328 + 1472)
    P8s = alloc("p8s", [128, 368], F32, at=wx + 35328 + 2944)
    assert 35328 + 3 * 1472 <= 41216
    SCV = alloc("scv", [128, 1536], F32, at=r1, tile=R1T[0])
    SPL = alloc("spl", [128, 2, 512], F32, at=r1 + 6144, tile=R1T[3])
    print("SBUF used", cur[0], "of", ARENA)

    PB = []
    pb45 = es.enter_context(nc.psum_tensor("pb45", [128, 1024], F32))
    for i in range(8):
        if i == 2:
            h = es.enter_context(nc.psum_tensor("pb2", [128, 1024], BF16))
        elif i == 4:
            h = pb45[:, 0:512]
        elif i == 5:
            h = pb45[:, 512:1024]
        else:
            h = es.enter_context(nc.psum_tensor("pb%d" % i, [128, 512], F32))
        PB.append(Buf(h, Tile("pb%d" % i, psum=True)))
    PT = PB[2]
    B3 = PB[3]
    PT2 = PB[7].h[:, :].bitcast(BF16)
    PT2t = PB[7].t
    B3dt = B3.t
    B3ac = B3.t
    B3cb = B3.t

    def tl(bufs):
        return [b.t if isinstance(b, Buf) else b for b in bufs]

    def fsz(ap):
        n = 1
        for d in ap.shape[1:]:
            n *= d
        return n

    def MM(out, lhsT, rhs, start, stop, r, w, first):
        n = fsz(rhs)
        S.op("pe", lambda e: e.matmul(out, lhsT=lhsT, rhs=rhs, start=start, stop=stop),
             tl(r), tl(w), acc=not first, cost=0.06 + max(n, 64) / 2000.0)

    def TR(out, in_, ident, r, w, first):
        S.op("pe", lambda e: e.transpose(out, in_, ident), tl(r) + [CT_], tl(w), acc=not first, cost=0.12)

    def ecost(eng, out):
        n = fsz(out)
        if eng == "pool":
            return 0.25 + n * 0.0017
        if eng == "act":
            return 0.25 + n * 0.00085
        return 0.15 + n * 0.00105

    def VT(eng, out, in0, in1, op, r, w, acc=False):
        S.op(eng, lambda e: e.tensor_tensor(out=out, in0=in0, in1=in1, op=op), tl(r), tl(w), acc=acc,
             cost=ecost(eng, out))

    def VS(eng, out, in0, s1, s2, op0, op1, r, w, acc=False):
        if s2 is None:
            S.op(eng, lambda e: e.tensor_scalar(out=out, in0=in0, scalar1=s1, scalar2=None, op0=op0),
                 tl(r), tl(w), acc=acc, cost=ecost(eng, out))
        else:
            S.op(eng, lambda e: e.tensor_scalar(out=out, in0=in0, scalar1=s1, scalar2=s2, op0=op0, op1=op1),
                 tl(r), tl(w), acc=acc, cost=ecost(eng, out))

    def STT(eng, out, in0, sc, in1, op0, op1, r, w, acc=False):
        S.op(eng, lambda e: e.scalar_tensor_tensor(out=out, in0=in0, scalar=sc, in1=in1, op0=op0, op1=op1),
             tl(r), tl(w), acc=acc, cost=ecost(eng, out))

    def CP(eng, out, in_, r, w, acc=False):
        if eng == "act":
            S.op(eng, lambda e: e.activation(out=out, in_=in_, func=AF.Copy), tl(r), tl(w), acc=acc,
                 cost=ecost(eng, out))
        else:
            S.op(eng, lambda e: e.tensor_copy(out=out, in_=in_), tl(r), tl(w), acc=acc, cost=ecost(eng, out))

    def ACT(out, in_, func, r, w, bias=None, scale=None, accum=None, acc=False):
        kw = {}
        if bias is not None:
            kw["bias"] = bias
        if scale is not None:
            kw["scale"] = scale
        if accum is not None:
            kw["accum_out"] = accum
        aset = "A" if func in (AF.Silu, AF.Tanh) else ("B" if func in (AF.Exp, AF.Ln) else None)
        S.op("act", lambda e: e.activation(out=out, in_=in_, func=func, **kw), tl(r), tl(w), acc=acc,
             cost=ecost("act", out), aset=aset)

    def MEMSET(eng, ap, val, w, acc=False):
        S.op(eng, lambda e: e.memset(ap, val), [], tl(w), acc=acc)

    def DMA(eng, out, in_, r, w, key, acc=False):
        S.op(eng, lambda e: e.dma_start(out=out, in_=in_), tl(r), tl(w), acc=acc, dma=key, cost=3.0)

    def bc(ap, shape, axis):
        return ap.unsqueeze(axis).to_broadcast(shape)

    winv = win_d.rearrange("(kc p) e -> p kc e", p=128)
    WXB = Tile("winx_b")
    DMA("pool", WINX.h[:, :, 0:1536], winv[:, :, 1024:2560], [], [WINX], WINX.t)
    DMA("pool", WINX.h[:, :, 1536:2576], winv[:, :, 2560:3600], [], [WXB], WXB)
    small = [(NPRE, npre_d), (CONVW, convw_d), (CONVB, convb_d), (SSMN, ssmn_d), (PSC, pscale_d),
             (DTB, dtb_d), (ABC, alog_d), (DSK, dskip_d), (NPOST, npost_d)]
    ctl = {}

    def ct(bf):
        k = id(bf)
        if k not in ctl:
            ctl[k] = Tile("c%d" % len(ctl))
        return ctl[k]

    for b_, d_ in small:
        if len(d_.shape) == 3:
            DMA("act", b_.h[:, :, :], d_[:, :, :], [], [ct(b_)], ct(b_))
        else:
            DMA("act", b_.h[:, :], d_[:, :], [], [ct(b_)], ct(b_))
    DMA("pool", PMIX.h[:, :, :], pmix_d.rearrange("k c d -> c k d"), [], [PMIX], PMIX.t)
    DMA("pool", WINZ.h[:, :, :], winv[:, :, 0:1024], [], [WINZ], WINZ.t)
    DMA("pool", WING.h[:, :, :], winv[:, :, 3600:5648], [], [WING], WING.t)
    DMA("pool", WB.h[:, :, :], wb_d.rearrange("(kc p) e -> p kc e", p=128), [], [WB], WB.t)
    DMA("pool", WA.h[:, :, :], wa_d.rearrange("(kc p) e -> p kc e", p=128), [], [WA], WA.t)
    DMA("pool", WO.h[:, :, :], wo_d.rearrange("(kc p) e -> p kc e", p=128), [], [WO], WO.t)

    def aff(bf, eng_ap, pattern, cmp, fill, base_, cm):
        S.op("pool", lambda e: e.affine_select(out=eng_ap, in_=eng_ap, pattern=pattern, compare_op=cmp,
                                                fill=fill, base=base_, channel_multiplier=cm), [ct(bf)], [ct(bf)])

    for I_ in (IDB, IDF):
        MEMSET("pool", I_.h[:, :], 0.0, [ct(I_)])
        aff(I_, I_.h[:, :], [[-1, 128]], ALU.not_equal, 1.0, 0, 1)
    MEMSET("pool", TRI.h[:, :], 1.0, [ct(TRI)])
    aff(TRI, TRI.h[:, :], [[1, 128]], ALU.is_ge, 0.0, 0, -1)
    MEMSET("pool", LST.h[:, :], 1.0, [ct(LST)])
    aff(LST, LST.h[:, :], [[-1, 128]], ALU.is_gt, 0.0, 0, 1)
    MEMSET("pool", ONE.h[:, :], 1.0, [ct(ONE)])
    MEMSET("pool", SSQ.h[:, :], 1.0, [ct(SSQ)])
    ssq3 = SSQ.h[:, :].rearrange("p (b j) -> p b j", j=8)
    aff(SSQ, ssq3, [[-8, 16], [0, 8]], ALU.is_ge, 0.0, 0, 1)
    aff(SSQ, ssq3, [[8, 16], [0, 8]], ALU.is_ge, 0.0, 7, -1)
    MEMSET("pool", SQM.h[:, :], 1.0, [ct(SQM)])
    aff(SQM, SQM.h[:, :], [[-8, 16]], ALU.is_ge, 0.0, 0, 1)
    aff(SQM, SQM.h[:, :], [[8, 16]], ALU.is_ge, 0.0, 7, -1)
    S.op("pool", lambda e: e.tensor_tensor(out=TRIS.h[:, :], in0=TRI.h[:, :], in1=SSQ.h[:, :], op=ALU.mult),
         [ct(TRI), ct(SSQ)], [ct(TRIS)])
    S.op("pool", lambda e: e.tensor_tensor(out=LSTS.h[:, :], in0=LST.h[:, :], in1=SSQ.h[:, :], op=ALU.mult),
         [ct(LST), ct(SSQ)], [ct(LSTS)])
    for N_, T_ in ((NEGM, TRI), (NEGMS, TRIS)):
        S.op("pool", (lambda N_, T_: (lambda e: e.tensor_scalar(out=N_.h[:, :], in0=T_.h[:, :], scalar1=-1.0,
                                                                  scalar2=30000.0, op0=ALU.add, op1=ALU.mult)))(N_, T_),
             [ct(T_)], [ct(N_)])
    for t_ in range(16):
        MEMSET("pool", INVC.h[:, t_:t_ + 1], 1.0 / (t_ + 1), [ct(INVC)], acc=(t_ > 0))
    S.op("act", lambda e: e.activation(out=ABC.h[:, :], in_=ABC.h[:, :], func=AF.Exp), [ct(ABC)], [ct(ABC)], aset="B")
    S.op("act", lambda e: e.mul(ABC.h[:, :], ABC.h[:, :], -1.0), [ct(ABC)], [ct(ABC)])
    S.op("pool", lambda e: e.memset(SCR.h[:, 3:4], 0.0), list(ctl.values()), [CT_, SCR.t])

    class _C:
        pass
    c = _C()

    def setpar(p):
        c.HT = HTs[p]
        c.XST = XSTs[p]
        c.YAT = Buf(XSTs[p].h, XSTs[p].t)
        c.BT = BTs[p]
        c.CTT = CTTs[p]
        c.YBT = YBTs[p]
        c.DTR = DTRs[p]
        c.DT = DTs_[p]
        c.ADT = ADTs_[p]
        c.EAC = EACs_[p]
        c.CD = CDs_[p]
        c.DDT = DDTs_[p]
        c.AHI = AHIs_[p]
        c.ALO = ALOs_[p]

    def load_x(ci):
        xb = XB[ci % 2]
        DMA("sp", xb.h[:, :], x_d[ci], [], [xb], xb.t)

    def proj_feature_major(ci, sample):
        L = 8 if sample else 128
        nseq = 16 if sample else 1
        hc = 3
        hp = 15
        if sample:
            xbc4 = XBC.h[:, :, :].rearrange("p t (b j) -> p t b j", j=11)
            ux4 = UXS_h[:, :, :].rearrange("p t (b j) -> p t b j", j=23)
            uxt = UXST
        else:
            xbc4 = XBC.h[:, :, 0:131].rearrange("p t (b j) -> p t b j", b=1)
            ux4 = UX.h[:, :, :].rearrange("p t (b j) -> p t b j", b=1)
            uxt = UX.t
        nb = 0
        for bl in range(3):
            bank = PB[nb % 2]
            nb += 1
            for j in range(4):
                tile = bl * 4 + j
                for kc in range(8):
                    MM(bank.h[:, j * 128:(j + 1) * 128], WINX.h[:, kc, tile * 128:(tile + 1) * 128], c.HT.h[:, kc, :],
                       kc == 0, kc == 7, [WINX, c.HT], [bank], first=(j == 0 and kc == 0))
            for j in range(4):
                tile = bl * 4 + j
                eng = "act" if bl % 2 == 0 else "dve"
                CP(eng, xbc4[:, tile, :, hc:hc + L],
                   bank.h[:, j * 128:(j + 1) * 128].rearrange("p (b j) -> p b j", j=L),
                   [bank], [XBC], acc=True)
        for kc in range(8):
            MM(B3.h[:, 0:16], c.HT.h[:, kc, :], WINX.h[:, kc, 1536:1552], kc == 0, kc == 7, [WXB, c.HT], [B3dt],
               first=(kc == 0))
        VT("dve", c.DTR.h[:, :], B3.h[:, 0:16], DTB.h[:, :], ALU.add, [B3dt, CT_], [c.DTR])
        bank = PB[nb % 2]
        nb += 1
        for j in range(4):
            for kc in range(8):
                MM(bank.h[:, j * 128:(j + 1) * 128], WINX.h[:, kc, 1552 + j * 128:1552 + (j + 1) * 128],
                   c.HT.h[:, kc, :], kc == 0, kc == 7, [WXB, c.HT], [bank], first=(j == 0 and kc == 0))
        for j in range(4):
            CP("dve", ux4[:, j, :, hp:hp + L], bank.h[:, j * 128:(j + 1) * 128].rearrange("p (b j) -> p b j", j=L),
               [bank], [uxt], acc=True)
        bank = PB[nb % 2]
        nb += 1
        for j in range(4):
            for kc in range(8):
                MM(bank.h[:, j * 128:(j + 1) * 128], WINX.h[:, kc, 2064 + j * 128:2064 + (j + 1) * 128],
                   c.HT.h[:, kc, :], kc == 0, kc == 7, [WXB, c.HT], [bank], first=(j == 0 and kc == 0))
        ACT(GS.h[:, :, :], bank.h[:, :].rearrange("p (k t) -> p k t", k=4), AF.Silu, [bank], [GS])
        return xbc4, ux4, uxt

    def conv_and_silu(xbc4, sample):
        L = 8 if sample else 128
        for pair in range(6):
            tiles = (2 * pair, 2 * pair + 1)
            accs = [CACC[(2 * pair) % 4], CACC[(2 * pair + 1) % 4]]
            a3s = [a.h[:, :].rearrange("p (b j) -> p b j", j=L) for a in accs]
            for i_, tile in enumerate(tiles):
                S.op("act", (lambda o_, i__, sc_, bi_: (lambda e: e.activation(out=o_, in_=i__, func=AF.Identity,
                                                                                   scale=sc_, bias=bi_)))(
                    a3s[i_], xbc4[:, tile, :, 0:L], CONVW.h[:, tile, 0:1], CONVB.h[:, tile:tile + 1]),
                    tl([XBC, CT_]), tl([accs[i_]]))
            for k in range(1, 4):
                for i_, tile in enumerate(tiles):
                    STT("dve", a3s[i_], xbc4[:, tile, :, k:k + L], CONVW.h[:, tile, k:k + 1], a3s[i_], ALU.mult,
                        ALU.add, [XBC, CT_, accs[i_]], [accs[i_]], acc=True)
            for i_, tile in enumerate(tiles):
                acc = accs[i_]
                if tile < 8:
                    ACT(c.XST.h[:, tile, :], acc.h[:, :], AF.Silu, [acc], [c.XST], acc=True)
                elif tile < 10:
                    ACT(c.BT.h[:, tile - 8, :], acc.h[:, :], AF.Silu, [acc], [c.BT], acc=True)
                else:
                    ACT(c.CTT.h[:, tile - 10, :], acc.h[:, :], AF.Silu, [acc], [c.CTT], acc=True)

    def pool_branch(ci, ux4, uxt, sample, mixbank=None):
        L = 8 if sample else 128
        nseq = 16 if sample else 1
        E = 15 + L
        P2, P4, P8 = (P2s, P4s, P8s) if sample else (P2p, P4p, P8p)
        p2 = P2.h[:, 0:nseq * (E - 1)].rearrange("p (b j) -> p b j", b=nseq)
        p4 = P4.h[:, 0:nseq * (E - 3)].rearrange("p (b j) -> p b j", b=nseq)
        p8 = P8.h[:, 0:nseq * (E - 7)].rearrange("p (b j) -> p b j", b=nseq)
        p16 = P16.h[:, :].rearrange("p (b j) -> p b j", b=nseq)
        for k, w in enumerate((2, 4, 8, 16)):
            u = ux4[:, k, :, :]
            eng = "dve" if k % 2 == 0 else "pool"
            VT(eng, p2, u[:, :, 1:E], u[:, :, 0:E - 1], ALU.add, [uxt], [P2])
            Sv = p2[:, :, 14:14 + L]
            rd = [P2]
            if w >= 4:
                VT(eng, p4, p2[:, :, 2:E - 1], p2[:, :, 0:E - 3], ALU.add, [P2], [P4])
                Sv = p4[:, :, 12:12 + L]
                rd = [P4]
            if w >= 8:
                VT(eng, p8, p4[:, :, 4:E - 3], p4[:, :, 0:E - 7], ALU.add, [P4], [P8])
                Sv = p8[:, :, 8:8 + L]
                rd = [P8]
            if w >= 16:
                VT(eng, p16, p8[:, :, 8:E - 7], p8[:, :, 0:E - 15], ALU.add, [P8], [P16])
                Sv = p16
                rd = [P16]
            pld3 = PLD.h[:, k, :].rearrange("p (b j) -> p b j", b=nseq)
            STT("dve", pld3, Sv, 1.0 / w, u[:, :, 15:15 + L], ALU.mult, ALU.subtract, rd + [uxt], [PLD], acc=True)
            if ci == 0 and not sample:
                VT(eng, SCR.h[:, 0:w - 1], Sv[:, 0, 0:w - 1], INVC.h[:, 0:w - 1], ALU.mult, rd + [CT_], [SCR])
                VT(eng, PLD.h[:, k, 0:w - 1], SCR.h[:, 0:w - 1], u[:, 0, 15:15 + w - 1], ALU.subtract,
                   [SCR, uxt], [PLD], acc=True)
        bank = mixbank if mixbank is not None else PB[0]
        for k in range(4):
            MM(bank.h[:, k * 128:(k + 1) * 128], PMIX.h[:, k, :], PLD.h[:, k, :], True, True, [PMIX, PLD], [bank],
               first=(k == 0))
        for k in range(4):
            STT("dve", c.YBT.h[:, k, :], bank.h[:, k * 128:(k + 1) * 128], PSC.h[:, k:k + 1], GS.h[:, k, :],
                ALU.mult, ALU.mult, [bank, GS, CT_], [c.YBT], acc=True)

    def dt_path(sample):
        tri = TRIS if sample else TRI
        ssq = SSQ if sample else ONE
        ACT(c.DTR.h[:, :], c.DTR.h[:, :], AF.Exp, [c.DTR], [c.DTR])
        ACT(c.DT.h[:, :], c.DTR.h[:, :], AF.Ln, [c.DTR], [c.DT], bias=1.0)
        VT("dve", c.ADT.h[:, :], c.DT.h[:, :], ABC.h[:, :], ALU.mult, [c.DT, CT_], [c.ADT])
        CP("dve", c.AHI.h[:, :], c.ADT.h[:, :], [c.ADT], [c.AHI])
        VT("dve", c.ALO.h[:, :], c.ADT.h[:, :], c.AHI.h[:, :], ALU.subtract, [c.ADT, c.AHI], [c.ALO])
        MM(B3.h[:, 16:32], tri.h[:, :], c.AHI.h[:, :], True, False, [CT_, c.AHI], [B3ac], first=True)
        MM(B3.h[:, 16:32], tri.h[:, :], c.ALO.h[:, :], False, True, [CT_, c.ALO], [B3ac], first=False)
        MM(B3.h[:, 32:48], ssq.h[:, :], c.AHI.h[:, :], True, False, [CT_, c.AHI], [B3ac], first=False)
        MM(B3.h[:, 32:48], ssq.h[:, :], c.ALO.h[:, :], False, True, [CT_, c.ALO], [B3ac], first=False)
        VS("dve", NACS.h[:, :], B3.h[:, 16:32], -1.0, None, ALU.mult, None, [B3ac], [NACS])
        ACT(c.EAC.h[:, :], B3.h[:, 16:32], AF.Exp, [B3ac], [c.EAC])
        ACT(c.CD.h[:, :], B3.h[:, 32:48], AF.Exp, [B3ac], [c.CD])
        VT("dve", DIF.h[:, :], B3.h[:, 32:48], NACS.h[:, :], ALU.add, [B3ac, NACS], [DIF])
        ACT(DTE.h[:, :], DIF.h[:, :], AF.Exp, [DIF], [DTE])
        VT("dve", c.DDT.h[:, :], c.DT.h[:, :], DTE.h[:, :], ALU.mult, [c.DT, DTE], [c.DDT])

    def to_token_major():
        for j in range(8):
            TR(PT2[:, j * 128:(j + 1) * 128], c.XST.h[:, j, :], IDB.h[:, :], [c.XST], [PT2t], first=(j == 0))
        CP("dve", BFA.h[:, :], PT2[:, :], [PT2t], [BFA])
        for g in range(2):
            TR(PT2[:, g * 128:(g + 1) * 128], c.BT.h[:, g, :], IDB.h[:, :], [c.BT], [PT2t], first=(g == 0))
        CP("act", BTOK.h[:, :], PT2[:, 0:256], [PT2t], [BTOK])

    def build_masks(sample, g):
        tri = TRIS if sample else TRI
        AMH_, AML_ = (AMH0, AML0) if g == 0 else (AMH, AML)
        VT("pool", AMH_.h[:, :, :], bc(tri.h[:, :], [128, 8, 128], 1), bc(c.AHI.h[:, 8 * g:8 * g + 8], [128, 8, 128], 2),
           ALU.mult, [CT_, c.AHI], [AMH_], acc=(g == 0))
        VT("pool", AML_.h[:, :, :], bc(tri.h[:, :], [128, 8, 128], 1), bc(c.ALO.h[:, 8 * g:8 * g + 8], [128, 8, 128], 2),
           ALU.mult, [CT_, c.ALO], [AML_], acc=(g == 0))

    def ssd_intra(ci, sample, g):
        tri = TRIS if sample else TRI
        lst = LSTS if sample else LST
        AMH_, AML_ = (AMH0, AML0) if g == 0 else (AMH, AML)
        for q in range(2):
            bank = PB[4 + q]
            MM(bank.h[:, :], lst.h[:, :], AMH_.h[:, 4 * q:4 * q + 4, :].rearrange("p h l -> p (h l)"), True, False,
               [CT_, AMH_], [bank], first=True)
            MM(bank.h[:, :], lst.h[:, :], AML_.h[:, 4 * q:4 * q + 4, :].rearrange("p h l -> p (h l)"), False, True,
               [CT_, AML_], [bank], first=False)
            ACT(DEC.h[:, 4 * q:4 * q + 4, :], bank.h[:, :].rearrange("p (h l) -> p h l", h=4), AF.Exp, [bank], [DEC],
                acc=(q == 1))
        VT("dve", MMT.h[:, :, :], DEC.h[:, :, :], bc(CBM.h[:, g, :], [128, 8, 128], 1), ALU.mult, [DEC, CBM], [MMT])
        bank = PB[6]
        for hh in range(8):
            h = 8 * g + hh
            MM(bank.h[:, hh * 64:(hh + 1) * 64], MMT.h[:, hh, :], XR.h[:, h * 64:(h + 1) * 64], True, True,
               [MMT, XR], [bank], first=(hh == 0))

    def ssd_prompt(ci):
        build_masks(False, 0)
        to_token_major()
        xs3 = BFA.h[:, :].rearrange("p (h d) -> p h d", d=64)
        VT("pool", XR.h[:, :].rearrange("p (h d) -> p h d", d=64), xs3, bc(c.DT.h[:, :], [128, 16, 64], 2), ALU.mult,
           [BFA, c.DT], [XR])
        build_masks(False, 1)
        VT("pool", XRD.h[:, :].rearrange("p (h d) -> p h d", d=64), xs3, bc(c.DDT.h[:, :], [128, 16, 64], 2), ALU.mult,
           [BFA, c.DDT], [XRD])
        for g in range(2):
            MM(B3.h[:, 64 + g * 128:64 + (g + 1) * 128], c.BT.h[:, g, :], c.CTT.h[:, g, :], True, True, [c.BT, c.CTT], [B3cb],
               first=(g == 0))
        VT("dve", CBM.h[:, :, :], B3.h[:, 64:320].rearrange("p (g l) -> p g l", g=2),
           bc(TRI.h[:, :], [128, 2, 128], 1), ALU.mult, [B3cb, CT_], [CBM])
        for g in range(2):
            ssd_intra(ci, False, g)
            ysl = Y.h[:, g * 512:(g + 1) * 512]
            y3 = ysl.rearrange("p (h d) -> p h d", d=64)
            if ci > 0:
                MM(PB[7].h[:, :], c.CTT.h[:, g, :], STBF.h[:, g * 512:(g + 1) * 512], True, True, [c.CTT, STBF], [PB[7]],
                   first=True)
                VT("dve", y3, PB[7].h[:, :].rearrange("p (h d) -> p h d", d=64),
                   bc(c.EAC.h[:, 8 * g:8 * g + 8], [128, 8, 64], 2), ALU.mult, [PB[7], c.EAC], [Y], acc=(g == 1))
                VT("dve", ysl, ysl, PB[6].h[:, :], ALU.add, [Y, PB[6]], [Y], acc=True)
            else:
                CP("dve", ysl, PB[6].h[:, :], [PB[6]], [Y], acc=(g == 1))
            VT("pool", YT.h[:, :].rearrange("p (h d) -> p h d", d=64),
               BFA.h[:, g * 512:(g + 1) * 512].rearrange("p (h d) -> p h d", d=64),
               bc(DSK.h[:, 8 * g:8 * g + 8], [128, 8, 64], 2), ALU.mult, [BFA, CT_], [YT])
            VT("dve", ysl, ysl, YT.h[:, :], ALU.add, [Y, YT], [Y], acc=True)
            MM(PB[7].h[:, :], BTOK.h[:, g * 128:(g + 1) * 128], XRD.h[:, g * 512:(g + 1) * 512], True, True,
               [BTOK, XRD], [PB[7]], first=True)
            ssl = STATE.h[:, g * 512:(g + 1) * 512]
            if ci > 0:
                s3 = ssl.rearrange("p (h d) -> p h d", d=64)
                VT("dve", s3, s3, bc(c.CD.h[:, 8 * g:8 * g + 8], [128, 8, 64], 2), ALU.mult, [STATE, c.CD], [STATE],
                   acc=True)
                VT("dve", ssl, ssl, PB[7].h[:, :], ALU.add, [STATE, PB[7]], [STATE], acc=True)
            else:
                CP("dve", ssl, PB[7].h[:, :], [PB[7]], [STATE], acc=(g == 1))
            if ci < 15:
                CP("act", STBF.h[:, g * 512:(g + 1) * 512], ssl, [STATE], [STBF], acc=(g == 1))

    def gates(cb):
        cs = slice(cb * 512, (cb + 1) * 512)
        for kc in range(8):
            MM(PB[6].h[:, :], c.HT.h[:, kc, :], WING.h[:, kc, cs], kc == 0, kc == 7, [c.HT, WING], [PB[6]],
               first=(kc == 0))
        for kc in range(8):
            MM(PB[7].h[:, :], c.HT.h[:, kc, :], WING.h[:, kc, 1024 + cb * 512:1024 + (cb + 1) * 512], kc == 0, kc == 7,
               [c.HT, WING], [PB[7]], first=(kc == 0))
        ACT(TA.h[:, :], PB[6].h[:, :], AF.Tanh, [PB[6]], [TA], scale=0.5)
        ACT(TB.h[:, :], PB[7].h[:, :], AF.Tanh, [PB[7]], [TB], scale=0.5)

    def pb_proj(cb):
        cs = slice(cb * 512, (cb + 1) * 512)
        for kc in range(4):
            MM(PB[5].h[:, :], c.YBT.h[:, kc, :], WB.h[:, kc, cs], kc == 0, kc == 3, [c.YBT, WB], [PB[5]], first=(kc == 0))

    def gate_norm_transpose():
        for g in range(2):
            bank = PB[4 + g]
            for kc in range(8):
                MM(bank.h[:, :], c.HT.h[:, kc, :], WINZ.h[:, kc, g * 512:(g + 1) * 512], kc == 0, kc == 7, [c.HT, WINZ],
                   [bank], first=(kc == 0))
            ACT(ZS.h[:, :], bank.h[:, :], AF.Silu, [bank], [ZS])
            ysl = Y.h[:, g * 512:(g + 1) * 512]
            VT("dve", ysl, ysl, ZS.h[:, :], ALU.mult, [Y, ZS], [Y], acc=True)
            ACT(ZS.h[:, :], ysl, AF.Square, [Y], [ZS, SSG], accum=SSG.h[:, g:g + 1], acc=(g == 1))
        gates(0)
        pb_proj(0)
        ACT(RG.h[:, :], SSG.h[:, :], AF.Ln, [SSG], [RG], bias=EPS, scale=1.0 / 512)
        ACT(RG.h[:, :], RG.h[:, :], AF.Exp, [RG], [RG], scale=-0.5)
        for g in range(2):
            VS("dve", BFA.h[:, g * 512:(g + 1) * 512], Y.h[:, g * 512:(g + 1) * 512], RG.h[:, g:g + 1], None, ALU.mult,
               None, [Y, RG], [BFA], acc=(g == 1))
        for j in range(8):
            TR(PT2[:, j * 128:(j + 1) * 128], BFA.h[:, j * 128:(j + 1) * 128], IDB.h[:, :], [BFA], [PT2t], first=(j == 0))
        VT("dve", c.YAT.h[:, :, :], PT2[:, :].rearrange("p (k t) -> p k t", k=8), bc(SSMN.h[:, :], [128, 8, 128], 2),
           ALU.mult, [PT2t, CT_], [c.YAT])

    def tail(ci):
        xb = XB[ci % 2]
        for cb in range(2):
            cs = slice(cb * 512, (cb + 1) * 512)
            for kc in range(8):
                MM(PB[4].h[:, :], c.YAT.h[:, kc, :], WA.h[:, kc, cs], kc == 0, kc == 7, [c.YAT, WA], [PB[4]], first=(kc == 0))
            STT("dve", QA.h[:, :], TA.h[:, :], 1.0, PB[4].h[:, :], ALU.add, ALU.mult, [TA, PB[4]], [QA])
            STT("dve", QB.h[:, :], TB.h[:, :], 1.0, PB[5].h[:, :], ALU.add, ALU.mult, [TB, PB[5]], [QB])
            if cb == 0:
                gates(1)
                pb_proj(1)
            VT("dve", MG.h[:, cs], QA.h[:, :], QB.h[:, :], ALU.add, [QA, QB], [MG], acc=(cb == 1))
        for j in range(8):
            TR(PT2[:, j * 128:(j + 1) * 128], MG.h[:, j * 128:(j + 1) * 128], IDB.h[:, :], [MG], [PT2t], first=(j == 0))
        S.op("act", lambda e: e.mul(MGT.h[:, :, :], PT2[:, :].rearrange("p (k t) -> p k t", k=8), 0.5),
             [PT2t], [MGT.t])
        for ob in range(2):
            bank = PB[4 + ob]
            for kc in range(8):
                MM(bank.h[:, :], MGT.h[:, kc, :], WO.h[:, kc, ob * 512:(ob + 1) * 512], kc == 0, kc == 7, [MGT, WO],
                   [bank], first=(kc == 0))
        ACT(TAJ.h[:, :], pb45[:, :], AF.Square, [PB[4], PB[5]], [TAJ, SSP], accum=SSP.h[:, 0:1])
        ACT(RP.h[:, :], SSP.h[:, 0:1], AF.Ln, [SSP], [RP], bias=EPS, scale=1.0 / 1024)
        ACT(RP.h[:, :], RP.h[:, :], AF.Exp, [RP], [RP], scale=-0.5)
        for ob in range(2):
            cs = slice(ob * 512, (ob + 1) * 512)
            STT("dve", OT.h[:, :], PB[4 + ob].h[:, :], RP.h[:, 0:1], NPOST.h[:, cs], ALU.mult, ALU.mult,
                [PB[4 + ob], RP, CT_], [OT])
            VT("dve", xb.h[:, cs], xb.h[:, cs], OT.h[:, :], ALU.add, [xb, OT], [xb], acc=True)
        DMA("sp", y_d[ci], xb.h[:, :], [xb], [OUTT], xb.t)

    def norm_pre(ci):
        xb = XB[ci % 2]
        ACT(XN.h[:, :], xb.h[:, :], AF.Square, [xb], [XN, SSPRE], accum=SSPRE.h[:, :])
        ACT(RPRE.h[:, :], SSPRE.h[:, :], AF.Ln, [SSPRE], [RPRE], bias=EPS, scale=1.0 / 1024)
        ACT(RPRE.h[:, :], RPRE.h[:, :], AF.Exp, [RPRE], [RPRE], scale=-0.5)
        VS("dve", XN.h[:, :], xb.h[:, :], RPRE.h[:, 0:1], None, ALU.mult, None, [xb, RPRE], [XN])
        for j in range(8):
            TR(PT.h[:, j * 128:(j + 1) * 128], XN.h[:, j * 128:(j + 1) * 128], IDB.h[:, :], [XN], [PT], first=(j == 0))
        VT("dve", c.HT.h[:, :, :], PT.h[:, :].rearrange("p (k t) -> p k t", k=8), bc(NPRE.h[:, :], [128, 8, 128], 2),
           ALU.mult, [PT, CT_], [c.HT])

    def out_fp32_T(src_fn, ncols, tiles, stage_ap_fn, stage_tiles, dram_ap, reads, key, banks=(0, 1)):
        done = 0
        nb = 0
        nt = len(tiles)
        while done < nt:
            n = min(4, nt - done)
            bank = PB[banks[nb % 2]]
            nb += 1
            for j in range(n):
                TR(bank.h[0:ncols, j * 128:(j + 1) * 128], src_fn(tiles[done + j]), IDF.h[:, :], reads, [bank],
                   first=(j == 0))
            CP("dve", stage_ap_fn(done * 128, (done + n) * 128), bank.h[0:ncols, 0:n * 128], [bank], stage_tiles,
               acc=(done > 0))
            done += n
        DMA("sp", dram_ap, stage_ap_fn(0, nt * 128), stage_tiles, [OUTT], key)

    def stage1(ci):
        setpar(ci % 2)
        norm_pre(ci)
        xbc4, ux4, uxt = proj_feature_major(ci, False)
        dt_path(False)
        conv_and_silu(xbc4, False)
        pool_branch(ci, ux4, uxt, False)
        if ci < 15:
            CP("pool", XBC.h[:, :, 0:3], XBC.h[:, :, 128:131], [XBC], [XBC], acc=True)
            CP("pool", UX.h[:, :, 0:15], UX.h[:, :, 128:143], [UX], [UX], acc=True)

    def stage23(ci):
        setpar(ci % 2)
        ssd_prompt(ci)
        gate_norm_transpose()
        tail(ci)

    MEMSET("dve", XBC.h[:, :, 0:3], 0.0, [XBC])
    MEMSET("dve", UX.h[:, :, 0:15], 0.0, [UX])
    load_x(0)
    stage1(0)
    for ci in range(nprompt):
        sa = None
        if ci + 1 < nprompt or do_sample:
            load_x(ci + 1)
        if ci + 1 < nprompt:
            S.begin()
            stage1(ci + 1)
            sa = S.end()
        S.begin()
        stage23(ci)
        sb = S.end()
        S.run_merged([sa, sb])

    XRF = alloc("xrf", [128, 512], F32, at=b_off["xr"], tile=XR.t)
    XRDF = alloc("xrdf", [128, 512], F32, at=b_off["xrd"], tile=XRD.t)

    def prompt_state_outputs():
        out_fp32_T(lambda t: XBC.h[:, t, 128:131], 3, list(range(8)), lambda a, b: Y.h[0:3, a:b], [Y.t],
                   convp_d[:, 0:1024], [XBC], Y.t, banks=(6, 7))
        out_fp32_T(lambda t: XBC.h[:, t, 128:131], 3, list(range(8, 12)), lambda a, b: ZS.h[0:3, a:b], [ZS.t],
                   convp_d[:, 1024:1536], [XBC], ZS.t, banks=(6, 7))
        out_fp32_T(lambda t: UX.h[:, t, 128:143], 15, list(range(4)), lambda a, b: XRF.h[0:15, a:b], [XR.t],
                   poolp_d[:, :], [UX], XR.t, banks=(6, 7))
        ssmp_v = ssmp_d.rearrange("(j p) n -> p j n", p=128)
        for half in range(2):
            bank = PB[6 + half]
            for j in range(4):
                jj = half * 4 + j
                TR(bank.h[:, j * 128:(j + 1) * 128], STATE.h[:, jj * 128:(jj + 1) * 128], IDF.h[:, :], [STATE], [bank],
                   first=(j == 0))
            CP("dve", XRDF.h[:, :], bank.h[:, :], [bank], [XRD])
            DMA("sp", ssmp_v[:, half * 4:(half + 1) * 4, :], XRDF.h[:, :].rearrange("p (j n) -> p j n", j=4), [XRD],
                [OUTT], XRD.t)

    sctx = {}

    def sample_front():
        ci = 16
        setpar(0)
        norm_pre(ci)
        SCVT = [R1T[0], R1T[1], R1T[2]]
        SPLT = [R1T[3], R1T[4]]
        DMA("act", SCV.h[0:48, :], sconv_d[:, :], [], SCVT, R1T[0])
        DMA("act", SPL.h[0:120, :, :], spool_d.rearrange("(h r) c -> r h c", h=2), [], SPLT, R1T[3])
        xbc4s = XBC.h[:, :, :].rearrange("p t (b j) -> p t b j", j=11)
        ux4s = UXS_h[:, :, :].rearrange("p t (b j) -> p t b j", j=23)
        for tile in range(12):
            bank = PB[4 + (tile % 2)]
            TR(bank.h[:, 0:48], SCV.h[0:48, tile * 128:(tile + 1) * 128], IDF.h[0:48, 0:48], SCVT, [bank], first=True)
            CP("dve", xbc4s[:, tile, :, 0:3], bank.h[:, 0:48].rearrange("p (b k) -> p b k", k=3), [bank], [XBC], acc=True)
        first_ux = True
        for k in range(4):
            for half in range(2):
                bank = PB[4 + half]
                TR(bank.h[:, 0:120], SPL.h[0:120, half, k * 128:(k + 1) * 128], IDF.h[0:120, 0:120], SPLT, [bank],
                   first=True)
                CP("dve", ux4s[:, k, 8 * half:8 * half + 8, 0:15], bank.h[:, 0:120].rearrange("p (b k) -> p b k", k=15),
                   [bank, STATE, STBF], [UXST, STATE, STBF], acc=not first_ux)
                first_ux = False
        xbc4, ux4, uxt = proj_feature_major(ci, True)
        conv_and_silu(xbc4, True)
        alias_tiles = [b.t for b in H0 + H0B + H0T + CTM + BM] + [CDC.t, CSTG.t, PSTG.t, OSTG.t, ADTX.t, P2s.t, P4s.t, P8s.t]
        S.op("pool", lambda e: e.memset(SCR.h[:, 0:1], 0.0), [], [WINX.t, WXB] + alias_tiles + [SCR.t])
        for tile in range(12):
            CP("pool", CSTG.h[:, tile, :].rearrange("p (b k) -> p b k", k=3), xbc4s[:, tile, :, 8:11], [XBC], [CSTG],
               acc=(tile > 0))
        out_fp32_T(lambda t: CSTG.h[:, t, :], 48, list(range(12)), lambda a, b: OSTG.h[0:48, a:b], [OSTG.t],
                   convs_d[:, :], [CSTG], OSTG.t)
        sctx.update(ux4=ux4, uxt=uxt, ux4s=ux4s)

    def sample_rest():
        ci = 16
        setpar(0)
        ux4, uxt, ux4s = sctx["ux4"], sctx["uxt"], sctx["ux4s"]
        ssv = sso = None
        dt_path(True)
        build_masks(True, 0)
        build_masks(True, 1)
        to_token_major()
        xs3 = BFA.h[:, :].rearrange("p (h d) -> p h d", d=64)
        VT("pool", XR.h[:, :].rearrange("p (h d) -> p h d", d=64), xs3, bc(c.DT.h[:, :], [128, 16, 64], 2), ALU.mult,
           [BFA, c.DT], [XR])
        VT("pool", XRD.h[:, :].rearrange("p (h d) -> p h d", d=64), xs3, bc(c.DDT.h[:, :], [128, 16, 64], 2), ALU.mult,
           [BFA, c.DDT], [XRD])
        for g in range(2):
            MM(B3.h[:, 64 + g * 128:64 + (g + 1) * 128], c.BT.h[:, g, :], c.CTT.h[:, g, :], True, True, [c.BT, c.CTT], [B3cb],
               first=(g == 0))
        VT("dve", CBM.h[:, :, :], B3.h[:, 64:320].rearrange("p (g l) -> p g l", g=2), bc(TRIS.h[:, :], [128, 2, 128], 1),
           ALU.mult, [B3cb, CT_], [CBM])
        for g in range(2):
            ssd_intra(ci, True, g)
            ysl = Y.h[:, g * 512:(g + 1) * 512]
            CP("dve", ysl, PB[6].h[:, :], [PB[6]], [Y], acc=(g == 1))
            VT("pool", YT.h[:, :].rearrange("p (h d) -> p h d", d=64),
               BFA.h[:, g * 512:(g + 1) * 512].rearrange("p (h d) -> p h d", d=64),
               bc(DSK.h[:, 8 * g:8 * g + 8], [128, 8, 64], 2), ALU.mult, [BFA, CT_], [YT])
            VT("dve", ysl, ysl, YT.h[:, :], ALU.add, [Y, YT], [Y], acc=True)
        VT("dve", ADTX.h[:, :].rearrange("p (h d) -> p h d", d=64), bc(c.ADT.h[:, :], [128, 16, 64], 2),
           bc(ONE.h[:, 0:16], [128, 16, 64], 2), ALU.mult, [c.ADT, CT_], [ADTX])
        for j in range(8):
            MM(PB[7].h[:, j * 16:(j + 1) * 16], ADTX.h[:, j * 128:(j + 1) * 128], SQM.h[:, :], True, True,
               [ADTX, CT_], [PB[7]], first=(j == 0))
        ACT(CDC.h[:, :, :], PB[7].h[:, 0:128].rearrange("p (j b) -> p j b", j=8), AF.Exp, [PB[7]], [CDC])
        for i in range(2):
            MEMSET("pool", CTM[i].h[:, :, :], 0.0, [CTM[i]])
        S.begin()
        ssv = sssm_d.rearrange("b (j p) n -> b p j n", p=128)
        sso = ssms_d.rearrange("b (j p) n -> b p j n", p=128)
        NB = 3
        S.op("pool", lambda e: e.memset(SCR.h[:, 1:2], 0.0), [], [R1T[0], R1T[1], R1T[2], H0[2].t, H0B[2].t, SCR.t])

        def h0_load(b):
            rb = b % NB
            DMA("sp", H0[rb].h[:, :, :], ssv[b], [], [H0[rb]], H0[rb].t)
            DMA("pool", H0B[rb].h[:, :, :], ssv[b], [], [H0B[rb]], H0B[rb].t)

        def stage_a(b):
            rb = b % NB
            r2_ = b % 2
            for j in range(8):
                TR(PT.h[:, j * 128:(j + 1) * 128], H0B[rb].h[:, j, :], IDB.h[:, :], [H0B[rb]], [PT], first=(j == 0))
            CP("act", H0T[r2_].h[:, :], PT.h[:, :], [PT], [H0T[r2_]])
            CP("act", CTM[r2_].h[:, :, 8 * b:8 * b + 8], c.CTT.h[:, :, 8 * b:8 * b + 8], [c.CTT], [CTM[r2_]], acc=True)
            ACT(BM[r2_].h[:, :], BTOK.h[:, :], AF.Copy, [BTOK, CT_], [BM[r2_]], scale=SQM.h[:, b:b + 1])

        def stage_b(b):
            rb = b % NB
            r2_ = b % 2
            for g in range(2):
                MM(PB[g].h[:, :], CTM[r2_].h[:, g, :], H0T[r2_].h[:, g * 512:(g + 1) * 512], b == 0, b == 15,
                   [CTM[r2_], H0T[r2_]], [PB[g]], first=(b == 0))
            for j in range(8):
                bank = PB[4 + j // 4]
                MM(bank.h[:, (j % 4) * 128:(j % 4 + 1) * 128], XRD.h[:, j * 128:(j + 1) * 128],
                   BM[r2_].h[:, (j // 4) * 128:(j // 4 + 1) * 128], True, True, [XRD, BM[r2_]], [bank], first=(j % 4 == 0))
            ACT(CTM[r2_].h[:, :, 8 * b:8 * b + 8], CTM[r2_].h[:, :, 8 * b:8 * b + 8], AF.Copy, [CTM[r2_]], [CTM[r2_]],
                scale=0.0, acc=True)
            for j in range(8):
                bank = PB[4 + j // 4]
                STT("dve", H0[rb].h[:, j, :], H0[rb].h[:, j, :], CDC.h[:, j, b:b + 1],
                    bank.h[:, (j % 4) * 128:(j % 4 + 1) * 128], ALU.mult, ALU.add, [H0[rb], CDC, bank], [H0[rb]], acc=True)
            DMA("act", sso[b], H0[rb].h[:, :, :], [H0[rb]], [OUTT], H0[rb].t)

        h0_load(0)
        h0_load(1)
        stage_a(0)
        for b in range(16):
            if b + 1 < 16:
                stage_a(b + 1)
            stage_b(b)
            if b + 2 < 16:
                h0_load(b + 2)
        S.op("pool", lambda e: e.memset(SCR.h[:, 2:3], 0.0), [], [R1T[0], R1T[1], R1T[2], H0[2].t, H0B[2].t, SCR.t])
        st_loop = S.end()
        S.begin()
        pool_branch(ci, ux4, uxt, True, mixbank=B3)
        for k in range(4):
            CP("pool", PSTG.h[:, k, :].rearrange("p (b r) -> p b r", r=15), ux4s[:, k, :, 8:23], [UXST], [PSTG],
               acc=(k > 0))
        for half in range(2):
            bank = PB[6 + half]
            for k in range(4):
                TR(bank.h[0:120, k * 128:(k + 1) * 128], PSTG.h[:, k, half * 120:(half + 1) * 120], IDF.h[:, :], [PSTG],
                   [bank], first=(k == 0))
            CP("dve", OSTG.h[0:120, half * 512:(half + 1) * 512], bank.h[0:120, :], [bank], [OSTG], acc=True)
        DMA("sp", pools_d.rearrange("(h r) c -> r h c", h=2), OSTG.h[0:120, 0:1024].rearrange("r (h c) -> r h c", h=2),
            [OSTG], [OUTT], OSTG.t)
        st_pool = S.end()
        S.run_merged([st_loop, st_pool])
        for g in range(2):
            ysl = Y.h[:, g * 512:(g + 1) * 512]
            VT("dve", YT.h[:, :].rearrange("p (h d) -> p h d", d=64), PB[g].h[:, :].rearrange("p (h d) -> p h d", d=64),
               bc(c.EAC.h[:, 8 * g:8 * g + 8], [128, 8, 64], 2), ALU.mult, [PB[g], c.EAC], [YT])
            VT("dve", ysl, ysl, YT.h[:, :], ALU.add, [Y, YT], [Y], acc=True)
        gate_norm_transpose()
        tail(ci)

    sa = sb = None
    if do_pstate:
        S.begin()
        prompt_state_outputs()
        sa = S.end()
    if do_sample:
        S.begin()
        sample_front()
        sb = S.end()
    S.run_merged([sa, sb])
    if do_sample:
        sample_rest()
    S.op("sp", None, [OUTT], [])

    S.finalize(nc, es)
    block = es.enter_context(nc.Block())

    @block.tensor
    def _(e):
        S.emit_engine(e, "pe")

    @block.scalar
    def _(e):
        S.emit_engine(e, "act")

    @block.vector
    def _(e):
        S.emit_engine(e, "dve")

    @block.gpsimd
    def _(e):
        S.emit_engine(e, "pool")

    @block.sync
    def _(e):
        S.emit_engine(e, "sp")

    es.close()
    return nc


_NC_CACHE = {}


def _prep_inputs(inp):
    f = lambda a: np.ascontiguousarray(np.asarray(a, dtype=np.float32))
    xp = f(inp["x_prompt"])
    xs = f(inp["x_sample"])
    sc = f(inp["state_conv"])[0]
    ss = f(inp["state_ssm"])[0]
    sp = f(inp["state_pool"])[0]
    shared = {
        "w_in": f(inp["w_in"])[0],
        "w_a": f(inp["w_branch_a"])[0],
        "w_b": f(inp["w_branch_b"])[0],
        "w_out": f(inp["w_out"])[0],
        "pool_mix": f(inp["pool_mix"])[0],
        "npre_col": f(f(inp["norm_pre"])[0].reshape(8, 128).T),
        "convw_col": f(f(inp["conv_w"])[0].reshape(4, 12, 128).transpose(2, 1, 0)),
        "convb_col": f(f(inp["conv_b"])[0].reshape(12, 128).T),
        "ssmn_col": f(f(inp["ssm_norm"])[0].reshape(8, 128).T),
        "pscale_col": f(f(inp["pool_scale"])[0].reshape(4, 128).T),
        "dtb_bc": f(np.broadcast_to(f(inp["dt_bias"])[0][None, :], (128, 16))),
        "alog_bc": f(np.broadcast_to(f(inp["a_log"])[0][None, :], (128, 16))),
        "dskip_bc": f(np.broadcast_to(f(inp["d_skip"])[0][None, :], (128, 16))),
        "npost_bc": f(np.broadcast_to(f(inp["norm_post"])[0][None, :], (128, 1024))),
    }
    maps = []
    for c in range(8):
        x = np.concatenate([xp[c].reshape(16, 128, 1024), xs[16 * c:16 * c + 16].reshape(1, 128, 1024)], axis=0)
        m = dict(shared)
        m["x"] = f(x)
        m["sconv"] = f(sc[16 * c:16 * c + 16].reshape(48, 1536))
        m["sssm"] = f(ss[16 * c:16 * c + 16].reshape(16, 1024, 128))
        m["spool"] = f(sp[16 * c:16 * c + 16].reshape(240, 512))
        maps.append(m)
    return maps


def kernel(**inputs):
    if "nc" not in _NC_CACHE:
        _NC_CACHE["nc"] = build_program()
    nc = _NC_CACHE["nc"]
    maps = _prep_inputs(inputs)
    res = run_bass_kernel_spmd(nc, maps, core_ids=list(range(8)))
    R = res.results
    yp = np.stack([R[c]["y"][:16].reshape(2048, 1024) for c in range(8)], axis=0)
    ys = np.concatenate([R[c]["y"][16].reshape(16, 8, 1024) for c in range(8)], axis=0)
    convp = np.stack([R[c]["convp"] for c in range(8)], axis=0)[None]
    ssmp = np.stack([R[c]["ssmp"].reshape(16, 64, 128) for c in range(8)], axis=0)[None]
    poolp = np.stack([R[c]["poolp"] for c in range(8)], axis=0)[None]
    convs = np.concatenate([R[c]["convs"].reshape(16, 3, 1536) for c in range(8)], axis=0)[None]
    ssms = np.concatenate([R[c]["ssms"].reshape(16, 16, 64, 128) for c in range(8)], axis=0)[None]
    pools = np.concatenate([R[c]["pools"].reshape(16, 15, 512) for c in range(8)], axis=0)[None]
    out = (yp, ys, convp, ssmp, poolp, convs, ssms, pools)
    return tuple(np.ascontiguousarray(o, dtype=np.float32) for o in out)
```

```python
import os
import numpy as np
from contextlib import ExitStack
import concourse.bass as bass
import concourse.mybir as mybir
from concourse.bass_utils import run_bass_kernel_spmd

F32 = mybir.dt.float32
BF16 = mybir.dt.bfloat16
AF = mybir.ActivationFunctionType
ALU = mybir.AluOpType

NCHUNK = 17
XLAT = float(os.environ.get("XLAT", 0.9))
SAME_ENG_ALL = int(os.environ.get("SAME_ENG_ALL", 1))
EPS = 1e-6


class Tile:
    __slots__ = ("name", "writers", "readers", "sem", "cnt", "collector", "psum")

    def __init__(self, name, collector=False, psum=False):
        self.psum = psum
        self.name = name
        self.writers = []
        self.readers = []
        self.sem = None
        self.cnt = 0
        self.collector = collector


class Op:
    __slots__ = ("eng", "fn", "reads", "writes", "acc", "dma", "depc", "depdma",
                 "need_inc", "ticket", "dticket", "seq", "dsem", "t_end")


class Sched:
    ENGS = ("pe", "act", "dve", "pool", "sp")
    DMA_SEM_MAX = 224
    ROT = int(os.environ.get("ROT", 1000))

    def __init__(self):
        self.ops = []
        self.per = {e: [] for e in self.ENGS}
        self.dma_keys = []
        self.cur = None
        self.eng_t = {e: 0.0 for e in self.ENGS}
        self.act_set = None

    def begin(self):
        self.cur = []
        return self.cur

    def end(self):
        st = self.cur
        self.cur = None
        return st

    def _est_start(self, a):
        eng, fn, reads, writes, acc, dma, cost, aset = a
        t = self.eng_t[eng]
        if aset is not None and aset != self.act_set:
            t += 1.3
        for tl_ in reads:
            for w in tl_.writers:
                te = w.t_end + (0.0 if w.eng == eng else XLAT)
                if te > t:
                    t = te
            if tl_.psum:
                for r in tl_.readers:
                    if r.eng != eng and r.t_end > t:
                        t = r.t_end
        for tl_ in writes:
            if tl_.collector:
                continue
            for w in tl_.writers:
                te = w.t_end + (0.0 if w.eng == eng else XLAT)
                if te > t:
                    t = te
            for r in tl_.readers:
                te = r.t_end + (0.0 if r.eng == eng else XLAT)
                if te > t:
                    t = te
        return t

    def run_merged(self, streams):
        streams = [st for st in streams if st]
        pos = [0] * len(streams)
        while True:
            best = -1
            bt = 1e30
            for i, st in enumerate(streams):
                if pos[i] < len(st):
                    t = self._est_start(st[pos[i]])
                    t += 0.05 * pos[i] / len(st)
                    if t < bt:
                        bt = t
                        best = i
            if best < 0:
                break
            a = streams[best][pos[best]]
            pos[best] += 1
            self.op(*a)

    def op(self, eng, fn, reads=(), writes=(), acc=False, dma=None, cost=0.3, aset=None):
        if self.cur is not None:
            self.cur.append((eng, fn, list(reads), list(writes), acc, dma, cost, aset))
            return None
        t_start = self._est_start((eng, fn, reads, writes, acc, dma, cost, aset))
        if aset is not None:
            self.act_set = aset
        o = Op()
        if dma is not None:
            self.eng_t[eng] = t_start + 0.1
            o.t_end = t_start + cost
        else:
            o.t_end = t_start + cost
            self.eng_t[eng] = o.t_end
        o.eng = eng
        o.fn = fn
        o.reads = list(reads)
        o.writes = list(writes)
        o.acc = acc
        o.dma = dma
        o.depc = []
        o.depdma = []
        o.need_inc = False
        o.ticket = 0
        o.dticket = 0
        if dma is not None:
            if dma.cnt == 0 and dma not in self.dma_keys:
                self.dma_keys.append(dma)
            o.dsem = dma.cnt // self.DMA_SEM_MAX
            dma.cnt += 16
            o.dticket = dma.cnt - o.dsem * self.DMA_SEM_MAX
        deps = {}
        for t in o.reads:
            for w in t.writers:
                deps[id(w)] = (w, True)
            if t.psum:
                for r in t.readers:
                    if r.eng != eng and id(r) not in deps:
                        deps[id(r)] = (r, False)
        for t in o.writes:
            if t.collector:
                continue
            for w in t.writers:
                if id(w) not in deps:
                    deps[id(w)] = (w, False)
            for r in t.readers:
                if id(r) not in deps:
                    deps[id(r)] = (r, None)
        for t in o.reads:
            t.readers.append(o)
        for t in o.writes:
            if t.collector or acc:
                t.writers.append(o)
            else:
                t.writers = [o]
                t.readers = []
        latest = {}
        for d, raw in deps.values():
            if d is o:
                continue
            if d.dma is not None:
                o.depdma.append(d)
                continue
            if d.eng == eng:
                if eng == "pe":
                    continue
                if raw is None and SAME_ENG_ALL < 1:
                    continue
                if raw is False and SAME_ENG_ALL < 1 and SAME_ENG_ALL > -1:
                    pass
                if raw is False and SAME_ENG_ALL < 0:
                    continue
                if raw is False and acc and d.acc:
                    continue
            c_ = latest.get(d.eng)
            if c_ is None or d.seq > c_.seq:
                latest[d.eng] = d
        for d in latest.values():
            d.need_inc = True
            o.depc.append(d)
        o.seq = len(self.ops)
        self.ops.append(o)
        self.per[eng].append(o)
        return o

    def finalize(self, nc, es):
        self.engsem = {e: [] for e in self.ENGS}
        nsem = 0
        for i, k in enumerate(self.dma_keys):
            n = (k.cnt + self.DMA_SEM_MAX - 1) // self.DMA_SEM_MAX
            k.sem = [es.enter_context(nc.semaphore("dq%d_%d" % (i, j))) for j in range(n)]
            nsem += n
        print("dma sems", nsem)
        for e in self.ENGS:
            c = 0
            for o in self.per[e]:
                if o.need_inc:
                    c += 1
                    o.ticket = c
            print("engine", e, "ops", len(self.per[e]), "tickets", c)
            nse = (c + self.ROT - 1) // self.ROT
            self.engsem[e] = [es.enter_context(nc.semaphore("sem_%s%d" % (e, j))) for j in range(max(nse, 1))]

    def emit_engine(self, e, eng):
        waited = {}
        nw = 0
        for o in self.per[eng]:
            waits = {}
            for d in o.depc:
                s = self.engsem[d.eng][(d.ticket - 1) // self.ROT]
                k = id(s)
                tv = (d.ticket - 1) % self.ROT + 1
                if waits.get(k, (None, 0))[1] < tv:
                    waits[k] = (s, tv)
            for d in o.depdma:
                s = d.dma.sem[d.dsem]
                k = id(s)
                if waits.get(k, (None, 0))[1] < d.dticket:
                    waits[k] = (s, d.dticket)
            for k, (s, v) in waits.items():
                if waited.get(k, 0) >= v:
                    continue
                waited[k] = v
                e.wait_ge(s, v)
                nw += 1
                if os.environ.get("DUMPW") and o.seq >= int(os.environ.get("DUMPW")):
                    print("W", eng, o.seq, getattr(s, "name", s), v)
            if os.environ.get("DUMPW") and o.seq >= int(os.environ.get("DUMPW")):
                print("OP", eng, o.seq, "inc" if o.need_inc else "", o.ticket)
            ins = o.fn(e) if o.fn is not None else None
            if o.need_inc:
                assert ins is not None
                ins.then_inc(self.engsem[eng][(o.ticket - 1) // self.ROT], 1)
            if o.dma is not None:
                ins.then_inc(o.dma.sem[o.dsem], 16)
        print("emit", eng, "ops", len(self.per[eng]), "waits", nw)


class Buf:
    __slots__ = ("h", "t")

    def __init__(self, h, t):
        self.h = h
        self.t = t


def build_program(dbg=None, nprompt=16, do_pstate=True, do_sample=True):
    nc = bass.Bass("TRN2", target_bir_lowering=False)
    S = Sched()
    es = ExitStack()

    def din(name, shape, dt=F32):
        return nc.dram_tensor(name, shape, dt, kind="ExternalInput").ap()

    def dout(name, shape, dt=F32):
        return nc.dram_tensor(name, shape, dt, kind="ExternalOutput").ap()

    x_d = din("x", [NCHUNK, 128, 1024])
    sconv_d = din("sconv", [48, 1536])
    sssm_d = din("sssm", [16, 1024, 128])
    spool_d = din("spool", [240, 512])
    win_d = din("w_in", [1024, 5648])
    wa_d = din("w_a", [1024, 1024])
    wb_d = din("w_b", [512, 1024])
    wo_d = din("w_out", [1024, 1024])
    pmix_d = din("pool_mix", [4, 128, 128])
    npre_d = din("npre_col", [128, 8])
    convw_d = din("convw_col", [128, 12, 4])
    convb_d = din("convb_col", [128, 12])
    ssmn_d = din("ssmn_col", [128, 8])
    pscale_d = din("pscale_col", [128, 4])
    dtb_d = din("dtb_bc", [128, 16])
    alog_d = din("alog_bc", [128, 16])
    dskip_d = din("dskip_bc", [128, 16])
    npost_d = din("npost_bc", [128, 1024])

    y_d = dout("y", [NCHUNK, 128, 1024])
    convp_d = dout("convp", [3, 1536])
    ssmp_d = dout("ssmp", [1024, 128])
    poolp_d = dout("poolp", [15, 512])
    convs_d = dout("convs", [48, 1536])
    ssms_d = dout("ssms", [16, 1024, 128])
    pools_d = dout("pools", [240, 512])
    dbg_outs = {}
    OUTT = Tile("dram_out", collector=True)

    ARENA = 212800
    arena = nc.alloc_sbuf_tensor("arena", [128, ARENA // 4], F32)
    base = nc.lookup_mloc(arena).addr
    cur = [0]

    def nbytes(shape, dt):
        n = 1
        for s in shape[1:]:
            n *= s
        return n * (2 if dt == BF16 else 4)

    def alloc(name, shape, dt, at=None, tile=None):
        sz = (nbytes(shape, dt) + 31) // 32 * 32
        if at is None:
            off = cur[0]
            cur[0] += sz
            assert cur[0] <= ARENA, (name, cur[0])
        else:
            off = at
        h = nc.alloc_sbuf_tensor_at(name, shape, dt, offset=base + off)
        b = Buf(h, tile if tile is not None else Tile(name))
        b_off[name] = off
        return b

    b_off = {}

    WINX = alloc("winx", [128, 8, 2576], BF16)
    WINZ = alloc("winz", [128, 8, 1024], BF16)
    WING = alloc("wing", [128, 8, 2048], BF16)
    WA = alloc("wa", [128, 8, 1024], BF16)
    WB = alloc("wb", [128, 4, 1024], BF16)
    WO = alloc("wo", [128, 8, 1024], BF16)
    PMIX = alloc("pmix", [128, 4, 128], BF16)
    CT_ = Tile("consts", collector=True)
    IDB = alloc("idb", [128, 128], BF16, tile=CT_)
    IDF = alloc("idf", [128, 128], F32, tile=CT_)
    TRI = alloc("tri", [128, 128], BF16, tile=CT_)
    LST = alloc("lst", [128, 128], BF16, tile=CT_)
    ONE = alloc("one", [128, 128], BF16, tile=CT_)
    TRIS = alloc("tris", [128, 128], BF16, tile=CT_)
    LSTS = alloc("lsts", [128, 128], BF16, tile=CT_)
    SSQ = alloc("ssq", [128, 128], BF16, tile=CT_)
    SQM = alloc("sqm", [128, 16], F32, tile=CT_)
    NPRE = alloc("npre", [128, 8], F32, tile=CT_)
    CONVW = alloc("convw", [128, 12, 4], F32, tile=CT_)
    CONVB = alloc("convb", [128, 12], F32, tile=CT_)
    SSMN = alloc("ssmn", [128, 8], F32, tile=CT_)
    PSC = alloc("psc", [128, 4], F32, tile=CT_)
    DTB = alloc("dtb", [128, 16], F32, tile=CT_)
    ABC = alloc("abc", [128, 16], F32, tile=CT_)
    DSK = alloc("dsk", [128, 16], F32, tile=CT_)
    NPOST = alloc("npostb", [128, 1024], F32, tile=CT_)
    INVC = alloc("invc", [128, 16], F32, tile=CT_)
    SCR = alloc("scr", [128, 16], F32)
    NEGM = alloc("negm", [128, 128], BF16, tile=CT_)
    NEGMS = alloc("negms", [128, 128], BF16, tile=CT_)
    NACS = alloc("nacs", [128, 16], F32)
    XB = [alloc("xb0", [128, 1024], F32), alloc("xb1", [128, 1024], F32)]
    BFA = alloc("bfa", [128, 1024], BF16)
    XN = alloc("xn", [128, 1024], BF16)
    HTs = [alloc("ht0", [128, 8, 128], BF16), alloc("ht1", [128, 8, 128], BF16)]
    XBC = alloc("xbc", [128, 12, 176], F32)
    UX = alloc("ux", [128, 4, 143], F32)
    CACC = [alloc("cacc%d" % i, [128, 128], F32) for i in range(4)]
    XSTs = [alloc("xst0", [128, 8, 128], BF16), alloc("xst1", [128, 8, 128], BF16)]
    BTs = [alloc("bt0", [128, 2, 128], BF16), alloc("bt1", [128, 2, 128], BF16)]
    CTTs = [alloc("ct0", [128, 2, 128], BF16), alloc("ct1", [128, 2, 128], BF16)]
    GS = alloc("gs", [128, 4, 128], F32)
    DTRs = [alloc("dtr0", [128, 16], F32), alloc("dtr1", [128, 16], F32)]
    DTs_ = [alloc("dt%d" % i, [128, 16], F32) for i in range(2)]
    ADTs_ = [alloc("adt%d" % i, [128, 16], F32) for i in range(2)]
    DIF = alloc("dif", [128, 16], F32)
    EACs_ = [alloc("eac%d" % i, [128, 16], F32) for i in range(2)]
    DTE = alloc("dte", [128, 16], F32)
    CDs_ = [alloc("cd%d" % i, [128, 16], F32) for i in range(2)]
    DDTs_ = [alloc("ddt%d" % i, [128, 16], F32) for i in range(2)]
    AHIs_ = [alloc("ahi%d" % i, [128, 16], BF16) for i in range(2)]
    ALOs_ = [alloc("alo%d" % i, [128, 16], BF16) for i in range(2)]
    BTOK = alloc("btok", [128, 256], BF16)
    XR = alloc("xr", [128, 1024], BF16)
    XRD = alloc("xrd", [128, 1024], BF16)
    MG = Buf(XR.h, XR.t)
    MGT_h = nc.alloc_sbuf_tensor_at("mgt", [128, 8, 128], BF16, offset=base + b_off["xrd"])
    MGT = Buf(MGT_h, XRD.t)
    R1T = [Tile("r1_%d" % i) for i in range(5)]
    r1 = cur[0]
    cur[0] += 10240
    AMH = alloc("amh", [128, 8, 128], BF16, at=r1, tile=R1T[0])
    AML = alloc("aml", [128, 8, 128], BF16, at=r1 + 2048, tile=R1T[1])
    DEC = alloc("dec", [128, 8, 128], BF16, at=r1 + 4096, tile=R1T[2])
    MMT = alloc("mmt", [128, 8, 128], BF16, at=r1 + 6144, tile=R1T[3])
    YT = alloc("yt", [128, 512], F32, at=r1 + 8192, tile=R1T[4])
    TA = alloc("ta", [128, 512], F32, at=r1, tile=R1T[0])
    TAJ = alloc("taj", [128, 1024], BF16, at=r1, tile=R1T[0])
    TB = alloc("tb", [128, 512], F32, at=r1 + 2048, tile=R1T[1])
    QA = alloc("qa", [128, 512], F32, at=r1 + 4096, tile=R1T[2])
    QB = alloc("qb", [128, 512], F32, at=r1 + 6144, tile=R1T[3])
    OT = alloc("ot", [128, 512], F32, at=r1 + 8192, tile=R1T[4])
    CBM = alloc("cbm", [128, 2, 128], BF16)
    STATE = alloc("state", [128, 1024], F32)
    STBF = alloc("stbf", [128, 1024], BF16)
    Y = alloc("y", [128, 1024], F32)
    ZS = alloc("zs", [128, 512], F32)
    AMH0 = alloc("amh0", [128, 8, 128], BF16, at=b_off["y"], tile=Y.t)
    AML0 = alloc("aml0", [128, 8, 128], BF16, at=b_off["y"] + 2048, tile=Y.t)
    P2p = alloc("p2", [128, 144], F32)
    P4p = alloc("p4", [128, 144], F32)
    P8p = alloc("p8", [128, 144], F32)
    P16 = alloc("p16", [128, 128], F32)
    PLD = alloc("pld", [128, 4, 128], BF16)
    YBTs = [alloc("ybt0", [128, 4, 128], BF16), alloc("ybt1", [128, 4, 128], BF16)]
    SSPRE = alloc("sspre", [128, 1], F32)
    RPRE = alloc("rpre", [128, 1], F32)
    SSG = alloc("ssg", [128, 2], F32)
    RG = alloc("rg", [128, 2], F32)
    SSP = alloc("ssp", [128, 2], F32)
    RP = alloc("rp", [128, 1], F32)
    UXS_h = nc.alloc_sbuf_tensor_at("uxs", [128, 4, 368], F32, offset=base + b_off["state"])
    assert b_off["stbf"] == b_off["state"] + 4096
    UXST = Tile("uxs")
    wx = b_off["winx"]
    H0 = [alloc("h0_%d" % i, [128, 8, 128], F32, at=wx + i * 4096) for i in range(2)]
    H0B = [alloc("h0b_%d" % i, [128, 8, 128], BF16, at=wx + 8192 + i * 2048) for i in range(2)]
    H0T = [alloc("h0t_%d" % i, [128, 1024], BF16, at=wx + 12288 + i * 2048) for i in range(2)]
    CTM = [alloc("ctm_%d" % i, [128, 2, 128], BF16, at=wx + 16384 + i * 512) for i in range(2)]
    BM = [alloc("bm_%d" % i, [128, 256], BF16, at=wx + 17408 + i * 512) for i in range(2)]
    H0.append(alloc("h0_2", [128, 8, 128], F32, at=r1))
    H0B.append(alloc("h0b_2", [128, 8, 128], BF16, at=r1 + 4096))
    CDC = alloc("cdc", [128, 8, 16], F32, at=wx + 18432)
    CSTG = alloc("cstg", [128, 12, 48], F32, at=wx + 18944)
    PSTG = alloc("pstg", [128, 4, 240], F32, at=wx + 18944 + 2304)
    OSTG = alloc("ostg", [128, 1536], F32, at=wx + 18944 + 2304 + 3840)
    ADTX = alloc("adtx", [128, 1024], F32, at=wx + 18944 + 2304 + 3840 + 6144)
    P2s = alloc("p2s", [128, 368], F32, at=wx + 35328)
    P4s = alloc("p4s", [128, 368], F32, at=wx + 35328 + 1472)
    P8s = alloc("p8s", [128, 368], F32, at=wx + 35328 + 2944)
    assert 35328 + 3 * 1472 <= 41216
    SCV = alloc("scv", [128, 1536], F32, at=r1, tile=R1T[0])
    SPL = alloc("spl", [128, 2, 512], F32, at=r1 + 6144, tile=R1T[3])
    print("SBUF used", cur[0], "of", ARENA)

    PB = []
    pb45 = es.enter_context(nc.psum_tensor("pb45", [128, 1024], F32))
    for i in range(8):
        if i == 2:
            h = es.enter_context(nc.psum_tensor("pb2", [128, 1024], BF16))
        elif i == 4:
            h = pb45[:, 0:512]
        elif i == 5:
            h = pb45[:, 512:1024]
        else:
            h = es.enter_context(nc.psum_tensor("pb%d" % i, [128, 512], F32))
        PB.append(Buf(h, Tile("pb%d" % i, psum=True)))
    PT = PB[2]
    B3 = PB[3]
    PT2 = PB[7].h[:, :].bitcast(BF16)
    PT2t = PB[7].t
    B3dt = B3.t
    B3ac = B3.t
    B3cb = B3.t

    def tl(bufs):
        return [b.t if isinstance(b, Buf) else b for b in bufs]

    def fsz(ap):
        n = 1
        for d in ap.shape[1:]:
            n *= d
        return n

    def MM(out, lhsT, rhs, start, stop, r, w, first):
        n = fsz(rhs)
        S.op("pe", lambda e: e.matmul(out, lhsT=lhsT, rhs=rhs, start=start, stop=stop),
             tl(r), tl(w), acc=not first, cost=0.06 + max(n, 64) / 2000.0)

    def TR(out, in_, ident, r, w, first):
        S.op("pe", lambda e: e.transpose(out, in_, ident), tl(r) + [CT_], tl(w), acc=not first, cost=0.12)

    def ecost(eng, out):
        n = fsz(out)
        if eng == "pool":
            return 0.25 + n * 0.0017
        if eng == "act":
            return 0.25 + n * 0.00085
        return 0.15 + n * 0.00105

    def VT(eng, out, in0, in1, op, r, w, acc=False):
        S.op(eng, lambda e: e.tensor_tensor(out=out, in0=in0, in1=in1, op=op), tl(r), tl(w), acc=acc,
             cost=ecost(eng, out))

    def VS(eng, out, in0, s1, s2, op0, op1, r, w, acc=False):
        if s2 is None:
            S.op(eng, lambda e: e.tensor_scalar(out=out, in0=in0, scalar1=s1, scalar2=None, op0=op0),
                 tl(r), tl(w), acc=acc, cost=ecost(eng, out))
        else:
            S.op(eng, lambda e: e.tensor_scalar(out=out, in0=in0, scalar1=s1, scalar2=s2, op0=op0, op1=op1),
                 tl(r), tl(w), acc=acc, cost=ecost(eng, out))

    def STT(eng, out, in0, sc, in1, op0, op1, r, w, acc=False):
        S.op(eng, lambda e: e.scalar_tensor_tensor(out=out, in0=in0, scalar=sc, in1=in1, op0=op0, op1=op1),
             tl(r), tl(w), acc=acc, cost=ecost(eng, out))

    def CP(eng, out, in_, r, w, acc=False):
        if eng == "act":
            S.op(eng, lambda e: e.activation(out=out, in_=in_, func=AF.Copy), tl(r), tl(w), acc=acc,
                 cost=ecost(eng, out))
        else:
            S.op(eng, lambda e: e.tensor_copy(out=out, in_=in_), tl(r), tl(w), acc=acc, cost=ecost(eng, out))

    def ACT(out, in_, func, r, w, bias=None, scale=None, accum=None, acc=False):
        kw = {}
        if bias is not None:
            kw["bias"] = bias
        if scale is not None:
            kw["scale"] = scale
        if accum is not None:
            kw["accum_out"] = accum
        aset = "A" if func in (AF.Silu, AF.Tanh) else ("B" if func in (AF.Exp, AF.Ln) else None)
        S.op("act", lambda e: e.activation(out=out, in_=in_, func=func, **kw), tl(r), tl(w), acc=acc,
             cost=ecost("act", out), aset=aset)

    def MEMSET(eng, ap, val, w, acc=False):
        S.op(eng, lambda e: e.memset(ap, val), [], tl(w), acc=acc)

    def DMA(eng, out, in_, r, w, key, acc=False):
        S.op(eng, lambda e: e.dma_start(out=out, in_=in_), tl(r), tl(w), acc=acc, dma=key, cost=3.0)

    def bc(ap, shape, axis):
        return ap.unsqueeze(axis).to_broadcast(shape)

    winv = win_d.rearrange("(kc p) e -> p kc e", p=128)
    WXB = Tile("winx_b")
    DMA("pool", WINX.h[:, :, 0:1536], winv[:, :, 1024:2560], [], [WINX], WINX.t)
    DMA("pool", WINX.h[:, :, 1536:2576], winv[:, :, 2560:3600], [], [WXB], WXB)
    small = [(NPRE, npre_d), (CONVW, convw_d), (CONVB, convb_d), (SSMN, ssmn_d), (PSC, pscale_d),
             (DTB, dtb_d), (ABC, alog_d), (DSK, dskip_d), (NPOST, npost_d)]
    ctl = {}

    def ct(bf):
        k = id(bf)
        if k not in ctl:
            ctl[k] = Tile("c%d" % len(ctl))
        return ctl[k]

    for b_, d_ in small:
        if len(d_.shape) == 3:
            DMA("act", b_.h[:, :, :], d_[:, :, :], [], [ct(b_)], ct(b_))
        else:
            DMA("act", b_.h[:, :], d_[:, :], [], [ct(b_)], ct(b_))
    DMA("pool", PMIX.h[:, :, :], pmix_d.rearrange("k c d -> c k d"), [], [PMIX], PMIX.t)
    DMA("pool", WINZ.h[:, :, :], winv[:, :, 0:1024], [], [WINZ], WINZ.t)
    DMA("pool", WING.h[:, :, :], winv[:, :, 3600:5648], [], [WING], WING.t)
    DMA("pool", WB.h[:, :, :], wb_d.rearrange("(kc p) e -> p kc e", p=128), [], [WB], WB.t)
    DMA("pool", WA.h[:, :, :], wa_d.rearrange("(kc p) e -> p kc e", p=128), [], [WA], WA.t)
    DMA("pool", WO.h[:, :, :], wo_d.rearrange("(kc p) e -> p kc e", p=128), [], [WO], WO.t)

    def aff(bf, eng_ap, pattern, cmp, fill, base_, cm):
        S.op("pool", lambda e: e.affine_select(out=eng_ap, in_=eng_ap, pattern=pattern, compare_op=cmp,
                                                fill=fill, base=base_, channel_multiplier=cm), [ct(bf)], [ct(bf)])

    for I_ in (IDB, IDF):
        MEMSET("pool", I_.h[:, :], 0.0, [ct(I_)])
        aff(I_, I_.h[:, :], [[-1, 128]], ALU.not_equal, 1.0, 0, 1)
    MEMSET("pool", TRI.h[:, :], 1.0, [ct(TRI)])
    aff(TRI, TRI.h[:, :], [[1, 128]], ALU.is_ge, 0.0, 0, -1)
    MEMSET("pool", LST.h[:, :], 1.0, [ct(LST)])
    aff(LST, LST.h[:, :], [[-1, 128]], ALU.is_gt, 0.0, 0, 1)
    MEMSET("pool", ONE.h[:, :], 1.0, [ct(ONE)])
    MEMSET("pool", SSQ.h[:, :], 1.0, [ct(SSQ)])
    ssq3 = SSQ.h[:, :].rearrange("p (b j) -> p b j", j=8)
    aff(SSQ, ssq3, [[-8, 16], [0, 8]], ALU.is_ge, 0.0, 0, 1)
    aff(SSQ, ssq3, [[8, 16], [0, 8]], ALU.is_ge, 0.0, 7, -1)
    MEMSET("pool", SQM.h[:, :], 1.0, [ct(SQM)])
    aff(SQM, SQM.h[:, :], [[-8, 16]], ALU.is_ge, 0.0, 0, 1)
    aff(SQM, SQM.h[:, :], [[8, 16]], ALU.is_ge, 0.0, 7, -1)
    S.op("pool", lambda e: e.tensor_tensor(out=TRIS.h[:, :], in0=TRI.h[:, :], in1=SSQ.h[:, :], op=ALU.mult),
         [ct(TRI), ct(SSQ)], [ct(TRIS)])
    S.op("pool", lambda e: e.tensor_tensor(out=LSTS.h[:, :], in0=LST.h[:, :], in1=SSQ.h[:, :], op=ALU.mult),
         [ct(LST), ct(SSQ)], [ct(LSTS)])
    for N_, T_ in ((NEGM, TRI), (NEGMS, TRIS)):
        S.op("pool", (lambda N_, T_: (lambda e: e.tensor_scalar(out=N_.h[:, :], in0=T_.h[:, :], scalar1=-1.0,
                                                                  scalar2=30000.0, op0=ALU.add, op1=ALU.mult)))(N_, T_),
             [ct(T_)], [ct(N_)])
    for t_ in range(16):
        MEMSET("pool", INVC.h[:, t_:t_ + 1], 1.0 / (t_ + 1), [ct(INVC)], acc=(t_ > 0))
    S.op("act", lambda e: e.activation(out=ABC.h[:, :], in_=ABC.h[:, :], func=AF.Exp), [ct(ABC)], [ct(ABC)], aset="B")
    S.op("act", lambda e: e.mul(ABC.h[:, :], ABC.h[:, :], -1.0), [ct(ABC)], [ct(ABC)])
    S.op("pool", lambda e: e.memset(SCR.h[:, 3:4], 0.0), list(ctl.values()), [CT_, SCR.t])

    class _C:
        pass
    c = _C()

    def setpar(p):
        c.HT = HTs[p]
        c.XST = XSTs[p]
        c.YAT = Buf(XSTs[p].h, XSTs[p].t)
        c.BT = BTs[p]
        c.CTT = CTTs[p]
        c.YBT = YBTs[p]
        c.DTR = DTRs[p]
        c.DT = DTs_[p]
        c.ADT = ADTs_[p]
        c.EAC = EACs_[p]
        c.CD = CDs_[p]
        c.DDT = DDTs_[p]
        c.AHI = AHIs_[p]
        c.ALO = ALOs_[p]

    def load_x(ci):
        xb = XB[ci % 2]
        DMA("sp", xb.h[:, :], x_d[ci], [], [xb], xb.t)

    def proj_feature_major(ci, sample):
        L = 8 if sample else 128
        nseq = 16 if sample else 1
        hc = 3
        hp = 15
        if sample:
            xbc4 = XBC.h[:, :, :].rearrange("p t (b j) -> p t b j", j=11)
            ux4 = UXS_h[:, :, :].rearrange("p t (b j) -> p t b j", j=23)
            uxt = UXST
        else:
            xbc4 = XBC.h[:, :, 0:131].rearrange("p t (b j) -> p t b j", b=1)
            ux4 = UX.h[:, :, :].rearrange("p t (b j) -> p t b j", b=1)
            uxt = UX.t
        nb = 0
        for bl in range(3):
            bank = PB[nb % 2]
            nb += 1
            for j in range(4):
                tile = bl * 4 + j
                for kc in range(8):
                    MM(bank.h[:, j * 128:(j + 1) * 128], WINX.h[:, kc, tile * 128:(tile + 1) * 128], c.HT.h[:, kc, :],
                       kc == 0, kc == 7, [WINX, c.HT], [bank], first=(j == 0 and kc == 0))
            for j in range(4):
                tile = bl * 4 + j
                eng = "act" if bl % 2 == 0 else "dve"
                CP(eng, xbc4[:, tile, :, hc:hc + L],
                   bank.h[:, j * 128:(j + 1) * 128].rearrange("p (b j) -> p b j", j=L),
                   [bank], [XBC], acc=True)
        for kc in range(8):
            MM(B3.h[:, 0:16], c.HT.h[:, kc, :], WINX.h[:, kc, 1536:1552], kc == 0, kc == 7, [WXB, c.HT], [B3dt],
               first=(kc == 0))
        VT("dve", c.DTR.h[:, :], B3.h[:, 0:16], DTB.h[:, :], ALU.add, [B3dt, CT_], [c.DTR])
        bank = PB[nb % 2]
        nb += 1
        for j in range(4):
            for kc in range(8):
                MM(bank.h[:, j * 128:(j + 1) * 128], WINX.h[:, kc, 1552 + j * 128:1552 + (j + 1) * 128],
                   c.HT.h[:, kc, :], kc == 0, kc == 7, [WXB, c.HT], [bank], first=(j == 0 and kc == 0))
        for j in range(4):
            CP("dve", ux4[:, j, :, hp:hp + L], bank.h[:, j * 128:(j + 1) * 128].rearrange("p (b j) -> p b j", j=L),
               [bank], [uxt], acc=True)
        bank = PB[nb % 2]
        nb += 1
        for j in range(4):
            for kc in range(8):
                MM(bank.h[:, j * 128:(j + 1) * 128], WINX.h[:, kc, 2064 + j * 128:2064 + (j + 1) * 128],
                   c.HT.h[:, kc, :], kc == 0, kc == 7, [WXB, c.HT], [bank], first=(j == 0 and kc == 0))
        ACT(GS.h[:, :, :], bank.h[:, :].rearrange("p (k t) -> p k t", k=4), AF.Silu, [bank], [GS])
        return xbc4, ux4, uxt

    def conv_and_silu(xbc4, sample):
        L = 8 if sample else 128
        for pair in range(6):
            tiles = (2 * pair, 2 * pair + 1)
            accs = [CACC[(2 * pair) % 4], CACC[(2 * pair + 1) % 4]]
            a3s = [a.h[:, :].rearrange("p (b j) -> p b j", j=L) for a in accs]
            for i_, tile in enumerate(tiles):
                VS("pool", a3s[i_], xbc4[:, tile, :, 0:L], CONVW.h[:, tile, 0:1], CONVB.h[:, tile:tile + 1], ALU.mult,
                   ALU.add, [XBC, CT_], [accs[i_]])
            for k in range(1, 4):
                for i_, tile in enumerate(tiles):
                    STT("dve", a3s[i_], xbc4[:, tile, :, k:k + L], CONVW.h[:, tile, k:k + 1], a3s[i_], ALU.mult,
                        ALU.add, [XBC, CT_, accs[i_]], [accs[i_]], acc=True)
            for i_, tile in enumerate(tiles):
                acc = accs[i_]
                if tile < 8:
                    ACT(c.XST.h[:, tile, :], acc.h[:, :], AF.Silu, [acc], [c.XST], acc=True)
                elif tile < 10:
                    ACT(c.BT.h[:, tile - 8, :], acc.h[:, :], AF.Silu, [acc], [c.BT], acc=True)
                else:
                    ACT(c.CTT.h[:, tile - 10, :], acc.h[:, :], AF.Silu, [acc], [c.CTT], acc=True)

    def pool_branch(ci, ux4, uxt, sample, mixbank=None):
        L = 8 if sample else 128
        nseq = 16 if sample else 1
        E = 15 + L
        P2, P4, P8 = (P2s, P4s, P8s) if sample else (P2p, P4p, P8p)
        p2 = P2.h[:, 0:nseq * (E - 1)].rearrange("p (b j) -> p b j", b=nseq)
        p4 = P4.h[:, 0:nseq * (E - 3)].rearrange("p (b j) -> p b j", b=nseq)
        p8 = P8.h[:, 0:nseq * (E - 7)].rearrange("p (b j) -> p b j", b=nseq)
        p16 = P16.h[:, :].rearrange("p (b j) -> p b j", b=nseq)
        for k, w in enumerate((2, 4, 8, 16)):
            u = ux4[:, k, :, :]
            eng = "dve" if k % 2 == 0 else "pool"
            VT(eng, p2, u[:, :, 1:E], u[:, :, 0:E - 1], ALU.add, [uxt], [P2])
            Sv = p2[:, :, 14:14 + L]
            rd = [P2]
            if w >= 4:
                VT(eng, p4, p2[:, :, 2:E - 1], p2[:, :, 0:E - 3], ALU.add, [P2], [P4])
                Sv = p4[:, :, 12:12 + L]
                rd = [P4]
            if w >= 8:
                VT(eng, p8, p4[:, :, 4:E - 3], p4[:, :, 0:E - 7], ALU.add, [P4], [P8])
                Sv = p8[:, :, 8:8 + L]
                rd = [P8]
            if w >= 16:
                VT(eng, p16, p8[:, :, 8:E - 7], p8[:, :, 0:E - 15], ALU.add, [P8], [P16])
                Sv = p16
                rd = [P16]
            pld3 = PLD.h[:, k, :].rearrange("p (b j) -> p b j", b=nseq)
            STT("dve", pld3, Sv, 1.0 / w, u[:, :, 15:15 + L], ALU.mult, ALU.subtract, rd + [uxt], [PLD], acc=True)
            if ci == 0 and not sample:
                VT(eng, SCR.h[:, 0:w - 1], Sv[:, 0, 0:w - 1], INVC.h[:, 0:w - 1], ALU.mult, rd + [CT_], [SCR])
                VT(eng, PLD.h[:, k, 0:w - 1], SCR.h[:, 0:w - 1], u[:, 0, 15:15 + w - 1], ALU.subtract,
                   [SCR, uxt], [PLD], acc=True)
        bank = mixbank if mixbank is not None else PB[0]
        for k in range(4):
            MM(bank.h[:, k * 128:(k + 1) * 128], PMIX.h[:, k, :], PLD.h[:, k, :], True, True, [PMIX, PLD], [bank],
               first=(k == 0))
        for k in range(4):
            STT("dve", c.YBT.h[:, k, :], bank.h[:, k * 128:(k + 1) * 128], PSC.h[:, k:k + 1], GS.h[:, k, :],
                ALU.mult, ALU.mult, [bank, GS, CT_], [c.YBT], acc=True)

    def dt_path(sample):
        tri = TRIS if sample else TRI
        ssq = SSQ if sample else ONE
        ACT(c.DTR.h[:, :], c.DTR.h[:, :], AF.Exp, [c.DTR], [c.DTR])
        ACT(c.DT.h[:, :], c.DTR.h[:, :], AF.Ln, [c.DTR], [c.DT], bias=1.0)
        VT("dve", c.ADT.h[:, :], c.DT.h[:, :], ABC.h[:, :], ALU.mult, [c.DT, CT_], [c.ADT])
        CP("dve", c.AHI.h[:, :], c.ADT.h[:, :], [c.ADT], [c.AHI])
        VT("dve", c.ALO.h[:, :], c.ADT.h[:, :], c.AHI.h[:, :], ALU.subtract, [c.ADT, c.AHI], [c.ALO])
        MM(B3.h[:, 16:32], tri.h[:, :], c.AHI.h[:, :], True, False, [CT_, c.AHI], [B3ac], first=True)
        MM(B3.h[:, 16:32], tri.h[:, :], c.ALO.h[:, :], False, True, [CT_, c.ALO], [B3ac], first=False)
        MM(B3.h[:, 32:48], ssq.h[:, :], c.AHI.h[:, :], True, False, [CT_, c.AHI], [B3ac], first=False)
        MM(B3.h[:, 32:48], ssq.h[:, :], c.ALO.h[:, :], False, True, [CT_, c.ALO], [B3ac], first=False)
        VS("dve", NACS.h[:, :], B3.h[:, 16:32], -1.0, None, ALU.mult, None, [B3ac], [NACS])
        ACT(c.EAC.h[:, :], B3.h[:, 16:32], AF.Exp, [B3ac], [c.EAC])
        ACT(c.CD.h[:, :], B3.h[:, 32:48], AF.Exp, [B3ac], [c.CD])
        VT("dve", DIF.h[:, :], B3.h[:, 32:48], NACS.h[:, :], ALU.add, [B3ac, NACS], [DIF])
        ACT(DTE.h[:, :], DIF.h[:, :], AF.Exp, [DIF], [DTE])
        VT("dve", c.DDT.h[:, :], c.DT.h[:, :], DTE.h[:, :], ALU.mult, [c.DT, DTE], [c.DDT])

    def to_token_major():
        for j in range(8):
            TR(PT2[:, j * 128:(j + 1) * 128], c.XST.h[:, j, :], IDB.h[:, :], [c.XST], [PT2t], first=(j == 0))
        CP("dve", BFA.h[:, :], PT2[:, :], [PT2t], [BFA])
        for g in range(2):
            TR(PT2[:, g * 128:(g + 1) * 128], c.BT.h[:, g, :], IDB.h[:, :], [c.BT], [PT2t], first=(g == 0))
        CP("act", BTOK.h[:, :], PT2[:, 0:256], [PT2t], [BTOK])

    def build_masks(sample, g):
        tri = TRIS if sample else TRI
        AMH_, AML_ = (AMH0, AML0) if g == 0 else (AMH, AML)
        VT("pool", AMH_.h[:, :, :], bc(tri.h[:, :], [128, 8, 128], 1), bc(c.AHI.h[:, 8 * g:8 * g + 8], [128, 8, 128], 2),
           ALU.mult, [CT_, c.AHI], [AMH_], acc=(g == 0))
        VT("pool", AML_.h[:, :, :], bc(tri.h[:, :], [128, 8, 128], 1), bc(c.ALO.h[:, 8 * g:8 * g + 8], [128, 8, 128], 2),
           ALU.mult, [CT_, c.ALO], [AML_], acc=(g == 0))

    def ssd_intra(ci, sample, g):
        tri = TRIS if sample else TRI
        lst = LSTS if sample else LST
        AMH_, AML_ = (AMH0, AML0) if g == 0 else (AMH, AML)
        for q in range(2):
            bank = PB[4 + q]
            MM(bank.h[:, :], lst.h[:, :], AMH_.h[:, 4 * q:4 * q + 4, :].rearrange("p h l -> p (h l)"), True, False,
               [CT_, AMH_], [bank], first=True)
            MM(bank.h[:, :], lst.h[:, :], AML_.h[:, 4 * q:4 * q + 4, :].rearrange("p h l -> p (h l)"), False, True,
               [CT_, AML_], [bank], first=False)
            ACT(DEC.h[:, 4 * q:4 * q + 4, :], bank.h[:, :].rearrange("p (h l) -> p h l", h=4), AF.Exp, [bank], [DEC],
                acc=(q == 1))
        VT("dve", MMT.h[:, :, :], DEC.h[:, :, :], bc(CBM.h[:, g, :], [128, 8, 128], 1), ALU.mult, [DEC, CBM], [MMT])
        bank = PB[6]
        for hh in range(8):
            h = 8 * g + hh
            MM(bank.h[:, hh * 64:(hh + 1) * 64], MMT.h[:, hh, :], XR.h[:, h * 64:(h + 1) * 64], True, True,
               [MMT, XR], [bank], first=(hh == 0))

    def ssd_prompt(ci):
        build_masks(False, 0)
        to_token_major()
        xs3 = BFA.h[:, :].rearrange("p (h d) -> p h d", d=64)
        VT("pool", XR.h[:, :].rearrange("p (h d) -> p h d", d=64), xs3, bc(c.DT.h[:, :], [128, 16, 64], 2), ALU.mult,
           [BFA, c.DT], [XR])
        build_masks(False, 1)
        VT("pool", XRD.h[:, :].rearrange("p (h d) -> p h d", d=64), xs3, bc(c.DDT.h[:, :], [128, 16, 64], 2), ALU.mult,
           [BFA, c.DDT], [XRD])
        for g in range(2):
            MM(B3.h[:, 64 + g * 128:64 + (g + 1) * 128], c.BT.h[:, g, :], c.CTT.h[:, g, :], True, True, [c.BT, c.CTT], [B3cb],
               first=(g == 0))
        VT("dve", CBM.h[:, :, :], B3.h[:, 64:320].rearrange("p (g l) -> p g l", g=2),
           bc(TRI.h[:, :], [128, 2, 128], 1), ALU.mult, [B3cb, CT_], [CBM])
        for g in range(2):
            ssd_intra(ci, False, g)
            ysl = Y.h[:, g * 512:(g + 1) * 512]
            y3 = ysl.rearrange("p (h d) -> p h d", d=64)
            if ci > 0:
                MM(PB[7].h[:, :], c.CTT.h[:, g, :], STBF.h[:, g * 512:(g + 1) * 512], True, True, [c.CTT, STBF], [PB[7]],
                   first=True)
                VT("dve", y3, PB[7].h[:, :].rearrange("p (h d) -> p h d", d=64),
                   bc(c.EAC.h[:, 8 * g:8 * g + 8], [128, 8, 64], 2), ALU.mult, [PB[7], c.EAC], [Y], acc=(g == 1))
                VT("dve", ysl, ysl, PB[6].h[:, :], ALU.add, [Y, PB[6]], [Y], acc=True)
            else:
                CP("dve", ysl, PB[6].h[:, :], [PB[6]], [Y], acc=(g == 1))
            VT("pool", YT.h[:, :].rearrange("p (h d) -> p h d", d=64),
               BFA.h[:, g * 512:(g + 1) * 512].rearrange("p (h d) -> p h d", d=64),
               bc(DSK.h[:, 8 * g:8 * g + 8], [128, 8, 64], 2), ALU.mult, [BFA, CT_], [YT])
            VT("dve", ysl, ysl, YT.h[:, :], ALU.add, [Y, YT], [Y], acc=True)
            MM(PB[7].h[:, :], BTOK.h[:, g * 128:(g + 1) * 128], XRD.h[:, g * 512:(g + 1) * 512], True, True,
               [BTOK, XRD], [PB[7]], first=True)
            ssl = STATE.h[:, g * 512:(g + 1) * 512]
            if ci > 0:
                s3 = ssl.rearrange("p (h d) -> p h d", d=64)
                VT("dve", s3, s3, bc(c.CD.h[:, 8 * g:8 * g + 8], [128, 8, 64], 2), ALU.mult, [STATE, c.CD], [STATE],
                   acc=True)
                VT("dve", ssl, ssl, PB[7].h[:, :], ALU.add, [STATE, PB[7]], [STATE], acc=True)
            else:
                CP("dve", ssl, PB[7].h[:, :], [PB[7]], [STATE], acc=(g == 1))
            if ci < 15:
                CP("pool", STBF.h[:, g * 512:(g + 1) * 512], ssl, [STATE], [STBF], acc=(g == 1))

    def gates(cb):
        cs = slice(cb * 512, (cb + 1) * 512)
        for kc in range(8):
            MM(PB[6].h[:, :], c.HT.h[:, kc, :], WING.h[:, kc, cs], kc == 0, kc == 7, [c.HT, WING], [PB[6]],
               first=(kc == 0))
        for kc in range(8):
            MM(PB[7].h[:, :], c.HT.h[:, kc, :], WING.h[:, kc, 1024 + cb * 512:1024 + (cb + 1) * 512], kc == 0, kc == 7,
               [c.HT, WING], [PB[7]], first=(kc == 0))
        ACT(TA.h[:, :], PB[6].h[:, :], AF.Tanh, [PB[6]], [TA], scale=0.5)
        ACT(TB.h[:, :], PB[7].h[:, :], AF.Tanh, [PB[7]], [TB], scale=0.5)

    def pb_proj(cb):
        cs = slice(cb * 512, (cb + 1) * 512)
        for kc in range(4):
            MM(PB[5].h[:, :], c.YBT.h[:, kc, :], WB.h[:, kc, cs], kc == 0, kc == 3, [c.YBT, WB], [PB[5]], first=(kc == 0))

    def gate_norm_transpose():
        for g in range(2):
            bank = PB[4 + g]
            for kc in range(8):
                MM(bank.h[:, :], c.HT.h[:, kc, :], WINZ.h[:, kc, g * 512:(g + 1) * 512], kc == 0, kc == 7, [c.HT, WINZ],
                   [bank], first=(kc == 0))
            ACT(ZS.h[:, :], bank.h[:, :], AF.Silu, [bank], [ZS])
            ysl = Y.h[:, g * 512:(g + 1) * 512]
            VT("dve", ysl, ysl, ZS.h[:, :], ALU.mult, [Y, ZS], [Y], acc=True)
            ACT(ZS.h[:, :], ysl, AF.Square, [Y], [ZS, SSG], accum=SSG.h[:, g:g + 1], acc=(g == 1))
        gates(0)
        pb_proj(0)
        ACT(RG.h[:, :], SSG.h[:, :], AF.Ln, [SSG], [RG], bias=EPS, scale=1.0 / 512)
        ACT(RG.h[:, :], RG.h[:, :], AF.Exp, [RG], [RG], scale=-0.5)
        VT("dve", BFA.h[:, :].rearrange("p (g d) -> p g d", g=2), Y.h[:, :].rearrange("p (g d) -> p g d", g=2),
           bc(RG.h[:, :], [128, 2, 512], 2), ALU.mult, [Y, RG], [BFA])
        for j in range(8):
            TR(PT2[:, j * 128:(j + 1) * 128], BFA.h[:, j * 128:(j + 1) * 128], IDB.h[:, :], [BFA], [PT2t], first=(j == 0))
        VT("dve", c.YAT.h[:, :, :], PT2[:, :].rearrange("p (k t) -> p k t", k=8), bc(SSMN.h[:, :], [128, 8, 128], 2),
           ALU.mult, [PT2t, CT_], [c.YAT])

    def tail(ci):
        xb = XB[ci % 2]
        for cb in range(2):
            cs = slice(cb * 512, (cb + 1) * 512)
            for kc in range(8):
                MM(PB[4].h[:, :], c.YAT.h[:, kc, :], WA.h[:, kc, cs], kc == 0, kc == 7, [c.YAT, WA], [PB[4]], first=(kc == 0))
            STT("dve", QA.h[:, :], TA.h[:, :], 1.0, PB[4].h[:, :], ALU.add, ALU.mult, [TA, PB[4]], [QA])
            STT("dve", QB.h[:, :], TB.h[:, :], 1.0, PB[5].h[:, :], ALU.add, ALU.mult, [TB, PB[5]], [QB])
            if cb == 0:
                gates(1)
                pb_proj(1)
            VT("dve", MG.h[:, cs], QA.h[:, :], QB.h[:, :], ALU.add, [QA, QB], [MG], acc=(cb == 1))
        for j in range(8):
            TR(PT2[:, j * 128:(j + 1) * 128], MG.h[:, j * 128:(j + 1) * 128], IDB.h[:, :], [MG], [PT2t], first=(j == 0))
        S.op("act", lambda e: e.mul(MGT.h[:, :, :], PT2[:, :].rearrange("p (k t) -> p k t", k=8), 0.5),
             [PT2t], [MGT.t])
        for ob in range(2):
            bank = PB[4 + ob]
            for kc in range(8):
                MM(bank.h[:, :], MGT.h[:, kc, :], WO.h[:, kc, ob * 512:(ob + 1) * 512], kc == 0, kc == 7, [MGT, WO],
                   [bank], first=(kc == 0))
        ACT(TAJ.h[:, :], pb45[:, :], AF.Square, [PB[4], PB[5]], [TAJ, SSP], accum=SSP.h[:, 0:1])
        ACT(RP.h[:, :], SSP.h[:, 0:1], AF.Ln, [SSP], [RP], bias=EPS, scale=1.0 / 1024)
        ACT(RP.h[:, :], RP.h[:, :], AF.Exp, [RP], [RP], scale=-0.5)
        for ob in range(2):
            cs = slice(ob * 512, (ob + 1) * 512)
            STT("dve", OT.h[:, :], PB[4 + ob].h[:, :], RP.h[:, 0:1], NPOST.h[:, cs], ALU.mult, ALU.mult,
                [PB[4 + ob], RP, CT_], [OT])
            VT("dve", xb.h[:, cs], xb.h[:, cs], OT.h[:, :], ALU.add, [xb, OT], [xb], acc=True)
        DMA("sp", y_d[ci], xb.h[:, :], [xb], [OUTT], xb.t)

    def norm_pre(ci):
        xb = XB[ci % 2]
        ACT(XN.h[:, :], xb.h[:, :], AF.Square, [xb], [XN, SSPRE], accum=SSPRE.h[:, :])
        ACT(RPRE.h[:, :], SSPRE.h[:, :], AF.Ln, [SSPRE], [RPRE], bias=EPS, scale=1.0 / 1024)
        ACT(RPRE.h[:, :], RPRE.h[:, :], AF.Exp, [RPRE], [RPRE], scale=-0.5)
        VS("dve", XN.h[:, :], xb.h[:, :], RPRE.h[:, 0:1], None, ALU.mult, None, [xb, RPRE], [XN])
        for j in range(8):
            TR(PT.h[:, j * 128:(j + 1) * 128], XN.h[:, j * 128:(j + 1) * 128], IDB.h[:, :], [XN], [PT], first=(j == 0))
        VT("dve", c.HT.h[:, :, :], PT.h[:, :].rearrange("p (k t) -> p k t", k=8), bc(NPRE.h[:, :], [128, 8, 128], 2),
           ALU.mult, [PT, CT_], [c.HT])

    def out_fp32_T(src_fn, ncols, tiles, stage_ap_fn, stage_tiles, dram_ap, reads, key, banks=(0, 1)):
        done = 0
        nb = 0
        nt = len(tiles)
        while done < nt:
            n = min(4, nt - done)
            bank = PB[banks[nb % 2]]
            nb += 1
            for j in range(n):
                TR(bank.h[0:ncols, j * 128:(j + 1) * 128], src_fn(tiles[done + j]), IDF.h[:, :], reads, [bank],
                   first=(j == 0))
            CP("dve", stage_ap_fn(done * 128, (done + n) * 128), bank.h[0:ncols, 0:n * 128], [bank], stage_tiles,
               acc=(done > 0))
            done += n
        DMA("sp", dram_ap, stage_ap_fn(0, nt * 128), stage_tiles, [OUTT], key)

    def stage1(ci):
        setpar(ci % 2)
        norm_pre(ci)
        xbc4, ux4, uxt = proj_feature_major(ci, False)
        dt_path(False)
        conv_and_silu(xbc4, False)
        pool_branch(ci, ux4, uxt, False)
        if ci < 15:
            CP("pool", XBC.h[:, :, 0:3], XBC.h[:, :, 128:131], [XBC], [XBC], acc=True)
            CP("pool", UX.h[:, :, 0:15], UX.h[:, :, 128:143], [UX], [UX], acc=True)

    def stage23(ci):
        setpar(ci % 2)
        ssd_prompt(ci)
        gate_norm_transpose()
        tail(ci)

    MEMSET("dve", XBC.h[:, :, 0:3], 0.0, [XBC])
    MEMSET("dve", UX.h[:, :, 0:15], 0.0, [UX])
    load_x(0)
    stage1(0)
    for ci in range(nprompt):
        sa = None
        if ci + 1 < nprompt or do_sample:
            load_x(ci + 1)
        if ci + 1 < nprompt:
            S.begin()
            stage1(ci + 1)
            sa = S.end()
        S.begin()
        stage23(ci)
        sb = S.end()
        S.run_merged([sa, sb])

    XRF = alloc("xrf", [128, 512], F32, at=b_off["xr"], tile=XR.t)
    XRDF = alloc("xrdf", [128, 512], F32, at=b_off["xrd"], tile=XRD.t)

    def prompt_state_outputs():
        out_fp32_T(lambda t: XBC.h[:, t, 128:131], 3, list(range(8)), lambda a, b: Y.h[0:3, a:b], [Y.t],
                   convp_d[:, 0:1024], [XBC], Y.t, banks=(6, 7))
        out_fp32_T(lambda t: XBC.h[:, t, 128:131], 3, list(range(8, 12)), lambda a, b: ZS.h[0:3, a:b], [ZS.t],
                   convp_d[:, 1024:1536], [XBC], ZS.t, banks=(6, 7))
        out_fp32_T(lambda t: UX.h[:, t, 128:143], 15, list(range(4)), lambda a, b: XRF.h[0:15, a:b], [XR.t],
                   poolp_d[:, :], [UX], XR.t, banks=(6, 7))
        ssmp_v = ssmp_d.rearrange("(j p) n -> p j n", p=128)
        for half in range(2):
            bank = PB[6 + half]
            for j in range(4):
                jj = half * 4 + j
                TR(bank.h[:, j * 128:(j + 1) * 128], STATE.h[:, jj * 128:(jj + 1) * 128], IDF.h[:, :], [STATE], [bank],
                   first=(j == 0))
            CP("dve", XRDF.h[:, :], bank.h[:, :], [bank], [XRD])
            DMA("sp", ssmp_v[:, half * 4:(half + 1) * 4, :], XRDF.h[:, :].rearrange("p (j n) -> p j n", j=4), [XRD],
                [OUTT], XRD.t)

    sctx = {}

    def sample_front():
        ci = 16
        setpar(0)
        norm_pre(ci)
        SCVT = [R1T[0], R1T[1], R1T[2]]
        SPLT = [R1T[3], R1T[4]]
        DMA("act", SCV.h[0:48, :], sconv_d[:, :], [], SCVT, R1T[0])
        DMA("act", SPL.h[0:120, :, :], spool_d.rearrange("(h r) c -> r h c", h=2), [], SPLT, R1T[3])
        xbc4s = XBC.h[:, :, :].rearrange("p t (b j) -> p t b j", j=11)
        ux4s = UXS_h[:, :, :].rearrange("p t (b j) -> p t b j", j=23)
        for tile in range(12):
            bank = PB[4 + (tile % 2)]
            TR(bank.h[:, 0:48], SCV.h[0:48, tile * 128:(tile + 1) * 128], IDF.h[0:48, 0:48], SCVT, [bank], first=True)
            CP("dve", xbc4s[:, tile, :, 0:3], bank.h[:, 0:48].rearrange("p (b k) -> p b k", k=3), [bank], [XBC], acc=True)
        first_ux = True
        for k in range(4):
            for half in range(2):
                bank = PB[4 + half]
                TR(bank.h[:, 0:120], SPL.h[0:120, half, k * 128:(k + 1) * 128], IDF.h[0:120, 0:120], SPLT, [bank],
                   first=True)
                CP("dve", ux4s[:, k, 8 * half:8 * half + 8, 0:15], bank.h[:, 0:120].rearrange("p (b k) -> p b k", k=15),
                   [bank, STATE, STBF], [UXST, STATE, STBF], acc=not first_ux)
                first_ux = False
        xbc4, ux4, uxt = proj_feature_major(ci, True)
        conv_and_silu(xbc4, True)
        alias_tiles = [b.t for b in H0 + H0B + H0T + CTM + BM] + [CDC.t, CSTG.t, PSTG.t, OSTG.t, ADTX.t, P2s.t, P4s.t, P8s.t]
        S.op("pool", lambda e: e.memset(SCR.h[:, 0:1], 0.0), [], [WINX.t, WXB] + alias_tiles + [SCR.t])
        for tile in range(12):
            CP("pool", CSTG.h[:, tile, :].rearrange("p (b k) -> p b k", k=3), xbc4s[:, tile, :, 8:11], [XBC], [CSTG],
               acc=(tile > 0))
        out_fp32_T(lambda t: CSTG.h[:, t, :], 48, list(range(12)), lambda a, b: OSTG.h[0:48, a:b], [OSTG.t],
                   convs_d[:, :], [CSTG], OSTG.t)
        sctx.update(ux4=ux4, uxt=uxt, ux4s=ux4s)

    def sample_rest():
        ci = 16
        setpar(0)
        ux4, uxt, ux4s = sctx["ux4"], sctx["uxt"], sctx["ux4s"]
        ssv = sso = None
        dt_path(True)
        build_masks(True, 0)
        build_masks(True, 1)
        to_token_major()
        xs3 = BFA.h[:, :].rearrange("p (h d) -> p h d", d=64)
        VT("pool", XR.h[:, :].rearrange("p (h d) -> p h d", d=64), xs3, bc(c.DT.h[:, :], [128, 16, 64], 2), ALU.mult,
           [BFA, c.DT], [XR])
        VT("pool", XRD.h[:, :].rearrange("p (h d) -> p h d", d=64), xs3, bc(c.DDT.h[:, :], [128, 16, 64], 2), ALU.mult,
           [BFA, c.DDT], [XRD])
        for g in range(2):
            MM(B3.h[:, 64 + g * 128:64 + (g + 1) * 128], c.BT.h[:, g, :], c.CTT.h[:, g, :], True, True, [c.BT, c.CTT], [B3cb],
               first=(g == 0))
        VT("dve", CBM.h[:, :, :], B3.h[:, 64:320].rearrange("p (g l) -> p g l", g=2), bc(TRIS.h[:, :], [128, 2, 128], 1),
           ALU.mult, [B3cb, CT_], [CBM])
        for g in range(2):
            ssd_intra(ci, True, g)
            ysl = Y.h[:, g * 512:(g + 1) * 512]
            CP("dve", ysl, PB[6].h[:, :], [PB[6]], [Y], acc=(g == 1))
            VT("pool", YT.h[:, :].rearrange("p (h d) -> p h d", d=64),
               BFA.h[:, g * 512:(g + 1) * 512].rearrange("p (h d) -> p h d", d=64),
               bc(DSK.h[:, 8 * g:8 * g + 8], [128, 8, 64], 2), ALU.mult, [BFA, CT_], [YT])
            VT("dve", ysl, ysl, YT.h[:, :], ALU.add, [Y, YT], [Y], acc=True)
        VT("dve", ADTX.h[:, :].rearrange("p (h d) -> p h d", d=64), bc(c.ADT.h[:, :], [128, 16, 64], 2),
           bc(ONE.h[:, 0:16], [128, 16, 64], 2), ALU.mult, [c.ADT, CT_], [ADTX])
        for j in range(8):
            MM(PB[7].h[:, j * 16:(j + 1) * 16], ADTX.h[:, j * 128:(j + 1) * 128], SQM.h[:, :], True, True,
               [ADTX, CT_], [PB[7]], first=(j == 0))
        ACT(CDC.h[:, :, :], PB[7].h[:, 0:128].rearrange("p (j b) -> p j b", j=8), AF.Exp, [PB[7]], [CDC])
        for i in range(2):
            MEMSET("pool", CTM[i].h[:, :, :], 0.0, [CTM[i]])
        S.begin()
        ssv = sssm_d.rearrange("b (j p) n -> b p j n", p=128)
        sso = ssms_d.rearrange("b (j p) n -> b p j n", p=128)
        NB = 3
        S.op("pool", lambda e: e.memset(SCR.h[:, 1:2], 0.0), [], [R1T[0], R1T[1], R1T[2], H0[2].t, H0B[2].t, SCR.t])

        def h0_load(b):
            rb = b % NB
            DMA("sp", H0[rb].h[:, :, :], ssv[b], [], [H0[rb]], H0[rb].t)
            DMA("pool", H0B[rb].h[:, :, :], ssv[b], [], [H0B[rb]], H0B[rb].t)

        def stage_a(b):
            rb = b % NB
            r2_ = b % 2
            for j in range(8):
                TR(PT.h[:, j * 128:(j + 1) * 128], H0B[rb].h[:, j, :], IDB.h[:, :], [H0B[rb]], [PT], first=(j == 0))
            CP("act", H0T[r2_].h[:, :], PT.h[:, :], [PT], [H0T[r2_]])
            CP("act", CTM[r2_].h[:, :, 8 * b:8 * b + 8], c.CTT.h[:, :, 8 * b:8 * b + 8], [c.CTT], [CTM[r2_]], acc=True)
            ACT(BM[r2_].h[:, :], BTOK.h[:, :], AF.Copy, [BTOK, CT_], [BM[r2_]], scale=SQM.h[:, b:b + 1])

        def stage_b(b):
            rb = b % NB
            r2_ = b % 2
            for g in range(2):
                MM(PB[g].h[:, :], CTM[r2_].h[:, g, :], H0T[r2_].h[:, g * 512:(g + 1) * 512], b == 0, b == 15,
                   [CTM[r2_], H0T[r2_]], [PB[g]], first=(b == 0))
            for j in range(8):
                bank = PB[4 + j // 4]
                MM(bank.h[:, (j % 4) * 128:(j % 4 + 1) * 128], XRD.h[:, j * 128:(j + 1) * 128],
                   BM[r2_].h[:, (j // 4) * 128:(j // 4 + 1) * 128], True, True, [XRD, BM[r2_]], [bank], first=(j % 4 == 0))
            ACT(CTM[r2_].h[:, :, 8 * b:8 * b + 8], CTM[r2_].h[:, :, 8 * b:8 * b + 8], AF.Copy, [CTM[r2_]], [CTM[r2_]],
                scale=0.0, acc=True)
            for j in range(8):
                bank = PB[4 + j // 4]
                STT("dve", H0[rb].h[:, j, :], H0[rb].h[:, j, :], CDC.h[:, j, b:b + 1],
                    bank.h[:, (j % 4) * 128:(j % 4 + 1) * 128], ALU.mult, ALU.add, [H0[rb], CDC, bank], [H0[rb]], acc=True)
            DMA("act", sso[b], H0[rb].h[:, :, :], [H0[rb]], [OUTT], H0[rb].t)

        h0_load(0)
        h0_load(1)
        stage_a(0)
        for b in range(16):
            if b + 1 < 16:
                stage_a(b + 1)
            stage_b(b)
            if b + 2 < 16:
                h0_load(b + 2)
        S.op("pool", lambda e: e.memset(SCR.h[:, 2:3], 0.0), [], [R1T[0], R1T[1], R1T[2], H0[2].t, H0B[2].t, SCR.t])
        st_loop = S.end()
        S.begin()
        pool_branch(ci, ux4, uxt, True, mixbank=B3)
        for k in range(4):
            CP("pool", PSTG.h[:, k, :].rearrange("p (b r) -> p b r", r=15), ux4s[:, k, :, 8:23], [UXST], [PSTG],
               acc=(k > 0))
        for half in range(2):
            bank = PB[6 + half]
            for k in range(4):
                TR(bank.h[0:120, k * 128:(k + 1) * 128], PSTG.h[:, k, half * 120:(half + 1) * 120], IDF.h[:, :], [PSTG],
                   [bank], first=(k == 0))
            CP("dve", OSTG.h[0:120, half * 512:(half + 1) * 512], bank.h[0:120, :], [bank], [OSTG], acc=True)
        DMA("sp", pools_d.rearrange("(h r) c -> r h c", h=2), OSTG.h[0:120, 0:1024].rearrange("r (h c) -> r h c", h=2),
            [OSTG], [OUTT], OSTG.t)
        st_pool = S.end()
        S.run_merged([st_loop, st_pool])
        for g in range(2):
            ysl = Y.h[:, g * 512:(g + 1) * 512]
            VT("dve", YT.h[:, :].rearrange("p (h d) -> p h d", d=64), PB[g].h[:, :].rearrange("p (h d) -> p h d", d=64),
               bc(c.EAC.h[:, 8 * g:8 * g + 8], [128, 8, 64], 2), ALU.mult, [PB[g], c.EAC], [YT])
            VT("dve", ysl, ysl, YT.h[:, :], ALU.add, [Y, YT], [Y], acc=True)
        gate_norm_transpose()
        tail(ci)

    sa = sb = None
    if do_pstate:
        S.begin()
        prompt_state_outputs()
        sa = S.end()
    if do_sample:
        S.begin()
        sample_front()
        sb = S.end()
    S.run_merged([sa, sb])
    if do_sample:
        sample_rest()
    S.op("sp", None, [OUTT], [])

    S.finalize(nc, es)
    block = es.enter_context(nc.Block())

    @block.tensor
    def _(e):
        S.emit_engine(e, "pe")

    @block.scalar
    def _(e):
        S.emit_engine(e, "act")

    @block.vector
    def _(e):
        S.emit_engine(e, "dve")

    @block.gpsimd
    def _(e):
        S.emit_engine(e, "pool")

    @block.sync
    def _(e):
        S.emit_engine(e, "sp")

    es.close()
    return nc


_NC_CACHE = {}


def _prep_inputs(inp):
    f = lambda a: np.ascontiguousarray(np.asarray(a, dtype=np.float32))
    xp = f(inp["x_prompt"])
    xs = f(inp["x_sample"])
    sc = f(inp["state_conv"])[0]
    ss = f(inp["state_ssm"])[0]
    sp = f(inp["state_pool"])[0]
    shared = {
        "w_in": f(inp["w_in"])[0],
        "w_a": f(inp["w_branch_a"])[0],
        "w_b": f(inp["w_branch_b"])[0],
        "w_out": f(inp["w_out"])[0],
        "pool_mix": f(inp["pool_mix"])[0],
        "npre_col": f(f(inp["norm_pre"])[0].reshape(8, 128).T),
        "convw_col": f(f(inp["conv_w"])[0].reshape(4, 12, 128).transpose(2, 1, 0)),
        "convb_col": f(f(inp["conv_b"])[0].reshape(12, 128).T),
        "ssmn_col": f(f(inp["ssm_norm"])[0].reshape(8, 128).T),
        "pscale_col": f(f(inp["pool_scale"])[0].reshape(4, 128).T),
        "dtb_bc": f(np.broadcast_to(f(inp["dt_bias"])[0][None, :], (128, 16))),
        "alog_bc": f(np.broadcast_to(f(inp["a_log"])[0][None, :], (128, 16))),
        "dskip_bc": f(np.broadcast_to(f(inp["d_skip"])[0][None, :], (128, 16))),
        "npost_bc": f(np.broadcast_to(f(inp["norm_post"])[0][None, :], (128, 1024))),
    }
    maps = []
    for c in range(8):
        x = np.concatenate([xp[c].reshape(16, 128, 1024), xs[16 * c:16 * c + 16].reshape(1, 128, 1024)], axis=0)
        m = dict(shared)
        m["x"] = f(x)
        m["sconv"] = f(sc[16 * c:16 * c + 16].reshape(48, 1536))
        m["sssm"] = f(ss[16 * c:16 * c + 16].reshape(16, 1024, 128))
        m["spool"] = f(sp[16 * c:16 * c + 16].reshape(240, 512))
        maps.append(m)
    return maps


def kernel(**inputs):
    if "nc" not in _NC_CACHE:
        _NC_CACHE["nc"] = build_program()
    nc = _NC_CACHE["nc"]
    maps = _prep_inputs(inputs)
    res = run_bass_kernel_spmd(nc, maps, core_ids=list(range(8)))
    R = res.results
    yp = np.stack([R[c]["y"][:16].reshape(2048, 1024) for c in range(8)], axis=0)
    ys = np.concatenate([R[c]["y"][16].reshape(16, 8, 1024) for c in range(8)], axis=0)
    convp = np.stack([R[c]["convp"] for c in range(8)], axis=0)[None]
    ssmp = np.stack([R[c]["ssmp"].reshape(16, 64, 128) for c in range(8)], axis=0)[None]
    poolp = np.stack([R[c]["poolp"] for c in range(8)], axis=0)[None]
    convs = np.concatenate([R[c]["convs"].reshape(16, 3, 1536) for c in range(8)], axis=0)[None]
    ssms = np.concatenate([R[c]["ssms"].reshape(16, 16, 64, 128) for c in range(8)], axis=0)[None]
    pools = np.concatenate([R[c]["pools"].reshape(16, 15, 512) for c in range(8)], axis=0)[None]
    out = (yp, ys, convp, ssmp, poolp, convs, ssms, pools)
    return tuple(np.ascontiguousarray(o, dtype=np.float32) for o in out)
```
